# Optimizing a Trainium2 kernel written in Bass

```python
import math, functools
import jax, jax.numpy as jnp
from jax import lax
import numpy as np

D_MODEL = 1024
BATCH = 16
SEQ = 4096
DEPTH = 4

GRID_W = 64
CTX_LEN = 256
NORM_EPS = 1e-6

ML_WIDTH = D_MODEL
ML_HEADS = 4
ML_HD = ML_WIDTH // ML_HEADS
ML_CHUNK = 64
ML_M_INIT = -1e30

LRU_WIDTH = D_MODEL
LRU_BLOCKS = 16
LRU_BD = LRU_WIDTH // LRU_BLOCKS
LRU_CONV = 4
LRU_C = 8.0

RW_WIDTH = D_MODEL
RW_HD = 64
RW_HEADS = RW_WIDTH // RW_HD
RW_DECAY_RANK = 64
RW_ICLR_RANK = 64
RW_GATE_RANK = 128
RW_DECAY_SCALE = math.exp(-0.5)
RW_LN_EPS = 64e-5

FFN_HIDDEN = -(-8 * D_MODEL // (3 * 256)) * 256

N_BRANCHES = 3
RW_SPLITS = (RW_WIDTH, RW_WIDTH, RW_WIDTH, 2 * RW_DECAY_RANK, 2 * RW_ICLR_RANK, RW_GATE_RANK)
RW_SEG = 3 * RW_WIDTH + 2 * RW_DECAY_RANK + 2 * RW_ICLR_RANK + RW_GATE_RANK
IN_NAMES = ('ml_q', 'ml_k', 'ml_v', 'ml_o', 'ml_i', 'ml_f', 'lru_x', 'lru_gate', 'rw', 'merge')
IN_SPLITS = (ML_WIDTH, ML_WIDTH, ML_WIDTH, ML_WIDTH, 2 * ML_HEADS, 2 * ML_HEADS, LRU_WIDTH, LRU_WIDTH, RW_SEG, N_BRANCHES * D_MODEL)
IN_WIDTH = sum(IN_SPLITS)

kernel_name = 'hybrid_mlstm_rglru_rwkv7_dit_block'


def split_cols(a, sizes):
    return jnp.split(a, np.cumsum(sizes)[:-1].tolist(), axis=-1)


def rms_norm(x, w):
    xf = x.astype(jnp.float32)
    y = xf * lax.rsqrt(jnp.mean(xf * xf, axis=-1, keepdims=True) + NORM_EPS)
    return (y * w).astype(x.dtype)


def adaln(cond, w, b):
    mod = jax.nn.silu(cond) @ w + b
    return [m[:, None, :] for m in jnp.split(mod, 6, axis=-1)]


def modulate(h, shift, scale):
    return h * (1 + scale) + shift


def grid_transpose(h, rows, cols):
    b, t, d = h.shape
    return h.reshape(b, rows, cols, d).transpose(0, 2, 1, 3).reshape(b, t, d)


def flip_seq(xs):
    return tuple(jnp.flip(a, axis=1) for a in xs)


def bidirectional(scan_f, scan_b, ins_c_f, ins_l_f, ins_c_b, ins_l_b, state0):
    y_c_f, s_f = scan_f(ins_c_f, state0)
    y_l_f, _ = scan_f(ins_l_f, s_f)
    y_c_b, s_b = scan_b(flip_seq(ins_c_b), state0)
    y_l_b, _ = scan_b(flip_seq(ins_l_b), s_b)
    return y_c_f + jnp.flip(y_c_b, axis=1), y_l_f + jnp.flip(y_l_b, axis=1)


def token_shift(x, mu):
    xp = jnp.pad(x, ((0, 0), (1, 1), (0, 0)))
    neighbours = 0.5 * (xp[:, :-2] + xp[:, 2:])
    return x + mu * (neighbours - x)


def dwconv(x, w, b):
    k, ch = w.shape
    left = k // 2
    y = lax.conv_general_dilated(x, w[:, None, :].astype(x.dtype), window_strides=(1,), padding=[(left, k - 1 - left)], dimension_numbers=('NWC', 'WIO', 'NWC'), feature_group_count=ch)
    return y + b


def mlstm_scan(inputs, state0):
    q, k, v, ig, fg = inputs
    b, t, h, dh = q.shape
    nc = t // ML_CHUNK

    def chunks(a):
        a = jnp.swapaxes(a, 1, 2)
        a = a.reshape(b, h, nc, ML_CHUNK, *a.shape[3:])
        return jnp.moveaxis(a, 2, 0)

    mask = jnp.tril(jnp.ones((ML_CHUNK, ML_CHUNK), dtype=bool))

    def step(carry, xs):
        c_st, n_st, m_st = carry
        qc, kc, vc, ic, lfc = xs
        f_cum = jnp.cumsum(lfc, axis=-1)
        logw = f_cum[..., :, None] - f_cum[..., None, :] + ic[..., None, :]
        logw = jnp.where(mask, logw, -jnp.inf)
        inter = f_cum + m_st[..., None]
        m_t = jnp.maximum(jnp.max(logw, axis=-1), inter)
        w_intra = jnp.exp(logw - m_t[..., None])
        w_inter = jnp.exp(inter - m_t)
        s = jnp.einsum('bhtd,bhsd->bhts', qc, kc) * w_intra
        num = jnp.einsum('bhts,bhsd->bhtd', s, vc) + w_inter[..., None] * jnp.einsum('bhvk,bhtk->bhtv', c_st, qc)
        den = jnp.sum(s, axis=-1) + w_inter * jnp.einsum('bhk,bhtk->bht', n_st, qc)
        h_out = num / jnp.maximum(jnp.abs(den), jnp.exp(-m_t))[..., None]
        f_end = f_cum[..., -1]
        log_end = f_end[..., None] - f_cum + ic
        m_new = jnp.maximum(f_end + m_st, jnp.max(log_end, axis=-1))
        w_end = jnp.exp(log_end - m_new[..., None])
        decay = jnp.exp(f_end + m_st - m_new)
        c_new = decay[..., None, None] * c_st + jnp.einsum('bhs,bhsv,bhsk->bhvk', w_end, vc, kc)
        n_new = decay[..., None] * n_st + jnp.einsum('bhs,bhsk->bhk', w_end, kc)
        return (c_new, n_new, m_new), h_out

    xs = (chunks(q), chunks(k), chunks(v), chunks(ig), chunks(jax.nn.log_sigmoid(fg)))
    state, hs = lax.scan(step, state0, xs)
    hs = jnp.moveaxis(hs, 0, 2).reshape(b, h, t, dh)
    return jnp.swapaxes(hs, 1, 2), state


def mlstm_branch(pc, pl, p):
    def prep(pieces):
        b, t = pieces['ml_q'].shape[:2]
        heads = lambda a: a.astype(jnp.float32).reshape(b, t, ML_HEADS, ML_HD)
        q = heads(pieces['ml_q'])
        k = heads(pieces['ml_k']) * ML_HD ** -0.5
        v = heads(pieces['ml_v'])
        ig = pieces['ml_i'].astype(jnp.float32).reshape(b, t, 2, ML_HEADS) + p['ml_ig_b']
        fg = pieces['ml_f'].astype(jnp.float32).reshape(b, t, 2, ML_HEADS) + p['ml_fg_b']
        return (q, k, v, ig[:, :, 0], fg[:, :, 0]), (q, k, v, ig[:, :, 1], fg[:, :, 1])

    c_f, c_b = prep(pc)
    l_f, l_b = prep(pl)
    b = pc['ml_q'].shape[0]
    state0 = (jnp.zeros((b, ML_HEADS, ML_HD, ML_HD), jnp.float32), jnp.zeros((b, ML_HEADS, ML_HD), jnp.float32), jnp.full((b, ML_HEADS), ML_M_INIT, jnp.float32))
    h_c, h_l = bidirectional(mlstm_scan, mlstm_scan, c_f, l_f, c_b, l_b, state0)
    norm_w = p['ml_norm_w'].reshape(ML_HEADS, ML_HD)

    def post(hh, pieces):
        b_, t_ = hh.shape[:2]
        hn = hh * lax.rsqrt(jnp.mean(hh * hh, axis=-1, keepdims=True) + NORM_EPS) * norm_w
        return hn.reshape(b_, t_, ML_WIDTH) * jax.nn.sigmoid(pieces['ml_o'])

    return post(h_c, pc), post(h_l, pl)


def _affine_combine(e1, e2):
    a1, b1 = e1
    a2, b2 = e2
    return a1 * a2, a2 * b1 + b2


def lru_scan(inputs, h0, gr_w, gr_b, gi_w, gi_b, lam):
    (u,) = inputs
    b, t, _ = u.shape
    ub = u.reshape(b, t, LRU_BLOCKS, LRU_BD)
    r = jax.nn.sigmoid(jnp.einsum('btnd,nde->btne', ub, gr_w).reshape(b, t, LRU_WIDTH) + gr_b)
    i = jax.nn.sigmoid(jnp.einsum('btnd,nde->btne', ub, gi_w).reshape(b, t, LRU_WIDTH) + gi_b)
    log_a = -LRU_C * r * jax.nn.softplus(-lam)
    a = jnp.exp(log_a)
    bx = jnp.sqrt(-jnp.expm1(2.0 * log_a)) * (i * u)
    a_cum, h = lax.associative_scan(_affine_combine, (a, bx), axis=1)
    h = h + a_cum * h0[:, None, :]
    return h, h[:, -1]


def rglru_branch(pc, pl, p):
    u_c = dwconv(pc['lru_x'], p['lru_conv_w'], p['lru_conv_b']).astype(jnp.float32)
    u_l = dwconv(pl['lru_x'], p['lru_conv_w'], p['lru_conv_b']).astype(jnp.float32)
    scan_f = functools.partial(lru_scan, gr_w=p['lru_gr_w'][0], gr_b=p['lru_gr_b'][0], gi_w=p['lru_gi_w'][0], gi_b=p['lru_gi_b'][0], lam=p['lru_lambda'][0])
    scan_b = functools.partial(lru_scan, gr_w=p['lru_gr_w'][1], gr_b=p['lru_gr_b'][1], gi_w=p['lru_gi_w'][1], gi_b=p['lru_gi_b'][1], lam=p['lru_lambda'][1])
    h0 = jnp.zeros((u_c.shape[0], LRU_WIDTH), jnp.float32)
    h_c, h_l = bidirectional(scan_f, scan_b, (u_c,), (u_l,), (u_c,), (u_l,), h0)
    return h_c * jax.nn.gelu(pc['lru_gate']), h_l * jax.nn.gelu(pl['lru_gate'])


def rwkv7_scan(inputs, state0):
    xs = tuple(jnp.moveaxis(a, 1, 0) for a in inputs)

    def step(s, x_t):
        r_t, w_t, k_t, v_t, kap_t, kapa_t = x_t
        s_kap = jnp.einsum('bhvk,bhk->bhv', s, kap_t)
        s = s * w_t[:, :, None, :] - s_kap[..., None] * kapa_t[:, :, None, :] + v_t[..., None] * k_t[:, :, None, :]
        return s, jnp.einsum('bhvk,bhk->bhv', s, r_t)

    state, ys = lax.scan(step, state0, xs)
    return jnp.moveaxis(ys, 0, 1), state


def rwkv7_branch(pc, pl, p):
    r_k = p['rw_r_k'].reshape(RW_HEADS, RW_HD)
    ln_w = p['rw_ln_w'].reshape(RW_HEADS, RW_HD)
    ln_b = p['rw_ln_b'].reshape(RW_HEADS, RW_HD)

    def prep(pieces):
        seg = token_shift(pieces['rw'].astype(jnp.float32), p['rw_mu'])
        r, k, v, dd, ad, gd = split_cols(seg, RW_SPLITS)
        b, t = r.shape[:2]
        heads = lambda a: a.reshape(b, t, RW_HEADS, RW_HD)
        kk = heads(k * p['rw_k_k'])
        kap = kk / jnp.maximum(jnp.sqrt(jnp.sum(kk * kk, axis=-1, keepdims=True)), 1e-12)
        per_dir = []
        for d in range(2):
            dd_d = dd[..., d * RW_DECAY_RANK:(d + 1) * RW_DECAY_RANK]
            ad_d = ad[..., d * RW_ICLR_RANK:(d + 1) * RW_ICLR_RANK]
            w = jnp.exp(-RW_DECAY_SCALE * jax.nn.sigmoid(p['rw_decay0'][d] + jnp.tanh(dd_d) @ p['rw_decay_up'][d]))
            a = jax.nn.sigmoid(p['rw_iclr0'][d] + ad_d @ p['rw_iclr_up'][d])
            kt = heads(k * (1 + (a - 1) * p['rw_k_a']))
            per_dir.append((heads(r), heads(w), kt, heads(v), kap, kap * heads(a)))
        g = jax.nn.sigmoid(gd) @ p['rw_gate_up']
        return per_dir, g

    def post(y, per_dir, g):
        r, kt_f, v = per_dir[0][0], per_dir[0][2], per_dir[0][3]
        kt_b = per_dir[1][2]
        mu = jnp.mean(y, axis=-1, keepdims=True)
        var = jnp.mean(jnp.square(y - mu), axis=-1, keepdims=True)
        yn = (y - mu) * lax.rsqrt(var + RW_LN_EPS) * ln_w + ln_b
        bonus = jnp.sum(r * (kt_f + kt_b) * r_k, axis=-1, keepdims=True) * v
        b, t = y.shape[:2]
        return (yn + bonus).reshape(b, t, RW_WIDTH) * g

    dirs_c, g_c = prep(pc)
    dirs_l, g_l = prep(pl)
    state0 = jnp.zeros((g_c.shape[0], RW_HEADS, RW_HD, RW_HD), jnp.float32)
    y_c, y_l = bidirectional(rwkv7_scan, rwkv7_scan, dirs_c[0], dirs_l[0], dirs_c[1], dirs_l[1], state0)
    return post(y_c, dirs_c, g_c), post(y_l, dirs_l, g_l)


def merge_branches(gates, y_ml, y_lru, y_rw, p):
    g_ml, g_lru, g_rw = jnp.split(gates, N_BRANCHES, axis=-1)
    y = (jax.nn.sigmoid(g_ml) * (y_ml @ p['out_ml']) + jax.nn.sigmoid(g_lru) * (y_lru @ p['out_lru']) + jax.nn.sigmoid(g_rw) * (y_rw @ p['out_rw']))
    return y @ p['w_out']


def hybrid_mixer(hc, hl, p, need_ctx):
    pc = dict(zip(IN_NAMES, split_cols(hc @ p['w_in'], IN_SPLITS)))
    pl = dict(zip(IN_NAMES, split_cols(hl @ p['w_in'], IN_SPLITS)))
    ml_c, ml_l = mlstm_branch(pc, pl, p)
    lru_c, lru_l = rglru_branch(pc, pl, p)
    rw_c, rw_l = rwkv7_branch(pc, pl, p)
    y_l = merge_branches(pl['merge'], ml_l, lru_l, rw_l, p).astype(hl.dtype)
    y_c = merge_branches(pc['merge'], ml_c, lru_c, rw_c, p).astype(hc.dtype) if need_ctx else None
    return y_c, y_l


def swiglu(h, w_in, w_out):
    gate, up = jnp.split(h @ w_in, 2, axis=-1)
    return (jax.nn.silu(gate) * up) @ w_out


def setup_inputs(seed: int = 0) -> dict:
    key = jax.random.key(seed)
    ks = jax.random.split(key, 37)
    L, D = DEPTH, D_MODEL

    def nrm(i, shape, scale):
        return scale * jax.random.normal(ks[i], shape, jnp.float32)

    u = jax.random.uniform(ks[18], (L, 2, LRU_WIDTH), jnp.float32, 0.9, 0.999)
    a_base = u ** (1.0 / LRU_C)
    return {
        'x': nrm(0, (BATCH, SEQ, D), 1.0),
        'c': nrm(1, (BATCH, D), 1.0),
        'ctx': nrm(2, (BATCH, CTX_LEN, D), 1.0),
        'c_ctx': nrm(3, (D,), 1.0),
        'w_mod': nrm(4, (L, D, 6 * D), 0.5 * D ** -0.5),
        'b_mod': nrm(5, (L, 6 * D), 0.01),
        'norm1_w': 1.0 + nrm(6, (L, D), 0.05),
        'norm2_w': 1.0 + nrm(7, (L, D), 0.05),
        'w_in': nrm(8, (L, D, IN_WIDTH), D ** -0.5),
        'ml_ig_b': nrm(9, (L, 2, ML_HEADS), 0.1),
        'ml_fg_b': jnp.linspace(3.0, 6.0, ML_HEADS, dtype=jnp.float32) + nrm(10, (L, 2, ML_HEADS), 0.1),
        'ml_norm_w': 1.0 + nrm(11, (L, ML_WIDTH), 0.05),
        'lru_conv_w': nrm(12, (L, LRU_CONV, LRU_WIDTH), LRU_CONV ** -0.5),
        'lru_conv_b': nrm(13, (L, LRU_WIDTH), 0.01),
        'lru_gr_w': nrm(14, (L, 2, LRU_BLOCKS, LRU_BD, LRU_BD), LRU_BD ** -0.5),
        'lru_gr_b': nrm(15, (L, 2, LRU_WIDTH), 0.01),
        'lru_gi_w': nrm(16, (L, 2, LRU_BLOCKS, LRU_BD, LRU_BD), LRU_BD ** -0.5),
        'lru_gi_b': nrm(17, (L, 2, LRU_WIDTH), 0.01),
        'lru_lambda': jnp.log(a_base) - jnp.log1p(-a_base),
        'rw_mu': jax.random.uniform(ks[19], (L, RW_SEG), jnp.float32),
        'rw_decay0': -1.0 + nrm(20, (L, 2, RW_WIDTH), 0.5),
        'rw_decay_up': nrm(21, (L, 2, RW_DECAY_RANK, RW_WIDTH), 0.1),
        'rw_iclr0': nrm(22, (L, 2, RW_WIDTH), 0.1),
        'rw_iclr_up': nrm(23, (L, 2, RW_ICLR_RANK, RW_WIDTH), 0.5 * RW_ICLR_RANK ** -0.5),
        'rw_gate_up': nrm(24, (L, RW_GATE_RANK, RW_WIDTH), RW_GATE_RANK ** -0.5),
        'rw_k_k': 0.85 + nrm(25, (L, RW_WIDTH), 0.05),
        'rw_k_a': 1.0 + nrm(26, (L, RW_WIDTH), 0.05),
        'rw_r_k': nrm(27, (L, RW_WIDTH), 0.1),
        'rw_ln_w': 1.0 + nrm(28, (L, RW_WIDTH), 0.05),
        'rw_ln_b': nrm(29, (L, RW_WIDTH), 0.01),
        'out_ml': nrm(30, (L, ML_WIDTH, D), ML_WIDTH ** -0.5),
        'out_lru': nrm(31, (L, LRU_WIDTH, D), LRU_WIDTH ** -0.5),
        'out_rw': nrm(32, (L, RW_WIDTH, D), RW_WIDTH ** -0.5),
        'w_out': nrm(33, (L, D, D), D ** -0.5),
        'w_ffn_in': nrm(34, (L, D, 2 * FFN_HIDDEN), D ** -0.5),
        'w_ffn_out': nrm(35, (L, FFN_HIDDEN, D), FFN_HIDDEN ** -0.5),
        'final_norm_w': 1.0 + nrm(36, (D,), 0.05),
    }


def reference(x, c, ctx, c_ctx, w_mod, b_mod, norm1_w, norm2_w, w_in, ml_ig_b, ml_fg_b, ml_norm_w, lru_conv_w, lru_conv_b, lru_gr_w, lru_gr_b, lru_gi_w, lru_gi_b, lru_lambda, rw_mu, rw_decay0, rw_decay_up, rw_iclr0, rw_iclr_up, rw_gate_up, rw_k_k, rw_k_a, rw_r_k, rw_ln_w, rw_ln_b, out_ml, out_lru, out_rw, w_out, w_ffn_in, w_ffn_out, final_norm_w):
    rows = x.shape[1] // GRID_W
    x_lat, x_ctx = x, ctx
    cond_ctx = c_ctx[None, :]
    for layer in range(DEPTH):
        last = layer == DEPTH - 1
        p = {
            'w_in': w_in[layer], 'ml_ig_b': ml_ig_b[layer], 'ml_fg_b': ml_fg_b[layer], 'ml_norm_w': ml_norm_w[layer],
            'lru_conv_w': lru_conv_w[layer], 'lru_conv_b': lru_conv_b[layer], 'lru_gr_w': lru_gr_w[layer],
            'lru_gr_b': lru_gr_b[layer], 'lru_gi_w': lru_gi_w[layer], 'lru_gi_b': lru_gi_b[layer],
            'lru_lambda': lru_lambda[layer], 'rw_mu': rw_mu[layer], 'rw_decay0': rw_decay0[layer],
            'rw_decay_up': rw_decay_up[layer], 'rw_iclr0': rw_iclr0[layer], 'rw_iclr_up': rw_iclr_up[layer],
            'rw_gate_up': rw_gate_up[layer], 'rw_k_k': rw_k_k[layer], 'rw_k_a': rw_k_a[layer], 'rw_r_k': rw_r_k[layer],
            'rw_ln_w': rw_ln_w[layer], 'rw_ln_b': rw_ln_b[layer], 'out_ml': out_ml[layer], 'out_lru': out_lru[layer],
            'out_rw': out_rw[layer], 'w_out': w_out[layer],
        }
        sh1, sc1, g1, sh2, sc2, g2 = adaln(c, w_mod[layer], b_mod[layer])
        csh1, csc1, cg1, csh2, csc2, cg2 = adaln(cond_ctx, w_mod[layer], b_mod[layer])
        h_lat = modulate(rms_norm(x_lat, norm1_w[layer]), sh1, sc1)
        h_ctx = modulate(rms_norm(x_ctx, norm1_w[layer]), csh1, csc1)
        column_order = layer % 2 == 1
        if column_order:
            h_lat = grid_transpose(h_lat, rows, GRID_W)
        y_ctx, y_lat = hybrid_mixer(h_ctx, h_lat, p, not last)
        if column_order:
            y_lat = grid_transpose(y_lat, GRID_W, rows)
        x_lat = x_lat + g1 * y_lat
        x_lat = x_lat + g2 * swiglu(modulate(rms_norm(x_lat, norm2_w[layer]), sh2, sc2), w_ffn_in[layer], w_ffn_out[layer])
        if not last:
            x_ctx = x_ctx + cg1 * y_ctx
            x_ctx = x_ctx + cg2 * swiglu(modulate(rms_norm(x_ctx, norm2_w[layer]), csh2, csc2), w_ffn_in[layer], w_ffn_out[layer])
    return rms_norm(x_lat, final_norm_w)
```

```python
import math
from contextlib import ExitStack, contextmanager
import numpy as np
import ml_dtypes
import concourse.bass as bass
import concourse.mybir as mybir
from concourse.bass_utils import run_bass_kernel_spmd

F32 = mybir.dt.float32
BF16 = mybir.dt.bfloat16
AF = mybir.ActivationFunctionType
ALU = mybir.AluOpType
AX = mybir.AxisListType

D = 1024
NCORES = 8
FFN = 2816
INW = 12688
RWSEG = 3456
EPS = 1e-6
RW_LN_EPS = 64e-5
RW_DECAY_SCALE = math.exp(-0.5)


class Buf:
    __slots__ = ("w", "r")

    def __init__(self):
        self.w = {}
        self.r = {}


class Tile:
    def __init__(self, h):
        self.h = h
        self.b = Buf()

    def __getitem__(self, k):
        return self.h[k]


class KB:
    def __init__(self, nc, n_dma=56):
        self.nc = nc
        self.E = {"pe": nc.tensor, "act": nc.scalar, "dve": nc.vector, "pool": nc.gpsimd, "sp": nc.sync}
        self.sem = {}
        self.cnt = {}
        for e in ("pe", "act", "dve", "pool"):
            self.sem[e] = nc.alloc_semaphore("c_" + e)
            self.cnt[e] = 0
        self.dsem = [nc.alloc_semaphore("d%d" % i) for i in range(n_dma)]
        self.dval = [0] * n_dma
        self.drr = 0
        self.seen = {}
        self.n_inst = 0
        self.es = None
        self.uid = 0

    def tile(self, shape, dt, name="t"):
        self.uid += 1
        nm = "%s_%d" % (name, self.uid)
        if self.es is not None:
            return Tile(self.es.enter_context(self.nc.sbuf_tensor(nm, list(shape), dt)))
        return Tile(self.nc.alloc_sbuf_tensor(nm, list(shape), dt))

    def psum(self, shape, dt, name="p"):
        self.uid += 1
        nm = "%s_%d" % (name, self.uid)
        if self.es is not None:
            return Tile(self.es.enter_context(self.nc.psum_tensor(nm, list(shape), dt)))
        return Tile(self.nc.alloc_psum_tensor(nm, list(shape), dt))

    @contextmanager
    def stage(self):
        es = ExitStack()
        self.es = es
        try:
            yield
            self.barrier()
        finally:
            self.es = None
            es.close()

    def _semh(self, key):
        return self.sem[key] if isinstance(key, str) else self.dsem[key]

    def _wait(self, e, key, val):
        if e == "pe" and key == "pe":
            return
        if self.seen.get((e, key), 0) >= val:
            return
        self.E[e].wait_ge(self._semh(key), val)
        self.seen[(e, key)] = val
        self.n_inst += 1

    @staticmethod
    def _bufs(lst):
        return [x.b if isinstance(x, Tile) else x for x in lst]

    def _deps(self, e, reads, writes):
        deps = {}
        for b in reads:
            for k, v in b.w.items():
                if deps.get(k, 0) < v:
                    deps[k] = v
        for b in writes:
            for k, v in b.w.items():
                if deps.get(k, 0) < v:
                    deps[k] = v
            for k, v in b.r.items():
                if deps.get(k, 0) < v:
                    deps[k] = v
        for k, v in deps.items():
            self._wait(e, k, v)

    def _mark(self, ticket, reads, writes):
        k, v = ticket
        for b in reads:
            b.r[k] = v
        for b in writes:
            b.w = {k: v}
            b.r = {}

    def op(self, e, fn, reads=(), writes=()):
        reads = self._bufs(reads)
        writes = self._bufs(writes)
        self._deps(e, reads, writes)
        ins = fn(self.E[e])
        self.cnt[e] += 1
        ins.then_inc(self.sem[e], 1)
        self.n_inst += 1
        self._mark((e, self.cnt[e]), reads, writes)
        return ins

    def dma(self, e, out, in_, reads=(), writes=(), **kw):
        reads = self._bufs(reads)
        writes = self._bufs(writes)
        self._deps(e, reads, writes)
        s = self.drr
        self.drr = (self.drr + 1) % len(self.dsem)
        if self.dval[s] > 0:
            self._wait(e, s, self.dval[s])
        ins = self.E[e].dma_start(out=out, in_=in_, **kw)
        self.dval[s] += 16
        ins.then_inc(self.dsem[s], 16)
        self.n_inst += 1
        self._mark((s, self.dval[s]), reads, writes)

    def barrier(self, engines=("pe", "act", "dve", "pool", "sp")):
        for e in engines:
            for s in range(len(self.dsem)):
                if self.dval[s] > 0:
                    self._wait(e, s, self.dval[s])
            for k in ("pe", "act", "dve", "pool"):
                if self.cnt[k] > 0 and k != e:
                    self._wait(e, k, self.cnt[k])

    def finish(self):
        self.barrier()


class Prog:
    def __init__(self, nc, cfg):
        self.nc = nc
        self.kb = KB(nc)
        self.cfg = cfg
        self.NB = cfg["NB"]
        self.CTX = cfg["CTX"]
        self.LAT = cfg["LAT"]
        self.L = cfg["DEPTH"]
        self.ROWS = self.LAT // 64
        self.TS = self.CTX + self.LAT
        self.T = self.NB * self.TS
        self.dbg = cfg.get("dbg", ())
        self.inp = {}
        self.scr = {}
        self.flip = 0

    def din(self, name, shape, dt=F32):
        self.inp[name] = self.nc.dram_tensor(name, list(shape), dt, kind="ExternalInput").ap()
        return self.inp[name]

    def dscr(self, name, shape, dt):
        kind = "ExternalOutput" if name in self.dbg else "Internal"
        self.scr[name] = self.nc.dram_tensor(name, list(shape), dt, kind=kind).ap()
        return self.scr[name]

    def alt(self):
        self.flip ^= 1
        return "act" if self.flip else "dve"

    def xr_rows(self, layer, b, pos0, n=128):
        XR = self.scr["XR"]
        if pos0 < self.CTX or layer % 2 == 0:
            return [(0, n, XR[b, pos0:pos0 + n, :])]
        lat = XR[b, self.CTX:self.TS, :].rearrange("(r c) d -> c r d", c=64)
        out = []
        m0 = pos0 - self.CTX
        m = m0
        while m < m0 + n:
            c = m // self.ROWS
            r0 = m % self.ROWS
            k = min(self.ROWS - r0, m0 + n - m)
            out.append((m - m0, k, lat[c, r0:r0 + k, :]))
            m += k
        return out

    def st_init(self):
        kb = self.kb
        with kb.stage():
            XR = self.scr["XR"]
            for b in range(self.NB):
                kb.dma("sp", XR[b, 0:self.CTX, :], self.inp["ctx"][b])
                kb.dma("sp", XR[b, self.CTX:self.TS, :], self.inp["x"][b])

    def st_adaln(self):
        kb, nc = self.kb, self.nc
        with kb.stage():
            ct = kb.tile([128, 8, 3], F32)
            sc = kb.tile([128, 8, 3], F32)
            cb = kb.tile([128, 24, 128], F32)
            kb.dma("sp", ct[:, :, :], self.inp["condT"], writes=[ct])
            kb.op("act", lambda e: e.activation(out=sc[:, :, :], in_=ct[:, :, :], func=AF.Silu), reads=[ct], writes=[sc])
            for kc in range(8):
                for r in range(3):
                    kb.op("dve", lambda e: e.tensor_copy(out=cb[:, kc * 3 + r, :], in_=sc[:, kc, r:r + 1].to_broadcast([128, 128])), reads=[sc], writes=[cb])
            wts = [kb.tile([128, 8, 512], F32) for _ in range(2)]
            bts = [kb.tile([128, 512], F32) for _ in range(2)]
            pss = [kb.psum([128, 512], F32) for _ in range(3)]
            ots = [kb.tile([128, 512], F32) for _ in range(3)]
            it = 0
            for l in range(self.L):
                for nch in range(12):
                    wt, bt = wts[it % 2], bts[it % 2]
                    it += 1
                    kb.dma("sp", wt[:, :, :], self.inp["w_mod"][l].rearrange("(kc p) n -> p kc n", p=128)[:, :, nch * 512:(nch + 1) * 512], writes=[wt])
                    kb.dma("sp", bt[:, :], self.inp["b_mod"][l:l + 1, nch * 512:(nch + 1) * 512].to_broadcast([128, 512]), writes=[bt])
                    for r in range(3):
                        ps, ot = pss[r], ots[r]
                        for kc in range(8):
                            kb.op("pe", lambda e: e.matmul(ps[:, :], lhsT=cb[:, kc * 3 + r, :], rhs=wt[:, kc, :], start=(kc == 0), stop=(kc == 7)), reads=[cb, wt], writes=[ps])
                        kb.op("dve", lambda e: e.tensor_tensor(out=ot[:, :], in0=ps[:, :], in1=bt[:, :], op=ALU.add), reads=[ps, bt], writes=[ot])
                        kb.dma("pool", self.scr["MOD"][l, r:r + 1, nch * 512:(nch + 1) * 512], ot[0:1, :], reads=[ot])

    def load_bcast(self, t, src_row):
        n = src_row.shape[-1]
        self.kb.dma("sp", t[:, 0:n], src_row.to_broadcast([128, n]), writes=[t])

    def transpose_store(self, hb, dst, col0, ident, psT, hT):
        kb = self.kb
        for kc in range(8):
            kb.op("pe", lambda e: e.transpose(psT[:, kc, :], hb[:, kc * 128:(kc + 1) * 128], ident[:, :]), reads=[hb, ident], writes=[psT])
        kb.op("act", lambda e: e.activation(out=hT[:, :, :], in_=psT[:, :, :], func=AF.Copy), reads=[psT], writes=[hT])
        kb.dma("pool", dst.rearrange("(kc p) t -> p kc t", p=128)[:, :, col0:col0 + 128], hT[:, :, :], reads=[hT])

    def st_norm(self, l, which, final=False):
        kb = self.kb
        with kb.stage():
            nw = kb.tile([128, D], F32)
            G = [kb.tile([128, D], F32) for _ in range(3)]
            SH = [kb.tile([128, D], F32) for _ in range(3)]
            ident = kb.tile([128, 128], BF16)
            kb.dma("sp", ident[:, :], self.inp["ident_bf"], writes=[ident])
            if final:
                self.load_bcast(nw, self.inp["final_norm_w"].rearrange("(o d) -> o d", o=1))
            else:
                nwsrc = self.inp["norm1_w" if which == 1 else "norm2_w"]
                self.load_bcast(nw, nwsrc[l:l + 1, :])
                shi, sci = (0, 1) if which == 1 else (3, 4)
                for r in range(3):
                    self.load_bcast(G[r], self.scr["MOD"][l, r:r + 1, sci * D:(sci + 1) * D])
                    self.load_bcast(SH[r], self.scr["MOD"][l, r:r + 1, shi * D:(shi + 1) * D])
                    kb.op("dve", lambda e: e.scalar_tensor_tensor(out=G[r][:, :], in0=G[r][:, :], scalar=1.0, in1=nw[:, :], op0=ALU.add, op1=ALU.mult), reads=[G[r], nw], writes=[G[r]])
            NBUF = 3
            xt = [kb.tile([128, D], F32) for _ in range(NBUF)]
            junk = [kb.tile([128, D], F32) for _ in range(NBUF)]
            ss = [kb.tile([128, 1], F32) for _ in range(NBUF)]
            rs = [kb.tile([128, 1], F32) for _ in range(NBUF)]
            hb = [kb.tile([128, D], BF16) for _ in range(NBUF)]
            hT = [kb.tile([128, 8, 128], BF16) for _ in range(NBUF)]
            psT = [kb.psum([128, 8, 128], BF16) for _ in range(2)]
            it = 0
            for b in range(self.NB):
                for j in range(self.TS // 128):
                    pos0 = j * 128
                    if final and pos0 < self.CTX:
                        continue
                    i = it % NBUF
                    it += 1
                    r = 2 if pos0 < self.CTX else b
                    x, jk, s_, r_, h_ = xt[i], junk[i], ss[i], rs[i], hb[i]
                    for (p0, n, ap) in self.xr_rows(l if not final else 0, b, pos0):
                        kb.dma("sp", x[p0:p0 + n, :], ap, writes=[x])
                    kb.op("act", lambda e: e.activation(out=jk[:, :], in_=x[:, :], func=AF.Square, accum_out=s_[:, :]), reads=[x], writes=[jk, s_])
                    kb.op("act", lambda e: e.activation(out=r_[:, :], in_=s_[:, :], func=AF.Sqrt, scale=1.0 / D, bias=self.eps_t[:, 0:1]), reads=[s_], writes=[r_])
                    kb.op("dve", lambda e: e.reciprocal(out=r_[:, :], in_=r_[:, :]), reads=[r_], writes=[r_])
                    if final:
                        kb.op("dve", lambda e: e.scalar_tensor_tensor(out=jk[:, :], in0=x[:, :], scalar=r_[:, 0:1], in1=nw[:, :], op0=ALU.mult, op1=ALU.mult), reads=[x, r_, nw], writes=[jk])
                        kb.dma("pool", self.out_ap[b, pos0 - self.CTX:pos0 - self.CTX + 128, :], jk[:, :], reads=[jk])
                        continue
                    kb.op("dve", lambda e: e.scalar_tensor_tensor(out=jk[:, :], in0=x[:, :], scalar=r_[:, 0:1], in1=G[r][:, :], op0=ALU.mult, op1=ALU.mult), reads=[x, r_, G[r]], writes=[jk])
                    kb.op("dve", lambda e: e.tensor_tensor(out=h_[:, :], in0=jk[:, :], in1=SH[r][:, :], op=ALU.add), reads=[jk, SH[r]], writes=[h_])
                    self.transpose_store(h_, self.scr["HT"], b * self.TS + pos0, ident, psT[it % 2], hT[i])

    def gemm(self, src, K, W, jobs, ng_max=2048):
        kb = self.kb
        T = self.T
        kp = min(K, 128)
        KC = (K + 127) // 128
        assert K == kp * KC
        if KC > 8:
            ng_max = 512
        with kb.stage():
            wb = [kb.tile([kp, KC, ng_max], BF16) for _ in range(2)]
            hbs = [kb.tile([kp, KC, 512], BF16) for _ in range(2)]
            self.g_ps = [kb.psum([128, 512], F32) for _ in range(4)]
            self.g_stF = [kb.tile([128, 512], F32) for _ in range(4)]
            self.g_stB = [kb.tile([128, 512], BF16) for _ in range(4)]
            self.g_tmp = [kb.tile([128, 512], F32) for _ in range(4)]
            self.g_i = 0
            src3 = src.rearrange("(kc p) t -> p kc t", p=kp)
            gi = 0
            hi = 0
            for (c0, ncols, mode, epi, prep) in jobs:
                if prep is not None:
                    prep()
                for g0 in range(c0, c0 + ncols, ng_max):
                    ng = min(ng_max, c0 + ncols - g0)
                    w = wb[gi % 2]
                    gi += 1
                    for kc in range(KC):
                        kb.dma("pool", w[:, kc, 0:ng], W[kc * kp:(kc + 1) * kp, g0:g0 + ng], writes=[w])
                    for t0 in range(0, T, 512):
                        tsz = min(512, T - t0)
                        h = hbs[hi % 2]
                        hi += 1
                        kb.dma("sp", h[:, :, 0:tsz], src3[:, :, t0:t0 + tsz], writes=[h])
                        if mode == "FM":
                            for n0 in range(0, ng, 128):
                                nsz = min(128, ng - n0)
                                ps = self.g_ps[self.g_i % 4]
                                for kc in range(KC):
                                    kb.op("pe", lambda e: e.matmul(ps[0:nsz, 0:tsz], lhsT=w[:, kc, n0:n0 + nsz], rhs=h[:, kc, 0:tsz], start=(kc == 0), stop=(kc == KC - 1)), reads=[w, h], writes=[ps])
                                epi(ps, g0 + n0 - c0, nsz, t0, tsz)
                                self.g_i += 1
                        else:
                            for ts in range(0, tsz, 128):
                                for n0 in range(0, ng, 512):
                                    nsz = min(512, ng - n0)
                                    ps = self.g_ps[self.g_i % 4]
                                    for kc in range(KC):
                                        kb.op("pe", lambda e: e.matmul(ps[:, 0:nsz], lhsT=h[:, kc, ts:ts + 128], rhs=w[:, kc, n0:n0 + nsz], start=(kc == 0), stop=(kc == KC - 1)), reads=[w, h], writes=[ps])
                                    epi(ps, g0 + n0 - c0, nsz, t0 + ts, 128)
                                    self.g_i += 1

    def epi_fm(self, dst, dt, func=AF.Copy, scale=1.0, bias_t=None):
        kb = self.kb

        def epi(ps, c, nsz, t0, tsz):
            st = (self.g_stF if dt == F32 else self.g_stB)[self.g_i % 4]
            if func == AF.Copy and bias_t is None and self.g_i % 2 == 0:
                kb.op("dve", lambda e: e.tensor_scalar(out=st[0:nsz, 0:tsz], in0=ps[0:nsz, 0:tsz], scalar1=float(scale), scalar2=None, op0=ALU.mult), reads=[ps], writes=[st])
            elif bias_t is None:
                kb.op("act", lambda e: e.activation(out=st[0:nsz, 0:tsz], in_=ps[0:nsz, 0:tsz], func=func, scale=float(scale)), reads=[ps], writes=[st])
            else:
                kb.op("act", lambda e: e.activation(out=st[0:nsz, 0:tsz], in_=ps[0:nsz, 0:tsz], func=func, scale=float(scale), bias=bias_t[0:nsz, c // 128:c // 128 + 1]), reads=[ps, bias_t], writes=[st])
            kb.dma("pool", dst[c:c + nsz, t0:t0 + tsz], st[0:nsz, 0:tsz], reads=[st])
        return epi

    def epi_tm(self, dst, dt, func=AF.Copy, scale=1.0):
        kb = self.kb

        def epi(ps, c, nsz, t0, tsz):
            st = (self.g_stF if dt == F32 else self.g_stB)[self.g_i % 4]
            if func == AF.Copy and self.g_i % 2 == 0:
                kb.op("dve", lambda e: e.tensor_scalar(out=st[0:tsz, 0:nsz], in0=ps[0:tsz, 0:nsz], scalar1=float(scale), scalar2=None, op0=ALU.mult), reads=[ps], writes=[st])
            else:
                kb.op("act", lambda e: e.activation(out=st[0:tsz, 0:nsz], in_=ps[0:tsz, 0:nsz], func=func, scale=float(scale)), reads=[ps], writes=[st])
            kb.dma("pool", dst[t0:t0 + tsz, c:c + nsz], st[0:tsz, 0:nsz], reads=[st])
        return epi

    def epi_gelu_fm(self, dst):
        kb = self.kb

        def epi(ps, c, nsz, t0, tsz):
            x = self.g_stF[self.g_i % 4]
            u = self.g_tmp[self.g_i % 4]
            st = self.g_stB[self.g_i % 4]
            kb.op("act", lambda e: e.activation(out=x[0:nsz, 0:tsz], in_=ps[0:nsz, 0:tsz], func=AF.Copy), reads=[ps], writes=[x])
            kb.op("dve", lambda e: e.tensor_tensor(out=u[0:nsz, 0:tsz], in0=x[0:nsz, 0:tsz], in1=x[0:nsz, 0:tsz], op=ALU.mult), reads=[x], writes=[u])
            kb.op("dve", lambda e: e.tensor_scalar(out=u[0:nsz, 0:tsz], in0=u[0:nsz, 0:tsz], scalar1=0.044715, scalar2=1.0, op0=ALU.mult, op1=ALU.add), reads=[u], writes=[u])
            kb.op("dve", lambda e: e.tensor_tensor(out=u[0:nsz, 0:tsz], in0=u[0:nsz, 0:tsz], in1=x[0:nsz, 0:tsz], op=ALU.mult), reads=[u, x], writes=[u])
            kb.op("act", lambda e: e.activation(out=u[0:nsz, 0:tsz], in_=u[0:nsz, 0:tsz], func=AF.Sigmoid, scale=1.5957691216057308), reads=[u], writes=[u])
            kb.op("dve", lambda e: e.tensor_tensor(out=st[0:nsz, 0:tsz], in0=u[0:nsz, 0:tsz], in1=x[0:nsz, 0:tsz], op=ALU.mult), reads=[u, x], writes=[st])
            kb.dma("pool", dst[c:c + nsz, t0:t0 + tsz], st[0:nsz, 0:tsz], reads=[st])
        return epi

    def st_win(self, l):
        S = self.scr
        W = self.inp["w_in"][l]
        none = None
        jobs = [
            (0, 1024, "FM", self.epi_fm(S["QT"], BF16), none),
            (1024, 1024, "FM", self.epi_fm(S["KT"], BF16, scale=1.0 / 16), none),
            (1024, 1024, "TM", self.epi_tm(S["KTM"], BF16, scale=1.0 / 16), none),
            (2048, 1024, "TM", self.epi_tm(S["VTM"], BF16), none),
            (3072, 1024, "TM", self.epi_tm(S["OG"], BF16, func=AF.Sigmoid), none),
            (4096, 16, "TM", self.epi_tm(S["IFG"], F32), none),
            (4112, 1024, "FM", self.epi_fm(S["LX"], F32), none),
            (5136, 1024, "FM", self.epi_gelu_fm(S["LG"]), none),
            (6160, RWSEG, "FM", self.epi_fm(S["RW"], F32), none),
            (9616, 3072, "FM", self.epi_fm(S["MG"], BF16, func=AF.Sigmoid), none),
        ]
        self.gemm(S["HT"], D, W, jobs)

    def epi_resid(self, l, gate_idx):
        kb = self.kb
        self.r_g = None

        def prep():
            self.r_g = [kb.tile([128, D], F32) for _ in range(3)]
            for r in range(3):
                self.load_bcast(self.r_g[r], self.scr["MOD"][l, r:r + 1, gate_idx * D:(gate_idx + 1) * D])
            self.r_x = [kb.tile([128, 512], F32) for _ in range(4)]

        def epi(ps, c, nsz, t0, tsz):
            b = t0 // self.TS
            pos0 = t0 - b * self.TS
            r = 2 if pos0 < self.CTX else b
            x = self.r_x[self.g_i % 4]
            st = self.g_stF[self.g_i % 4]
            rows = self.xr_rows(l, b, pos0)
            for (p0, n, ap) in rows:
                kb.dma("sp", x[p0:p0 + n, 0:nsz], ap[:, c:c + nsz], writes=[x])
            kb.op("dve", lambda e: e.tensor_tensor(out=st[:, 0:nsz], in0=ps[:, 0:nsz], in1=self.r_g[r][:, c:c + nsz], op=ALU.mult), reads=[ps, self.r_g[r]], writes=[st])
            kb.op("dve", lambda e: e.tensor_tensor(out=st[:, 0:nsz], in0=st[:, 0:nsz], in1=x[:, 0:nsz], op=ALU.add), reads=[st, x], writes=[st])
            for (p0, n, ap) in rows:
                kb.dma("pool", ap[:, c:c + nsz], st[p0:p0 + n, 0:nsz], reads=[st])
        return epi, prep

    def st_wout(self, l):
        epi, prep = self.epi_resid(l, 2)
        self.gemm(self.scr["YM"], D, self.inp["w_out"][l], [(0, D, "TM", epi, prep)])

    def st_ffn(self, l):
        kb = self.kb
        S = self.scr
        self.st_norm(l, 2)
        W = self.inp["w_ffn_in"][l]
        self.gemm(S["HT"], D, W, [(0, FFN, "FM", self.epi_fm(S["AG"], BF16, func=AF.Silu), None)])

        def prep():
            self.f_g = [kb.tile([128, 512], BF16) for _ in range(4)]

        def epi_up(ps, c, nsz, t0, tsz):
            g = self.f_g[self.g_i % 4]
            st = self.g_stB[self.g_i % 4]
            kb.dma("sp", g[0:nsz, 0:tsz], S["AG"][c:c + nsz, t0:t0 + tsz], writes=[g])
            kb.op("dve", lambda e: e.tensor_tensor(out=st[0:nsz, 0:tsz], in0=ps[0:nsz, 0:tsz], in1=g[0:nsz, 0:tsz], op=ALU.mult), reads=[ps, g], writes=[st])
            kb.dma("pool", S["AT"][c:c + nsz, t0:t0 + tsz], st[0:nsz, 0:tsz], reads=[st])
        self.gemm(S["HT"], D, W[:, FFN:2 * FFN], [(0, FFN, "FM", epi_up, prep)])
        epi, prep2 = self.epi_resid(l, 5)
        self.gemm(S["AT"], FFN, self.inp["w_ffn_out"][l], [(0, D, "TM", epi, prep2)])

    def st_merge(self, l):
        kb = self.kb
        S = self.scr
        srcs = [("YML", "out_ml"), ("YLRU", "out_lru"), ("YRW", "out_rw")]
        for bi, (ys, wn) in enumerate(srcs):
            def prep():
                self.m_g = [kb.tile([128, 512], BF16) for _ in range(4)]
                self.m_a = [kb.tile([128, 512], F32) for _ in range(4)]

            def epi(ps, c, nsz, t0, tsz, bi=bi):
                g = self.m_g[self.g_i % 4]
                a = self.m_a[self.g_i % 4]
                kb.dma("sp", g[0:nsz, 0:tsz], S["MG"][bi * D + c:bi * D + c + nsz, t0:t0 + tsz], writes=[g])
                if bi == 0:
                    st = self.g_stF[self.g_i % 4]
                    kb.op("dve", lambda e: e.tensor_tensor(out=st[0:nsz, 0:tsz], in0=ps[0:nsz, 0:tsz], in1=g[0:nsz, 0:tsz], op=ALU.mult), reads=[ps, g], writes=[st])
                    kb.dma("pool", S["YACC"][c:c + nsz, t0:t0 + tsz], st[0:nsz, 0:tsz], reads=[st])
                else:
                    kb.dma("sp", a[0:nsz, 0:tsz], S["YACC"][c:c + nsz, t0:t0 + tsz], writes=[a])
                    st = self.g_stF[self.g_i % 4] if bi == 1 else self.g_stB[self.g_i % 4]
                    tmp = self.g_tmp[self.g_i % 4]
                    kb.op("dve", lambda e: e.tensor_tensor(out=tmp[0:nsz, 0:tsz], in0=ps[0:nsz, 0:tsz], in1=g[0:nsz, 0:tsz], op=ALU.mult), reads=[ps, g], writes=[tmp])
                    kb.op("dve", lambda e: e.tensor_tensor(out=st[0:nsz, 0:tsz], in0=tmp[0:nsz, 0:tsz], in1=a[0:nsz, 0:tsz], op=ALU.add), reads=[tmp, a], writes=[st])
                    dst = S["YACC"] if bi == 1 else S["YM"]
                    kb.dma("pool", dst[c:c + nsz, t0:t0 + tsz], st[0:nsz, 0:tsz], reads=[st])
            self.gemm(S[ys], D, self.inp[wn][l], [(0, D, "FM", epi, prep)])

    def st_mlstm(self, l):
        kb = self.kb
        S = self.scr
        TS, NB = self.TS, self.NB
        nck = TS // 128
        ctxc = self.CTX // 128
        with kb.stage():
            identF = kb.tile([128, 128], F32)
            ones = kb.tile([128, 128], F32)
            tri = [kb.tile([128, 128], F32) for _ in range(2)]
            negm = [kb.tile([128, 128], F32) for _ in range(2)]
            GB = kb.tile([128, 16], F32)
            kb.dma("sp", identF[:, :], self.inp["ident_f"], writes=[identF])
            kb.dma("sp", ones[:, :], self.inp["ones_f"], writes=[ones])
            for d in range(2):
                kb.dma("sp", tri[d][:, :], self.inp["tri"][d], writes=[tri[d]])
                kb.dma("sp", negm[d][:, :], self.inp["negm"][d], writes=[negm[d]])
            kb.dma("sp", GB[:, 0:8], self.inp["ml_ig_b"][l].rearrange("(o a) b -> o (a b)", o=1).to_broadcast([128, 8]), writes=[GB])
            kb.dma("sp", GB[:, 8:16], self.inp["ml_fg_b"][l].rearrange("(o a) b -> o (a b)", o=1).to_broadcast([128, 8]), writes=[GB])
            NBUF = 2
            qT = [kb.tile([128, 8, 128], BF16) for _ in range(NBUF)]
            kT = [kb.tile([128, 8, 128], BF16) for _ in range(NBUF)]
            kTM = [kb.tile([128, D], BF16) for _ in range(NBUF)]
            VA = [kb.tile([128, 4, 257], BF16) for _ in range(NBUF)]
            IFt = [kb.tile([128, 16], F32) for _ in range(NBUF)]
            for i in range(NBUF):
                kb.op("dve", lambda e: e.memset(VA[i][:, :, :], 1.0), writes=[VA[i]])
            gx = kb.tile([128, 16], F32)
            lf = kb.tile([128, 4], F32)
            e1 = kb.tile([128, 4], F32)
            lfB = kb.tile([128, 4, 128], F32)
            fc = kb.tile([128, 8], F32)
            cs = kb.tile([128, 4], F32)
            ef = kb.tile([128, 4], F32)
            ev = kb.tile([128, 4], F32)
            eT = kb.tile([128, 4], F32)
            tmp4 = kb.tile([128, 4], F32)
            C32 = [kb.tile([128, 2, 257], F32) for _ in range(4)]
            Cbf = [kb.tile([128, 2, 257], BF16) for _ in range(4)]
            DT = [kb.tile([128, 128], F32) for _ in range(2)]
            AT = [kb.tile([128, 128], BF16) for _ in range(2)]
            tI = [kb.tile([128, 257], F32) for _ in range(2)]
            ND = [kb.tile([128, 257], F32) for _ in range(2)]
            den = [kb.tile([128, 1], F32) for _ in range(2)]
            VS = [kb.tile([128, 257], BF16) for _ in range(2)]
            HO = [kb.tile([128, D], F32) for _ in range(2)]
            ps_g = kb.psum([128, 8], F32)
            psA = kb.psum([128, 128], F32)
            psF = kb.psum([128, 128], F32)
            psI = kb.psum([128, 257], F32)
            psC = kb.psum([128, 257], F32)
            psD = [kb.psum([128, 257], F32) for _ in range(2)]
            it = 0
            hh = 0
            for d in range(2):
                for b in range(NB):
                    for h in range(4):
                        kb.op("dve", lambda e: e.memset(C32[h][:, :, :], 0.0), writes=[C32[h]])
                        kb.op("dve", lambda e: e.memset(Cbf[h][:, :, :], 0.0), writes=[Cbf[h]])
                    cl = list(range(ctxc)) + list(range(ctxc, nck)) if d == 0 else list(range(ctxc - 1, -1, -1)) + list(range(nck - 1, ctxc - 1, -1))
                    for c in cl:
                        i = it % NBUF
                        it += 1
                        col0 = b * TS + c * 128
                        q_, k_, km_, va_, if_ = qT[i], kT[i], kTM[i], VA[i], IFt[i]
                        kb.dma("sp", q_[:, :, :], S["QT"].rearrange("(kc p) t -> p kc t", p=128)[:, :, col0:col0 + 128], writes=[q_])
                        kb.dma("sp", k_[:, :, :], S["KT"].rearrange("(kc p) t -> p kc t", p=128)[:, :, col0:col0 + 128], writes=[k_])
                        kb.dma("sp", km_[:, :], S["KTM"][col0:col0 + 128, :], writes=[km_])
                        kb.dma("sp", va_[:, :, 0:256], S["VTM"][col0:col0 + 128, :].rearrange("t (h e) -> t h e", h=4), writes=[va_])
                        kb.dma("sp", if_[:, :], S["IFG"][col0:col0 + 128, :], writes=[if_])
                        kb.op("dve", lambda e: e.tensor_tensor(out=gx[:, :], in0=if_[:, :], in1=GB[:, :], op=ALU.add), reads=[if_, GB], writes=[gx])
                        i4 = gx[:, d * 4:d * 4 + 4]
                        f4 = gx[:, 8 + d * 4:12 + d * 4]
                        kb.op("act", lambda e: e.activation(out=e1[:, :], in_=f4, func=AF.Exp, scale=-1.0), reads=[gx], writes=[e1])
                        kb.op("act", lambda e: e.activation(out=e1[:, :], in_=e1[:, :], func=AF.Ln, bias=self.one_t[:, 0:1]), reads=[e1], writes=[e1])
                        kb.op("dve", lambda e: e.tensor_scalar(out=lf[:, :], in0=e1[:, :], scalar1=-1.0, scalar2=None, op0=ALU.mult), reads=[e1], writes=[lf])
                        kb.op("dve", lambda e: e.tensor_copy(out=lfB[:, :, :], in_=lf[:, 0:4].unsqueeze(2).to_broadcast([128, 4, 128])), reads=[lf], writes=[lfB])
                        kb.op("pe", lambda e: e.matmul(ps_g[:, 0:4], lhsT=tri[d][:, :], rhs=lf[:, :], start=True, stop=True), reads=[tri[d], lf], writes=[ps_g])
                        kb.op("pe", lambda e: e.matmul(ps_g[:, 4:8], lhsT=ones[:, :], rhs=lf[:, :], start=True, stop=True), reads=[ones, lf], writes=[ps_g])
                        kb.op("dve", lambda e: e.tensor_copy(out=fc[:, :], in_=ps_g[:, :]), reads=[ps_g], writes=[fc])
                        kb.op("dve", lambda e: e.tensor_tensor(out=cs[:, :], in0=i4, in1=fc[:, 0:4], op=ALU.subtract), reads=[gx, fc], writes=[cs])
                        kb.op("act", lambda e: e.activation(out=ef[:, :], in_=fc[:, 0:4], func=AF.Exp), reads=[fc], writes=[ef])
                        kb.op("dve", lambda e: e.tensor_tensor(out=tmp4[:, :], in0=cs[:, :], in1=fc[:, 4:8], op=ALU.add), reads=[cs, fc], writes=[tmp4])
                        kb.op("act", lambda e: e.activation(out=ev[:, :], in_=tmp4[:, :], func=AF.Exp), reads=[tmp4], writes=[ev])
                        kb.op("act", lambda e: e.activation(out=eT[:, :], in_=fc[:, 4:8], func=AF.Exp), reads=[fc], writes=[eT])
                        ho = HO[it % 2]
                        for h in range(4):
                            j2 = hh % 2
                            hh += 1
                            dt_, at_, ti_, nd_, dn_, vs_ = DT[j2], AT[j2], tI[j2], ND[j2], den[j2], VS[j2]
                            for j in range(2):
                                kb.op("pe", lambda e: e.matmul(psA[:, :], lhsT=k_[:, 2 * h + j, :], rhs=q_[:, 2 * h + j, :], start=(j == 0), stop=(j == 1)), reads=[k_, q_], writes=[psA])
                            kb.op("pe", lambda e: e.matmul(psF[:, :], lhsT=lfB[:, h, :], rhs=tri[d][:, :], start=True, stop=False), reads=[lfB, tri[d]], writes=[psF])
                            kb.op("pe", lambda e: e.matmul(psF[:, :], lhsT=identF[:, :], rhs=negm[d][:, :], start=False, stop=True), reads=[identF, negm[d]], writes=[psF])
                            kb.op("act", lambda e: e.activation(out=dt_[:, :], in_=psF[:, :], func=AF.Exp, bias=cs[:, h:h + 1]), reads=[psF, cs], writes=[dt_])
                            kb.op("dve", lambda e: e.tensor_tensor(out=at_[:, :], in0=psA[:, :], in1=dt_[:, :], op=ALU.mult), reads=[psA, dt_], writes=[at_])
                            kb.op("pe", lambda e: e.matmul(psI[:, :], lhsT=at_[:, :], rhs=va_[:, h, :], start=True, stop=True), reads=[at_, va_], writes=[psI])
                            for j in range(2):
                                kb.op("pe", lambda e: e.matmul(psC[:, :], lhsT=q_[:, 2 * h + j, :], rhs=Cbf[h][:, j, :], start=(j == 0), stop=(j == 1)), reads=[q_, Cbf[h]], writes=[psC])
                            kb.op("act", lambda e: e.activation(out=ti_[:, :], in_=psI[:, :], func=AF.Copy), reads=[psI], writes=[ti_])
                            kb.op("dve", lambda e: e.scalar_tensor_tensor(out=nd_[:, :], in0=psC[:, :], scalar=ef[:, h:h + 1], in1=ti_[:, :], op0=ALU.mult, op1=ALU.add), reads=[psC, ef, ti_], writes=[nd_])
                            kb.op("act", lambda e: e.activation(out=dn_[:, :], in_=nd_[:, 256:257], func=AF.Abs), reads=[nd_], writes=[dn_])
                            kb.op("dve", lambda e: e.tensor_scalar(out=dn_[:, :], in0=dn_[:, :], scalar1=1.0, scalar2=None, op0=ALU.max), reads=[dn_], writes=[dn_])
                            kb.op("dve", lambda e: e.reciprocal(out=dn_[:, :], in_=dn_[:, :]), reads=[dn_], writes=[dn_])
                            kb.op("act", lambda e: e.activation(out=ho[:, h * 256:(h + 1) * 256], in_=nd_[:, 0:256], func=AF.Copy, scale=dn_[:, 0:1]), reads=[nd_, dn_], writes=[ho])
                            kb.op("dve", lambda e: e.tensor_scalar(out=vs_[:, :], in0=va_[:, h, :], scalar1=ev[:, h:h + 1], scalar2=None, op0=ALU.mult), reads=[va_, ev], writes=[vs_])
                            for j in range(2):
                                kb.op("pe", lambda e: e.matmul(psD[j][:, :], lhsT=km_[:, h * 256 + j * 128:h * 256 + (j + 1) * 128], rhs=vs_[:, :], start=True, stop=True), reads=[km_, vs_], writes=[psD[j]])
                                kb.op("dve", lambda e: e.scalar_tensor_tensor(out=C32[h][:, j, :], in0=C32[h][:, j, :], scalar=eT[:, h:h + 1], in1=psD[j][:, :], op0=ALU.mult, op1=ALU.add), reads=[C32[h], eT, psD[j]], writes=[C32[h]])
                            kb.op("act", lambda e: e.activation(out=Cbf[h][:, :, :], in_=C32[h][:, :, :], func=AF.Copy), reads=[C32[h]], writes=[Cbf[h]])
                        kb.dma("pool", S["HML%d" % d][col0:col0 + 128, :], ho[:, :], reads=[ho])

    def st_mlstm_post(self, l):
        kb = self.kb
        S = self.scr
        with kb.stage():
            ident = kb.tile([128, 128], BF16)
            kb.dma("sp", ident[:, :], self.inp["ident_bf"], writes=[ident])
            nw = kb.tile([128, D], F32)
            self.load_bcast(nw, self.inp["ml_norm_w"][l:l + 1, :])
            NBUF = 2
            hf = [kb.tile([128, D], F32) for _ in range(NBUF)]
            hbk = [kb.tile([128, D], F32) for _ in range(NBUF)]
            og = [kb.tile([128, D], BF16) for _ in range(NBUF)]
            junk = [kb.tile([128, 256], F32) for _ in range(NBUF)]
            ms = [kb.tile([128, 4], F32) for _ in range(NBUF)]
            yb = [kb.tile([128, D], BF16) for _ in range(NBUF)]
            hT = [kb.tile([128, 8, 128], BF16) for _ in range(NBUF)]
            psT = [kb.psum([128, 8, 128], BF16) for _ in range(2)]
            for tix in range(self.T // 128):
                i = tix % NBUF
                col0 = tix * 128
                a, b_, o_, jk, m_, y_ = hf[i], hbk[i], og[i], junk[i], ms[i], yb[i]
                kb.dma("sp", a[:, :], S["HML0"][col0:col0 + 128, :], writes=[a])
                kb.dma("sp", b_[:, :], S["HML1"][col0:col0 + 128, :], writes=[b_])
                kb.dma("sp", o_[:, :], S["OG"][col0:col0 + 128, :], writes=[o_])
                kb.op("dve", lambda e: e.tensor_tensor(out=a[:, :], in0=a[:, :], in1=b_[:, :], op=ALU.add), reads=[a, b_], writes=[a])
                for h in range(4):
                    kb.op("act", lambda e: e.activation(out=jk[:, :], in_=a[:, h * 256:(h + 1) * 256], func=AF.Square, accum_out=m_[:, h:h + 1]), reads=[a], writes=[jk, m_])
                kb.op("act", lambda e: e.activation(out=m_[:, :], in_=m_[:, :], func=AF.Sqrt, scale=1.0 / 256, bias=self.eps_t[:, 0:1]), reads=[m_], writes=[m_])
                kb.op("dve", lambda e: e.reciprocal(out=m_[:, :], in_=m_[:, :]), reads=[m_], writes=[m_])
                kb.op("dve", lambda e: e.tensor_tensor(out=a[:, :].rearrange("p (h e) -> p h e", h=4), in0=a[:, :].rearrange("p (h e) -> p h e", h=4), in1=m_[:, 0:4].unsqueeze(2).to_broadcast([128, 4, 256]), op=ALU.mult), reads=[a, m_], writes=[a])
                kb.op("dve", lambda e: e.tensor_tensor(out=a[:, :], in0=a[:, :], in1=nw[:, :], op=ALU.mult), reads=[a, nw], writes=[a])
                kb.op("dve", lambda e: e.tensor_tensor(out=y_[:, :], in0=a[:, :], in1=o_[:, :], op=ALU.mult), reads=[a, o_], writes=[y_])
                self.transpose_store(y_, S["YML"], col0, ident, psT[tix % 2], hT[i])

    def st_lru(self, l):
        kb = self.kb
        S = self.scr
        TS, NB, CTX = self.TS, self.NB, self.CTX
        segs = [(0, CTX), (CTX, TS)]
        with kb.stage():
            cw = kb.tile([128, 8, 4], F32)
            cbias = kb.tile([128, 8], F32)
            grb = kb.tile([128, 2, 8], F32)
            gib = kb.tile([128, 2, 8], F32)
            lam = kb.tile([128, 2, 8], F32)
            cc_ = kb.tile([128, 2, 8], F32)
            kb.dma("sp", cw[:, :, :], self.inp["lru_conv_wT"][l], writes=[cw])
            kb.dma("sp", cbias[:, :], self.inp["lru_conv_bT"][l], writes=[cbias])
            kb.dma("sp", grb[:, :, :], self.inp["lru_gr_bT"][l], writes=[grb])
            kb.dma("sp", gib[:, :, :], self.inp["lru_gi_bT"][l], writes=[gib])
            kb.dma("sp", lam[:, :, :], self.inp["lru_lambdaT"][l], writes=[lam])
            kb.op("act", lambda e: e.activation(out=cc_[:, :, :], in_=lam[:, :, :], func=AF.Exp, scale=-1.0), reads=[lam], writes=[cc_])
            kb.op("act", lambda e: e.activation(out=cc_[:, :, :], in_=cc_[:, :, :], func=AF.Ln, bias=self.one_t[:, 0:1]), reads=[cc_], writes=[cc_])
            kb.op("dve", lambda e: e.tensor_scalar(out=cc_[:, :, :], in0=cc_[:, :, :], scalar1=-8.0, scalar2=None, op0=ALU.mult), reads=[cc_], writes=[cc_])
            wbd = [[[kb.tile([128, 128], F32) for _ in range(2)] for _ in range(2)] for _ in range(2)]
            x = [kb.tile([128, TS], F32) for _ in range(2)]
            u = [kb.tile([128, TS], F32) for _ in range(2)]
            lg = [kb.tile([128, TS], BF16) for _ in range(2)]
            aa = kb.tile([128, TS], F32)
            bx = kb.tile([128, TS], F32)
            hf = kb.tile([128, TS], F32)
            hb = kb.tile([128, TS], F32)
            yo = [kb.tile([128, TS], BF16) for _ in range(2)]
            rr = [kb.tile([128, 512], F32) for _ in range(2)]
            ii = [kb.tile([128, 512], F32) for _ in range(2)]
            a2 = [kb.tile([128, 512], F32) for _ in range(2)]
            psr = [kb.psum([128, 512], F32) for _ in range(2)]
            psi = [kb.psum([128, 512], F32) for _ in range(2)]
            it = 0
            for cc in range(8):
                wv = wbd[cc % 2]
                for d in range(2):
                    for g, nm in enumerate(("lru_gr_w", "lru_gi_w")):
                        w = wv[d][g]
                        kb.op("dve", lambda e: e.memset(w[:, :], 0.0), writes=[w])
                        for blk in range(2):
                            kb.dma("sp", w[blk * 64:(blk + 1) * 64, blk * 64:(blk + 1) * 64], self.inp[nm][l, d, 2 * cc + blk], writes=[w])
                for b in range(NB):
                    i = it % 2
                    it += 1
                    x_, u_, lg_, yo_ = x[i], u[i], lg[i], yo[i]
                    kb.dma("sp", x_[:, :], S["LX"][cc * 128:(cc + 1) * 128, b * TS:(b + 1) * TS], writes=[x_])
                    kb.dma("sp", lg_[:, :], S["LG"][cc * 128:(cc + 1) * 128, b * TS:(b + 1) * TS], writes=[lg_])
                    for (s0, s1) in segs:
                        kb.op("dve", lambda e: e.tensor_scalar(out=u_[:, s0:s1], in0=x_[:, s0:s1], scalar1=cw[:, cc, 2:3], scalar2=cbias[:, cc:cc + 1], op0=ALU.mult, op1=ALU.add), reads=[x_, cw, cbias], writes=[u_])
                        kb.op("dve", lambda e: e.scalar_tensor_tensor(out=u_[:, s0 + 2:s1], in0=x_[:, s0:s1 - 2], scalar=cw[:, cc, 0:1], in1=u_[:, s0 + 2:s1], op0=ALU.mult, op1=ALU.add), reads=[x_, cw, u_], writes=[u_])
                        kb.op("dve", lambda e: e.scalar_tensor_tensor(out=u_[:, s0 + 1:s1], in0=x_[:, s0:s1 - 1], scalar=cw[:, cc, 1:2], in1=u_[:, s0 + 1:s1], op0=ALU.mult, op1=ALU.add), reads=[x_, cw, u_], writes=[u_])
                        kb.op("dve", lambda e: e.scalar_tensor_tensor(out=u_[:, s0:s1 - 1], in0=x_[:, s0 + 1:s1], scalar=cw[:, cc, 3:4], in1=u_[:, s0:s1 - 1], op0=ALU.mult, op1=ALU.add), reads=[x_, cw, u_], writes=[u_])
                    for d in range(2):
                        for t0 in range(0, TS, 512):
                            tsz = min(512, TS - t0)
                            j = (t0 // 512) % 2
                            r_, i_, a2_ = rr[j], ii[j], a2[j]
                            kb.op("pe", lambda e: e.matmul(psr[j][:, 0:tsz], lhsT=wv[d][0][:, :], rhs=u_[:, t0:t0 + tsz], start=True, stop=True), reads=[wv[d][0], u_], writes=[psr[j]])
                            kb.op("pe", lambda e: e.matmul(psi[j][:, 0:tsz], lhsT=wv[d][1][:, :], rhs=u_[:, t0:t0 + tsz], start=True, stop=True), reads=[wv[d][1], u_], writes=[psi[j]])
                            kb.op("act", lambda e: e.activation(out=r_[:, 0:tsz], in_=psr[j][:, 0:tsz], func=AF.Sigmoid, bias=grb[:, d, cc:cc + 1]), reads=[psr[j], grb], writes=[r_])
                            kb.op("act", lambda e: e.activation(out=i_[:, 0:tsz], in_=psi[j][:, 0:tsz], func=AF.Sigmoid, bias=gib[:, d, cc:cc + 1]), reads=[psi[j], gib], writes=[i_])
                            kb.op("act", lambda e: e.activation(out=aa[:, t0:t0 + tsz], in_=r_[:, 0:tsz], func=AF.Exp, scale=cc_[:, d, cc:cc + 1]), reads=[r_, cc_], writes=[aa])
                            kb.op("dve", lambda e: e.tensor_tensor(out=a2_[:, 0:tsz], in0=aa[:, t0:t0 + tsz], in1=aa[:, t0:t0 + tsz], op=ALU.mult), reads=[aa], writes=[a2_])
                            kb.op("act", lambda e: e.activation(out=a2_[:, 0:tsz], in_=a2_[:, 0:tsz], func=AF.Sqrt, scale=-1.0, bias=self.one_t[:, 0:1]), reads=[a2_], writes=[a2_])
                            kb.op("dve", lambda e: e.tensor_tensor(out=i_[:, 0:tsz], in0=i_[:, 0:tsz], in1=u_[:, t0:t0 + tsz], op=ALU.mult), reads=[i_, u_], writes=[i_])
                            kb.op("dve", lambda e: e.tensor_tensor(out=bx[:, t0:t0 + tsz], in0=i_[:, 0:tsz], in1=a2_[:, 0:tsz], op=ALU.mult), reads=[i_, a2_], writes=[bx])
                        if d == 0:
                            kb.op("dve", lambda e: e.tensor_tensor_scan(out=hf[:, :], data0=aa[:, :], data1=bx[:, :], initial=0.0, op0=ALU.mult, op1=ALU.add), reads=[aa, bx], writes=[hf])
                        else:
                            kb.op("dve", lambda e: e.tensor_tensor_scan(out=hb[:, 0:CTX][:, ::-1], data0=aa[:, 0:CTX][:, ::-1], data1=bx[:, 0:CTX][:, ::-1], initial=0.0, op0=ALU.mult, op1=ALU.add), reads=[aa, bx], writes=[hb])
                            kb.op("dve", lambda e: e.tensor_tensor_scan(out=hb[:, CTX:TS][:, ::-1], data0=aa[:, CTX:TS][:, ::-1], data1=bx[:, CTX:TS][:, ::-1], initial=hb[:, 0:1], op0=ALU.mult, op1=ALU.add), reads=[aa, bx, hb], writes=[hb])
                    kb.op("dve", lambda e: e.tensor_tensor(out=hf[:, :], in0=hf[:, :], in1=hb[:, :], op=ALU.add), reads=[hf, hb], writes=[hf])
                    kb.op("dve", lambda e: e.tensor_tensor(out=yo_[:, :], in0=hf[:, :], in1=lg_[:, :], op=ALU.mult), reads=[hf, lg_], writes=[yo_])
                    kb.dma("pool", S["YLRU"][cc * 128:(cc + 1) * 128, b * TS:(b + 1) * TS], yo_[:, :], reads=[yo_])

    def st_rwkv(self, l):
        self.st_rw_shift(l)
        self.st_rw_lowrank(l)
        self.st_rw_core(l)

    def st_rw_shift(self, l):
        kb = self.kb
        S = self.scr
        TS, NB, CTX = self.TS, self.NB, self.CTX
        segs = [(0, CTX), (CTX, TS)]
        with kb.stage():
            mu = kb.tile([128, 27], F32)
            kb.dma("sp", mu[:, :], self.inp["rw_muT"][l], writes=[mu])
            x = [kb.tile([128, TS], F32) for _ in range(2)]
            tm = [kb.tile([128, TS], F32) for _ in range(2)]
            ob = [kb.tile([128, TS], BF16) for _ in range(2)]
            it = 0
            for ch in range(27):
                for b in range(NB):
                    i = it % 2
                    it += 1
                    x_, t_, o_ = x[i], tm[i], ob[i]
                    kb.dma("sp", x_[:, :], S["RW"][ch * 128:(ch + 1) * 128, b * TS:(b + 1) * TS], writes=[x_])
                    for (s0, s1) in segs:
                        kb.op("dve", lambda e: e.tensor_tensor(out=t_[:, s0 + 1:s1 - 1], in0=x_[:, s0:s1 - 2], in1=x_[:, s0 + 2:s1], op=ALU.add), reads=[x_], writes=[t_])
                        kb.op("dve", lambda e: e.tensor_copy(out=t_[:, s0:s0 + 1], in_=x_[:, s0 + 1:s0 + 2]), reads=[x_], writes=[t_])
                        kb.op("dve", lambda e: e.tensor_copy(out=t_[:, s1 - 1:s1], in_=x_[:, s1 - 2:s1 - 1]), reads=[x_], writes=[t_])
                    kb.op("dve", lambda e: e.scalar_tensor_tensor(out=t_[:, :], in0=t_[:, :], scalar=0.5, in1=x_[:, :], op0=ALU.mult, op1=ALU.subtract), reads=[t_, x_], writes=[t_])
                    kb.op("dve", lambda e: e.scalar_tensor_tensor(out=t_[:, :], in0=t_[:, :], scalar=mu[:, ch:ch + 1], in1=x_[:, :], op0=ALU.mult, op1=ALU.add), reads=[t_, x_, mu], writes=[t_])
                    cols = slice(b * TS, (b + 1) * TS)
                    if ch < 24:
                        kb.dma("pool", S["RS"][ch * 128:(ch + 1) * 128, cols], t_[:, :], reads=[t_])
                    else:
                        fn = (AF.Tanh, AF.Copy, AF.Sigmoid)[ch - 24]
                        dst = (S["TD"], S["AD"], S["GD"])[ch - 24]
                        kb.op("act", lambda e: e.activation(out=o_[:, :], in_=t_[:, :], func=fn), reads=[t_], writes=[o_])
                        kb.dma("pool", dst[:, cols], o_[:, :], reads=[o_])

    def st_rw_lowrank(self, l):
        kb = self.kb
        S = self.scr
        for d in range(2):
            holder = {}

            def prep(d=d):
                holder["d0"] = kb.tile([128, 8], F32)
                holder["i0"] = kb.tile([128, 8], F32)
                kb.dma("sp", holder["d0"][:, :], self.inp["rw_decay0T"][l][:, d, :], writes=[holder["d0"]])
                kb.dma("sp", holder["i0"][:, :], self.inp["rw_iclr0T"][l][:, d, :], writes=[holder["i0"]])

            def epi_b(dst, key):
                def epi(ps, c, nsz, t0, tsz):
                    st = self.g_stF[self.g_i % 4]
                    bt = holder[key]
                    kb.op("act", lambda e: e.activation(out=st[0:nsz, 0:tsz], in_=ps[0:nsz, 0:tsz], func=AF.Sigmoid, bias=bt[0:nsz, c // 128:c // 128 + 1]), reads=[ps, bt], writes=[st])
                    kb.dma("pool", dst[c:c + nsz, t0:t0 + tsz], st[0:nsz, 0:tsz], reads=[st])
                return epi
            self.gemm(S["TD"][d * 64:(d + 1) * 64, :], 64, self.inp["rw_decay_up"][l, d], [(0, D, "FM", epi_b(S["SG%d" % d], "d0"), prep)])
            self.gemm(S["AD"][d * 64:(d + 1) * 64, :], 64, self.inp["rw_iclr_up"][l, d], [(0, D, "FM", epi_b(S["AA%d" % d], "i0"), prep)])
        self.gemm(S["GD"], 128, self.inp["rw_gate_up"][l], [(0, D, "FM", self.epi_fm(S["GG"], F32), None)])

    def st_rw_core(self, l):
        kb = self.kb
        S = self.scr
        TS, NB, CTX = self.TS, self.NB, self.CTX
        TB = min(CTX, 512)
        nblk = TS // TB
        cblk = CTX // TB
        ncb = TB // 64
        DS = RW_DECAY_SCALE
        with kb.stage():
            identF = kb.tile([128, 128], F32)
            identB = kb.tile([128, 128], BF16)
            bones = kb.tile([128, 128], F32)
            maskA = [kb.tile([128, 128], F32) for _ in range(2)]
            maskN = [kb.tile([64, 64], F32) for _ in range(2)]
            rmask = [kb.tile([128, TB], F32) for _ in range(2)]
            kb.dma("sp", identF[:, :], self.inp["ident_f"], writes=[identF])
            kb.dma("sp", identB[:, :], self.inp["ident_bf"], writes=[identB])
            kb.dma("sp", bones[:, :], self.inp["bones_f"], writes=[bones])
            for d in range(2):
                kb.dma("sp", maskA[d][:, :], self.inp["maskA"][d], writes=[maskA[d]])
                kb.dma("sp", maskN[d][:, :], self.inp["maskN"][d], writes=[maskN[d]])
                kb.dma("sp", rmask[d][:, :], self.inp["rmask"][d][:, 0:TB], writes=[rmask[d]])
            prm = {}
            for nm in ("rw_k_kT", "rw_k_aT", "rw_r_kT", "rw_ln_wT", "rw_ln_bT"):
                prm[nm] = kb.tile([128, 8], F32)
                kb.dma("sp", prm[nm][:, :], self.inp[nm][l], writes=[prm[nm]])
            omk = kb.tile([128, 8], F32)
            kb.op("dve", lambda e: e.tensor_scalar(out=omk[:, :], in0=prm["rw_k_aT"][:, :], scalar1=-1.0, scalar2=1.0, op0=ALU.mult, op1=ALU.add), reads=[prm["rw_k_aT"]], writes=[omk])
            omk2 = kb.tile([128, 8], F32)
            kb.op("dve", lambda e: e.tensor_scalar(out=omk2[:, :], in0=omk[:, :], scalar1=2.0, scalar2=None, op0=ALU.mult), reads=[omk], writes=[omk2])
            lneps = kb.tile([128, 1], F32)
            kb.op("dve", lambda e: e.memset(lneps[:, :], RW_LN_EPS), writes=[lneps])
            NBUF = 2
            mk = lambda dt=F32: [kb.tile([128, TB], dt) for _ in range(NBUF)]
            rT, kT, vT, sg, aT = mk(), mk(), mk(), mk(), mk()
            kap, w1, w2, E1, sq = mk(), mk(), mk(), mk(), mk()
            QC = [kb.tile([128, ncb, 128], BF16) for _ in range(NBUF)]
            KC = [kb.tile([128, ncb, 128], BF16) for _ in range(NBUF)]
            YT = [kb.tile([64, 2, TB], F32) for _ in range(NBUF)]
            ST32 = kb.tile([128, 64], F32)
            ST = kb.tile([128, 64], BF16)
            AT = kb.tile([128, 2, 128], BF16)
            P = [kb.tile([64, 2, 64], F32) for _ in range(2)]
            PT = [kb.tile([64, 2, 64], F32) for _ in range(2)]
            TT = [kb.tile([64, 2, 64], F32) for _ in range(2)]
            TTp = kb.tile([64, 2, 128], F32)
            kb.op("dve", lambda e: e.memset(TTp[:, :, :], 0.0), writes=[TTp])
            UV = kb.tile([128, 2, 64], BF16)
            nZ = kb.tile([64, 2, 64], F32)
            KTT = kb.tile([128, 128], BF16)
            stmp = kb.tile([128, 64], F32)
            psA = kb.psum([128, 2, 128], F32)
            psN = kb.psum([64, 2, 64], F32)
            psM = [kb.psum([64, 2, 64], F32) for _ in range(2)]
            psZ = kb.psum([128, 2, 64], F32)
            psY = kb.psum([64, 2, 64], F32)
            psK = kb.psum([128, 128], BF16)
            psS = kb.psum([128, 512], F32)
            v3 = lambda t: t[:, :].rearrange("p (c e) -> p c e", e=64)
            ybuf = Buf()
            it = 0
            for cc in range(8):
                rows = slice(cc * 128, (cc + 1) * 128)
                for b in range(NB):
                    for d in range(2):
                        kb.op("dve", lambda e: e.memset(ST32[:, :], 0.0), writes=[ST32])
                        kb.op("dve", lambda e: e.memset(ST[:, :], 0.0), writes=[ST])
                        bl = list(range(nblk)) if d == 0 else list(range(cblk - 1, -1, -1)) + list(range(nblk - 1, cblk - 1, -1))
                        for blk in bl:
                            i = it % NBUF
                            it += 1
                            cols = slice(b * TS + blk * TB, b * TS + (blk + 1) * TB)
                            r_, k_, v_, sg_, a_ = rT[i], kT[i], vT[i], sg[i], aT[i]
                            kap_, w1_, w2_, E1_, sq_ = kap[i], w1[i], w2[i], E1[i], sq[i]
                            QC_, KC_, yt = QC[i], KC[i], YT[i]
                            kb.dma("sp", r_[:, :], S["RS"][rows, cols], writes=[r_])
                            kb.dma("sp", k_[:, :], S["RS"][D + cc * 128:D + (cc + 1) * 128, cols], writes=[k_])
                            kb.dma("sp", v_[:, :], S["RS"][2 * D + cc * 128:2 * D + (cc + 1) * 128, cols], writes=[v_])
                            kb.dma("sp", sg_[:, :], S["SG%d" % d][rows, cols], writes=[sg_])
                            kb.dma("sp", a_[:, :], S["AA%d" % d][rows, cols], writes=[a_])
                            kb.op("dve", lambda e: e.tensor_scalar(out=kap_[:, :], in0=k_[:, :], scalar1=prm["rw_k_kT"][:, cc:cc + 1], scalar2=None, op0=ALU.mult), reads=[k_, prm["rw_k_kT"]], writes=[kap_])
                            kb.op("dve", lambda e: e.tensor_tensor(out=sq_[:, :], in0=kap_[:, :], in1=kap_[:, :], op=ALU.mult), reads=[kap_], writes=[sq_])
                            kb.op("pe", lambda e: e.matmul(psS[:, 0:TB], lhsT=bones[:, :], rhs=sq_[:, :], start=True, stop=True), reads=[bones, sq_], writes=[psS])
                            kb.op("act", lambda e: e.activation(out=sq_[:, :], in_=psS[:, 0:TB], func=AF.Sqrt), reads=[psS], writes=[sq_])
                            kb.op("dve", lambda e: e.tensor_scalar(out=sq_[:, :], in0=sq_[:, :], scalar1=1e-12, scalar2=None, op0=ALU.max), reads=[sq_], writes=[sq_])
                            kb.op("dve", lambda e: e.reciprocal(out=sq_[:, :], in_=sq_[:, :]), reads=[sq_], writes=[sq_])
                            kb.op("dve", lambda e: e.tensor_tensor(out=kap_[:, :], in0=kap_[:, :], in1=sq_[:, :], op=ALU.mult), reads=[kap_, sq_], writes=[kap_])
                            kb.op("dve", lambda e: e.tensor_scalar(out=w1_[:, :], in0=a_[:, :], scalar1=prm["rw_k_aT"][:, cc:cc + 1], scalar2=omk[:, cc:cc + 1], op0=ALU.mult, op1=ALU.add), reads=[a_, prm["rw_k_aT"], omk], writes=[w1_])
                            kb.op("dve", lambda e: e.tensor_tensor(out=w1_[:, :], in0=w1_[:, :], in1=k_[:, :], op=ALU.mult), reads=[w1_, k_], writes=[w1_])
                            if d == 0:
                                kb.op("dve", lambda e: e.tensor_tensor_scan(out=w2_[:, :], data0=rmask[0][:, :], data1=sg_[:, :], initial=0.0, op0=ALU.mult, op1=ALU.add), reads=[rmask[0], sg_], writes=[w2_])
                            else:
                                kb.op("dve", lambda e: e.tensor_tensor_scan(out=w2_[:, ::-1], data0=rmask[1][:, ::-1], data1=sg_[:, ::-1], initial=0.0, op0=ALU.mult, op1=ALU.add), reads=[rmask[1], sg_], writes=[w2_])
                            kb.op("act", lambda e: e.activation(out=E1_[:, :], in_=w2_[:, :], func=AF.Exp, scale=-DS), reads=[w2_], writes=[E1_])
                            kb.op("dve", lambda e: e.tensor_tensor(out=QC_[:, :, 64:128], in0=v3(r_), in1=v3(E1_), op=ALU.mult), reads=[r_, E1_], writes=[QC_])
                            kb.op("dve", lambda e: e.tensor_tensor(out=sg_[:, :], in0=w2_[:, :], in1=sg_[:, :], op=ALU.subtract), reads=[w2_, sg_], writes=[sg_])
                            kb.op("act", lambda e: e.activation(out=sg_[:, :], in_=sg_[:, :], func=AF.Exp, scale=-DS), reads=[sg_], writes=[sg_])
                            kb.op("dve", lambda e: e.tensor_tensor(out=QC_[:, :, 0:64], in0=v3(kap_), in1=v3(sg_), op=ALU.mult), reads=[kap_, sg_], writes=[QC_])
                            kb.op("act", lambda e: e.activation(out=w2_[:, :], in_=w2_[:, :], func=AF.Exp, scale=DS), reads=[w2_], writes=[w2_])
                            kb.op("dve", lambda e: e.tensor_tensor(out=KC_[:, :, 0:64], in0=v3(w1_), in1=v3(w2_), op=ALU.mult), reads=[w1_, w2_], writes=[KC_])
                            kb.op("dve", lambda e: e.tensor_tensor(out=a_[:, :], in0=a_[:, :], in1=kap_[:, :], op=ALU.mult), reads=[a_, kap_], writes=[a_])
                            kb.op("dve", lambda e: e.tensor_tensor(out=KC_[:, :, 64:128], in0=v3(a_), in1=v3(w2_), op=ALU.mult), reads=[a_, w2_], writes=[KC_])
                            for c in (range(ncb) if d == 0 else range(ncb - 1, -1, -1)):
                                p0 = c * 64
                                for h in range(2):
                                    hs = slice(h * 64, (h + 1) * 64)
                                    kb.op("pe", lambda e: e.matmul(psA[:, h, :], lhsT=KC_[hs, c, :], rhs=QC_[hs, c, :], start=True, stop=True), reads=[KC_, QC_], writes=[psA])
                                    kb.op("pe", lambda e: e.matmul(psN[:, h, :], lhsT=QC_[hs, c, 0:64], rhs=KC_[hs, c, 64:128], start=True, stop=True), reads=[KC_, QC_], writes=[psN])
                                    kb.op("pe", lambda e: e.matmul(psM[1][:, h, :], lhsT=KC_[hs, c, 64:128], rhs=QC_[hs, c, 0:64], start=True, stop=True), reads=[KC_, QC_], writes=[psM[1]])
                                    kb.op("pe", lambda e: e.transpose(psZ[0:64, h, :], v_[hs, p0:p0 + 64], identF[hs, hs]), reads=[v_, identF], writes=[psZ])
                                kb.op("pe", lambda e: e.transpose(psK[:, :], KC_[:, c, :], identB[:, :]), reads=[KC_, identB], writes=[psK])
                                kb.op("dve", lambda e: e.tensor_tensor(out=AT[:, :, :], in0=psA[:, :, :], in1=maskA[d][:, :].unsqueeze(1).to_broadcast([128, 2, 128]), op=ALU.mult), reads=[psA, maskA[d]], writes=[AT])
                                kb.op("act", lambda e: e.activation(out=UV[0:64, :, :], in_=psZ[0:64, :, :], func=AF.Copy), reads=[psZ], writes=[UV])
                                kb.op("act", lambda e: e.activation(out=KTT[:, :], in_=psK[:, :], func=AF.Copy), reads=[psK], writes=[KTT])
                                kb.op("dve", lambda e: e.scalar_tensor_tensor(out=P[0][:, :, :], in0=psN[:, :, :], scalar=-1.0, in1=maskN[d][:, :].unsqueeze(1).to_broadcast([64, 2, 64]), op0=ALU.mult, op1=ALU.mult), reads=[psN, maskN[d]], writes=[P[0]])
                                kb.op("dve", lambda e: e.scalar_tensor_tensor(out=PT[0][:, :, :], in0=psM[1][:, :, :], scalar=-1.0, in1=maskA[d][0:64, 0:64].unsqueeze(1).to_broadcast([64, 2, 64]), op0=ALU.mult, op1=ALU.mult), reads=[psM[1], maskA[d]], writes=[PT[0]])
                                kb.op("dve", lambda e: e.tensor_tensor(out=TT[0][:, :, :], in0=PT[0][:, :, :], in1=identF[0:64, 0:64].unsqueeze(1).to_broadcast([64, 2, 64]), op=ALU.add), reads=[PT[0], identF], writes=[TT[0]])
                                for m in range(1, 6):
                                    pc_, pn_ = P[(m - 1) % 2], P[m % 2]
                                    tc_, tn_ = PT[(m - 1) % 2], PT[m % 2]
                                    for h in range(2):
                                        kb.op("pe", lambda e: e.matmul(psM[0][:, h, :], lhsT=tc_[:, h, :], rhs=pc_[:, h, :], start=True, stop=True), reads=[tc_, pc_], writes=[psM[0]])
                                    if m < 5:
                                        for h in range(2):
                                            kb.op("pe", lambda e: e.matmul(psM[1][:, h, :], lhsT=pc_[:, h, :], rhs=tc_[:, h, :], start=True, stop=True), reads=[tc_, pc_], writes=[psM[1]])
                                    kb.op("act", lambda e: e.activation(out=pn_[:, :, :], in_=psM[0][:, :, :], func=AF.Copy), reads=[psM[0]], writes=[pn_])
                                    if m < 5:
                                        kb.op("dve", lambda e: e.tensor_copy(out=tn_[:, :, :], in_=psM[1][:, :, :]), reads=[psM[1]], writes=[tn_])
                                    tt_c, tt_n = TT[(m - 1) % 2], TT[m % 2]
                                    for h in range(2):
                                        kb.op("pe", lambda e: e.matmul(psN[:, h, :], lhsT=pn_[:, h, :], rhs=tt_c[:, h, :], start=True, stop=True), reads=[pn_, tt_c], writes=[psN])
                                    if m < 5:
                                        kb.op("dve", lambda e: e.tensor_tensor(out=tt_n[:, :, :], in0=psN[:, :, :], in1=tt_c[:, :, :], op=ALU.add), reads=[psN, tt_c], writes=[tt_n])
                                    else:
                                        kb.op("dve", lambda e: e.tensor_tensor(out=TTp[:, :, 64:128], in0=psN[:, :, :], in1=tt_c[:, :, :], op=ALU.add), reads=[psN, tt_c], writes=[TTp])
                                for h in range(2):
                                    hs = slice(h * 64, (h + 1) * 64)
                                    kb.op("pe", lambda e: e.matmul(psY[:, h, :], lhsT=QC_[hs, c, 0:64], rhs=ST[hs, :], start=True, stop=False), reads=[QC_, ST], writes=[psY])
                                    kb.op("pe", lambda e: e.matmul(psY[:, h, :], lhsT=AT[0:64, h, 0:64], rhs=UV[0:64, h, :], start=False, stop=True), reads=[AT, UV], writes=[psY])
                                kb.op("dve", lambda e: e.tensor_scalar(out=nZ[:, :, :], in0=psY[:, :, :], scalar1=-1.0, scalar2=None, op0=ALU.mult), reads=[psY], writes=[nZ])
                                for h in range(2):
                                    kb.op("pe", lambda e: e.matmul(psZ[:, h, :], lhsT=TTp[:, h, :], rhs=nZ[:, h, :], start=True, stop=True), reads=[TTp, nZ], writes=[psZ])
                                kb.op("act", lambda e: e.activation(out=UV[64:128, :, :], in_=psZ[64:128, :, :], func=AF.Copy), reads=[psZ], writes=[UV])
                                for h in range(2):
                                    hs = slice(h * 64, (h + 1) * 64)
                                    kb.op("pe", lambda e: e.matmul(psY[:, h, :], lhsT=ST[hs, :], rhs=QC_[hs, c, 64:128], start=True, stop=False), reads=[QC_, ST], writes=[psY])
                                    kb.op("pe", lambda e: e.matmul(psY[:, h, :], lhsT=UV[:, h, :], rhs=AT[:, h, 64:128], start=False, stop=True), reads=[AT, UV], writes=[psY])
                                kb.op("act", lambda e: e.activation(out=yt[:, :, p0:p0 + 64], in_=psY[:, :, :], func=AF.Copy), reads=[psY], writes=[yt])
                                kb.op("pe", lambda e: e.matmul(psS[:, 0:64], lhsT=KTT[:, :], rhs=UV[:, 1, :], start=True, stop=True), reads=[KTT, UV], writes=[psS])
                                kb.op("pe", lambda e: e.matmul(psS[0:64, 0:64], lhsT=KTT[:, 0:64], rhs=UV[:, 0, :], start=True, stop=True), reads=[KTT, UV], writes=[psS])
                                wcol = p0 + 63 if d == 0 else p0
                                kb.op("dve", lambda e: e.tensor_tensor(out=stmp[:, :], in0=psS[:, 0:64], in1=ST32[:, :], op=ALU.add), reads=[psS, ST32], writes=[stmp])
                                kb.op("dve", lambda e: e.tensor_scalar(out=ST32[:, :], in0=stmp[:, :], scalar1=E1_[:, wcol:wcol + 1], scalar2=None, op0=ALU.mult), reads=[stmp, E1_], writes=[ST32])
                                kb.op("act", lambda e: e.activation(out=ST[:, :], in_=ST32[:, :], func=AF.Copy), reads=[ST32], writes=[ST])
                            kb.dma("pool", S["YT%d" % d][rows, cols].rearrange("(h v) t -> v h t", h=2), yt[:, :, :], reads=[yt], writes=[ybuf])
            kb.barrier()
            PB = 512
            mk2 = lambda dt=F32: [kb.tile([128, PB], dt) for _ in range(2)]
            r2, k2, v2, g2, a0, a1, y0, y1, tq, uq = mk2(), mk2(), mk2(), mk2(), mk2(), mk2(), mk2(), mk2(), mk2(), mk2()
            yo = mk2(BF16)
            it = 0
            for cc in range(8):
                rows = slice(cc * 128, (cc + 1) * 128)
                for t0 in range(0, self.T, PB):
                    tsz = min(PB, self.T - t0)
                    i = it % 2
                    it += 1
                    cs_ = slice(t0, t0 + tsz)
                    ld = [(r2[i], S["RS"][rows, cs_]), (k2[i], S["RS"][D + cc * 128:D + (cc + 1) * 128, cs_]), (v2[i], S["RS"][2 * D + cc * 128:2 * D + (cc + 1) * 128, cs_]),
                          (g2[i], S["GG"][rows, cs_]), (a0[i], S["AA0"][rows, cs_]), (a1[i], S["AA1"][rows, cs_]), (y0[i], S["YT0"][rows, cs_]), (y1[i], S["YT1"][rows, cs_])]
                    for tl, src in ld:
                        kb.dma("sp", tl[:, 0:tsz], src, writes=[tl])
                    r_, k_, v_, g_, a0_, a1_, y_, y1_, tq_, uq_, yo_ = r2[i], k2[i], v2[i], g2[i], a0[i], a1[i], y0[i], y1[i], tq[i], uq[i], yo[i]
                    sl = slice(0, tsz)
                    kb.op("dve", lambda e: e.tensor_tensor(out=a0_[:, sl], in0=a0_[:, sl], in1=a1_[:, sl], op=ALU.add), reads=[a0_, a1_], writes=[a0_])
                    kb.op("dve", lambda e: e.tensor_scalar(out=a0_[:, sl], in0=a0_[:, sl], scalar1=prm["rw_k_aT"][:, cc:cc + 1], scalar2=omk2[:, cc:cc + 1], op0=ALU.mult, op1=ALU.add), reads=[a0_, prm["rw_k_aT"], omk2], writes=[a0_])
                    kb.op("dve", lambda e: e.tensor_tensor(out=a0_[:, sl], in0=a0_[:, sl], in1=k_[:, sl], op=ALU.mult), reads=[a0_, k_], writes=[a0_])
                    kb.op("dve", lambda e: e.scalar_tensor_tensor(out=a0_[:, sl], in0=a0_[:, sl], scalar=prm["rw_r_kT"][:, cc:cc + 1], in1=r_[:, sl], op0=ALU.mult, op1=ALU.mult), reads=[a0_, prm["rw_r_kT"], r_], writes=[a0_])
                    kb.op("pe", lambda e: e.matmul(psS[:, sl], lhsT=bones[:, :], rhs=a0_[:, sl], start=True, stop=True), reads=[bones, a0_], writes=[psS])
                    kb.op("dve", lambda e: e.tensor_tensor(out=a1_[:, sl], in0=psS[:, sl], in1=v_[:, sl], op=ALU.mult), reads=[psS, v_], writes=[a1_])
                    kb.op("dve", lambda e: e.tensor_tensor(out=y_[:, sl], in0=y_[:, sl], in1=y1_[:, sl], op=ALU.add), reads=[y_, y1_], writes=[y_])
                    kb.op("pe", lambda e: e.matmul(psS[:, sl], lhsT=bones[:, :], rhs=y_[:, sl], start=True, stop=True), reads=[bones, y_], writes=[psS])
                    kb.op("dve", lambda e: e.scalar_tensor_tensor(out=tq_[:, sl], in0=psS[:, sl], scalar=-1.0 / 64, in1=y_[:, sl], op0=ALU.mult, op1=ALU.add), reads=[psS, y_], writes=[tq_])
                    kb.op("dve", lambda e: e.tensor_tensor(out=uq_[:, sl], in0=tq_[:, sl], in1=tq_[:, sl], op=ALU.mult), reads=[tq_], writes=[uq_])
                    kb.op("pe", lambda e: e.matmul(psS[:, sl], lhsT=bones[:, :], rhs=uq_[:, sl], start=True, stop=True), reads=[bones, uq_], writes=[psS])
                    kb.op("act", lambda e: e.activation(out=uq_[:, sl], in_=psS[:, sl], func=AF.Sqrt, scale=1.0 / 64, bias=lneps[:, 0:1]), reads=[psS, lneps], writes=[uq_])
                    kb.op("dve", lambda e: e.reciprocal(out=uq_[:, sl], in_=uq_[:, sl]), reads=[uq_], writes=[uq_])
                    kb.op("dve", lambda e: e.tensor_tensor(out=tq_[:, sl], in0=tq_[:, sl], in1=uq_[:, sl], op=ALU.mult), reads=[tq_, uq_], writes=[tq_])
                    kb.op("dve", lambda e: e.tensor_scalar(out=tq_[:, sl], in0=tq_[:, sl], scalar1=prm["rw_ln_wT"][:, cc:cc + 1], scalar2=prm["rw_ln_bT"][:, cc:cc + 1], op0=ALU.mult, op1=ALU.add), reads=[tq_, prm["rw_ln_wT"], prm["rw_ln_bT"]], writes=[tq_])
                    kb.op("dve", lambda e: e.tensor_tensor(out=tq_[:, sl], in0=tq_[:, sl], in1=a1_[:, sl], op=ALU.add), reads=[tq_, a1_], writes=[tq_])
                    kb.op("dve", lambda e: e.tensor_tensor(out=yo_[:, sl], in0=tq_[:, sl], in1=g_[:, sl], op=ALU.mult), reads=[tq_, g_], writes=[yo_])
                    kb.dma("pool", S["YRW"][rows, cs_], yo_[:, sl], reads=[yo_])

    def build(self):
        nc, kb = self.nc, self.kb
        NB, CTX, LAT, L, T, TS = self.NB, self.CTX, self.LAT, self.L, self.T, self.TS
        self.din("x", [NB, LAT, D])
        self.din("ctx", [NB, CTX, D])
        self.din("condT", [128, 8, 3])
        self.din("w_mod", [L, D, 6 * D])
        self.din("b_mod", [L, 6 * D])
        self.din("norm1_w", [L, D])
        self.din("norm2_w", [L, D])
        self.din("w_in", [L, D, INW])
        self.din("ml_ig_b", [L, 2, 4])
        self.din("ml_fg_b", [L, 2, 4])
        self.din("ml_norm_w", [L, D])
        self.din("lru_conv_wT", [L, 128, 8, 4])
        self.din("lru_conv_bT", [L, 128, 8])
        self.din("lru_gr_w", [L, 2, 16, 64, 64])
        self.din("lru_gr_bT", [L, 128, 2, 8])
        self.din("lru_gi_w", [L, 2, 16, 64, 64])
        self.din("lru_gi_bT", [L, 128, 2, 8])
        self.din("lru_lambdaT", [L, 128, 2, 8])
        self.din("rw_muT", [L, 128, 27])
        self.din("rw_decay0T", [L, 128, 2, 8])
        self.din("rw_iclr0T", [L, 128, 2, 8])
        self.din("rw_decay_up", [L, 2, 64, D])
        self.din("rw_iclr_up", [L, 2, 64, D])
        self.din("rw_gate_up", [L, 128, D])
        for nm in ("rw_k_kT", "rw_k_aT", "rw_r_kT", "rw_ln_wT", "rw_ln_bT"):
            self.din(nm, [L, 128, 8])
        self.din("bones_f", [128, 128])
        self.din("maskA", [2, 128, 128])
        self.din("maskN", [2, 64, 64])
        self.din("rmask", [2, 128, TS])
        self.din("out_ml", [L, D, D])
        self.din("out_lru", [L, D, D])
        self.din("out_rw", [L, D, D])
        self.din("w_out", [L, D, D])
        self.din("w_ffn_in", [L, D, 2 * FFN])
        self.din("w_ffn_out", [L, FFN, D])
        self.din("final_norm_w", [D])
        self.din("ident_bf", [128, 128], BF16)
        self.din("ident_f", [128, 128])
        self.din("ones_f", [128, 128])
        self.din("tri", [2, 128, 128])
        self.din("negm", [2, 128, 128])
        self.out_ap = self.nc.dram_tensor("out", [NB, LAT, D], F32, kind="ExternalOutput").ap()
        self.dscr("XR", [NB, TS, D], F32)
        self.dscr("MOD", [L, 3, 6 * D], F32)
        self.dscr("HT", [D, T], BF16)
        self.dscr("QT", [D, T], BF16)
        self.dscr("KT", [D, T], BF16)
        self.dscr("KTM", [T, D], BF16)
        self.dscr("VTM", [T, D], BF16)
        self.dscr("OG", [T, D], BF16)
        self.dscr("IFG", [T, 16], F32)
        self.dscr("LX", [D, T], F32)
        self.dscr("LG", [D, T], BF16)
        self.dscr("RW", [RWSEG, T], F32)
        self.dscr("MG", [3 * D, T], BF16)
        self.dscr("HML0", [T, D], F32)
        self.dscr("HML1", [T, D], F32)
        self.dscr("YML", [D, T], BF16)
        self.dscr("YLRU", [D, T], BF16)
        self.dscr("YRW", [D, T], BF16)
        self.dscr("YACC", [D, T], F32)
        self.dscr("YM", [D, T], BF16)
        self.dscr("RS", [3 * D, T], F32)
        self.dscr("TD", [128, T], BF16)
        self.dscr("AD", [128, T], BF16)
        self.dscr("GD", [128, T], BF16)
        for d in range(2):
            self.dscr("SG%d" % d, [D, T], F32)
            self.dscr("AA%d" % d, [D, T], F32)
            self.dscr("YT%d" % d, [D, T], F32)
        self.dscr("GG", [D, T], F32)
        self.dscr("AG", [FFN, T], BF16)
        self.dscr("AT", [FFN, T], BF16)
        self.eps_t = kb.tile([128, 1], F32, "eps")
        self.one_t = kb.tile([128, 1], F32, "one")
        kb.op("dve", lambda e: e.memset(self.eps_t[:, :], EPS), writes=[self.eps_t])
        kb.op("dve", lambda e: e.memset(self.one_t[:, :], 1.0), writes=[self.one_t])
        upto = self.cfg.get("upto", None)
        seq = [("init", lambda: self.st_init()), ("adaln", lambda: self.st_adaln())]
        for l in range(L):
            seq += [("norm1", lambda l=l: self.st_norm(l, 1)), ("win", lambda l=l: self.st_win(l)),
                    ("mlstm", lambda l=l: self.st_mlstm(l)), ("mlpost", lambda l=l: self.st_mlstm_post(l)),
                    ("lru", lambda l=l: self.st_lru(l)), ("rwkv", lambda l=l: self.st_rwkv(l)),
                    ("merge", lambda l=l: self.st_merge(l)), ("wout", lambda l=l: self.st_wout(l)),
                    ("ffn", lambda l=l: self.st_ffn(l))]
        seq += [("final", lambda: self.st_norm(0, 1, final=True))]
        skip = self.cfg.get("skip", ())
        for name, fn in seq:
            if name not in skip:
                fn()
            if upto is not None and name == upto:
                break
        kb.finish()


def host_consts():
    c = {}
    c["ident_bf"] = np.eye(128, dtype=np.float32).astype(ml_dtypes.bfloat16)
    c["ident_f"] = np.eye(128, dtype=np.float32)
    c["ones_f"] = np.ones((128, 128), np.float32)
    s = np.arange(128)[:, None]
    t = np.arange(128)[None, :]
    tri = np.stack([(s <= t), (s >= t)]).astype(np.float32)
    c["tri"] = tri
    c["negm"] = ((1.0 - tri) * -30000.0).astype(np.float32)
    bo = np.zeros((128, 128), np.float32)
    bo[:64, :64] = 1.0
    bo[64:, 64:] = 1.0
    c["bones_f"] = bo
    j = np.arange(64)[:, None]
    t = np.arange(64)[None, :]
    mA = []
    mN = []
    for d in range(2):
        strict = (j < t) if d == 0 else (j > t)
        incl = (j <= t) if d == 0 else (j >= t)
        blk = np.concatenate([strict, incl], 1).astype(np.float32)
        mA.append(np.concatenate([blk, blk], 0))
        mN.append(strict.T.astype(np.float32))
    c["maskA"] = np.stack(mA)
    c["maskN"] = np.stack(mN)
    return c


def chanT(a):
    a = np.asarray(a, np.float32)
    lead = a.shape[:-1]
    a = a.reshape(lead + (8, 128))
    return np.ascontiguousarray(np.moveaxis(a, -1, 0))


def make_in_maps(inputs, cfg, ncores):
    NB, L = cfg["NB"], cfg["DEPTH"]
    consts = host_consts()
    f = lambda k: np.ascontiguousarray(np.asarray(inputs[k], np.float32)[:L])
    shared = {
        "w_mod": f("w_mod"), "b_mod": f("b_mod"), "norm1_w": f("norm1_w"), "norm2_w": f("norm2_w"),
        "w_in": f("w_in"), "ml_ig_b": f("ml_ig_b"), "ml_fg_b": f("ml_fg_b"), "ml_norm_w": f("ml_norm_w"),
        "lru_gr_w": f("lru_gr_w"), "lru_gi_w": f("lru_gi_w"),
        "out_ml": f("out_ml"), "out_lru": f("out_lru"), "out_rw": f("out_rw"), "w_out": f("w_out"),
        "w_ffn_in": f("w_ffn_in"), "w_ffn_out": f("w_ffn_out"),
        "final_norm_w": np.asarray(inputs["final_norm_w"], np.float32),
    }
    cw = np.asarray(inputs["lru_conv_w"], np.float32)[:L]
    shared["lru_conv_wT"] = np.ascontiguousarray(np.stack([np.moveaxis(chanT(cw[l]), 1, 2) for l in range(L)]))
    shared["lru_conv_bT"] = np.ascontiguousarray(np.stack([chanT(np.asarray(inputs["lru_conv_b"], np.float32)[l]) for l in range(L)]))
    for k in ("lru_gr_b", "lru_gi_b", "lru_lambda"):
        a = np.asarray(inputs[k], np.float32)[:L]
        shared[k + "T"] = np.ascontiguousarray(np.stack([chanT(a[l]) for l in range(L)]))
    TS = cfg["CTX"] + cfg["LAT"]
    tt = np.arange(TS)
    rm = np.stack([(tt % 64 != 0), (tt % 64 != 63)]).astype(np.float32)
    shared["rmask"] = np.ascontiguousarray(np.broadcast_to(rm[:, None, :], (2, 128, TS)))
    mu = np.asarray(inputs["rw_mu"], np.float32)[:L]
    shared["rw_muT"] = np.ascontiguousarray(mu.reshape(L, 27, 128).transpose(0, 2, 1))
    for k in ("rw_decay0", "rw_iclr0"):
        a = np.asarray(inputs[k], np.float32)[:L]
        shared[k + "T"] = np.ascontiguousarray(np.stack([chanT(a[l]) for l in range(L)]))
    for k in ("rw_k_k", "rw_k_a", "rw_r_k", "rw_ln_w", "rw_ln_b"):
        a = np.asarray(inputs[k], np.float32)[:L]
        shared[k + "T"] = np.ascontiguousarray(np.stack([chanT(a[l]) for l in range(L)]))
    for k in ("rw_decay_up", "rw_iclr_up", "rw_gate_up"):
        shared[k] = f(k)
    shared.update(consts)
    x = np.asarray(inputs["x"], np.float32)
    ctx = np.asarray(inputs["ctx"], np.float32)
    c = np.asarray(inputs["c"], np.float32)
    cc = np.asarray(inputs["c_ctx"], np.float32)
    maps = []
    for i in range(ncores):
        m = dict(shared)
        m["x"] = np.ascontiguousarray(x[i * NB:(i + 1) * NB])
        m["ctx"] = np.ascontiguousarray(ctx[i * NB:(i + 1) * NB])
        cond = np.concatenate([c[i * NB:(i + 1) * NB], cc[None, :]], 0)
        m["condT"] = np.ascontiguousarray(cond.T.reshape(8, 128, 3).transpose(1, 0, 2))
        maps.append(m)
    return maps


def kernel(**inputs):
    cfg = {"NB": 2, "CTX": 256, "LAT": 4096, "DEPTH": 4}
    nc = bass.Bass("TRN2", target_bir_lowering=False)
    p = Prog(nc, cfg)
    p.build()
    maps = make_in_maps(inputs, cfg, NCORES)
    res = run_bass_kernel_spmd(nc, maps, core_ids=list(range(NCORES)))
    return np.concatenate([np.asarray(r["out"], np.float32) for r in res.results], axis=0)
```

```python
import math
from contextlib import ExitStack, contextmanager
import numpy as np
import ml_dtypes
import concourse.bass as bass
import concourse.mybir as mybir
from concourse.bass_utils import run_bass_kernel_spmd

F32 = mybir.dt.float32
BF16 = mybir.dt.bfloat16
AF = mybir.ActivationFunctionType
ALU = mybir.AluOpType
AX = mybir.AxisListType

D = 1024
NCORES = 8
FFN = 2816
INW = 12688
RWSEG = 3456
EPS = 1e-6
RW_LN_EPS = 64e-5
RW_DECAY_SCALE = math.exp(-0.5)


class Buf:
    __slots__ = ("w", "r", "prow")

    def __init__(self):
        self.w = {}
        self.r = {}
        self.prow = None


class Tile:
    def __init__(self, h):
        self.h = h
        self.b = Buf()

    def __getitem__(self, k):
        return self.h[k]


class KB:
    def __init__(self, nc, n_dma=56):
        self.nc = nc
        self.E = {"pe": nc.tensor, "act": nc.scalar, "dve": nc.vector, "pool": nc.gpsimd, "sp": nc.sync}
        self.sem = {}
        self.cnt = {}
        for e in ("pe", "act", "dve", "pool"):
            self.sem[e] = nc.alloc_semaphore("c_" + e)
            self.cnt[e] = 0
        self.dsem = [nc.alloc_semaphore("d%d" % i) for i in range(n_dma)]
        self.dval = [0] * n_dma
        self.drr = 0
        self.drr_sw = 0
        self.seen = {}
        self.n_inst = 0
        self.es = None
        self.uid = 0

    def tile(self, shape, dt, name="t"):
        self.uid += 1
        nm = "%s_%d" % (name, self.uid)
        if self.es is not None:
            return Tile(self.es.enter_context(self.nc.sbuf_tensor(nm, list(shape), dt)))
        return Tile(self.nc.alloc_sbuf_tensor(nm, list(shape), dt))

    def psum(self, shape, dt, name="p"):
        self.uid += 1
        nm = "%s_%d" % (name, self.uid)
        if self.es is not None:
            return Tile(self.es.enter_context(self.nc.psum_tensor(nm, list(shape), dt)))
        return Tile(self.nc.alloc_psum_tensor(nm, list(shape), dt))

    @contextmanager
    def stage(self):
        es = ExitStack()
        self.es = es
        try:
            yield
            self.barrier()
        finally:
            self.es = None
            es.close()

    def _semh(self, key):
        return self.sem[key] if isinstance(key, str) else self.dsem[key]

    def _wait(self, e, key, val):
        if e == "pe" and key == "pe":
            return
        if self.seen.get((e, key), 0) >= val:
            return
        self.E[e].wait_ge(self._semh(key), val)
        self.seen[(e, key)] = val
        self.n_inst += 1

    @staticmethod
    def _bufs(lst):
        return [x.b if isinstance(x, Tile) else x for x in lst]

    def _deps(self, e, reads, writes):
        deps = {}
        for b in reads:
            for k, v in b.w.items():
                if deps.get(k, 0) < v:
                    deps[k] = v
        for b in writes:
            for k, v in b.w.items():
                if deps.get(k, 0) < v:
                    deps[k] = v
            for k, v in b.r.items():
                if deps.get(k, 0) < v:
                    deps[k] = v
        for k, v in deps.items():
            self._wait(e, k, v)

    def _mark(self, ticket, reads, writes):
        k, v = ticket
        for b in reads:
            b.r[k] = v
        for b in writes:
            b.w = {k: v}
            b.r = {}

    def op(self, e, fn, reads=(), writes=(), row=None):
        reads = self._bufs(reads)
        writes = self._bufs(writes)
        if e == "pe":
            if row is not None and any(b.prow is not None and b.prow != row for b in writes):
                self.pe_fence()
            for b in writes:
                b.prow = row
        self._deps(e, reads, writes)
        ins = fn(self.E[e])
        self.cnt[e] += 1
        ins.then_inc(self.sem[e], 1)
        self.n_inst += 1
        self._mark((e, self.cnt[e]), reads, writes)
        return ins

    def dma(self, e, out, in_, reads=(), writes=(), **kw):
        reads = self._bufs(reads)
        writes = self._bufs(writes)
        self._deps(e, reads, writes)
        half = len(self.dsem) // 2
        if e == "pool":
            s = half + self.drr_sw
            self.drr_sw = (self.drr_sw + 1) % (len(self.dsem) - half)
        else:
            s = self.drr
            self.drr = (self.drr + 1) % half
        if self.dval[s] > 0:
            self._wait(e, s, self.dval[s])
        ins = self.E[e].dma_start(out=out, in_=in_, **kw)
        self.dval[s] += 16
        ins.then_inc(self.dsem[s], 16)
        self.n_inst += 1
        self._mark((s, self.dval[s]), reads, writes)

    def pe_fence(self):
        if self.cnt["pe"] > 0:
            self.E["pe"].wait_ge(self.sem["pe"], self.cnt["pe"])
            self.n_inst += 1

    def barrier(self, engines=("pe", "act", "dve", "pool", "sp")):
        for e in engines:
            for s in range(len(self.dsem)):
                if self.dval[s] > 0:
                    self._wait(e, s, self.dval[s])
            for k in ("pe", "act", "dve", "pool"):
                if self.cnt[k] > 0 and k != e:
                    self._wait(e, k, self.cnt[k])

    def finish(self):
        self.barrier()


class Prog:
    def __init__(self, nc, cfg):
        self.nc = nc
        self.kb = KB(nc)
        self.cfg = cfg
        self.NB = cfg["NB"]
        self.CTX = cfg["CTX"]
        self.LAT = cfg["LAT"]
        self.L = cfg["DEPTH"]
        self.ROWS = self.LAT // 64
        self.TS = self.CTX + self.LAT
        self.T = self.NB * self.TS
        self.dbg = cfg.get("dbg", ())
        self.inp = {}
        self.scr = {}
        self.flip = 0

    def din(self, name, shape, dt=F32):
        self.inp[name] = self.nc.dram_tensor(name, list(shape), dt, kind="ExternalInput").ap()
        return self.inp[name]

    def dscr(self, name, shape, dt):
        kind = "ExternalOutput" if name in self.dbg else "Internal"
        self.scr[name] = self.nc.dram_tensor(name, list(shape), dt, kind=kind).ap()
        return self.scr[name]

    def alt(self):
        self.flip ^= 1
        return "act" if self.flip else "dve"

    def xr_rows(self, layer, b, pos0, n=128):
        XR = self.scr["XR"]
        if pos0 < self.CTX or layer % 2 == 0:
            return [(0, n, XR[b, pos0:pos0 + n, :])]
        lat = XR[b, self.CTX:self.TS, :].rearrange("(r c) d -> c r d", c=64)
        out = []
        m0 = pos0 - self.CTX
        m = m0
        while m < m0 + n:
            c = m // self.ROWS
            r0 = m % self.ROWS
            k = min(self.ROWS - r0, m0 + n - m)
            out.append((m - m0, k, lat[c, r0:r0 + k, :]))
            m += k
        return out

    def st_init(self):
        kb = self.kb
        with kb.stage():
            XR = self.scr["XR"]
            for b in range(self.NB):
                kb.dma("sp", XR[b, 0:self.CTX, :], self.inp["ctx"][b])
                kb.dma("sp", XR[b, self.CTX:self.TS, :], self.inp["x"][b])

    def st_adaln(self):
        kb, nc = self.kb, self.nc
        with kb.stage():
            ct = kb.tile([128, 8, 3], F32)
            sc = kb.tile([128, 8, 3], F32)
            cb = kb.tile([128, 24, 128], F32)
            kb.dma("sp", ct[:, :, :], self.inp["condT"], writes=[ct])
            kb.op("act", lambda e: e.activation(out=sc[:, :, :], in_=ct[:, :, :], func=AF.Silu), reads=[ct], writes=[sc])
            for kc in range(8):
                for r in range(3):
                    kb.op("dve", lambda e: e.tensor_copy(out=cb[:, kc * 3 + r, :], in_=sc[:, kc, r:r + 1].to_broadcast([128, 128])), reads=[sc], writes=[cb])
            wts = [kb.tile([128, 8, 512], F32) for _ in range(2)]
            bts = [kb.tile([128, 512], F32) for _ in range(2)]
            pss = [kb.psum([128, 512], F32) for _ in range(3)]
            ots = [kb.tile([128, 512], F32) for _ in range(3)]
            it = 0
            for l in range(self.L):
                for nch in range(12):
                    wt, bt = wts[it % 2], bts[it % 2]
                    it += 1
                    kb.dma("sp", wt[:, :, :], self.inp["w_mod"][l].rearrange("(kc p) n -> p kc n", p=128)[:, :, nch * 512:(nch + 1) * 512], writes=[wt])
                    kb.dma("sp", bt[:, :], self.inp["b_mod"][l:l + 1, nch * 512:(nch + 1) * 512].to_broadcast([128, 512]), writes=[bt])
                    for r in range(3):
                        ps, ot = pss[r], ots[r]
                        for kc in range(8):
                            kb.op("pe", lambda e: e.matmul(ps[:, :], lhsT=cb[:, kc * 3 + r, :], rhs=wt[:, kc, :], start=(kc == 0), stop=(kc == 7)), reads=[cb, wt], writes=[ps])
                        kb.op("dve", lambda e: e.tensor_tensor(out=ot[:, :], in0=ps[:, :], in1=bt[:, :], op=ALU.add), reads=[ps, bt], writes=[ot])
                        kb.dma("pool", self.scr["MOD"][l, r:r + 1, nch * 512:(nch + 1) * 512], ot[0:1, :], reads=[ot])

    def load_bcast(self, t, src_row):
        n = src_row.shape[-1]
        self.kb.dma("sp", t[:, 0:n], src_row.to_broadcast([128, n]), writes=[t])

    def transpose_store(self, hb, dst, col0, ident, psT, hT):
        kb = self.kb
        for kc in range(8):
            kb.op("pe", lambda e: e.transpose(psT[:, kc, :], hb[:, kc * 128:(kc + 1) * 128], ident[:, :]), reads=[hb, ident], writes=[psT])
        kb.op("act", lambda e: e.activation(out=hT[:, :, :], in_=psT[:, :, :], func=AF.Copy), reads=[psT], writes=[hT])
        kb.dma("pool", dst.rearrange("(kc p) t -> p kc t", p=128)[:, :, col0:col0 + 128], hT[:, :, :], reads=[hT])

    def st_norm(self, l, which, final=False):
        kb = self.kb
        with kb.stage():
            nw = kb.tile([128, D], F32)
            G = [kb.tile([128, D], F32) for _ in range(3)]
            SH = [kb.tile([128, D], F32) for _ in range(3)]
            ident = kb.tile([128, 128], BF16)
            kb.dma("sp", ident[:, :], self.inp["ident_bf"], writes=[ident])
            if final:
                self.load_bcast(nw, self.inp["final_norm_w"].rearrange("(o d) -> o d", o=1))
            else:
                nwsrc = self.inp["norm1_w" if which == 1 else "norm2_w"]
                self.load_bcast(nw, nwsrc[l:l + 1, :])
                shi, sci = (0, 1) if which == 1 else (3, 4)
                for r in range(3):
                    self.load_bcast(G[r], self.scr["MOD"][l, r:r + 1, sci * D:(sci + 1) * D])
                    self.load_bcast(SH[r], self.scr["MOD"][l, r:r + 1, shi * D:(shi + 1) * D])
                    kb.op("dve", lambda e: e.scalar_tensor_tensor(out=G[r][:, :], in0=G[r][:, :], scalar=1.0, in1=nw[:, :], op0=ALU.add, op1=ALU.mult), reads=[G[r], nw], writes=[G[r]])
            NBUF = 3
            xt = [kb.tile([128, D], F32) for _ in range(NBUF)]
            junk = [kb.tile([128, D], F32) for _ in range(NBUF)]
            ss = [kb.tile([128, 1], F32) for _ in range(NBUF)]
            rs = [kb.tile([128, 1], F32) for _ in range(NBUF)]
            hb = [kb.tile([128, D], BF16) for _ in range(NBUF)]
            hT = [kb.tile([128, 8, 128], BF16) for _ in range(NBUF)]
            psT = [kb.psum([128, 8, 128], BF16) for _ in range(2)]
            it = 0
            for b in range(self.NB):
                for j in range(self.TS // 128):
                    pos0 = j * 128
                    if final and pos0 < self.CTX:
                        continue
                    i = it % NBUF
                    it += 1
                    r = 2 if pos0 < self.CTX else b
                    x, jk, s_, r_, h_ = xt[i], junk[i], ss[i], rs[i], hb[i]
                    for (p0, n, ap) in self.xr_rows(l if not final else 0, b, pos0):
                        kb.dma("sp", x[p0:p0 + n, :], ap, writes=[x])
                    kb.op("act", lambda e: e.activation(out=jk[:, :], in_=x[:, :], func=AF.Square, accum_out=s_[:, :]), reads=[x], writes=[jk, s_])
                    kb.op("act", lambda e: e.activation(out=r_[:, :], in_=s_[:, :], func=AF.Sqrt, scale=1.0 / D, bias=self.eps_t[:, 0:1]), reads=[s_], writes=[r_])
                    kb.op("dve", lambda e: e.reciprocal(out=r_[:, :], in_=r_[:, :]), reads=[r_], writes=[r_])
                    if final:
                        kb.op("dve", lambda e: e.scalar_tensor_tensor(out=jk[:, :], in0=x[:, :], scalar=r_[:, 0:1], in1=nw[:, :], op0=ALU.mult, op1=ALU.mult), reads=[x, r_, nw], writes=[jk])
                        kb.dma("pool", self.out_ap[b, pos0 - self.CTX:pos0 - self.CTX + 128, :], jk[:, :], reads=[jk])
                        continue
                    kb.op("dve", lambda e: e.scalar_tensor_tensor(out=jk[:, :], in0=x[:, :], scalar=r_[:, 0:1], in1=G[r][:, :], op0=ALU.mult, op1=ALU.mult), reads=[x, r_, G[r]], writes=[jk])
                    kb.op("dve", lambda e: e.tensor_tensor(out=h_[:, :], in0=jk[:, :], in1=SH[r][:, :], op=ALU.add), reads=[jk, SH[r]], writes=[h_])
                    self.transpose_store(h_, self.scr["HT"], b * self.TS + pos0, ident, psT[it % 2], hT[i])

    def gemm(self, src, K, W, jobs, ng_max=2048):
        kb = self.kb
        T = self.T
        kp = min(K, 128)
        KC = (K + 127) // 128
        assert K == kp * KC
        if KC > 8:
            ng_max = 512
        with kb.stage():
            wb = [kb.tile([kp, KC, ng_max], BF16) for _ in range(2)]
            hbs = [kb.tile([kp, KC, 512], BF16) for _ in range(2)]
            self.g_ps = [kb.psum([128, 512], F32) for _ in range(4)]
            self.g_stF = [kb.tile([128, 512], F32) for _ in range(4)]
            self.g_stB = [kb.tile([128, 512], BF16) for _ in range(4)]
            self.g_tmp = [kb.tile([128, 512], F32) for _ in range(4)]
            self.g_i = 0
            src3 = src.rearrange("(kc p) t -> p kc t", p=kp)
            gi = 0
            hi = 0
            for (c0, ncols, mode, epi, prep) in jobs:
                if prep is not None:
                    prep()
                for g0 in range(c0, c0 + ncols, ng_max):
                    ng = min(ng_max, c0 + ncols - g0)
                    w = wb[gi % 2]
                    gi += 1
                    for kc in range(KC):
                        kb.dma("pool", w[:, kc, 0:ng], W[kc * kp:(kc + 1) * kp, g0:g0 + ng], writes=[w])
                    for t0 in range(0, T, 512):
                        tsz = min(512, T - t0)
                        h = hbs[hi % 2]
                        hi += 1
                        kb.dma("sp", h[:, :, 0:tsz], src3[:, :, t0:t0 + tsz], writes=[h])
                        if mode == "FM":
                            for n0 in range(0, ng, 128):
                                nsz = min(128, ng - n0)
                                ps = self.g_ps[self.g_i % 4]
                                for kc in range(KC):
                                    kb.op("pe", lambda e: e.matmul(ps[0:nsz, 0:tsz], lhsT=w[:, kc, n0:n0 + nsz], rhs=h[:, kc, 0:tsz], start=(kc == 0), stop=(kc == KC - 1)), reads=[w, h], writes=[ps])
                                epi(ps, g0 + n0 - c0, nsz, t0, tsz)
                                self.g_i += 1
                        else:
                            for ts in range(0, tsz, 128):
                                for n0 in range(0, ng, 512):
                                    nsz = min(512, ng - n0)
                                    ps = self.g_ps[self.g_i % 4]
                                    for kc in range(KC):
                                        kb.op("pe", lambda e: e.matmul(ps[:, 0:nsz], lhsT=h[:, kc, ts:ts + 128], rhs=w[:, kc, n0:n0 + nsz], start=(kc == 0), stop=(kc == KC - 1)), reads=[w, h], writes=[ps])
                                    epi(ps, g0 + n0 - c0, nsz, t0 + ts, 128)
                                    self.g_i += 1

    def epi_fm(self, dst, dt, func=AF.Copy, scale=1.0, bias_t=None):
        kb = self.kb

        def epi(ps, c, nsz, t0, tsz):
            st = (self.g_stF if dt == F32 else self.g_stB)[self.g_i % 4]
            if func == AF.Copy and bias_t is None and self.g_i % 2 == 0:
                kb.op("dve", lambda e: e.tensor_scalar(out=st[0:nsz, 0:tsz], in0=ps[0:nsz, 0:tsz], scalar1=float(scale), scalar2=None, op0=ALU.mult), reads=[ps], writes=[st])
            elif bias_t is None:
                kb.op("act", lambda e: e.activation(out=st[0:nsz, 0:tsz], in_=ps[0:nsz, 0:tsz], func=func, scale=float(scale)), reads=[ps], writes=[st])
            else:
                kb.op("act", lambda e: e.activation(out=st[0:nsz, 0:tsz], in_=ps[0:nsz, 0:tsz], func=func, scale=float(scale), bias=bias_t[0:nsz, c // 128:c // 128 + 1]), reads=[ps, bias_t], writes=[st])
            kb.dma("pool", dst[c:c + nsz, t0:t0 + tsz], st[0:nsz, 0:tsz], reads=[st])
        return epi

    def epi_tm(self, dst, dt, func=AF.Copy, scale=1.0):
        kb = self.kb

        def epi(ps, c, nsz, t0, tsz):
            st = (self.g_stF if dt == F32 else self.g_stB)[self.g_i % 4]
            if func == AF.Copy and self.g_i % 2 == 0:
                kb.op("dve", lambda e: e.tensor_scalar(out=st[0:tsz, 0:nsz], in0=ps[0:tsz, 0:nsz], scalar1=float(scale), scalar2=None, op0=ALU.mult), reads=[ps], writes=[st])
            else:
                kb.op("act", lambda e: e.activation(out=st[0:tsz, 0:nsz], in_=ps[0:tsz, 0:nsz], func=func, scale=float(scale)), reads=[ps], writes=[st])
            kb.dma("pool", dst[t0:t0 + tsz, c:c + nsz], st[0:tsz, 0:nsz], reads=[st])
        return epi

    def epi_gelu_fm(self, dst):
        kb = self.kb

        def epi(ps, c, nsz, t0, tsz):
            x = self.g_stF[self.g_i % 4]
            u = self.g_tmp[self.g_i % 4]
            st = self.g_stB[self.g_i % 4]
            kb.op("act", lambda e: e.activation(out=x[0:nsz, 0:tsz], in_=ps[0:nsz, 0:tsz], func=AF.Copy), reads=[ps], writes=[x])
            kb.op("dve", lambda e: e.tensor_tensor(out=u[0:nsz, 0:tsz], in0=x[0:nsz, 0:tsz], in1=x[0:nsz, 0:tsz], op=ALU.mult), reads=[x], writes=[u])
            kb.op("dve", lambda e: e.tensor_scalar(out=u[0:nsz, 0:tsz], in0=u[0:nsz, 0:tsz], scalar1=0.044715, scalar2=1.0, op0=ALU.mult, op1=ALU.add), reads=[u], writes=[u])
            kb.op("dve", lambda e: e.tensor_tensor(out=u[0:nsz, 0:tsz], in0=u[0:nsz, 0:tsz], in1=x[0:nsz, 0:tsz], op=ALU.mult), reads=[u, x], writes=[u])
            kb.op("act", lambda e: e.activation(out=u[0:nsz, 0:tsz], in_=u[0:nsz, 0:tsz], func=AF.Sigmoid, scale=1.5957691216057308), reads=[u], writes=[u])
            kb.op("dve", lambda e: e.tensor_tensor(out=st[0:nsz, 0:tsz], in0=u[0:nsz, 0:tsz], in1=x[0:nsz, 0:tsz], op=ALU.mult), reads=[u, x], writes=[st])
            kb.dma("pool", dst[c:c + nsz, t0:t0 + tsz], st[0:nsz, 0:tsz], reads=[st])
        return epi

    def st_win(self, l):
        S = self.scr
        W = self.inp["w_in"][l]
        none = None
        jobs = [
            (0, 1024, "FM", self.epi_fm(S["QT"], BF16), none),
            (1024, 1024, "FM", self.epi_fm(S["KT"], BF16, scale=1.0 / 16), none),
            (1024, 1024, "TM", self.epi_tm(S["KTM"], BF16, scale=1.0 / 16), none),
            (2048, 1024, "TM", self.epi_tm(S["VTM"], BF16), none),
            (3072, 1024, "TM", self.epi_tm(S["OG"], BF16, func=AF.Sigmoid), none),
            (4096, 16, "TM", self.epi_tm(S["IFG"], F32), none),
            (4112, 1024, "FM", self.epi_fm(S["LX"], F32), none),
            (5136, 1024, "FM", self.epi_gelu_fm(S["LG"]), none),
            (6160, RWSEG, "FM", self.epi_fm(S["RW"], F32), none),
            (9616, 3072, "FM", self.epi_fm(S["MG"], BF16, func=AF.Sigmoid), none),
        ]
        self.gemm(S["HT"], D, W, jobs)

    def epi_resid(self, l, gate_idx):
        kb = self.kb
        self.r_g = None

        def prep():
            self.r_g = [kb.tile([128, D], F32) for _ in range(3)]
            for r in range(3):
                self.load_bcast(self.r_g[r], self.scr["MOD"][l, r:r + 1, gate_idx * D:(gate_idx + 1) * D])
            self.r_x = [kb.tile([128, 512], F32) for _ in range(4)]

        def epi(ps, c, nsz, t0, tsz):
            b = t0 // self.TS
            pos0 = t0 - b * self.TS
            r = 2 if pos0 < self.CTX else b
            x = self.r_x[self.g_i % 4]
            st = self.g_stF[self.g_i % 4]
            rows = self.xr_rows(l, b, pos0)
            for (p0, n, ap) in rows:
                kb.dma("sp", x[p0:p0 + n, 0:nsz], ap[:, c:c + nsz], writes=[x])
            kb.op("dve", lambda e: e.tensor_tensor(out=st[:, 0:nsz], in0=ps[:, 0:nsz], in1=self.r_g[r][:, c:c + nsz], op=ALU.mult), reads=[ps, self.r_g[r]], writes=[st])
            kb.op("dve", lambda e: e.tensor_tensor(out=st[:, 0:nsz], in0=st[:, 0:nsz], in1=x[:, 0:nsz], op=ALU.add), reads=[st, x], writes=[st])
            for (p0, n, ap) in rows:
                kb.dma("pool", ap[:, c:c + nsz], st[p0:p0 + n, 0:nsz], reads=[st])
        return epi, prep

    def st_wout(self, l):
        epi, prep = self.epi_resid(l, 2)
        self.gemm(self.scr["YM"], D, self.inp["w_out"][l], [(0, D, "TM", epi, prep)])

    def st_ffn(self, l):
        kb = self.kb
        S = self.scr
        self.st_norm(l, 2)
        W = self.inp["w_ffn_in"][l]
        self.gemm(S["HT"], D, W, [(0, FFN, "FM", self.epi_fm(S["AG"], BF16, func=AF.Silu), None)])

        def prep():
            self.f_g = [kb.tile([128, 512], BF16) for _ in range(4)]

        def epi_up(ps, c, nsz, t0, tsz):
            g = self.f_g[self.g_i % 4]
            st = self.g_stB[self.g_i % 4]
            kb.dma("sp", g[0:nsz, 0:tsz], S["AG"][c:c + nsz, t0:t0 + tsz], writes=[g])
            kb.op("dve", lambda e: e.tensor_tensor(out=st[0:nsz, 0:tsz], in0=ps[0:nsz, 0:tsz], in1=g[0:nsz, 0:tsz], op=ALU.mult), reads=[ps, g], writes=[st])
            kb.dma("pool", S["AT"][c:c + nsz, t0:t0 + tsz], st[0:nsz, 0:tsz], reads=[st])
        self.gemm(S["HT"], D, W[:, FFN:2 * FFN], [(0, FFN, "FM", epi_up, prep)])
        epi, prep2 = self.epi_resid(l, 5)
        self.gemm(S["AT"], FFN, self.inp["w_ffn_out"][l], [(0, D, "TM", epi, prep2)])

    def st_merge(self, l):
        kb = self.kb
        S = self.scr
        srcs = [("YML", "out_ml"), ("YLRU", "out_lru"), ("YRW", "out_rw")]
        for bi, (ys, wn) in enumerate(srcs):
            def prep():
                self.m_g = [kb.tile([128, 512], BF16) for _ in range(4)]
                self.m_a = [kb.tile([128, 512], F32) for _ in range(4)]

            def epi(ps, c, nsz, t0, tsz, bi=bi):
                g = self.m_g[self.g_i % 4]
                a = self.m_a[self.g_i % 4]
                kb.dma("sp", g[0:nsz, 0:tsz], S["MG"][bi * D + c:bi * D + c + nsz, t0:t0 + tsz], writes=[g])
                if bi == 0:
                    st = self.g_stF[self.g_i % 4]
                    kb.op("dve", lambda e: e.tensor_tensor(out=st[0:nsz, 0:tsz], in0=ps[0:nsz, 0:tsz], in1=g[0:nsz, 0:tsz], op=ALU.mult), reads=[ps, g], writes=[st])
                    kb.dma("pool", S["YACC"][c:c + nsz, t0:t0 + tsz], st[0:nsz, 0:tsz], reads=[st])
                else:
                    kb.dma("sp", a[0:nsz, 0:tsz], S["YACC"][c:c + nsz, t0:t0 + tsz], writes=[a])
                    st = self.g_stF[self.g_i % 4] if bi == 1 else self.g_stB[self.g_i % 4]
                    tmp = self.g_tmp[self.g_i % 4]
                    kb.op("dve", lambda e: e.tensor_tensor(out=tmp[0:nsz, 0:tsz], in0=ps[0:nsz, 0:tsz], in1=g[0:nsz, 0:tsz], op=ALU.mult), reads=[ps, g], writes=[tmp])
                    kb.op("dve", lambda e: e.tensor_tensor(out=st[0:nsz, 0:tsz], in0=tmp[0:nsz, 0:tsz], in1=a[0:nsz, 0:tsz], op=ALU.add), reads=[tmp, a], writes=[st])
                    dst = S["YACC"] if bi == 1 else S["YM"]
                    kb.dma("pool", dst[c:c + nsz, t0:t0 + tsz], st[0:nsz, 0:tsz], reads=[st])
            self.gemm(S[ys], D, self.inp[wn][l], [(0, D, "FM", epi, prep)])

    def st_mlstm(self, l):
        kb = self.kb
        S = self.scr
        TS, NB = self.TS, self.NB
        nck = TS // 128
        ctxc = self.CTX // 128
        with kb.stage():
            identF = kb.tile([128, 128], F32)
            ones = kb.tile([128, 128], F32)
            tri = [kb.tile([128, 128], F32) for _ in range(2)]
            negm = [kb.tile([128, 128], F32) for _ in range(2)]
            GB = kb.tile([128, 16], F32)
            kb.dma("sp", identF[:, :], self.inp["ident_f"], writes=[identF])
            kb.dma("sp", ones[:, :], self.inp["ones_f"], writes=[ones])
            for d in range(2):
                kb.dma("sp", tri[d][:, :], self.inp["tri"][d], writes=[tri[d]])
                kb.dma("sp", negm[d][:, :], self.inp["negm"][d], writes=[negm[d]])
            kb.dma("sp", GB[:, 0:8], self.inp["ml_ig_b"][l].rearrange("(o a) b -> o (a b)", o=1).to_broadcast([128, 8]), writes=[GB])
            kb.dma("sp", GB[:, 8:16], self.inp["ml_fg_b"][l].rearrange("(o a) b -> o (a b)", o=1).to_broadcast([128, 8]), writes=[GB])
            NBUF = 2
            qT = [kb.tile([128, 8, 128], BF16) for _ in range(NBUF)]
            kT = [kb.tile([128, 8, 128], BF16) for _ in range(NBUF)]
            kTM = [kb.tile([128, D], BF16) for _ in range(NBUF)]
            VA = [kb.tile([128, 4, 257], BF16) for _ in range(NBUF)]
            IFt = [kb.tile([128, 16], F32) for _ in range(NBUF)]
            for i in range(NBUF):
                kb.op("dve", lambda e: e.memset(VA[i][:, :, :], 1.0), writes=[VA[i]])
            gx = kb.tile([128, 16], F32)
            lf = kb.tile([128, 4], F32)
            e1 = kb.tile([128, 4], F32)
            lfB = kb.tile([128, 4, 128], F32)
            fc = kb.tile([128, 8], F32)
            cs = kb.tile([128, 4], F32)
            ef = kb.tile([128, 4], F32)
            ev = kb.tile([128, 4], F32)
            eT = kb.tile([128, 4], F32)
            tmp4 = kb.tile([128, 4], F32)
            C32 = [kb.tile([128, 2, 257], F32) for _ in range(4)]
            Cbf = [kb.tile([128, 2, 257], BF16) for _ in range(4)]
            DT = [kb.tile([128, 128], F32) for _ in range(2)]
            AT = [kb.tile([128, 128], BF16) for _ in range(2)]
            tI = [kb.tile([128, 257], F32) for _ in range(2)]
            ND = [kb.tile([128, 257], F32) for _ in range(2)]
            den = [kb.tile([128, 1], F32) for _ in range(2)]
            VS = [kb.tile([128, 257], BF16) for _ in range(2)]
            HO = [kb.tile([128, D], F32) for _ in range(2)]
            ps_g = kb.psum([128, 8], F32)
            psA = kb.psum([128, 128], F32)
            psF = kb.psum([128, 128], F32)
            psI = kb.psum([128, 257], F32)
            psC = kb.psum([128, 257], F32)
            psD = [kb.psum([128, 257], F32) for _ in range(2)]
            it = 0
            hh = 0
            for d in range(2):
                for b in range(NB):
                    for h in range(4):
                        kb.op("dve", lambda e: e.memset(C32[h][:, :, :], 0.0), writes=[C32[h]])
                        kb.op("dve", lambda e: e.memset(Cbf[h][:, :, :], 0.0), writes=[Cbf[h]])
                    cl = list(range(ctxc)) + list(range(ctxc, nck)) if d == 0 else list(range(ctxc - 1, -1, -1)) + list(range(nck - 1, ctxc - 1, -1))
                    for c in cl:
                        i = it % NBUF
                        it += 1
                        col0 = b * TS + c * 128
                        q_, k_, km_, va_, if_ = qT[i], kT[i], kTM[i], VA[i], IFt[i]
                        kb.dma("sp", q_[:, :, :], S["QT"].rearrange("(kc p) t -> p kc t", p=128)[:, :, col0:col0 + 128], writes=[q_])
                        kb.dma("sp", k_[:, :, :], S["KT"].rearrange("(kc p) t -> p kc t", p=128)[:, :, col0:col0 + 128], writes=[k_])
                        kb.dma("sp", km_[:, :], S["KTM"][col0:col0 + 128, :], writes=[km_])
                        kb.dma("sp", va_[:, :, 0:256], S["VTM"][col0:col0 + 128, :].rearrange("t (h e) -> t h e", h=4), writes=[va_])
                        kb.dma("sp", if_[:, :], S["IFG"][col0:col0 + 128, :], writes=[if_])
                        kb.op("dve", lambda e: e.tensor_tensor(out=gx[:, :], in0=if_[:, :], in1=GB[:, :], op=ALU.add), reads=[if_, GB], writes=[gx])
                        i4 = gx[:, d * 4:d * 4 + 4]
                        f4 = gx[:, 8 + d * 4:12 + d * 4]
                        kb.op("act", lambda e: e.activation(out=e1[:, :], in_=f4, func=AF.Exp, scale=-1.0), reads=[gx], writes=[e1])
                        kb.op("act", lambda e: e.activation(out=e1[:, :], in_=e1[:, :], func=AF.Ln, bias=self.one_t[:, 0:1]), reads=[e1], writes=[e1])
                        kb.op("dve", lambda e: e.tensor_scalar(out=lf[:, :], in0=e1[:, :], scalar1=-1.0, scalar2=None, op0=ALU.mult), reads=[e1], writes=[lf])
                        kb.op("dve", lambda e: e.tensor_copy(out=lfB[:, :, :], in_=lf[:, 0:4].unsqueeze(2).to_broadcast([128, 4, 128])), reads=[lf], writes=[lfB])
                        kb.op("pe", lambda e: e.matmul(ps_g[:, 0:4], lhsT=tri[d][:, :], rhs=lf[:, :], start=True, stop=True), reads=[tri[d], lf], writes=[ps_g])
                        kb.op("pe", lambda e: e.matmul(ps_g[:, 4:8], lhsT=ones[:, :], rhs=lf[:, :], start=True, stop=True), reads=[ones, lf], writes=[ps_g])
                        kb.op("dve", lambda e: e.tensor_copy(out=fc[:, :], in_=ps_g[:, :]), reads=[ps_g], writes=[fc])
                        kb.op("dve", lambda e: e.tensor_tensor(out=cs[:, :], in0=i4, in1=fc[:, 0:4], op=ALU.subtract), reads=[gx, fc], writes=[cs])
                        kb.op("act", lambda e: e.activation(out=ef[:, :], in_=fc[:, 0:4], func=AF.Exp), reads=[fc], writes=[ef])
                        kb.op("dve", lambda e: e.tensor_tensor(out=tmp4[:, :], in0=cs[:, :], in1=fc[:, 4:8], op=ALU.add), reads=[cs, fc], writes=[tmp4])
                        kb.op("act", lambda e: e.activation(out=ev[:, :], in_=tmp4[:, :], func=AF.Exp), reads=[tmp4], writes=[ev])
                        kb.op("act", lambda e: e.activation(out=eT[:, :], in_=fc[:, 4:8], func=AF.Exp), reads=[fc], writes=[eT])
                        ho = HO[it % 2]
                        for h in range(4):
                            j2 = hh % 2
                            hh += 1
                            dt_, at_, ti_, nd_, dn_, vs_ = DT[j2], AT[j2], tI[j2], ND[j2], den[j2], VS[j2]
                            for j in range(2):
                                kb.op("pe", lambda e: e.matmul(psA[:, :], lhsT=k_[:, 2 * h + j, :], rhs=q_[:, 2 * h + j, :], start=(j == 0), stop=(j == 1)), reads=[k_, q_], writes=[psA])
                            kb.op("pe", lambda e: e.matmul(psF[:, :], lhsT=lfB[:, h, :], rhs=tri[d][:, :], start=True, stop=False), reads=[lfB, tri[d]], writes=[psF])
                            kb.op("pe", lambda e: e.matmul(psF[:, :], lhsT=identF[:, :], rhs=negm[d][:, :], start=False, stop=True), reads=[identF, negm[d]], writes=[psF])
                            kb.op("act", lambda e: e.activation(out=dt_[:, :], in_=psF[:, :], func=AF.Exp, bias=cs[:, h:h + 1]), reads=[psF, cs], writes=[dt_])
                            kb.op("dve", lambda e: e.tensor_tensor(out=at_[:, :], in0=psA[:, :], in1=dt_[:, :], op=ALU.mult), reads=[psA, dt_], writes=[at_])
                            kb.op("pe", lambda e: e.matmul(psI[:, :], lhsT=at_[:, :], rhs=va_[:, h, :], start=True, stop=True), reads=[at_, va_], writes=[psI])
                            for j in range(2):
                                kb.op("pe", lambda e: e.matmul(psC[:, :], lhsT=q_[:, 2 * h + j, :], rhs=Cbf[h][:, j, :], start=(j == 0), stop=(j == 1)), reads=[q_, Cbf[h]], writes=[psC])
                            kb.op("act", lambda e: e.activation(out=ti_[:, :], in_=psI[:, :], func=AF.Copy), reads=[psI], writes=[ti_])
                            kb.op("dve", lambda e: e.scalar_tensor_tensor(out=nd_[:, :], in0=psC[:, :], scalar=ef[:, h:h + 1], in1=ti_[:, :], op0=ALU.mult, op1=ALU.add), reads=[psC, ef, ti_], writes=[nd_])
                            kb.op("act", lambda e: e.activation(out=dn_[:, :], in_=nd_[:, 256:257], func=AF.Abs), reads=[nd_], writes=[dn_])
                            kb.op("dve", lambda e: e.tensor_scalar(out=dn_[:, :], in0=dn_[:, :], scalar1=1.0, scalar2=None, op0=ALU.max), reads=[dn_], writes=[dn_])
                            kb.op("dve", lambda e: e.reciprocal(out=dn_[:, :], in_=dn_[:, :]), reads=[dn_], writes=[dn_])
                            kb.op("act", lambda e: e.activation(out=ho[:, h * 256:(h + 1) * 256], in_=nd_[:, 0:256], func=AF.Copy, scale=dn_[:, 0:1]), reads=[nd_, dn_], writes=[ho])
                            kb.op("dve", lambda e: e.tensor_scalar(out=vs_[:, :], in0=va_[:, h, :], scalar1=ev[:, h:h + 1], scalar2=None, op0=ALU.mult), reads=[va_, ev], writes=[vs_])
                            for j in range(2):
                                kb.op("pe", lambda e: e.matmul(psD[j][:, :], lhsT=km_[:, h * 256 + j * 128:h * 256 + (j + 1) * 128], rhs=vs_[:, :], start=True, stop=True), reads=[km_, vs_], writes=[psD[j]])
                                kb.op("dve", lambda e: e.scalar_tensor_tensor(out=C32[h][:, j, :], in0=C32[h][:, j, :], scalar=eT[:, h:h + 1], in1=psD[j][:, :], op0=ALU.mult, op1=ALU.add), reads=[C32[h], eT, psD[j]], writes=[C32[h]])
                            kb.op("act", lambda e: e.activation(out=Cbf[h][:, :, :], in_=C32[h][:, :, :], func=AF.Copy), reads=[C32[h]], writes=[Cbf[h]])
                        kb.dma("pool", S["HML%d" % d][col0:col0 + 128, :], ho[:, :], reads=[ho])

    def st_mlstm_post(self, l):
        kb = self.kb
        S = self.scr
        with kb.stage():
            ident = kb.tile([128, 128], BF16)
            kb.dma("sp", ident[:, :], self.inp["ident_bf"], writes=[ident])
            nw = kb.tile([128, D], F32)
            self.load_bcast(nw, self.inp["ml_norm_w"][l:l + 1, :])
            NBUF = 2
            hf = [kb.tile([128, D], F32) for _ in range(NBUF)]
            hbk = [kb.tile([128, D], F32) for _ in range(NBUF)]
            og = [kb.tile([128, D], BF16) for _ in range(NBUF)]
            junk = [kb.tile([128, 256], F32) for _ in range(NBUF)]
            ms = [kb.tile([128, 4], F32) for _ in range(NBUF)]
            yb = [kb.tile([128, D], BF16) for _ in range(NBUF)]
            hT = [kb.tile([128, 8, 128], BF16) for _ in range(NBUF)]
            psT = [kb.psum([128, 8, 128], BF16) for _ in range(2)]
            for tix in range(self.T // 128):
                i = tix % NBUF
                col0 = tix * 128
                a, b_, o_, jk, m_, y_ = hf[i], hbk[i], og[i], junk[i], ms[i], yb[i]
                kb.dma("sp", a[:, :], S["HML0"][col0:col0 + 128, :], writes=[a])
                kb.dma("sp", b_[:, :], S["HML1"][col0:col0 + 128, :], writes=[b_])
                kb.dma("sp", o_[:, :], S["OG"][col0:col0 + 128, :], writes=[o_])
                kb.op("dve", lambda e: e.tensor_tensor(out=a[:, :], in0=a[:, :], in1=b_[:, :], op=ALU.add), reads=[a, b_], writes=[a])
                for h in range(4):
                    kb.op("act", lambda e: e.activation(out=jk[:, :], in_=a[:, h * 256:(h + 1) * 256], func=AF.Square, accum_out=m_[:, h:h + 1]), reads=[a], writes=[jk, m_])
                kb.op("act", lambda e: e.activation(out=m_[:, :], in_=m_[:, :], func=AF.Sqrt, scale=1.0 / 256, bias=self.eps_t[:, 0:1]), reads=[m_], writes=[m_])
                kb.op("dve", lambda e: e.reciprocal(out=m_[:, :], in_=m_[:, :]), reads=[m_], writes=[m_])
                kb.op("dve", lambda e: e.tensor_tensor(out=a[:, :].rearrange("p (h e) -> p h e", h=4), in0=a[:, :].rearrange("p (h e) -> p h e", h=4), in1=m_[:, 0:4].unsqueeze(2).to_broadcast([128, 4, 256]), op=ALU.mult), reads=[a, m_], writes=[a])
                kb.op("dve", lambda e: e.tensor_tensor(out=a[:, :], in0=a[:, :], in1=nw[:, :], op=ALU.mult), reads=[a, nw], writes=[a])
                kb.op("dve", lambda e: e.tensor_tensor(out=y_[:, :], in0=a[:, :], in1=o_[:, :], op=ALU.mult), reads=[a, o_], writes=[y_])
                self.transpose_store(y_, S["YML"], col0, ident, psT[tix % 2], hT[i])

    def st_lru(self, l):
        kb = self.kb
        S = self.scr
        TS, NB, CTX = self.TS, self.NB, self.CTX
        segs = [(0, CTX), (CTX, TS)]
        with kb.stage():
            cw = kb.tile([128, 8, 4], F32)
            cbias = kb.tile([128, 8], F32)
            grb = kb.tile([128, 2, 8], F32)
            gib = kb.tile([128, 2, 8], F32)
            lam = kb.tile([128, 2, 8], F32)
            cc_ = kb.tile([128, 2, 8], F32)
            kb.dma("sp", cw[:, :, :], self.inp["lru_conv_wT"][l], writes=[cw])
            kb.dma("sp", cbias[:, :], self.inp["lru_conv_bT"][l], writes=[cbias])
            kb.dma("sp", grb[:, :, :], self.inp["lru_gr_bT"][l], writes=[grb])
            kb.dma("sp", gib[:, :, :], self.inp["lru_gi_bT"][l], writes=[gib])
            kb.dma("sp", lam[:, :, :], self.inp["lru_lambdaT"][l], writes=[lam])
            kb.op("act", lambda e: e.activation(out=cc_[:, :, :], in_=lam[:, :, :], func=AF.Exp, scale=-1.0), reads=[lam], writes=[cc_])
            kb.op("act", lambda e: e.activation(out=cc_[:, :, :], in_=cc_[:, :, :], func=AF.Ln, bias=self.one_t[:, 0:1]), reads=[cc_], writes=[cc_])
            kb.op("dve", lambda e: e.tensor_scalar(out=cc_[:, :, :], in0=cc_[:, :, :], scalar1=-8.0, scalar2=None, op0=ALU.mult), reads=[cc_], writes=[cc_])
            wbd = [[[kb.tile([128, 128], F32) for _ in range(2)] for _ in range(2)] for _ in range(2)]
            x = [kb.tile([128, TS], F32) for _ in range(2)]
            u = [kb.tile([128, TS], F32) for _ in range(2)]
            lg = [kb.tile([128, TS], BF16) for _ in range(2)]
            aa = kb.tile([128, TS], F32)
            bx = kb.tile([128, TS], F32)
            hf = kb.tile([128, TS], F32)
            hb = kb.tile([128, TS], F32)
            yo = [kb.tile([128, TS], BF16) for _ in range(2)]
            rr = [kb.tile([128, 512], F32) for _ in range(2)]
            ii = [kb.tile([128, 512], F32) for _ in range(2)]
            a2 = [kb.tile([128, 512], F32) for _ in range(2)]
            psr = [kb.psum([128, 512], F32) for _ in range(2)]
            psi = [kb.psum([128, 512], F32) for _ in range(2)]
            it = 0
            for cc in range(8):
                wv = wbd[cc % 2]
                for d in range(2):
                    for g, nm in enumerate(("lru_gr_w", "lru_gi_w")):
                        w = wv[d][g]
                        kb.op("dve", lambda e: e.memset(w[:, :], 0.0), writes=[w])
                        for blk in range(2):
                            kb.dma("sp", w[blk * 64:(blk + 1) * 64, blk * 64:(blk + 1) * 64], self.inp[nm][l, d, 2 * cc + blk], writes=[w])
                for b in range(NB):
                    i = it % 2
                    it += 1
                    x_, u_, lg_, yo_ = x[i], u[i], lg[i], yo[i]
                    kb.dma("sp", x_[:, :], S["LX"][cc * 128:(cc + 1) * 128, b * TS:(b + 1) * TS], writes=[x_])
                    kb.dma("sp", lg_[:, :], S["LG"][cc * 128:(cc + 1) * 128, b * TS:(b + 1) * TS], writes=[lg_])
                    for (s0, s1) in segs:
                        kb.op("dve", lambda e: e.tensor_scalar(out=u_[:, s0:s1], in0=x_[:, s0:s1], scalar1=cw[:, cc, 2:3], scalar2=cbias[:, cc:cc + 1], op0=ALU.mult, op1=ALU.add), reads=[x_, cw, cbias], writes=[u_])
                        kb.op("dve", lambda e: e.scalar_tensor_tensor(out=u_[:, s0 + 2:s1], in0=x_[:, s0:s1 - 2], scalar=cw[:, cc, 0:1], in1=u_[:, s0 + 2:s1], op0=ALU.mult, op1=ALU.add), reads=[x_, cw, u_], writes=[u_])
                        kb.op("dve", lambda e: e.scalar_tensor_tensor(out=u_[:, s0 + 1:s1], in0=x_[:, s0:s1 - 1], scalar=cw[:, cc, 1:2], in1=u_[:, s0 + 1:s1], op0=ALU.mult, op1=ALU.add), reads=[x_, cw, u_], writes=[u_])
                        kb.op("dve", lambda e: e.scalar_tensor_tensor(out=u_[:, s0:s1 - 1], in0=x_[:, s0 + 1:s1], scalar=cw[:, cc, 3:4], in1=u_[:, s0:s1 - 1], op0=ALU.mult, op1=ALU.add), reads=[x_, cw, u_], writes=[u_])
                    for d in range(2):
                        for t0 in range(0, TS, 512):
                            tsz = min(512, TS - t0)
                            j = (t0 // 512) % 2
                            r_, i_, a2_ = rr[j], ii[j], a2[j]
                            kb.op("pe", lambda e: e.matmul(psr[j][:, 0:tsz], lhsT=wv[d][0][:, :], rhs=u_[:, t0:t0 + tsz], start=True, stop=True), reads=[wv[d][0], u_], writes=[psr[j]])
                            kb.op("pe", lambda e: e.matmul(psi[j][:, 0:tsz], lhsT=wv[d][1][:, :], rhs=u_[:, t0:t0 + tsz], start=True, stop=True), reads=[wv[d][1], u_], writes=[psi[j]])
                            kb.op("act", lambda e: e.activation(out=r_[:, 0:tsz], in_=psr[j][:, 0:tsz], func=AF.Sigmoid, bias=grb[:, d, cc:cc + 1]), reads=[psr[j], grb], writes=[r_])
                            kb.op("act", lambda e: e.activation(out=i_[:, 0:tsz], in_=psi[j][:, 0:tsz], func=AF.Sigmoid, bias=gib[:, d, cc:cc + 1]), reads=[psi[j], gib], writes=[i_])
                            kb.op("act", lambda e: e.activation(out=aa[:, t0:t0 + tsz], in_=r_[:, 0:tsz], func=AF.Exp, scale=cc_[:, d, cc:cc + 1]), reads=[r_, cc_], writes=[aa])
                            kb.op("dve", lambda e: e.tensor_tensor(out=a2_[:, 0:tsz], in0=aa[:, t0:t0 + tsz], in1=aa[:, t0:t0 + tsz], op=ALU.mult), reads=[aa], writes=[a2_])
                            kb.op("act", lambda e: e.activation(out=a2_[:, 0:tsz], in_=a2_[:, 0:tsz], func=AF.Sqrt, scale=-1.0, bias=self.one_t[:, 0:1]), reads=[a2_], writes=[a2_])
                            kb.op("dve", lambda e: e.tensor_tensor(out=i_[:, 0:tsz], in0=i_[:, 0:tsz], in1=u_[:, t0:t0 + tsz], op=ALU.mult), reads=[i_, u_], writes=[i_])
                            kb.op("dve", lambda e: e.tensor_tensor(out=bx[:, t0:t0 + tsz], in0=i_[:, 0:tsz], in1=a2_[:, 0:tsz], op=ALU.mult), reads=[i_, a2_], writes=[bx])
                        if d == 0:
                            kb.op("dve", lambda e: e.tensor_tensor_scan(out=hf[:, :], data0=aa[:, :], data1=bx[:, :], initial=0.0, op0=ALU.mult, op1=ALU.add), reads=[aa, bx], writes=[hf])
                        else:
                            kb.op("dve", lambda e: e.tensor_tensor_scan(out=hb[:, 0:CTX][:, ::-1], data0=aa[:, 0:CTX][:, ::-1], data1=bx[:, 0:CTX][:, ::-1], initial=0.0, op0=ALU.mult, op1=ALU.add), reads=[aa, bx], writes=[hb])
                            kb.op("dve", lambda e: e.tensor_tensor_scan(out=hb[:, CTX:TS][:, ::-1], data0=aa[:, CTX:TS][:, ::-1], data1=bx[:, CTX:TS][:, ::-1], initial=hb[:, 0:1], op0=ALU.mult, op1=ALU.add), reads=[aa, bx, hb], writes=[hb])
                    kb.op("dve", lambda e: e.tensor_tensor(out=hf[:, :], in0=hf[:, :], in1=hb[:, :], op=ALU.add), reads=[hf, hb], writes=[hf])
                    kb.op("dve", lambda e: e.tensor_tensor(out=yo_[:, :], in0=hf[:, :], in1=lg_[:, :], op=ALU.mult), reads=[hf, lg_], writes=[yo_])
                    kb.dma("pool", S["YLRU"][cc * 128:(cc + 1) * 128, b * TS:(b + 1) * TS], yo_[:, :], reads=[yo_])

    def st_rwkv(self, l):
        self.st_rw_shift(l)
        self.st_rw_lowrank(l)
        self.st_rw_core(l)

    def st_rw_shift(self, l):
        kb = self.kb
        S = self.scr
        TS, NB, CTX = self.TS, self.NB, self.CTX
        segs = [(0, CTX), (CTX, TS)]
        with kb.stage():
            mu = kb.tile([128, 27], F32)
            kb.dma("sp", mu[:, :], self.inp["rw_muT"][l], writes=[mu])
            x = [kb.tile([128, TS], F32) for _ in range(2)]
            tm = [kb.tile([128, TS], F32) for _ in range(2)]
            ob = [kb.tile([128, TS], BF16) for _ in range(2)]
            it = 0
            for ch in range(27):
                for b in range(NB):
                    i = it % 2
                    it += 1
                    x_, t_, o_ = x[i], tm[i], ob[i]
                    kb.dma("sp", x_[:, :], S["RW"][ch * 128:(ch + 1) * 128, b * TS:(b + 1) * TS], writes=[x_])
                    for (s0, s1) in segs:
                        kb.op("dve", lambda e: e.tensor_tensor(out=t_[:, s0 + 1:s1 - 1], in0=x_[:, s0:s1 - 2], in1=x_[:, s0 + 2:s1], op=ALU.add), reads=[x_], writes=[t_])
                        kb.op("dve", lambda e: e.tensor_copy(out=t_[:, s0:s0 + 1], in_=x_[:, s0 + 1:s0 + 2]), reads=[x_], writes=[t_])
                        kb.op("dve", lambda e: e.tensor_copy(out=t_[:, s1 - 1:s1], in_=x_[:, s1 - 2:s1 - 1]), reads=[x_], writes=[t_])
                    kb.op("dve", lambda e: e.scalar_tensor_tensor(out=t_[:, :], in0=t_[:, :], scalar=0.5, in1=x_[:, :], op0=ALU.mult, op1=ALU.subtract), reads=[t_, x_], writes=[t_])
                    kb.op("dve", lambda e: e.scalar_tensor_tensor(out=t_[:, :], in0=t_[:, :], scalar=mu[:, ch:ch + 1], in1=x_[:, :], op0=ALU.mult, op1=ALU.add), reads=[t_, x_, mu], writes=[t_])
                    cols = slice(b * TS, (b + 1) * TS)
                    if ch < 24:
                        kb.dma("pool", S["RS"][ch * 128:(ch + 1) * 128, cols], t_[:, :], reads=[t_])
                    else:
                        fn = (AF.Tanh, AF.Copy, AF.Sigmoid)[ch - 24]
                        dst = (S["TD"], S["AD"], S["GD"])[ch - 24]
                        kb.op("act", lambda e: e.activation(out=o_[:, :], in_=t_[:, :], func=fn), reads=[t_], writes=[o_])
                        kb.dma("pool", dst[:, cols], o_[:, :], reads=[o_])

    def st_rw_lowrank(self, l):
        kb = self.kb
        S = self.scr
        for d in range(2):
            holder = {}

            def prep(d=d):
                holder["d0"] = kb.tile([128, 8], F32)
                holder["i0"] = kb.tile([128, 8], F32)
                kb.dma("sp", holder["d0"][:, :], self.inp["rw_decay0T"][l][:, d, :], writes=[holder["d0"]])
                kb.dma("sp", holder["i0"][:, :], self.inp["rw_iclr0T"][l][:, d, :], writes=[holder["i0"]])

            def epi_b(dst, key):
                def epi(ps, c, nsz, t0, tsz):
                    st = self.g_stF[self.g_i % 4]
                    bt = holder[key]
                    kb.op("act", lambda e: e.activation(out=st[0:nsz, 0:tsz], in_=ps[0:nsz, 0:tsz], func=AF.Sigmoid, bias=bt[0:nsz, c // 128:c // 128 + 1]), reads=[ps, bt], writes=[st])
                    kb.dma("pool", dst[c:c + nsz, t0:t0 + tsz], st[0:nsz, 0:tsz], reads=[st])
                return epi
            self.gemm(S["TD"][d * 64:(d + 1) * 64, :], 64, self.inp["rw_decay_up"][l, d], [(0, D, "FM", epi_b(S["SG%d" % d], "d0"), prep)])
            self.gemm(S["AD"][d * 64:(d + 1) * 64, :], 64, self.inp["rw_iclr_up"][l, d], [(0, D, "FM", epi_b(S["AA%d" % d], "i0"), prep)])
        self.gemm(S["GD"], 128, self.inp["rw_gate_up"][l], [(0, D, "FM", self.epi_fm(S["GG"], F32), None)])

    def st_rw_core(self, l):
        kb = self.kb
        S = self.scr
        TS, NB, CTX = self.TS, self.NB, self.CTX
        TB = min(CTX, 256)
        nblk = TS // TB
        cblk = CTX // TB
        ncb = TB // 64
        DS = RW_DECAY_SCALE
        v3 = lambda t: t[:, :].rearrange("p (c e) -> p c e", e=64)
        h2 = lambda ap: ap.rearrange("p (h c) -> p h c", h=2)
        with kb.stage():
            identF = kb.tile([128, 128], F32)
            identB = kb.tile([128, 128], BF16)
            bones = kb.tile([128, 128], F32)
            maskA = [kb.tile([128, 128], F32) for _ in range(2)]
            maskN = [kb.tile([64, 64], F32) for _ in range(2)]
            rmask = [kb.tile([128, TB], F32) for _ in range(2)]
            kb.dma("sp", identF[:, :], self.inp["ident_f"], writes=[identF])
            kb.dma("sp", identB[:, :], self.inp["ident_bf"], writes=[identB])
            kb.dma("sp", bones[:, :], self.inp["bones_f"], writes=[bones])
            for d in range(2):
                kb.dma("sp", maskA[d][:, :], self.inp["maskA"][d], writes=[maskA[d]])
                kb.dma("sp", maskN[d][:, :], self.inp["maskN"][d], writes=[maskN[d]])
                kb.dma("sp", rmask[d][:, :], self.inp["rmask"][d][:, 0:TB], writes=[rmask[d]])
            prm = {}
            for nm in ("rw_k_kT", "rw_k_aT"):
                prm[nm] = kb.tile([128, 8], F32)
                kb.dma("sp", prm[nm][:, :], self.inp[nm][l], writes=[prm[nm]])
            omk = kb.tile([128, 8], F32)
            kb.op("dve", lambda e: e.tensor_scalar(out=omk[:, :], in0=prm["rw_k_aT"][:, :], scalar1=-1.0, scalar2=1.0, op0=ALU.mult, op1=ALU.add), reads=[prm["rw_k_aT"]], writes=[omk])

            NQ = 2 * ncb
            bq = lambda ap, p: ap.unsqueeze(1).to_broadcast([p, NQ, 64])
            W = {}
            mk = lambda dt=F32: [kb.tile([128, TB], dt) for _ in range(2)]
            for nm in ("rT", "kT", "vT", "sg", "aT", "kap", "w1", "w2", "E1", "sq"):
                W[nm] = mk()
            QC = [kb.tile([128, ncb, 128], BF16) for _ in range(2)]
            KC = [kb.tile([128, ncb, 128], BF16) for _ in range(2)]
            YT = [kb.tile([64, 2, TB], F32) for _ in range(2)]
            AT4 = [kb.tile([128, ncb, 2, 128], BF16) for _ in range(2)]
            UV4 = [kb.tile([128, ncb, 2, 64], BF16) for _ in range(2)]
            KTT4 = [kb.tile([128, ncb, 128], BF16) for _ in range(2)]
            TTp = [kb.tile([64, NQ, 128], F32) for _ in range(2)]
            for i in range(2):
                kb.op("dve", lambda e: e.memset(TTp[i][:, :, :], 0.0), writes=[TTp[i]])
            P = [kb.tile([64, NQ, 64], F32) for _ in range(2)]
            PT = [kb.tile([64, NQ, 64], F32) for _ in range(2)]
            TT = [kb.tile([64, NQ, 64], F32) for _ in range(2)]
            ST32 = kb.tile([128, 64], F32)
            STp = [kb.tile([128, 64], BF16) for _ in range(2)]
            nZ = kb.tile([64, 2, 64], F32)
            stmp = kb.tile([128, 64], F32)
            psM = [kb.psum([64, NQ, 64], F32) for _ in range(2)]
            psN = kb.psum([64, NQ, 64], F32)
            psA = kb.psum([128, 256], F32)
            psV = kb.psum([64, NQ, 64], F32)
            psK = kb.psum([128, ncb, 128], BF16)
            psY = kb.psum([64, 2, 64], F32)
            psU = kb.psum([128, 2, 64], F32)
            psA3 = h2(psA[:, :])

            def gen1(cc, b, d, blk, i):
                rows = slice(cc * 128, (cc + 1) * 128)
                cols = slice(b * TS + blk * TB, b * TS + (blk + 1) * TB)
                r_, k_, v_, sg_, a_ = W["rT"][i], W["kT"][i], W["vT"][i], W["sg"][i], W["aT"][i]
                kap_, w1_, w2_, E1_, sq_ = W["kap"][i], W["w1"][i], W["w2"][i], W["E1"][i], W["sq"][i]
                QC_, KC_ = QC[i], KC[i]
                kb.dma("sp", r_[:, :], S["RS"][rows, cols], writes=[r_])
                kb.dma("sp", k_[:, :], S["RS"][D + cc * 128:D + (cc + 1) * 128, cols], writes=[k_])
                kb.dma("sp", v_[:, :], S["RS"][2 * D + cc * 128:2 * D + (cc + 1) * 128, cols], writes=[v_])
                kb.dma("sp", sg_[:, :], S["SG%d" % d][rows, cols], writes=[sg_])
                kb.dma("sp", a_[:, :], S["AA%d" % d][rows, cols], writes=[a_])
                yield
                kb.op("dve", lambda e: e.tensor_scalar(out=kap_[:, :], in0=k_[:, :], scalar1=prm["rw_k_kT"][:, cc:cc + 1], scalar2=None, op0=ALU.mult), reads=[k_, prm["rw_k_kT"]], writes=[kap_])
                kb.op("dve", lambda e: e.tensor_tensor(out=sq_[:, :], in0=kap_[:, :], in1=kap_[:, :], op=ALU.mult), reads=[kap_], writes=[sq_])
                kb.op("pe", lambda e: e.matmul(psA[:, 0:TB], lhsT=bones[:, :], rhs=sq_[:, :], start=True, stop=True), reads=[bones, sq_], writes=[psA])
                kb.op("dve", lambda e: e.tensor_scalar(out=w1_[:, :], in0=a_[:, :], scalar1=prm["rw_k_aT"][:, cc:cc + 1], scalar2=omk[:, cc:cc + 1], op0=ALU.mult, op1=ALU.add), reads=[a_, prm["rw_k_aT"], omk], writes=[w1_])
                kb.op("dve", lambda e: e.tensor_tensor(out=w1_[:, :], in0=w1_[:, :], in1=k_[:, :], op=ALU.mult), reads=[w1_, k_], writes=[w1_])
                if d == 0:
                    kb.op("dve", lambda e: e.tensor_tensor_scan(out=w2_[:, :], data0=rmask[0][:, :], data1=sg_[:, :], initial=0.0, op0=ALU.mult, op1=ALU.add), reads=[rmask[0], sg_], writes=[w2_])
                else:
                    kb.op("dve", lambda e: e.tensor_tensor_scan(out=w2_[:, ::-1], data0=rmask[1][:, ::-1], data1=sg_[:, ::-1], initial=0.0, op0=ALU.mult, op1=ALU.add), reads=[rmask[1], sg_], writes=[w2_])
                yield
                kb.op("act", lambda e: e.activation(out=sq_[:, :], in_=psA[:, 0:TB], func=AF.Sqrt), reads=[psA], writes=[sq_])
                kb.op("act", lambda e: e.activation(out=E1_[:, :], in_=w2_[:, :], func=AF.Exp, scale=-DS), reads=[w2_], writes=[E1_])
                kb.op("dve", lambda e: e.tensor_scalar(out=sq_[:, :], in0=sq_[:, :], scalar1=1e-12, scalar2=None, op0=ALU.max), reads=[sq_], writes=[sq_])
                kb.op("dve", lambda e: e.reciprocal(out=sq_[:, :], in_=sq_[:, :]), reads=[sq_], writes=[sq_])
                kb.op("dve", lambda e: e.tensor_tensor(out=kap_[:, :], in0=kap_[:, :], in1=sq_[:, :], op=ALU.mult), reads=[kap_, sq_], writes=[kap_])
                kb.op("dve", lambda e: e.tensor_tensor(out=sg_[:, :], in0=w2_[:, :], in1=sg_[:, :], op=ALU.subtract), reads=[w2_, sg_], writes=[sg_])
                yield
                kb.op("act", lambda e: e.activation(out=sg_[:, :], in_=sg_[:, :], func=AF.Exp, scale=-DS), reads=[sg_], writes=[sg_])
                kb.op("act", lambda e: e.activation(out=w2_[:, :], in_=w2_[:, :], func=AF.Exp, scale=DS), reads=[w2_], writes=[w2_])
                kb.op("dve", lambda e: e.tensor_tensor(out=QC_[:, :, 64:128], in0=v3(r_), in1=v3(E1_), op=ALU.mult), reads=[r_, E1_], writes=[QC_])
                kb.op("dve", lambda e: e.tensor_tensor(out=a_[:, :], in0=a_[:, :], in1=kap_[:, :], op=ALU.mult), reads=[a_, kap_], writes=[a_])
                yield
                kb.op("dve", lambda e: e.tensor_tensor(out=QC_[:, :, 0:64], in0=v3(kap_), in1=v3(sg_), op=ALU.mult), reads=[kap_, sg_], writes=[QC_])
                kb.op("dve", lambda e: e.tensor_tensor(out=KC_[:, :, 0:64], in0=v3(w1_), in1=v3(w2_), op=ALU.mult), reads=[w1_, w2_], writes=[KC_])
                kb.op("dve", lambda e: e.tensor_tensor(out=KC_[:, :, 64:128], in0=v3(a_), in1=v3(w2_), op=ALU.mult), reads=[a_, w2_], writes=[KC_])
                yield
                for h in range(2):
                    hs = slice(h * 64, (h + 1) * 64)
                    for c in range(ncb):
                        q = 2 * c + h
                        kb.op("pe", lambda e: e.matmul(psM[0][:, q, :], lhsT=QC_[hs, c, 0:64], rhs=KC_[hs, c, 64:128], start=True, stop=True), reads=[KC_, QC_], writes=[psM[0]], row=h * 64)
                        kb.op("pe", lambda e: e.matmul(psM[1][:, q, :], lhsT=KC_[hs, c, 64:128], rhs=QC_[hs, c, 0:64], start=True, stop=True), reads=[KC_, QC_], writes=[psM[1]], row=h * 64)
                yield
                kb.op("dve", lambda e: e.scalar_tensor_tensor(out=P[0][:, :, :], in0=psM[0][:, :, :], scalar=-1.0, in1=bq(maskN[d][:, :], 64), op0=ALU.mult, op1=ALU.mult), reads=[psM[0], maskN[d]], writes=[P[0]])
                kb.op("dve", lambda e: e.scalar_tensor_tensor(out=PT[0][:, :, :], in0=psM[1][:, :, :], scalar=-1.0, in1=bq(maskA[d][0:64, 0:64], 64), op0=ALU.mult, op1=ALU.mult), reads=[psM[1], maskA[d]], writes=[PT[0]])
                kb.op("dve", lambda e: e.tensor_tensor(out=TT[0][:, :, :], in0=PT[0][:, :, :], in1=bq(identF[0:64, 0:64], 64), op=ALU.add), reads=[PT[0], identF], writes=[TT[0]])
                yield
                extra = []
                for c in range(ncb):
                    def ex_a(c=c):
                        for h in range(2):
                            hs = slice(h * 64, (h + 1) * 64)
                            kb.op("pe", lambda e: e.matmul(psA3[:, h, :], lhsT=KC_[hs, c, :], rhs=QC_[hs, c, :], start=True, stop=True), reads=[KC_, QC_], writes=[psA], row=h * 64)
                        kb.op("dve", lambda e: e.tensor_tensor(out=AT4[i][:, c, :, :], in0=psA3, in1=maskA[d][:, :].unsqueeze(1).to_broadcast([128, 2, 128]), op=ALU.mult), reads=[psA, maskA[d]], writes=[AT4[i]])
                    extra.append(ex_a)

                def ex_v():
                    for h in range(2):
                        hs = slice(h * 64, (h + 1) * 64)
                        for c in range(ncb):
                            kb.op("pe", lambda e: e.transpose(psV[:, 2 * c + h, :], v_[hs, c * 64:(c + 1) * 64], identF[hs, hs]), reads=[v_, identF], writes=[psV], row=h * 64)
                    kb.op("act", lambda e: e.activation(out=UV4[i][0:64, :, :, :].rearrange("p c h e -> p (c h) e"), in_=psV[:, :, :], func=AF.Copy), reads=[psV], writes=[UV4[i]])
                extra.append(ex_v)

                def ex_k():
                    for c in range(ncb):
                        kb.op("pe", lambda e: e.transpose(psK[:, c, :], KC_[:, c, :], identB[:, :]), reads=[KC_, identB], writes=[psK])
                    kb.op("act", lambda e: e.activation(out=KTT4[i][:, :, :], in_=psK[:, :, :], func=AF.Copy), reads=[psK], writes=[KTT4[i]])
                extra.append(ex_k)
                for m in range(1, 6):
                    pc_, pn_ = P[(m - 1) % 2], P[m % 2]
                    tc_, tn_ = PT[(m - 1) % 2], PT[m % 2]
                    for q in range(NQ):
                        kb.op("pe", lambda e: e.matmul(psM[0][:, q, :], lhsT=tc_[:, q, :], rhs=pc_[:, q, :], start=True, stop=True), reads=[tc_, pc_], writes=[psM[0]], row=0)
                    if m < 5:
                        for q in range(NQ):
                            kb.op("pe", lambda e: e.matmul(psM[1][:, q, :], lhsT=pc_[:, q, :], rhs=tc_[:, q, :], start=True, stop=True), reads=[tc_, pc_], writes=[psM[1]], row=0)
                    if extra:
                        extra.pop(0)()
                    yield
                    kb.op("act", lambda e: e.activation(out=pn_[:, :, :], in_=psM[0][:, :, :], func=AF.Copy), reads=[psM[0]], writes=[pn_])
                    if m < 5:
                        kb.op("dve", lambda e: e.tensor_copy(out=tn_[:, :, :], in_=psM[1][:, :, :]), reads=[psM[1]], writes=[tn_])
                    yield
                    tt_c, tt_n = TT[(m - 1) % 2], TT[m % 2]
                    for q in range(NQ):
                        kb.op("pe", lambda e: e.matmul(psN[:, q, :], lhsT=pn_[:, q, :], rhs=tt_c[:, q, :], start=True, stop=True), reads=[pn_, tt_c], writes=[psN])
                    if extra:
                        extra.pop(0)()
                    yield
                    if m < 5:
                        kb.op("dve", lambda e: e.tensor_tensor(out=tt_n[:, :, :], in0=psN[:, :, :], in1=tt_c[:, :, :], op=ALU.add), reads=[psN, tt_c], writes=[tt_n])
                    else:
                        kb.op("dve", lambda e: e.tensor_tensor(out=TTp[i][:, :, 64:128], in0=psN[:, :, :], in1=tt_c[:, :, :], op=ALU.add), reads=[psN, tt_c], writes=[TTp[i]])
                    yield
                while extra:
                    extra.pop(0)()
                    yield

            def gen2(cc, b, d, blk, i):
                rows = slice(cc * 128, (cc + 1) * 128)
                cols = slice(b * TS + blk * TB, b * TS + (blk + 1) * TB)
                QC_, KC_, yt, E1_ = QC[i], KC[i], YT[i], W["E1"][i]
                AT_, UV_, KTT_, TTp_ = AT4[i], UV4[i], KTT4[i], TTp[i]
                for c in (range(ncb) if d == 0 else range(ncb - 1, -1, -1)):
                    p0 = c * 64
                    for h in range(2):
                        hs = slice(h * 64, (h + 1) * 64)
                        kb.op("pe", lambda e: e.matmul(psY[:, h, :], lhsT=QC_[:, c, 0:64], rhs=STp[h][:, :], start=True, stop=False), reads=[QC_, STp[h]], writes=[psY])
                        kb.op("pe", lambda e: e.matmul(psY[:, h, :], lhsT=AT_[0:64, c, h, 0:64], rhs=UV_[0:64, c, h, :], start=False, stop=True), reads=[AT_, UV_], writes=[psY])
                    yield
                    kb.op("dve", lambda e: e.tensor_scalar(out=nZ[:, :, :], in0=psY[:, :, :], scalar1=-1.0, scalar2=None, op0=ALU.mult), reads=[psY], writes=[nZ])
                    yield
                    for h in range(2):
                        kb.op("pe", lambda e: e.matmul(psU[:, h, :], lhsT=TTp_[:, 2 * c + h, :], rhs=nZ[:, h, :], start=True, stop=True), reads=[TTp_, nZ], writes=[psU])
                    yield
                    kb.op("act", lambda e: e.activation(out=UV_[64:128, c, :, :], in_=psU[64:128, :, :], func=AF.Copy), reads=[psU], writes=[UV_])
                    yield
                    for h in range(2):
                        hs = slice(h * 64, (h + 1) * 64)
                        kb.op("pe", lambda e: e.matmul(psY[:, h, :], lhsT=STp[h][:, :], rhs=QC_[:, c, 64:128], start=True, stop=False), reads=[QC_, STp[h]], writes=[psY])
                        kb.op("pe", lambda e: e.matmul(psY[:, h, :], lhsT=UV_[:, c, h, :], rhs=AT_[:, c, h, 64:128], start=False, stop=True), reads=[AT_, UV_], writes=[psY])
                    kb.op("pe", lambda e: e.matmul(psU[:, 0, :], lhsT=KTT_[:, c, :], rhs=UV_[:, c, 1, :], start=True, stop=True), reads=[KTT_, UV_], writes=[psU])
                    kb.op("pe", lambda e: e.matmul(psU[0:64, 0, :], lhsT=KTT_[:, c, 0:64], rhs=UV_[:, c, 0, :], start=True, stop=True), reads=[KTT_, UV_], writes=[psU])
                    yield
                    wcol = p0 + 63 if d == 0 else p0
                    kb.op("dve", lambda e: e.tensor_tensor(out=stmp[:, :], in0=psU[:, 0, :], in1=ST32[:, :], op=ALU.add), reads=[psU, ST32], writes=[stmp])
                    kb.op("act", lambda e: e.activation(out=yt[:, :, p0:p0 + 64], in_=psY[:, :, :], func=AF.Copy), reads=[psY], writes=[yt])
                    kb.op("dve", lambda e: e.tensor_scalar(out=ST32[:, :], in0=stmp[:, :], scalar1=E1_[:, wcol:wcol + 1], scalar2=None, op0=ALU.mult), reads=[stmp, E1_], writes=[ST32])
                    yield
                    kb.op("act", lambda e: e.activation(out=STp[0][0:64, :], in_=ST32[0:64, :], func=AF.Copy), reads=[ST32], writes=[STp[0]])
                    kb.op("dve", lambda e: e.tensor_copy(out=STp[1][64:128, :], in_=ST32[64:128, :]), reads=[ST32], writes=[STp[1]])
                    yield
                kb.dma("pool", S["YT%d" % d][rows, cols].rearrange("(h v) t -> v h t", h=2), yt[:, :, :], reads=[yt])

            def drain(g):
                for _ in g:
                    pass

            seqb = []
            for cc in range(8):
                for b in range(NB):
                    for d in range(2):
                        bl = list(range(nblk)) if d == 0 else list(range(cblk - 1, -1, -1)) + list(range(nblk - 1, cblk - 1, -1))
                        for n_, blk in enumerate(bl):
                            seqb.append((cc, b, d, blk, n_ == 0))
            drain(gen1(*seqb[0][:4], 0))
            for k, (cc, b, d, blk, first) in enumerate(seqb):
                if first:
                    kb.op("dve", lambda e: e.memset(ST32[:, :], 0.0), writes=[ST32])
                    for h_ in range(2):
                        kb.op("dve", lambda e: e.memset(STp[h_][:, :], 0.0), writes=[STp[h_]])
                gens = [gen2(cc, b, d, blk, k % 2)]
                if k + 1 < len(seqb):
                    if self.cfg.get("rw_interleave", True):
                        gens.append(gen1(*seqb[k + 1][:4], (k + 1) % 2))
                    else:
                        drain(gen1(*seqb[k + 1][:4], (k + 1) % 2))
                while gens:
                    for g in list(gens):
                        try:
                            next(g)
                        except StopIteration:
                            gens.remove(g)
        self.st_rw_post(l)

    def st_rw_post(self, l):
        kb = self.kb
        S = self.scr
        with kb.stage():
            bones = kb.tile([128, 128], F32)
            kb.dma("sp", bones[:, :], self.inp["bones_f"], writes=[bones])
            prm = {}
            for nm in ("rw_k_aT", "rw_r_kT", "rw_ln_wT", "rw_ln_bT"):
                prm[nm] = kb.tile([128, 8], F32)
                kb.dma("sp", prm[nm][:, :], self.inp[nm][l], writes=[prm[nm]])
            omk2 = kb.tile([128, 8], F32)
            kb.op("dve", lambda e: e.tensor_scalar(out=omk2[:, :], in0=prm["rw_k_aT"][:, :], scalar1=-2.0, scalar2=2.0, op0=ALU.mult, op1=ALU.add), reads=[prm["rw_k_aT"]], writes=[omk2])
            lneps = kb.tile([128, 1], F32)
            kb.op("dve", lambda e: e.memset(lneps[:, :], RW_LN_EPS), writes=[lneps])
            psB = [kb.psum([128, 512], F32) for _ in range(3)]
            PB = 512
            mk2 = lambda dt=F32: [kb.tile([128, PB], dt) for _ in range(2)]
            r2, k2, v2, g2, a0, a1, y0, y1, tq, uq = mk2(), mk2(), mk2(), mk2(), mk2(), mk2(), mk2(), mk2(), mk2(), mk2()
            yo = mk2(BF16)
            it = 0
            for cc in range(8):
                rows = slice(cc * 128, (cc + 1) * 128)
                for t0 in range(0, self.T, PB):
                    tsz = min(PB, self.T - t0)
                    i = it % 2
                    it += 1
                    cs_ = slice(t0, t0 + tsz)
                    ld = [(r2[i], S["RS"][rows, cs_]), (k2[i], S["RS"][D + cc * 128:D + (cc + 1) * 128, cs_]), (v2[i], S["RS"][2 * D + cc * 128:2 * D + (cc + 1) * 128, cs_]),
                          (g2[i], S["GG"][rows, cs_]), (a0[i], S["AA0"][rows, cs_]), (a1[i], S["AA1"][rows, cs_]), (y0[i], S["YT0"][rows, cs_]), (y1[i], S["YT1"][rows, cs_])]
                    for tl, src in ld:
                        kb.dma("sp", tl[:, 0:tsz], src, writes=[tl])
                    r_, k_, v_, g_, a0_, a1_, y_, y1_, tq_, uq_, yo_ = r2[i], k2[i], v2[i], g2[i], a0[i], a1[i], y0[i], y1[i], tq[i], uq[i], yo[i]
                    sl = slice(0, tsz)
                    p1, p2, p3 = psB
                    kb.op("dve", lambda e: e.tensor_tensor(out=a0_[:, sl], in0=a0_[:, sl], in1=a1_[:, sl], op=ALU.add), reads=[a0_, a1_], writes=[a0_])
                    kb.op("dve", lambda e: e.tensor_scalar(out=a0_[:, sl], in0=a0_[:, sl], scalar1=prm["rw_k_aT"][:, cc:cc + 1], scalar2=omk2[:, cc:cc + 1], op0=ALU.mult, op1=ALU.add), reads=[a0_, prm["rw_k_aT"], omk2], writes=[a0_])
                    kb.op("dve", lambda e: e.tensor_tensor(out=a0_[:, sl], in0=a0_[:, sl], in1=k_[:, sl], op=ALU.mult), reads=[a0_, k_], writes=[a0_])
                    kb.op("dve", lambda e: e.scalar_tensor_tensor(out=a0_[:, sl], in0=a0_[:, sl], scalar=prm["rw_r_kT"][:, cc:cc + 1], in1=r_[:, sl], op0=ALU.mult, op1=ALU.mult), reads=[a0_, prm["rw_r_kT"], r_], writes=[a0_])
                    kb.op("pe", lambda e: e.matmul(p1[:, sl], lhsT=bones[:, :], rhs=a0_[:, sl], start=True, stop=True), reads=[bones, a0_], writes=[p1])
                    kb.op("dve", lambda e: e.tensor_tensor(out=y_[:, sl], in0=y_[:, sl], in1=y1_[:, sl], op=ALU.add), reads=[y_, y1_], writes=[y_])
                    kb.op("pe", lambda e: e.matmul(p2[:, sl], lhsT=bones[:, :], rhs=y_[:, sl], start=True, stop=True), reads=[bones, y_], writes=[p2])
                    kb.op("dve", lambda e: e.tensor_tensor(out=a1_[:, sl], in0=p1[:, sl], in1=v_[:, sl], op=ALU.mult), reads=[p1, v_], writes=[a1_])
                    kb.op("dve", lambda e: e.scalar_tensor_tensor(out=tq_[:, sl], in0=p2[:, sl], scalar=-1.0 / 64, in1=y_[:, sl], op0=ALU.mult, op1=ALU.add), reads=[p2, y_], writes=[tq_])
                    kb.op("act", lambda e: e.activation(out=uq_[:, sl], in_=tq_[:, sl], func=AF.Square), reads=[tq_], writes=[uq_])
                    kb.op("pe", lambda e: e.matmul(p3[:, sl], lhsT=bones[:, :], rhs=uq_[:, sl], start=True, stop=True), reads=[bones, uq_], writes=[p3])
                    kb.op("act", lambda e: e.activation(out=uq_[:, sl], in_=p3[:, sl], func=AF.Sqrt, scale=1.0 / 64, bias=lneps[:, 0:1]), reads=[p3, lneps], writes=[uq_])
                    kb.op("dve", lambda e: e.reciprocal(out=uq_[:, sl], in_=uq_[:, sl]), reads=[uq_], writes=[uq_])
                    kb.op("dve", lambda e: e.tensor_tensor(out=tq_[:, sl], in0=tq_[:, sl], in1=uq_[:, sl], op=ALU.mult), reads=[tq_, uq_], writes=[tq_])
                    kb.op("act", lambda e: e.activation(out=tq_[:, sl], in_=tq_[:, sl], func=AF.Identity, scale=prm["rw_ln_wT"][:, cc:cc + 1], bias=prm["rw_ln_bT"][:, cc:cc + 1]), reads=[tq_, prm["rw_ln_wT"], prm["rw_ln_bT"]], writes=[tq_])
                    kb.op("dve", lambda e: e.tensor_tensor(out=tq_[:, sl], in0=tq_[:, sl], in1=a1_[:, sl], op=ALU.add), reads=[tq_, a1_], writes=[tq_])
                    kb.op("dve", lambda e: e.tensor_tensor(out=yo_[:, sl], in0=tq_[:, sl], in1=g_[:, sl], op=ALU.mult), reads=[tq_, g_], writes=[yo_])
                    kb.dma("pool", S["YRW"][rows, cs_], yo_[:, sl], reads=[yo_])

    def build(self):
        nc, kb = self.nc, self.kb
        NB, CTX, LAT, L, T, TS = self.NB, self.CTX, self.LAT, self.L, self.T, self.TS
        self.din("x", [NB, LAT, D])
        self.din("ctx", [NB, CTX, D])
        self.din("condT", [128, 8, 3])
        self.din("w_mod", [L, D, 6 * D])
        self.din("b_mod", [L, 6 * D])
        self.din("norm1_w", [L, D])
        self.din("norm2_w", [L, D])
        self.din("w_in", [L, D, INW])
        self.din("ml_ig_b", [L, 2, 4])
        self.din("ml_fg_b", [L, 2, 4])
        self.din("ml_norm_w", [L, D])
        self.din("lru_conv_wT", [L, 128, 8, 4])
        self.din("lru_conv_bT", [L, 128, 8])
        self.din("lru_gr_w", [L, 2, 16, 64, 64])
        self.din("lru_gr_bT", [L, 128, 2, 8])
        self.din("lru_gi_w", [L, 2, 16, 64, 64])
        self.din("lru_gi_bT", [L, 128, 2, 8])
        self.din("lru_lambdaT", [L, 128, 2, 8])
        self.din("rw_muT", [L, 128, 27])
        self.din("rw_decay0T", [L, 128, 2, 8])
        self.din("rw_iclr0T", [L, 128, 2, 8])
        self.din("rw_decay_up", [L, 2, 64, D])
        self.din("rw_iclr_up", [L, 2, 64, D])
        self.din("rw_gate_up", [L, 128, D])
        for nm in ("rw_k_kT", "rw_k_aT", "rw_r_kT", "rw_ln_wT", "rw_ln_bT"):
            self.din(nm, [L, 128, 8])
        self.din("bones_f", [128, 128])
        self.din("maskA", [2, 128, 128])
        self.din("maskN", [2, 64, 64])
        self.din("rmask", [2, 128, TS])
        self.din("out_ml", [L, D, D])
        self.din("out_lru", [L, D, D])
        self.din("out_rw", [L, D, D])
        self.din("w_out", [L, D, D])
        self.din("w_ffn_in", [L, D, 2 * FFN])
        self.din("w_ffn_out", [L, FFN, D])
        self.din("final_norm_w", [D])
        self.din("ident_bf", [128, 128], BF16)
        self.din("ident_f", [128, 128])
        self.din("ones_f", [128, 128])
        self.din("tri", [2, 128, 128])
        self.din("negm", [2, 128, 128])
        self.out_ap = self.nc.dram_tensor("out", [NB, LAT, D], F32, kind="ExternalOutput").ap()
        self.dscr("XR", [NB, TS, D], F32)
        self.dscr("MOD", [L, 3, 6 * D], F32)
        self.dscr("HT", [D, T], BF16)
        self.dscr("QT", [D, T], BF16)
        self.dscr("KT", [D, T], BF16)
        self.dscr("KTM", [T, D], BF16)
        self.dscr("VTM", [T, D], BF16)
        self.dscr("OG", [T, D], BF16)
        self.dscr("IFG", [T, 16], F32)
        self.dscr("LX", [D, T], F32)
        self.dscr("LG", [D, T], BF16)
        self.dscr("RW", [RWSEG, T], F32)
        self.dscr("MG", [3 * D, T], BF16)
        self.dscr("HML0", [T, D], F32)
        self.dscr("HML1", [T, D], F32)
        self.dscr("YML", [D, T], BF16)
        self.dscr("YLRU", [D, T], BF16)
        self.dscr("YRW", [D, T], BF16)
        self.dscr("YACC", [D, T], F32)
        self.dscr("YM", [D, T], BF16)
        self.dscr("RS", [3 * D, T], F32)
        self.dscr("TD", [128, T], BF16)
        self.dscr("AD", [128, T], BF16)
        self.dscr("GD", [128, T], BF16)
        for d in range(2):
            self.dscr("SG%d" % d, [D, T], F32)
            self.dscr("AA%d" % d, [D, T], F32)
            self.dscr("YT%d" % d, [D, T], F32)
        self.dscr("GG", [D, T], F32)
        self.dscr("AG", [FFN, T], BF16)
        self.dscr("AT", [FFN, T], BF16)
        self.eps_t = kb.tile([128, 1], F32, "eps")
        self.one_t = kb.tile([128, 1], F32, "one")
        kb.op("dve", lambda e: e.memset(self.eps_t[:, :], EPS), writes=[self.eps_t])
        kb.op("dve", lambda e: e.memset(self.one_t[:, :], 1.0), writes=[self.one_t])
        upto = self.cfg.get("upto", None)
        seq = [("init", lambda: self.st_init()), ("adaln", lambda: self.st_adaln())]
        for l in range(L):
            seq += [("norm1", lambda l=l: self.st_norm(l, 1)), ("win", lambda l=l: self.st_win(l)),
                    ("mlstm", lambda l=l: self.st_mlstm(l)), ("mlpost", lambda l=l: self.st_mlstm_post(l)),
                    ("lru", lambda l=l: self.st_lru(l)), ("rwkv", lambda l=l: self.st_rwkv(l)),
                    ("merge", lambda l=l: self.st_merge(l)), ("wout", lambda l=l: self.st_wout(l)),
                    ("ffn", lambda l=l: self.st_ffn(l))]
        seq += [("final", lambda: self.st_norm(0, 1, final=True))]
        skip = self.cfg.get("skip", ())
        for name, fn in seq:
            if name not in skip:
                fn()
            if upto is not None and name == upto:
                break
        kb.finish()


def host_consts():
    c = {}
    c["ident_bf"] = np.eye(128, dtype=np.float32).astype(ml_dtypes.bfloat16)
    c["ident_f"] = np.eye(128, dtype=np.float32)
    c["ones_f"] = np.ones((128, 128), np.float32)
    s = np.arange(128)[:, None]
    t = np.arange(128)[None, :]
    tri = np.stack([(s <= t), (s >= t)]).astype(np.float32)
    c["tri"] = tri
    c["negm"] = ((1.0 - tri) * -30000.0).astype(np.float32)
    bo = np.zeros((128, 128), np.float32)
    bo[:64, :64] = 1.0
    bo[64:, 64:] = 1.0
    c["bones_f"] = bo
    j = np.arange(64)[:, None]
    t = np.arange(64)[None, :]
    mA = []
    mN = []
    for d in range(2):
        strict = (j < t) if d == 0 else (j > t)
        incl = (j <= t) if d == 0 else (j >= t)
        blk = np.concatenate([strict, incl], 1).astype(np.float32)
        mA.append(np.concatenate([blk, blk], 0))
        mN.append(strict.T.astype(np.float32))
    c["maskA"] = np.stack(mA)
    c["maskN"] = np.stack(mN)
    return c


def chanT(a):
    a = np.asarray(a, np.float32)
    lead = a.shape[:-1]
    a = a.reshape(lead + (8, 128))
    return np.ascontiguousarray(np.moveaxis(a, -1, 0))


def make_in_maps(inputs, cfg, ncores):
    NB, L = cfg["NB"], cfg["DEPTH"]
    consts = host_consts()
    f = lambda k: np.ascontiguousarray(np.asarray(inputs[k], np.float32)[:L])
    shared = {
        "w_mod": f("w_mod"), "b_mod": f("b_mod"), "norm1_w": f("norm1_w"), "norm2_w": f("norm2_w"),
        "w_in": f("w_in"), "ml_ig_b": f("ml_ig_b"), "ml_fg_b": f("ml_fg_b"), "ml_norm_w": f("ml_norm_w"),
        "lru_gr_w": f("lru_gr_w"), "lru_gi_w": f("lru_gi_w"),
        "out_ml": f("out_ml"), "out_lru": f("out_lru"), "out_rw": f("out_rw"), "w_out": f("w_out"),
        "w_ffn_in": f("w_ffn_in"), "w_ffn_out": f("w_ffn_out"),
        "final_norm_w": np.asarray(inputs["final_norm_w"], np.float32),
    }
    cw = np.asarray(inputs["lru_conv_w"], np.float32)[:L]
    shared["lru_conv_wT"] = np.ascontiguousarray(np.stack([np.moveaxis(chanT(cw[l]), 1, 2) for l in range(L)]))
    shared["lru_conv_bT"] = np.ascontiguousarray(np.stack([chanT(np.asarray(inputs["lru_conv_b"], np.float32)[l]) for l in range(L)]))
    for k in ("lru_gr_b", "lru_gi_b", "lru_lambda"):
        a = np.asarray(inputs[k], np.float32)[:L]
        shared[k + "T"] = np.ascontiguousarray(np.stack([chanT(a[l]) for l in range(L)]))
    TS = cfg["CTX"] + cfg["LAT"]
    tt = np.arange(TS)
    rm = np.stack([(tt % 64 != 0), (tt % 64 != 63)]).astype(np.float32)
    shared["rmask"] = np.ascontiguousarray(np.broadcast_to(rm[:, None, :], (2, 128, TS)))
    mu = np.asarray(inputs["rw_mu"], np.float32)[:L]
    shared["rw_muT"] = np.ascontiguousarray(mu.reshape(L, 27, 128).transpose(0, 2, 1))
    for k in ("rw_decay0", "rw_iclr0"):
        a = np.asarray(inputs[k], np.float32)[:L]
        shared[k + "T"] = np.ascontiguousarray(np.stack([chanT(a[l]) for l in range(L)]))
    for k in ("rw_k_k", "rw_k_a", "rw_r_k", "rw_ln_w", "rw_ln_b"):
        a = np.asarray(inputs[k], np.float32)[:L]
        shared[k + "T"] = np.ascontiguousarray(np.stack([chanT(a[l]) for l in range(L)]))
    for k in ("rw_decay_up", "rw_iclr_up", "rw_gate_up"):
        shared[k] = f(k)
    shared.update(consts)
    x = np.asarray(inputs["x"], np.float32)
    ctx = np.asarray(inputs["ctx"], np.float32)
    c = np.asarray(inputs["c"], np.float32)
    cc = np.asarray(inputs["c_ctx"], np.float32)
    maps = []
    for i in range(ncores):
        m = dict(shared)
        m["x"] = np.ascontiguousarray(x[i * NB:(i + 1) * NB])
        m["ctx"] = np.ascontiguousarray(ctx[i * NB:(i + 1) * NB])
        cond = np.concatenate([c[i * NB:(i + 1) * NB], cc[None, :]], 0)
        m["condT"] = np.ascontiguousarray(cond.T.reshape(8, 128, 3).transpose(1, 0, 2))
        maps.append(m)
    return maps


def kernel(**inputs):
    cfg = {"NB": 2, "CTX": 256, "LAT": 4096, "DEPTH": 4}
    nc = bass.Bass("TRN2", target_bir_lowering=False)
    p = Prog(nc, cfg)
    p.build()
    maps = make_in_maps(inputs, cfg, NCORES)
    res = run_bass_kernel_spmd(nc, maps, core_ids=list(range(NCORES)))
    return np.concatenate([np.asarray(r["out"], np.float32) for r in res.results], axis=0)
```

```python
import math
from contextlib import ExitStack, contextmanager
import numpy as np
import ml_dtypes
import concourse.bass as bass
import concourse.mybir as mybir
from concourse.bass_utils import run_bass_kernel_spmd

F32 = mybir.dt.float32
BF16 = mybir.dt.bfloat16
AF = mybir.ActivationFunctionType
ALU = mybir.AluOpType
AX = mybir.AxisListType

D = 1024
NCORES = 8
FFN = 2816
INW = 12688
RWSEG = 3456
EPS = 1e-6
RW_LN_EPS = 64e-5
RW_DECAY_SCALE = math.exp(-0.5)


class Buf:
    __slots__ = ("w", "r", "prow")

    def __init__(self):
        self.w = {}
        self.r = {}
        self.prow = None


class Tile:
    def __init__(self, h):
        self.h = h
        self.b = Buf()

    def __getitem__(self, k):
        return self.h[k]


class KB:
    def __init__(self, nc, n_dma=56):
        self.nc = nc
        self.E = {"pe": nc.tensor, "act": nc.scalar, "dve": nc.vector, "pool": nc.gpsimd, "sp": nc.sync}
        self.sem = {}
        self.cnt = {}
        for e in ("pe", "act", "dve", "pool"):
            self.sem[e] = nc.alloc_semaphore("c_" + e)
            self.cnt[e] = 0
        self.dsem = [nc.alloc_semaphore("d%d" % i) for i in range(n_dma)]
        self.dval = [0] * n_dma
        self.drr = 0
        self.drr_sw = 0
        self.seen = {}
        self.n_inst = 0
        self.es = None
        self.uid = 0

    def tile(self, shape, dt, name="t"):
        self.uid += 1
        nm = "%s_%d" % (name, self.uid)
        if self.es is not None:
            return Tile(self.es.enter_context(self.nc.sbuf_tensor(nm, list(shape), dt)))
        return Tile(self.nc.alloc_sbuf_tensor(nm, list(shape), dt))

    def psum(self, shape, dt, name="p"):
        self.uid += 1
        nm = "%s_%d" % (name, self.uid)
        if self.es is not None:
            return Tile(self.es.enter_context(self.nc.psum_tensor(nm, list(shape), dt)))
        return Tile(self.nc.alloc_psum_tensor(nm, list(shape), dt))

    @contextmanager
    def stage(self):
        es = ExitStack()
        self.es = es
        try:
            yield
            self.barrier()
        finally:
            self.es = None
            es.close()

    def _semh(self, key):
        return self.sem[key] if isinstance(key, str) else self.dsem[key]

    def _wait(self, e, key, val):
        if e == "pe" and key == "pe":
            return
        if self.seen.get((e, key), 0) >= val:
            return
        self.E[e].wait_ge(self._semh(key), val)
        self.seen[(e, key)] = val
        self.n_inst += 1

    @staticmethod
    def _bufs(lst):
        return [x.b if isinstance(x, Tile) else x for x in lst]

    def _deps(self, e, reads, writes):
        deps = {}
        for b in reads:
            for k, v in b.w.items():
                if deps.get(k, 0) < v:
                    deps[k] = v
        for b in writes:
            for k, v in b.w.items():
                if deps.get(k, 0) < v:
                    deps[k] = v
            for k, v in b.r.items():
                if deps.get(k, 0) < v:
                    deps[k] = v
        for k, v in deps.items():
            self._wait(e, k, v)

    def _mark(self, ticket, reads, writes):
        k, v = ticket
        for b in reads:
            b.r[k] = v
        for b in writes:
            b.w = {k: v}
            b.r = {}

    def op(self, e, fn, reads=(), writes=(), row=None):
        reads = self._bufs(reads)
        writes = self._bufs(writes)
        if e == "pe":
            if row is not None and any(b.prow is not None and b.prow != row for b in writes):
                self.pe_fence()
            for b in writes:
                b.prow = row
        self._deps(e, reads, writes)
        ins = fn(self.E[e])
        self.cnt[e] += 1
        ins.then_inc(self.sem[e], 1)
        self.n_inst += 1
        self._mark((e, self.cnt[e]), reads, writes)
        return ins

    def dma(self, e, out, in_, reads=(), writes=(), **kw):
        reads = self._bufs(reads)
        writes = self._bufs(writes)
        self._deps(e, reads, writes)
        half = len(self.dsem) // 2
        if e == "pool":
            s = half + self.drr_sw
            self.drr_sw = (self.drr_sw + 1) % (len(self.dsem) - half)
        else:
            s = self.drr
            self.drr = (self.drr + 1) % half
        if self.dval[s] > 0:
            self._wait(e, s, self.dval[s])
        ins = self.E[e].dma_start(out=out, in_=in_, **kw)
        self.dval[s] += 16
        ins.then_inc(self.dsem[s], 16)
        self.n_inst += 1
        self._mark((s, self.dval[s]), reads, writes)

    def pe_fence(self):
        if self.cnt["pe"] > 0:
            self.E["pe"].wait_ge(self.sem["pe"], self.cnt["pe"])
            self.n_inst += 1

    def barrier(self, engines=("pe", "act", "dve", "pool", "sp")):
        for e in engines:
            for s in range(len(self.dsem)):
                if self.dval[s] > 0:
                    self._wait(e, s, self.dval[s])
            for k in ("pe", "act", "dve", "pool"):
                if self.cnt[k] > 0 and k != e:
                    self._wait(e, k, self.cnt[k])

    def finish(self):
        self.barrier()


class Prog:
    def __init__(self, nc, cfg):
        self.nc = nc
        self.kb = KB(nc)
        self.cfg = cfg
        self.NB = cfg["NB"]
        self.CTX = cfg["CTX"]
        self.LAT = cfg["LAT"]
        self.L = cfg["DEPTH"]
        self.ROWS = self.LAT // 64
        self.TS = self.CTX + self.LAT
        self.T = self.NB * self.TS
        self.dbg = cfg.get("dbg", ())
        self.inp = {}
        self.scr = {}
        self.flip = 0

    def din(self, name, shape, dt=F32):
        self.inp[name] = self.nc.dram_tensor(name, list(shape), dt, kind="ExternalInput").ap()
        return self.inp[name]

    def dscr(self, name, shape, dt):
        kind = "ExternalOutput" if name in self.dbg else "Internal"
        self.scr[name] = self.nc.dram_tensor(name, list(shape), dt, kind=kind).ap()
        return self.scr[name]

    def alt(self):
        self.flip ^= 1
        return "act" if self.flip else "dve"

    def xr_rows(self, layer, b, pos0, n=128):
        XR = self.scr["XR"]
        if pos0 < self.CTX or layer % 2 == 0:
            return [(0, n, XR[b, pos0:pos0 + n, :])]
        lat = XR[b, self.CTX:self.TS, :].rearrange("(r c) d -> c r d", c=64)
        out = []
        m0 = pos0 - self.CTX
        m = m0
        while m < m0 + n:
            c = m // self.ROWS
            r0 = m % self.ROWS
            k = min(self.ROWS - r0, m0 + n - m)
            out.append((m - m0, k, lat[c, r0:r0 + k, :]))
            m += k
        return out

    def st_init(self):
        kb = self.kb
        with kb.stage():
            XR = self.scr["XR"]
            for b in range(self.NB):
                kb.dma("sp", XR[b, 0:self.CTX, :], self.inp["ctx"][b])
                kb.dma("sp", XR[b, self.CTX:self.TS, :], self.inp["x"][b])

    def st_adaln(self):
        kb, nc = self.kb, self.nc
        with kb.stage():
            ct = kb.tile([128, 8, 3], F32)
            sc = kb.tile([128, 8, 3], F32)
            cb = kb.tile([128, 24, 128], F32)
            kb.dma("sp", ct[:, :, :], self.inp["condT"], writes=[ct])
            kb.op("act", lambda e: e.activation(out=sc[:, :, :], in_=ct[:, :, :], func=AF.Silu), reads=[ct], writes=[sc])
            for kc in range(8):
                for r in range(3):
                    kb.op("dve", lambda e: e.tensor_copy(out=cb[:, kc * 3 + r, :], in_=sc[:, kc, r:r + 1].to_broadcast([128, 128])), reads=[sc], writes=[cb])
            wts = [kb.tile([128, 8, 512], F32) for _ in range(2)]
            bts = [kb.tile([128, 512], F32) for _ in range(2)]
            pss = [kb.psum([128, 512], F32) for _ in range(3)]
            ots = [kb.tile([128, 512], F32) for _ in range(3)]
            it = 0
            for l in range(self.L):
                for nch in range(12):
                    wt, bt = wts[it % 2], bts[it % 2]
                    it += 1
                    kb.dma("sp", wt[:, :, :], self.inp["w_mod"][l].rearrange("(kc p) n -> p kc n", p=128)[:, :, nch * 512:(nch + 1) * 512], writes=[wt])
                    kb.dma("sp", bt[:, :], self.inp["b_mod"][l:l + 1, nch * 512:(nch + 1) * 512].to_broadcast([128, 512]), writes=[bt])
                    for r in range(3):
                        ps, ot = pss[r], ots[r]
                        for kc in range(8):
                            kb.op("pe", lambda e: e.matmul(ps[:, :], lhsT=cb[:, kc * 3 + r, :], rhs=wt[:, kc, :], start=(kc == 0), stop=(kc == 7)), reads=[cb, wt], writes=[ps])
                        kb.op("dve", lambda e: e.tensor_tensor(out=ot[:, :], in0=ps[:, :], in1=bt[:, :], op=ALU.add), reads=[ps, bt], writes=[ot])
                        kb.dma("pool", self.scr["MOD"][l, r:r + 1, nch * 512:(nch + 1) * 512], ot[0:1, :], reads=[ot])

    def load_bcast(self, t, src_row):
        n = src_row.shape[-1]
        self.kb.dma("sp", t[:, 0:n], src_row.to_broadcast([128, n]), writes=[t])

    def transpose_store(self, hb, dst, col0, ident, psT, hT):
        kb = self.kb
        for kc in range(8):
            kb.op("pe", lambda e: e.transpose(psT[:, kc, :], hb[:, kc * 128:(kc + 1) * 128], ident[:, :]), reads=[hb, ident], writes=[psT])
        kb.op("act", lambda e: e.activation(out=hT[:, :, :], in_=psT[:, :, :], func=AF.Copy), reads=[psT], writes=[hT])
        kb.dma("act", dst.rearrange("(kc p) t -> p kc t", p=128)[:, :, col0:col0 + 128], hT[:, :, :], reads=[hT])

    def st_norm(self, l, which, final=False):
        kb = self.kb
        with kb.stage():
            nw = kb.tile([128, D], F32)
            G = [kb.tile([128, D], F32) for _ in range(3)]
            SH = [kb.tile([128, D], F32) for _ in range(3)]
            ident = kb.tile([128, 128], BF16)
            kb.dma("sp", ident[:, :], self.inp["ident_bf"], writes=[ident])
            if final:
                self.load_bcast(nw, self.inp["final_norm_w"].rearrange("(o d) -> o d", o=1))
            else:
                nwsrc = self.inp["norm1_w" if which == 1 else "norm2_w"]
                self.load_bcast(nw, nwsrc[l:l + 1, :])
                shi, sci = (0, 1) if which == 1 else (3, 4)
                for r in range(3):
                    self.load_bcast(G[r], self.scr["MOD"][l, r:r + 1, sci * D:(sci + 1) * D])
                    self.load_bcast(SH[r], self.scr["MOD"][l, r:r + 1, shi * D:(shi + 1) * D])
                    kb.op("dve", lambda e: e.scalar_tensor_tensor(out=G[r][:, :], in0=G[r][:, :], scalar=1.0, in1=nw[:, :], op0=ALU.add, op1=ALU.mult), reads=[G[r], nw], writes=[G[r]])
            NBUF = 3
            xt = [kb.tile([128, D], F32) for _ in range(NBUF)]
            junk = [kb.tile([128, D], F32) for _ in range(NBUF)]
            ss = [kb.tile([128, 1], F32) for _ in range(NBUF)]
            rs = [kb.tile([128, 1], F32) for _ in range(NBUF)]
            hb = [kb.tile([128, D], BF16) for _ in range(NBUF)]
            hT = [kb.tile([128, 8, 128], BF16) for _ in range(NBUF)]
            psT = [kb.psum([128, 8, 128], BF16) for _ in range(2)]
            it = 0
            for b in range(self.NB):
                for j in range(self.TS // 128):
                    pos0 = j * 128
                    if final and pos0 < self.CTX:
                        continue
                    i = it % NBUF
                    it += 1
                    r = 2 if pos0 < self.CTX else b
                    x, jk, s_, r_, h_ = xt[i], junk[i], ss[i], rs[i], hb[i]
                    for (p0, n, ap) in self.xr_rows(l if not final else 0, b, pos0):
                        kb.dma("sp", x[p0:p0 + n, :], ap, writes=[x])
                    kb.op("act", lambda e: e.activation(out=jk[:, :], in_=x[:, :], func=AF.Square, accum_out=s_[:, :]), reads=[x], writes=[jk, s_])
                    kb.op("act", lambda e: e.activation(out=r_[:, :], in_=s_[:, :], func=AF.Sqrt, scale=1.0 / D, bias=self.eps_t[:, 0:1]), reads=[s_], writes=[r_])
                    kb.op("dve", lambda e: e.reciprocal(out=r_[:, :], in_=r_[:, :]), reads=[r_], writes=[r_])
                    if final:
                        kb.op("dve", lambda e: e.scalar_tensor_tensor(out=jk[:, :], in0=x[:, :], scalar=r_[:, 0:1], in1=nw[:, :], op0=ALU.mult, op1=ALU.mult), reads=[x, r_, nw], writes=[jk])
                        kb.dma("act", self.out_ap[b, pos0 - self.CTX:pos0 - self.CTX + 128, :], jk[:, :], reads=[jk])
                        continue
                    kb.op("dve", lambda e: e.scalar_tensor_tensor(out=jk[:, :], in0=x[:, :], scalar=r_[:, 0:1], in1=G[r][:, :], op0=ALU.mult, op1=ALU.mult), reads=[x, r_, G[r]], writes=[jk])
                    kb.op("dve", lambda e: e.tensor_tensor(out=h_[:, :], in0=jk[:, :], in1=SH[r][:, :], op=ALU.add), reads=[jk, SH[r]], writes=[h_])
                    self.transpose_store(h_, self.scr["HT"], b * self.TS + pos0, ident, psT[it % 2], hT[i])

    def gemm(self, src, K, W, jobs, ng_max=2048):
        kb = self.kb
        T = self.T
        kp = min(K, 128)
        KC = (K + 127) // 128
        assert K == kp * KC
        if KC > 8:
            ng_max = 512
        with kb.stage():
            wb = [kb.tile([kp, KC, ng_max], BF16) for _ in range(2)]
            hbs = [kb.tile([kp, KC, 512], BF16) for _ in range(2)]
            self.g_ps = [kb.psum([128, 512], F32) for _ in range(4)]
            self.g_stF = [kb.tile([128, 512], F32) for _ in range(4)]
            self.g_stB = [kb.tile([128, 512], BF16) for _ in range(4)]
            self.g_tmp = [kb.tile([128, 512], F32) for _ in range(4)]
            self.g_i = 0
            src3 = src.rearrange("(kc p) t -> p kc t", p=kp)
            gi = 0
            hi = 0
            for (c0, ncols, mode, epi, prep) in jobs:
                if prep is not None:
                    prep()
                for g0 in range(c0, c0 + ncols, ng_max):
                    ng = min(ng_max, c0 + ncols - g0)
                    w = wb[gi % 2]
                    gi += 1
                    for kc in range(KC):
                        kb.dma("pool", w[:, kc, 0:ng], W[kc * kp:(kc + 1) * kp, g0:g0 + ng], writes=[w])
                    for t0 in range(0, T, 512):
                        tsz = min(512, T - t0)
                        h = hbs[hi % 2]
                        hi += 1
                        kb.dma("sp", h[:, :, 0:tsz], src3[:, :, t0:t0 + tsz], writes=[h])
                        if mode == "FM":
                            for n0 in range(0, ng, 128):
                                nsz = min(128, ng - n0)
                                ps = self.g_ps[self.g_i % 4]
                                for kc in range(KC):
                                    kb.op("pe", lambda e: e.matmul(ps[0:nsz, 0:tsz], lhsT=w[:, kc, n0:n0 + nsz], rhs=h[:, kc, 0:tsz], start=(kc == 0), stop=(kc == KC - 1)), reads=[w, h], writes=[ps])
                                epi(ps, g0 + n0 - c0, nsz, t0, tsz)
                                self.g_i += 1
                        else:
                            for ts in range(0, tsz, 128):
                                for n0 in range(0, ng, 512):
                                    nsz = min(512, ng - n0)
                                    ps = self.g_ps[self.g_i % 4]
                                    for kc in range(KC):
                                        kb.op("pe", lambda e: e.matmul(ps[:, 0:nsz], lhsT=h[:, kc, ts:ts + 128], rhs=w[:, kc, n0:n0 + nsz], start=(kc == 0), stop=(kc == KC - 1)), reads=[w, h], writes=[ps])
                                    epi(ps, g0 + n0 - c0, nsz, t0 + ts, 128)
                                    self.g_i += 1

    def epi_fm(self, dst, dt, func=AF.Copy, scale=1.0, bias_t=None):
        kb = self.kb

        def epi(ps, c, nsz, t0, tsz):
            st = (self.g_stF if dt == F32 else self.g_stB)[self.g_i % 4]
            if func == AF.Copy and bias_t is None and self.g_i % 2 == 0:
                kb.op("dve", lambda e: e.tensor_scalar(out=st[0:nsz, 0:tsz], in0=ps[0:nsz, 0:tsz], scalar1=float(scale), scalar2=None, op0=ALU.mult), reads=[ps], writes=[st])
            elif bias_t is None:
                kb.op("act", lambda e: e.activation(out=st[0:nsz, 0:tsz], in_=ps[0:nsz, 0:tsz], func=func, scale=float(scale)), reads=[ps], writes=[st])
            else:
                kb.op("act", lambda e: e.activation(out=st[0:nsz, 0:tsz], in_=ps[0:nsz, 0:tsz], func=func, scale=float(scale), bias=bias_t[0:nsz, c // 128:c // 128 + 1]), reads=[ps, bias_t], writes=[st])
            kb.dma("act", dst[c:c + nsz, t0:t0 + tsz], st[0:nsz, 0:tsz], reads=[st])
        return epi

    def epi_tm(self, dst, dt, func=AF.Copy, scale=1.0):
        kb = self.kb

        def epi(ps, c, nsz, t0, tsz):
            st = (self.g_stF if dt == F32 else self.g_stB)[self.g_i % 4]
            if func == AF.Copy and self.g_i % 2 == 0:
                kb.op("dve", lambda e: e.tensor_scalar(out=st[0:tsz, 0:nsz], in0=ps[0:tsz, 0:nsz], scalar1=float(scale), scalar2=None, op0=ALU.mult), reads=[ps], writes=[st])
            else:
                kb.op("act", lambda e: e.activation(out=st[0:tsz, 0:nsz], in_=ps[0:tsz, 0:nsz], func=func, scale=float(scale)), reads=[ps], writes=[st])
            kb.dma("act", dst[t0:t0 + tsz, c:c + nsz], st[0:tsz, 0:nsz], reads=[st])
        return epi

    def epi_gelu_fm(self, dst):
        kb = self.kb

        def epi(ps, c, nsz, t0, tsz):
            x = self.g_stF[self.g_i % 4]
            u = self.g_tmp[self.g_i % 4]
            st = self.g_stB[self.g_i % 4]
            kb.op("act", lambda e: e.activation(out=x[0:nsz, 0:tsz], in_=ps[0:nsz, 0:tsz], func=AF.Copy), reads=[ps], writes=[x])
            kb.op("dve", lambda e: e.tensor_tensor(out=u[0:nsz, 0:tsz], in0=x[0:nsz, 0:tsz], in1=x[0:nsz, 0:tsz], op=ALU.mult), reads=[x], writes=[u])
            kb.op("dve", lambda e: e.tensor_scalar(out=u[0:nsz, 0:tsz], in0=u[0:nsz, 0:tsz], scalar1=0.044715, scalar2=1.0, op0=ALU.mult, op1=ALU.add), reads=[u], writes=[u])
            kb.op("dve", lambda e: e.tensor_tensor(out=u[0:nsz, 0:tsz], in0=u[0:nsz, 0:tsz], in1=x[0:nsz, 0:tsz], op=ALU.mult), reads=[u, x], writes=[u])
            kb.op("act", lambda e: e.activation(out=u[0:nsz, 0:tsz], in_=u[0:nsz, 0:tsz], func=AF.Sigmoid, scale=1.5957691216057308), reads=[u], writes=[u])
            kb.op("dve", lambda e: e.tensor_tensor(out=st[0:nsz, 0:tsz], in0=u[0:nsz, 0:tsz], in1=x[0:nsz, 0:tsz], op=ALU.mult), reads=[u, x], writes=[st])
            kb.dma("act", dst[c:c + nsz, t0:t0 + tsz], st[0:nsz, 0:tsz], reads=[st])
        return epi

    def st_win(self, l):
        S = self.scr
        W = self.inp["w_in"][l]
        none = None
        jobs = [
            (0, 1024, "FM", self.epi_fm(S["QT"], BF16), none),
            (1024, 1024, "FM", self.epi_fm(S["KT"], BF16, scale=1.0 / 16), none),
            (1024, 1024, "TM", self.epi_tm(S["KTM"], BF16, scale=1.0 / 16), none),
            (2048, 1024, "TM", self.epi_tm(S["VTM"], BF16), none),
            (3072, 1024, "TM", self.epi_tm(S["OG"], BF16, func=AF.Sigmoid), none),
            (4096, 16, "TM", self.epi_tm(S["IFG"], F32), none),
            (4112, 1024, "FM", self.epi_fm(S["LX"], F32), none),
            (5136, 1024, "FM", self.epi_gelu_fm(S["LG"]), none),
            (6160, RWSEG, "FM", self.epi_fm(S["RW"], F32), none),
            (9616, 3072, "FM", self.epi_fm(S["MG"], BF16, func=AF.Sigmoid), none),
        ]
        self.gemm(S["HT"], D, W, jobs)

    def epi_resid(self, l, gate_idx):
        kb = self.kb
        self.r_g = None

        def prep():
            self.r_g = [kb.tile([128, D], F32) for _ in range(3)]
            for r in range(3):
                self.load_bcast(self.r_g[r], self.scr["MOD"][l, r:r + 1, gate_idx * D:(gate_idx + 1) * D])
            self.r_x = [kb.tile([128, 512], F32) for _ in range(4)]

        def epi(ps, c, nsz, t0, tsz):
            b = t0 // self.TS
            pos0 = t0 - b * self.TS
            r = 2 if pos0 < self.CTX else b
            x = self.r_x[self.g_i % 4]
            st = self.g_stF[self.g_i % 4]
            rows = self.xr_rows(l, b, pos0)
            for (p0, n, ap) in rows:
                kb.dma("sp", x[p0:p0 + n, 0:nsz], ap[:, c:c + nsz], writes=[x])
            kb.op("dve", lambda e: e.tensor_tensor(out=st[:, 0:nsz], in0=ps[:, 0:nsz], in1=self.r_g[r][:, c:c + nsz], op=ALU.mult), reads=[ps, self.r_g[r]], writes=[st])
            kb.op("dve", lambda e: e.tensor_tensor(out=st[:, 0:nsz], in0=st[:, 0:nsz], in1=x[:, 0:nsz], op=ALU.add), reads=[st, x], writes=[st])
            for (p0, n, ap) in rows:
                kb.dma("act", ap[:, c:c + nsz], st[p0:p0 + n, 0:nsz], reads=[st])
        return epi, prep

    def st_wout(self, l):
        epi, prep = self.epi_resid(l, 2)
        self.gemm(self.scr["YM"], D, self.inp["w_out"][l], [(0, D, "TM", epi, prep)])

    def st_ffn(self, l):
        kb = self.kb
        S = self.scr
        self.st_norm(l, 2)
        W = self.inp["w_ffn_in"][l]
        self.gemm(S["HT"], D, W, [(0, FFN, "FM", self.epi_fm(S["AG"], BF16, func=AF.Silu), None)])

        def prep():
            self.f_g = [kb.tile([128, 512], BF16) for _ in range(4)]

        def epi_up(ps, c, nsz, t0, tsz):
            g = self.f_g[self.g_i % 4]
            st = self.g_stB[self.g_i % 4]
            kb.dma("sp", g[0:nsz, 0:tsz], S["AG"][c:c + nsz, t0:t0 + tsz], writes=[g])
            kb.op("dve", lambda e: e.tensor_tensor(out=st[0:nsz, 0:tsz], in0=ps[0:nsz, 0:tsz], in1=g[0:nsz, 0:tsz], op=ALU.mult), reads=[ps, g], writes=[st])
            kb.dma("act", S["AT"][c:c + nsz, t0:t0 + tsz], st[0:nsz, 0:tsz], reads=[st])
        self.gemm(S["HT"], D, W[:, FFN:2 * FFN], [(0, FFN, "FM", epi_up, prep)])
        epi, prep2 = self.epi_resid(l, 5)
        self.gemm(S["AT"], FFN, self.inp["w_ffn_out"][l], [(0, D, "TM", epi, prep2)])

    def st_merge(self, l):
        kb = self.kb
        S = self.scr
        srcs = [("YML", "out_ml"), ("YLRU", "out_lru"), ("YRW", "out_rw")]
        for bi, (ys, wn) in enumerate(srcs):
            def prep():
                self.m_g = [kb.tile([128, 512], BF16) for _ in range(4)]
                self.m_a = [kb.tile([128, 512], F32) for _ in range(4)]

            def epi(ps, c, nsz, t0, tsz, bi=bi):
                g = self.m_g[self.g_i % 4]
                a = self.m_a[self.g_i % 4]
                kb.dma("sp", g[0:nsz, 0:tsz], S["MG"][bi * D + c:bi * D + c + nsz, t0:t0 + tsz], writes=[g])
                if bi == 0:
                    st = self.g_stF[self.g_i % 4]
                    kb.op("dve", lambda e: e.tensor_tensor(out=st[0:nsz, 0:tsz], in0=ps[0:nsz, 0:tsz], in1=g[0:nsz, 0:tsz], op=ALU.mult), reads=[ps, g], writes=[st])
                    kb.dma("act", S["YACC"][c:c + nsz, t0:t0 + tsz], st[0:nsz, 0:tsz], reads=[st])
                else:
                    kb.dma("sp", a[0:nsz, 0:tsz], S["YACC"][c:c + nsz, t0:t0 + tsz], writes=[a])
                    st = self.g_stF[self.g_i % 4] if bi == 1 else self.g_stB[self.g_i % 4]
                    tmp = self.g_tmp[self.g_i % 4]
                    kb.op("dve", lambda e: e.tensor_tensor(out=tmp[0:nsz, 0:tsz], in0=ps[0:nsz, 0:tsz], in1=g[0:nsz, 0:tsz], op=ALU.mult), reads=[ps, g], writes=[tmp])
                    kb.op("dve", lambda e: e.tensor_tensor(out=st[0:nsz, 0:tsz], in0=tmp[0:nsz, 0:tsz], in1=a[0:nsz, 0:tsz], op=ALU.add), reads=[tmp, a], writes=[st])
                    dst = S["YACC"] if bi == 1 else S["YM"]
                    kb.dma("act", dst[c:c + nsz, t0:t0 + tsz], st[0:nsz, 0:tsz], reads=[st])
            self.gemm(S[ys], D, self.inp[wn][l], [(0, D, "FM", epi, prep)])

    def st_mlstm(self, l):
        kb = self.kb
        S = self.scr
        TS, NB = self.TS, self.NB
        nck = TS // 128
        ctxc = self.CTX // 128
        with kb.stage():
            identF = kb.tile([128, 128], F32)
            ones = kb.tile([128, 128], F32)
            tri = [kb.tile([128, 128], F32) for _ in range(2)]
            negm = [kb.tile([128, 128], F32) for _ in range(2)]
            GB = kb.tile([128, 16], F32)
            kb.dma("sp", identF[:, :], self.inp["ident_f"], writes=[identF])
            kb.dma("sp", ones[:, :], self.inp["ones_f"], writes=[ones])
            for d in range(2):
                kb.dma("sp", tri[d][:, :], self.inp["tri"][d], writes=[tri[d]])
                kb.dma("sp", negm[d][:, :], self.inp["negm"][d], writes=[negm[d]])
            kb.dma("sp", GB[:, 0:8], self.inp["ml_ig_b"][l].rearrange("(o a) b -> o (a b)", o=1).to_broadcast([128, 8]), writes=[GB])
            kb.dma("sp", GB[:, 8:16], self.inp["ml_fg_b"][l].rearrange("(o a) b -> o (a b)", o=1).to_broadcast([128, 8]), writes=[GB])
            NBUF = 2
            qT = [kb.tile([128, 8, 128], BF16) for _ in range(NBUF)]
            kT = [kb.tile([128, 8, 128], BF16) for _ in range(NBUF)]
            kTM = [kb.tile([128, D], BF16) for _ in range(NBUF)]
            VA = [kb.tile([128, 4, 257], BF16) for _ in range(NBUF)]
            IFt = [kb.tile([128, 16], F32) for _ in range(NBUF)]
            for i in range(NBUF):
                kb.op("dve", lambda e: e.memset(VA[i][:, :, :], 1.0), writes=[VA[i]])
            gx = kb.tile([128, 16], F32)
            lf = kb.tile([128, 4], F32)
            e1 = kb.tile([128, 4], F32)
            lfB = kb.tile([128, 4, 128], F32)
            fc = kb.tile([128, 8], F32)
            cs = kb.tile([128, 4], F32)
            ef = kb.tile([128, 4], F32)
            ev = kb.tile([128, 4], F32)
            eT = kb.tile([128, 4], F32)
            tmp4 = kb.tile([128, 4], F32)
            C32 = [kb.tile([128, 2, 257], F32) for _ in range(4)]
            Cbf = [kb.tile([128, 2, 257], BF16) for _ in range(4)]
            DT = [kb.tile([128, 128], F32) for _ in range(2)]
            AT = [kb.tile([128, 128], BF16) for _ in range(2)]
            tI = [kb.tile([128, 257], F32) for _ in range(2)]
            ND = [kb.tile([128, 257], F32) for _ in range(2)]
            den = [kb.tile([128, 1], F32) for _ in range(2)]
            VS = [kb.tile([128, 257], BF16) for _ in range(2)]
            HO = [kb.tile([128, D], F32) for _ in range(2)]
            ps_g = kb.psum([128, 8], F32)
            psA = kb.psum([128, 128], F32)
            psF = kb.psum([128, 128], F32)
            psI = kb.psum([128, 257], F32)
            psC = kb.psum([128, 257], F32)
            psD = [kb.psum([128, 257], F32) for _ in range(2)]
            it = 0
            hh = 0
            for d in range(2):
                for b in range(NB):
                    for h in range(4):
                        kb.op("dve", lambda e: e.memset(C32[h][:, :, :], 0.0), writes=[C32[h]])
                        kb.op("dve", lambda e: e.memset(Cbf[h][:, :, :], 0.0), writes=[Cbf[h]])
                    cl = list(range(ctxc)) + list(range(ctxc, nck)) if d == 0 else list(range(ctxc - 1, -1, -1)) + list(range(nck - 1, ctxc - 1, -1))
                    for c in cl:
                        i = it % NBUF
                        it += 1
                        col0 = b * TS + c * 128
                        q_, k_, km_, va_, if_ = qT[i], kT[i], kTM[i], VA[i], IFt[i]
                        kb.dma("sp", q_[:, :, :], S["QT"].rearrange("(kc p) t -> p kc t", p=128)[:, :, col0:col0 + 128], writes=[q_])
                        kb.dma("sp", k_[:, :, :], S["KT"].rearrange("(kc p) t -> p kc t", p=128)[:, :, col0:col0 + 128], writes=[k_])
                        kb.dma("sp", km_[:, :], S["KTM"][col0:col0 + 128, :], writes=[km_])
                        kb.dma("sp", va_[:, :, 0:256], S["VTM"][col0:col0 + 128, :].rearrange("t (h e) -> t h e", h=4), writes=[va_])
                        kb.dma("sp", if_[:, :], S["IFG"][col0:col0 + 128, :], writes=[if_])
                        kb.op("dve", lambda e: e.tensor_tensor(out=gx[:, :], in0=if_[:, :], in1=GB[:, :], op=ALU.add), reads=[if_, GB], writes=[gx])
                        i4 = gx[:, d * 4:d * 4 + 4]
                        f4 = gx[:, 8 + d * 4:12 + d * 4]
                        kb.op("act", lambda e: e.activation(out=e1[:, :], in_=f4, func=AF.Exp, scale=-1.0), reads=[gx], writes=[e1])
                        kb.op("act", lambda e: e.activation(out=e1[:, :], in_=e1[:, :], func=AF.Ln, bias=self.one_t[:, 0:1]), reads=[e1], writes=[e1])
                        kb.op("dve", lambda e: e.tensor_scalar(out=lf[:, :], in0=e1[:, :], scalar1=-1.0, scalar2=None, op0=ALU.mult), reads=[e1], writes=[lf])
                        kb.op("dve", lambda e: e.tensor_copy(out=lfB[:, :, :], in_=lf[:, 0:4].unsqueeze(2).to_broadcast([128, 4, 128])), reads=[lf], writes=[lfB])
                        kb.op("pe", lambda e: e.matmul(ps_g[:, 0:4], lhsT=tri[d][:, :], rhs=lf[:, :], start=True, stop=True), reads=[tri[d], lf], writes=[ps_g])
                        kb.op("pe", lambda e: e.matmul(ps_g[:, 4:8], lhsT=ones[:, :], rhs=lf[:, :], start=True, stop=True), reads=[ones, lf], writes=[ps_g])
                        kb.op("dve", lambda e: e.tensor_copy(out=fc[:, :], in_=ps_g[:, :]), reads=[ps_g], writes=[fc])
                        kb.op("dve", lambda e: e.tensor_tensor(out=cs[:, :], in0=i4, in1=fc[:, 0:4], op=ALU.subtract), reads=[gx, fc], writes=[cs])
                        kb.op("act", lambda e: e.activation(out=ef[:, :], in_=fc[:, 0:4], func=AF.Exp), reads=[fc], writes=[ef])
                        kb.op("dve", lambda e: e.tensor_tensor(out=tmp4[:, :], in0=cs[:, :], in1=fc[:, 4:8], op=ALU.add), reads=[cs, fc], writes=[tmp4])
                        kb.op("act", lambda e: e.activation(out=ev[:, :], in_=tmp4[:, :], func=AF.Exp), reads=[tmp4], writes=[ev])
                        kb.op("act", lambda e: e.activation(out=eT[:, :], in_=fc[:, 4:8], func=AF.Exp), reads=[fc], writes=[eT])
                        ho = HO[it % 2]
                        for h in range(4):
                            j2 = hh % 2
                            hh += 1
                            dt_, at_, ti_, nd_, dn_, vs_ = DT[j2], AT[j2], tI[j2], ND[j2], den[j2], VS[j2]
                            for j in range(2):
                                kb.op("pe", lambda e: e.matmul(psA[:, :], lhsT=k_[:, 2 * h + j, :], rhs=q_[:, 2 * h + j, :], start=(j == 0), stop=(j == 1)), reads=[k_, q_], writes=[psA])
                            kb.op("pe", lambda e: e.matmul(psF[:, :], lhsT=lfB[:, h, :], rhs=tri[d][:, :], start=True, stop=False), reads=[lfB, tri[d]], writes=[psF])
                            kb.op("pe", lambda e: e.matmul(psF[:, :], lhsT=identF[:, :], rhs=negm[d][:, :], start=False, stop=True), reads=[identF, negm[d]], writes=[psF])
                            kb.op("act", lambda e: e.activation(out=dt_[:, :], in_=psF[:, :], func=AF.Exp, bias=cs[:, h:h + 1]), reads=[psF, cs], writes=[dt_])
                            kb.op("dve", lambda e: e.tensor_tensor(out=at_[:, :], in0=psA[:, :], in1=dt_[:, :], op=ALU.mult), reads=[psA, dt_], writes=[at_])
                            kb.op("pe", lambda e: e.matmul(psI[:, :], lhsT=at_[:, :], rhs=va_[:, h, :], start=True, stop=True), reads=[at_, va_], writes=[psI])
                            for j in range(2):
                                kb.op("pe", lambda e: e.matmul(psC[:, :], lhsT=q_[:, 2 * h + j, :], rhs=Cbf[h][:, j, :], start=(j == 0), stop=(j == 1)), reads=[q_, Cbf[h]], writes=[psC])
                            kb.op("act", lambda e: e.activation(out=ti_[:, :], in_=psI[:, :], func=AF.Copy), reads=[psI], writes=[ti_])
                            kb.op("dve", lambda e: e.scalar_tensor_tensor(out=nd_[:, :], in0=psC[:, :], scalar=ef[:, h:h + 1], in1=ti_[:, :], op0=ALU.mult, op1=ALU.add), reads=[psC, ef, ti_], writes=[nd_])
                            kb.op("act", lambda e: e.activation(out=dn_[:, :], in_=nd_[:, 256:257], func=AF.Abs), reads=[nd_], writes=[dn_])
                            kb.op("dve", lambda e: e.tensor_scalar(out=dn_[:, :], in0=dn_[:, :], scalar1=1.0, scalar2=None, op0=ALU.max), reads=[dn_], writes=[dn_])
                            kb.op("dve", lambda e: e.reciprocal(out=dn_[:, :], in_=dn_[:, :]), reads=[dn_], writes=[dn_])
                            kb.op("act", lambda e: e.activation(out=ho[:, h * 256:(h + 1) * 256], in_=nd_[:, 0:256], func=AF.Copy, scale=dn_[:, 0:1]), reads=[nd_, dn_], writes=[ho])
                            kb.op("dve", lambda e: e.tensor_scalar(out=vs_[:, :], in0=va_[:, h, :], scalar1=ev[:, h:h + 1], scalar2=None, op0=ALU.mult), reads=[va_, ev], writes=[vs_])
                            for j in range(2):
                                kb.op("pe", lambda e: e.matmul(psD[j][:, :], lhsT=km_[:, h * 256 + j * 128:h * 256 + (j + 1) * 128], rhs=vs_[:, :], start=True, stop=True), reads=[km_, vs_], writes=[psD[j]])
                                kb.op("dve", lambda e: e.scalar_tensor_tensor(out=C32[h][:, j, :], in0=C32[h][:, j, :], scalar=eT[:, h:h + 1], in1=psD[j][:, :], op0=ALU.mult, op1=ALU.add), reads=[C32[h], eT, psD[j]], writes=[C32[h]])
                            kb.op("act", lambda e: e.activation(out=Cbf[h][:, :, :], in_=C32[h][:, :, :], func=AF.Copy), reads=[C32[h]], writes=[Cbf[h]])
                        kb.dma("pool", S["HML%d" % d][col0:col0 + 128, :], ho[:, :], reads=[ho])

    def st_mlstm_post(self, l):
        kb = self.kb
        S = self.scr
        with kb.stage():
            ident = kb.tile([128, 128], BF16)
            kb.dma("sp", ident[:, :], self.inp["ident_bf"], writes=[ident])
            nw = kb.tile([128, D], F32)
            self.load_bcast(nw, self.inp["ml_norm_w"][l:l + 1, :])
            NBUF = 2
            hf = [kb.tile([128, D], F32) for _ in range(NBUF)]
            hbk = [kb.tile([128, D], F32) for _ in range(NBUF)]
            og = [kb.tile([128, D], BF16) for _ in range(NBUF)]
            junk = [kb.tile([128, 256], F32) for _ in range(NBUF)]
            ms = [kb.tile([128, 4], F32) for _ in range(NBUF)]
            yb = [kb.tile([128, D], BF16) for _ in range(NBUF)]
            hT = [kb.tile([128, 8, 128], BF16) for _ in range(NBUF)]
            psT = [kb.psum([128, 8, 128], BF16) for _ in range(2)]
            for tix in range(self.T // 128):
                i = tix % NBUF
                col0 = tix * 128
                a, b_, o_, jk, m_, y_ = hf[i], hbk[i], og[i], junk[i], ms[i], yb[i]
                kb.dma("sp", a[:, :], S["HML0"][col0:col0 + 128, :], writes=[a])
                kb.dma("sp", b_[:, :], S["HML1"][col0:col0 + 128, :], writes=[b_])
                kb.dma("sp", o_[:, :], S["OG"][col0:col0 + 128, :], writes=[o_])
                kb.op("dve", lambda e: e.tensor_tensor(out=a[:, :], in0=a[:, :], in1=b_[:, :], op=ALU.add), reads=[a, b_], writes=[a])
                for h in range(4):
                    kb.op("act", lambda e: e.activation(out=jk[:, :], in_=a[:, h * 256:(h + 1) * 256], func=AF.Square, accum_out=m_[:, h:h + 1]), reads=[a], writes=[jk, m_])
                kb.op("act", lambda e: e.activation(out=m_[:, :], in_=m_[:, :], func=AF.Sqrt, scale=1.0 / 256, bias=self.eps_t[:, 0:1]), reads=[m_], writes=[m_])
                kb.op("dve", lambda e: e.reciprocal(out=m_[:, :], in_=m_[:, :]), reads=[m_], writes=[m_])
                kb.op("dve", lambda e: e.tensor_tensor(out=a[:, :].rearrange("p (h e) -> p h e", h=4), in0=a[:, :].rearrange("p (h e) -> p h e", h=4), in1=m_[:, 0:4].unsqueeze(2).to_broadcast([128, 4, 256]), op=ALU.mult), reads=[a, m_], writes=[a])
                kb.op("dve", lambda e: e.tensor_tensor(out=a[:, :], in0=a[:, :], in1=nw[:, :], op=ALU.mult), reads=[a, nw], writes=[a])
                kb.op("dve", lambda e: e.tensor_tensor(out=y_[:, :], in0=a[:, :], in1=o_[:, :], op=ALU.mult), reads=[a, o_], writes=[y_])
                self.transpose_store(y_, S["YML"], col0, ident, psT[tix % 2], hT[i])

    def st_lru(self, l):
        kb = self.kb
        S = self.scr
        TS, NB, CTX = self.TS, self.NB, self.CTX
        segs = [(0, CTX), (CTX, TS)]
        with kb.stage():
            cw = kb.tile([128, 8, 4], F32)
            cbias = kb.tile([128, 8], F32)
            grb = kb.tile([128, 2, 8], F32)
            gib = kb.tile([128, 2, 8], F32)
            lam = kb.tile([128, 2, 8], F32)
            cc_ = kb.tile([128, 2, 8], F32)
            kb.dma("sp", cw[:, :, :], self.inp["lru_conv_wT"][l], writes=[cw])
            kb.dma("sp", cbias[:, :], self.inp["lru_conv_bT"][l], writes=[cbias])
            kb.dma("sp", grb[:, :, :], self.inp["lru_gr_bT"][l], writes=[grb])
            kb.dma("sp", gib[:, :, :], self.inp["lru_gi_bT"][l], writes=[gib])
            kb.dma("sp", lam[:, :, :], self.inp["lru_lambdaT"][l], writes=[lam])
            kb.op("act", lambda e: e.activation(out=cc_[:, :, :], in_=lam[:, :, :], func=AF.Exp, scale=-1.0), reads=[lam], writes=[cc_])
            kb.op("act", lambda e: e.activation(out=cc_[:, :, :], in_=cc_[:, :, :], func=AF.Ln, bias=self.one_t[:, 0:1]), reads=[cc_], writes=[cc_])
            kb.op("dve", lambda e: e.tensor_scalar(out=cc_[:, :, :], in0=cc_[:, :, :], scalar1=-8.0, scalar2=None, op0=ALU.mult), reads=[cc_], writes=[cc_])
            wbd = [[[kb.tile([128, 128], F32) for _ in range(2)] for _ in range(2)] for _ in range(2)]
            x = [kb.tile([128, TS], F32) for _ in range(2)]
            u = [kb.tile([128, TS], F32) for _ in range(2)]
            lg = [kb.tile([128, TS], BF16) for _ in range(2)]
            aa = kb.tile([128, TS], F32)
            bx = kb.tile([128, TS], F32)
            hf = kb.tile([128, TS], F32)
            hb = kb.tile([128, TS], F32)
            yo = [kb.tile([128, TS], BF16) for _ in range(2)]
            rr = [kb.tile([128, 512], F32) for _ in range(2)]
            ii = [kb.tile([128, 512], F32) for _ in range(2)]
            a2 = [kb.tile([128, 512], F32) for _ in range(2)]
            psr = [kb.psum([128, 512], F32) for _ in range(2)]
            psi = [kb.psum([128, 512], F32) for _ in range(2)]
            it = 0
            for cc in range(8):
                wv = wbd[cc % 2]
                for d in range(2):
                    for g, nm in enumerate(("lru_gr_w", "lru_gi_w")):
                        w = wv[d][g]
                        kb.op("dve", lambda e: e.memset(w[:, :], 0.0), writes=[w])
                        for blk in range(2):
                            kb.dma("sp", w[blk * 64:(blk + 1) * 64, blk * 64:(blk + 1) * 64], self.inp[nm][l, d, 2 * cc + blk], writes=[w])
                for b in range(NB):
                    i = it % 2
                    it += 1
                    x_, u_, lg_, yo_ = x[i], u[i], lg[i], yo[i]
                    kb.dma("sp", x_[:, :], S["LX"][cc * 128:(cc + 1) * 128, b * TS:(b + 1) * TS], writes=[x_])
                    kb.dma("sp", lg_[:, :], S["LG"][cc * 128:(cc + 1) * 128, b * TS:(b + 1) * TS], writes=[lg_])
                    for (s0, s1) in segs:
                        kb.op("dve", lambda e: e.tensor_scalar(out=u_[:, s0:s1], in0=x_[:, s0:s1], scalar1=cw[:, cc, 2:3], scalar2=cbias[:, cc:cc + 1], op0=ALU.mult, op1=ALU.add), reads=[x_, cw, cbias], writes=[u_])
                        kb.op("dve", lambda e: e.scalar_tensor_tensor(out=u_[:, s0 + 2:s1], in0=x_[:, s0:s1 - 2], scalar=cw[:, cc, 0:1], in1=u_[:, s0 + 2:s1], op0=ALU.mult, op1=ALU.add), reads=[x_, cw, u_], writes=[u_])
                        kb.op("dve", lambda e: e.scalar_tensor_tensor(out=u_[:, s0 + 1:s1], in0=x_[:, s0:s1 - 1], scalar=cw[:, cc, 1:2], in1=u_[:, s0 + 1:s1], op0=ALU.mult, op1=ALU.add), reads=[x_, cw, u_], writes=[u_])
                        kb.op("dve", lambda e: e.scalar_tensor_tensor(out=u_[:, s0:s1 - 1], in0=x_[:, s0 + 1:s1], scalar=cw[:, cc, 3:4], in1=u_[:, s0:s1 - 1], op0=ALU.mult, op1=ALU.add), reads=[x_, cw, u_], writes=[u_])
                    for d in range(2):
                        for t0 in range(0, TS, 512):
                            tsz = min(512, TS - t0)
                            j = (t0 // 512) % 2
                            r_, i_, a2_ = rr[j], ii[j], a2[j]
                            kb.op("pe", lambda e: e.matmul(psr[j][:, 0:tsz], lhsT=wv[d][0][:, :], rhs=u_[:, t0:t0 + tsz], start=True, stop=True), reads=[wv[d][0], u_], writes=[psr[j]])
                            kb.op("pe", lambda e: e.matmul(psi[j][:, 0:tsz], lhsT=wv[d][1][:, :], rhs=u_[:, t0:t0 + tsz], start=True, stop=True), reads=[wv[d][1], u_], writes=[psi[j]])
                            kb.op("act", lambda e: e.activation(out=r_[:, 0:tsz], in_=psr[j][:, 0:tsz], func=AF.Sigmoid, bias=grb[:, d, cc:cc + 1]), reads=[psr[j], grb], writes=[r_])
                            kb.op("act", lambda e: e.activation(out=i_[:, 0:tsz], in_=psi[j][:, 0:tsz], func=AF.Sigmoid, bias=gib[:, d, cc:cc + 1]), reads=[psi[j], gib], writes=[i_])
                            kb.op("act", lambda e: e.activation(out=aa[:, t0:t0 + tsz], in_=r_[:, 0:tsz], func=AF.Exp, scale=cc_[:, d, cc:cc + 1]), reads=[r_, cc_], writes=[aa])
                            kb.op("dve", lambda e: e.tensor_tensor(out=a2_[:, 0:tsz], in0=aa[:, t0:t0 + tsz], in1=aa[:, t0:t0 + tsz], op=ALU.mult), reads=[aa], writes=[a2_])
                            kb.op("act", lambda e: e.activation(out=a2_[:, 0:tsz], in_=a2_[:, 0:tsz], func=AF.Sqrt, scale=-1.0, bias=self.one_t[:, 0:1]), reads=[a2_], writes=[a2_])
                            kb.op("dve", lambda e: e.tensor_tensor(out=i_[:, 0:tsz], in0=i_[:, 0:tsz], in1=u_[:, t0:t0 + tsz], op=ALU.mult), reads=[i_, u_], writes=[i_])
                            kb.op("dve", lambda e: e.tensor_tensor(out=bx[:, t0:t0 + tsz], in0=i_[:, 0:tsz], in1=a2_[:, 0:tsz], op=ALU.mult), reads=[i_, a2_], writes=[bx])
                        if d == 0:
                            kb.op("dve", lambda e: e.tensor_tensor_scan(out=hf[:, :], data0=aa[:, :], data1=bx[:, :], initial=0.0, op0=ALU.mult, op1=ALU.add), reads=[aa, bx], writes=[hf])
                        else:
                            kb.op("dve", lambda e: e.tensor_tensor_scan(out=hb[:, 0:CTX][:, ::-1], data0=aa[:, 0:CTX][:, ::-1], data1=bx[:, 0:CTX][:, ::-1], initial=0.0, op0=ALU.mult, op1=ALU.add), reads=[aa, bx], writes=[hb])
                            kb.op("dve", lambda e: e.tensor_tensor_scan(out=hb[:, CTX:TS][:, ::-1], data0=aa[:, CTX:TS][:, ::-1], data1=bx[:, CTX:TS][:, ::-1], initial=hb[:, 0:1], op0=ALU.mult, op1=ALU.add), reads=[aa, bx, hb], writes=[hb])
                    kb.op("dve", lambda e: e.tensor_tensor(out=hf[:, :], in0=hf[:, :], in1=hb[:, :], op=ALU.add), reads=[hf, hb], writes=[hf])
                    kb.op("dve", lambda e: e.tensor_tensor(out=yo_[:, :], in0=hf[:, :], in1=lg_[:, :], op=ALU.mult), reads=[hf, lg_], writes=[yo_])
                    kb.dma("pool", S["YLRU"][cc * 128:(cc + 1) * 128, b * TS:(b + 1) * TS], yo_[:, :], reads=[yo_])

    def st_rwkv(self, l):
        self.st_rw_shift(l)
        self.st_rw_lowrank(l)
        self.st_rw_core(l)

    def st_rw_shift(self, l):
        kb = self.kb
        S = self.scr
        TS, NB, CTX = self.TS, self.NB, self.CTX
        segs = [(0, CTX), (CTX, TS)]
        with kb.stage():
            mu = kb.tile([128, 27], F32)
            kb.dma("sp", mu[:, :], self.inp["rw_muT"][l], writes=[mu])
            x = [kb.tile([128, TS], F32) for _ in range(2)]
            tm = [kb.tile([128, TS], F32) for _ in range(2)]
            ob = [kb.tile([128, TS], BF16) for _ in range(2)]
            it = 0
            for ch in range(27):
                for b in range(NB):
                    i = it % 2
                    it += 1
                    x_, t_, o_ = x[i], tm[i], ob[i]
                    kb.dma("sp", x_[:, :], S["RW"][ch * 128:(ch + 1) * 128, b * TS:(b + 1) * TS], writes=[x_])
                    for (s0, s1) in segs:
                        kb.op("dve", lambda e: e.tensor_tensor(out=t_[:, s0 + 1:s1 - 1], in0=x_[:, s0:s1 - 2], in1=x_[:, s0 + 2:s1], op=ALU.add), reads=[x_], writes=[t_])
                        kb.op("dve", lambda e: e.tensor_copy(out=t_[:, s0:s0 + 1], in_=x_[:, s0 + 1:s0 + 2]), reads=[x_], writes=[t_])
                        kb.op("dve", lambda e: e.tensor_copy(out=t_[:, s1 - 1:s1], in_=x_[:, s1 - 2:s1 - 1]), reads=[x_], writes=[t_])
                    kb.op("dve", lambda e: e.scalar_tensor_tensor(out=t_[:, :], in0=t_[:, :], scalar=0.5, in1=x_[:, :], op0=ALU.mult, op1=ALU.subtract), reads=[t_, x_], writes=[t_])
                    kb.op("dve", lambda e: e.scalar_tensor_tensor(out=t_[:, :], in0=t_[:, :], scalar=mu[:, ch:ch + 1], in1=x_[:, :], op0=ALU.mult, op1=ALU.add), reads=[t_, x_, mu], writes=[t_])
                    cols = slice(b * TS, (b + 1) * TS)
                    if ch < 24:
                        kb.dma("pool", S["RS"][ch * 128:(ch + 1) * 128, cols], t_[:, :], reads=[t_])
                    else:
                        fn = (AF.Tanh, AF.Copy, AF.Sigmoid)[ch - 24]
                        dst = (S["TD"], S["AD"], S["GD"])[ch - 24]
                        kb.op("act", lambda e: e.activation(out=o_[:, :], in_=t_[:, :], func=fn), reads=[t_], writes=[o_])
                        kb.dma("pool", dst[:, cols], o_[:, :], reads=[o_])

    def st_rw_lowrank(self, l):
        kb = self.kb
        S = self.scr
        for d in range(2):
            holder = {}

            def prep(d=d):
                holder["d0"] = kb.tile([128, 8], F32)
                holder["i0"] = kb.tile([128, 8], F32)
                kb.dma("sp", holder["d0"][:, :], self.inp["rw_decay0T"][l][:, d, :], writes=[holder["d0"]])
                kb.dma("sp", holder["i0"][:, :], self.inp["rw_iclr0T"][l][:, d, :], writes=[holder["i0"]])

            def epi_b(dst, key):
                def epi(ps, c, nsz, t0, tsz):
                    st = self.g_stF[self.g_i % 4]
                    bt = holder[key]
                    kb.op("act", lambda e: e.activation(out=st[0:nsz, 0:tsz], in_=ps[0:nsz, 0:tsz], func=AF.Sigmoid, bias=bt[0:nsz, c // 128:c // 128 + 1]), reads=[ps, bt], writes=[st])
                    kb.dma("act", dst[c:c + nsz, t0:t0 + tsz], st[0:nsz, 0:tsz], reads=[st])
                return epi
            self.gemm(S["TD"][d * 64:(d + 1) * 64, :], 64, self.inp["rw_decay_up"][l, d], [(0, D, "FM", epi_b(S["SG%d" % d], "d0"), prep)])
            self.gemm(S["AD"][d * 64:(d + 1) * 64, :], 64, self.inp["rw_iclr_up"][l, d], [(0, D, "FM", epi_b(S["AA%d" % d], "i0"), prep)])
        self.gemm(S["GD"], 128, self.inp["rw_gate_up"][l], [(0, D, "FM", self.epi_fm(S["GG"], F32), None)])

    def st_rw_core(self, l):
        kb = self.kb
        S = self.scr
        TS, NB, CTX = self.TS, self.NB, self.CTX
        TB = min(CTX, 256)
        nblk = TS // TB
        cblk = CTX // TB
        ncb = TB // 64
        DS = RW_DECAY_SCALE
        v3 = lambda t: t[:, :].rearrange("p (c e) -> p c e", e=64)
        h2 = lambda ap: ap.rearrange("p (h c) -> p h c", h=2)
        with kb.stage():
            identF = kb.tile([128, 128], F32)
            identB = kb.tile([128, 128], BF16)
            bones = kb.tile([128, 128], F32)
            maskA = [kb.tile([128, 128], F32) for _ in range(2)]
            maskN = [kb.tile([64, 64], F32) for _ in range(2)]
            rmask = [kb.tile([128, TB], F32) for _ in range(2)]
            kb.dma("sp", identF[:, :], self.inp["ident_f"], writes=[identF])
            kb.dma("sp", identB[:, :], self.inp["ident_bf"], writes=[identB])
            kb.dma("sp", bones[:, :], self.inp["bones_f"], writes=[bones])
            for d in range(2):
                kb.dma("sp", maskA[d][:, :], self.inp["maskA"][d], writes=[maskA[d]])
                kb.dma("sp", maskN[d][:, :], self.inp["maskN"][d], writes=[maskN[d]])
                kb.dma("sp", rmask[d][:, :], self.inp["rmask"][d][:, 0:TB], writes=[rmask[d]])
            prm = {}
            for nm in ("rw_k_kT", "rw_k_aT"):
                prm[nm] = kb.tile([128, 8], F32)
                kb.dma("sp", prm[nm][:, :], self.inp[nm][l], writes=[prm[nm]])
            omk = kb.tile([128, 8], F32)
            kb.op("dve", lambda e: e.tensor_scalar(out=omk[:, :], in0=prm["rw_k_aT"][:, :], scalar1=-1.0, scalar2=1.0, op0=ALU.mult, op1=ALU.add), reads=[prm["rw_k_aT"]], writes=[omk])

            NQ = 2 * ncb
            bq = lambda ap, p: ap.unsqueeze(1).to_broadcast([p, NQ, 64])
            W = {}
            mk = lambda dt=F32: [kb.tile([128, TB], dt) for _ in range(2)]
            for nm in ("rT", "kT", "vT", "sg", "aT", "kap", "w1", "w2", "E1", "sq"):
                W[nm] = mk()
            QC = [kb.tile([128, ncb, 128], BF16) for _ in range(2)]
            KC = [kb.tile([128, ncb, 128], BF16) for _ in range(2)]
            YT = [kb.tile([64, 2, TB], F32) for _ in range(2)]
            AT4 = [kb.tile([128, ncb, 2, 128], BF16) for _ in range(2)]
            UV4 = [kb.tile([128, ncb, 2, 64], BF16) for _ in range(2)]
            KTT4 = [kb.tile([128, ncb, 128], BF16) for _ in range(2)]
            TTp = [kb.tile([64, NQ, 128], F32) for _ in range(2)]
            for i in range(2):
                kb.op("dve", lambda e: e.memset(TTp[i][:, :, :], 0.0), writes=[TTp[i]])
            P = [kb.tile([64, NQ, 64], F32) for _ in range(2)]
            PT = [kb.tile([64, NQ, 64], F32) for _ in range(2)]
            TT = [kb.tile([64, NQ, 64], F32) for _ in range(2)]
            Pb = [kb.tile([64, NQ, 64], BF16) for _ in range(2)]
            PTb = [kb.tile([64, NQ, 64], BF16) for _ in range(2)]
            TTb = kb.tile([64, NQ, 64], BF16)
            ST32 = kb.tile([128, 64], F32)
            STp = [kb.tile([128, 64], BF16) for _ in range(2)]
            nZ = kb.tile([64, 2, 64], F32)
            stmp = kb.tile([128, 64], F32)
            psM = [kb.psum([64, NQ, 64], F32) for _ in range(2)]
            psN = kb.psum([64, NQ, 64], F32)
            psA = kb.psum([128, 256], F32)
            psV = kb.psum([64, NQ, 64], F32)
            psK = kb.psum([128, ncb, 128], BF16)
            psY = kb.psum([64, 2, 64], F32)
            psU = kb.psum([128, 2, 64], F32)
            psA3 = h2(psA[:, :])

            def gen1(cc, b, d, blk, i):
                rows = slice(cc * 128, (cc + 1) * 128)
                cols = slice(b * TS + blk * TB, b * TS + (blk + 1) * TB)
                r_, k_, v_, sg_, a_ = W["rT"][i], W["kT"][i], W["vT"][i], W["sg"][i], W["aT"][i]
                kap_, w1_, w2_, E1_, sq_ = W["kap"][i], W["w1"][i], W["w2"][i], W["E1"][i], W["sq"][i]
                QC_, KC_ = QC[i], KC[i]
                kb.dma("sp", r_[:, :], S["RS"][rows, cols], writes=[r_])
                kb.dma("sp", k_[:, :], S["RS"][D + cc * 128:D + (cc + 1) * 128, cols], writes=[k_])
                kb.dma("sp", v_[:, :], S["RS"][2 * D + cc * 128:2 * D + (cc + 1) * 128, cols], writes=[v_])
                kb.dma("sp", sg_[:, :], S["SG%d" % d][rows, cols], writes=[sg_])
                kb.dma("sp", a_[:, :], S["AA%d" % d][rows, cols], writes=[a_])
                yield
                kb.op("dve", lambda e: e.tensor_scalar(out=kap_[:, :], in0=k_[:, :], scalar1=prm["rw_k_kT"][:, cc:cc + 1], scalar2=None, op0=ALU.mult), reads=[k_, prm["rw_k_kT"]], writes=[kap_])
                kb.op("dve", lambda e: e.tensor_tensor(out=sq_[:, :], in0=kap_[:, :], in1=kap_[:, :], op=ALU.mult), reads=[kap_], writes=[sq_])
                kb.op("pe", lambda e: e.matmul(psA[:, 0:TB], lhsT=bones[:, :], rhs=sq_[:, :], start=True, stop=True), reads=[bones, sq_], writes=[psA])
                kb.op("dve", lambda e: e.tensor_scalar(out=w1_[:, :], in0=a_[:, :], scalar1=prm["rw_k_aT"][:, cc:cc + 1], scalar2=omk[:, cc:cc + 1], op0=ALU.mult, op1=ALU.add), reads=[a_, prm["rw_k_aT"], omk], writes=[w1_])
                kb.op("dve", lambda e: e.tensor_tensor(out=w1_[:, :], in0=w1_[:, :], in1=k_[:, :], op=ALU.mult), reads=[w1_, k_], writes=[w1_])
                if d == 0:
                    kb.op("dve", lambda e: e.tensor_tensor_scan(out=w2_[:, :], data0=rmask[0][:, :], data1=sg_[:, :], initial=0.0, op0=ALU.mult, op1=ALU.add), reads=[rmask[0], sg_], writes=[w2_])
                else:
                    kb.op("dve", lambda e: e.tensor_tensor_scan(out=w2_[:, ::-1], data0=rmask[1][:, ::-1], data1=sg_[:, ::-1], initial=0.0, op0=ALU.mult, op1=ALU.add), reads=[rmask[1], sg_], writes=[w2_])
                yield
                kb.op("act", lambda e: e.activation(out=sq_[:, :], in_=psA[:, 0:TB], func=AF.Sqrt), reads=[psA], writes=[sq_])
                kb.op("act", lambda e: e.activation(out=E1_[:, :], in_=w2_[:, :], func=AF.Exp, scale=-DS), reads=[w2_], writes=[E1_])
                kb.op("dve", lambda e: e.tensor_scalar(out=sq_[:, :], in0=sq_[:, :], scalar1=1e-12, scalar2=None, op0=ALU.max), reads=[sq_], writes=[sq_])
                kb.op("dve", lambda e: e.reciprocal(out=sq_[:, :], in_=sq_[:, :]), reads=[sq_], writes=[sq_])
                kb.op("dve", lambda e: e.tensor_tensor(out=kap_[:, :], in0=kap_[:, :], in1=sq_[:, :], op=ALU.mult), reads=[kap_, sq_], writes=[kap_])
                kb.op("dve", lambda e: e.tensor_tensor(out=sg_[:, :], in0=w2_[:, :], in1=sg_[:, :], op=ALU.subtract), reads=[w2_, sg_], writes=[sg_])
                yield
                kb.op("act", lambda e: e.activation(out=sg_[:, :], in_=sg_[:, :], func=AF.Exp, scale=-DS), reads=[sg_], writes=[sg_])
                kb.op("act", lambda e: e.activation(out=w2_[:, :], in_=w2_[:, :], func=AF.Exp, scale=DS), reads=[w2_], writes=[w2_])
                kb.op("dve", lambda e: e.tensor_tensor(out=QC_[:, :, 64:128], in0=v3(r_), in1=v3(E1_), op=ALU.mult), reads=[r_, E1_], writes=[QC_])
                kb.op("dve", lambda e: e.tensor_tensor(out=a_[:, :], in0=a_[:, :], in1=kap_[:, :], op=ALU.mult), reads=[a_, kap_], writes=[a_])
                yield
                kb.op("dve", lambda e: e.tensor_tensor(out=QC_[:, :, 0:64], in0=v3(kap_), in1=v3(sg_), op=ALU.mult), reads=[kap_, sg_], writes=[QC_])
                kb.op("dve", lambda e: e.tensor_tensor(out=KC_[:, :, 0:64], in0=v3(w1_), in1=v3(w2_), op=ALU.mult), reads=[w1_, w2_], writes=[KC_])
                kb.op("dve", lambda e: e.tensor_tensor(out=KC_[:, :, 64:128], in0=v3(a_), in1=v3(w2_), op=ALU.mult), reads=[a_, w2_], writes=[KC_])
                yield
                for h in range(2):
                    hs = slice(h * 64, (h + 1) * 64)
                    for c in range(ncb):
                        q = 2 * c + h
                        kb.op("pe", lambda e: e.matmul(psM[0][:, q, :], lhsT=QC_[hs, c, 0:64], rhs=KC_[hs, c, 64:128], start=True, stop=True), reads=[KC_, QC_], writes=[psM[0]], row=h * 64)
                        kb.op("pe", lambda e: e.matmul(psM[1][:, q, :], lhsT=KC_[hs, c, 64:128], rhs=QC_[hs, c, 0:64], start=True, stop=True), reads=[KC_, QC_], writes=[psM[1]], row=h * 64)
                yield
                kb.op("dve", lambda e: e.scalar_tensor_tensor(out=P[0][:, :, :], in0=psM[0][:, :, :], scalar=-1.0, in1=bq(maskN[d][:, :], 64), op0=ALU.mult, op1=ALU.mult), reads=[psM[0], maskN[d]], writes=[P[0]])
                kb.op("dve", lambda e: e.scalar_tensor_tensor(out=PT[0][:, :, :], in0=psM[1][:, :, :], scalar=-1.0, in1=bq(maskA[d][0:64, 0:64], 64), op0=ALU.mult, op1=ALU.mult), reads=[psM[1], maskA[d]], writes=[PT[0]])
                kb.op("dve", lambda e: e.tensor_tensor(out=TT[0][:, :, :], in0=PT[0][:, :, :], in1=bq(identF[0:64, 0:64], 64), op=ALU.add), reads=[PT[0], identF], writes=[TT[0]])
                yield
                extra = []
                for c in range(ncb):
                    def ex_a(c=c):
                        for h in range(2):
                            hs = slice(h * 64, (h + 1) * 64)
                            kb.op("pe", lambda e: e.matmul(psA3[:, h, :], lhsT=KC_[hs, c, :], rhs=QC_[hs, c, :], start=True, stop=True), reads=[KC_, QC_], writes=[psA], row=h * 64)
                        kb.op("dve", lambda e: e.tensor_tensor(out=AT4[i][:, c, :, :], in0=psA3, in1=maskA[d][:, :].unsqueeze(1).to_broadcast([128, 2, 128]), op=ALU.mult), reads=[psA, maskA[d]], writes=[AT4[i]])
                    extra.append(ex_a)

                def ex_v():
                    for h in range(2):
                        hs = slice(h * 64, (h + 1) * 64)
                        for c in range(ncb):
                            kb.op("pe", lambda e: e.transpose(psV[:, 2 * c + h, :], v_[hs, c * 64:(c + 1) * 64], identF[hs, hs]), reads=[v_, identF], writes=[psV], row=h * 64)
                    kb.op("act", lambda e: e.activation(out=UV4[i][0:64, :, :, :].rearrange("p c h e -> p (c h) e"), in_=psV[:, :, :], func=AF.Copy), reads=[psV], writes=[UV4[i]])
                extra.append(ex_v)

                def ex_k():
                    for c in range(ncb):
                        kb.op("pe", lambda e: e.transpose(psK[:, c, :], KC_[:, c, :], identB[:, :]), reads=[KC_, identB], writes=[psK])
                    kb.op("act", lambda e: e.activation(out=KTT4[i][:, :, :], in_=psK[:, :, :], func=AF.Copy), reads=[psK], writes=[KTT4[i]])
                extra.append(ex_k)
                BL = 3
                for m in range(1, 6):
                    lo = m >= BL
                    if lo:
                        pc_, pn_ = Pb[(m - 1) % 2], Pb[m % 2]
                        tc_, tn_ = PTb[(m - 1) % 2], PTb[m % 2]
                    else:
                        pc_, pn_ = P[(m - 1) % 2], P[m % 2]
                        tc_, tn_ = PT[(m - 1) % 2], PT[m % 2]
                    if m + 1 == BL:
                        tn_ = PTb[m % 2]
                    for q in range(NQ):
                        kb.op("pe", lambda e: e.matmul(psM[0][:, q, :], lhsT=tc_[:, q, :], rhs=pc_[:, q, :], start=True, stop=True), reads=[tc_, pc_], writes=[psM[0]], row=0)
                    if m < 5:
                        for q in range(NQ):
                            kb.op("pe", lambda e: e.matmul(psM[1][:, q, :], lhsT=pc_[:, q, :], rhs=tc_[:, q, :], start=True, stop=True), reads=[tc_, pc_], writes=[psM[1]], row=0)
                    if extra:
                        extra.pop(0)()
                    yield
                    kb.op("act", lambda e: e.activation(out=pn_[:, :, :], in_=psM[0][:, :, :], func=AF.Copy), reads=[psM[0]], writes=[pn_])
                    if m < 5:
                        kb.op("dve", lambda e: e.tensor_copy(out=tn_[:, :, :], in_=psM[1][:, :, :]), reads=[psM[1]], writes=[tn_])
                    if m + 1 == BL:
                        pb_ = Pb[m % 2]
                        kb.op("pool", lambda e: e.tensor_copy(out=pb_[:, :, :], in_=pn_[:, :, :]), reads=[pn_], writes=[pb_])
                    yield
                    tt_c, tt_n = TT[(m - 1) % 2], TT[m % 2]
                    tt_r = TTb if lo else tt_c
                    for q in range(NQ):
                        kb.op("pe", lambda e: e.matmul(psN[:, q, :], lhsT=pn_[:, q, :], rhs=tt_r[:, q, :], start=True, stop=True), reads=[pn_, tt_r], writes=[psN], row=0)
                    if extra:
                        extra.pop(0)()
                    yield
                    if m < 5:
                        kb.op("dve", lambda e: e.tensor_tensor(out=tt_n[:, :, :], in0=psN[:, :, :], in1=tt_c[:, :, :], op=ALU.add), reads=[psN, tt_c], writes=[tt_n])
                        if m + 1 >= BL:
                            kb.op("pool", lambda e: e.tensor_copy(out=TTb[:, :, :], in_=tt_n[:, :, :]), reads=[tt_n], writes=[TTb])
                    else:
                        kb.op("dve", lambda e: e.tensor_tensor(out=TTp[i][:, :, 64:128], in0=psN[:, :, :], in1=tt_c[:, :, :], op=ALU.add), reads=[psN, tt_c], writes=[TTp[i]])
                    yield
                while extra:
                    extra.pop(0)()
                    yield

            def gen2(cc, b, d, blk, i):
                rows = slice(cc * 128, (cc + 1) * 128)
                cols = slice(b * TS + blk * TB, b * TS + (blk + 1) * TB)
                QC_, KC_, yt, E1_ = QC[i], KC[i], YT[i], W["E1"][i]
                AT_, UV_, KTT_, TTp_ = AT4[i], UV4[i], KTT4[i], TTp[i]
                for c in (range(ncb) if d == 0 else range(ncb - 1, -1, -1)):
                    p0 = c * 64
                    for h in range(2):
                        hs = slice(h * 64, (h + 1) * 64)
                        kb.op("pe", lambda e: e.matmul(psY[:, h, :], lhsT=QC_[:, c, 0:64], rhs=STp[h][:, :], start=True, stop=False), reads=[QC_, STp[h]], writes=[psY])
                        kb.op("pe", lambda e: e.matmul(psY[:, h, :], lhsT=AT_[0:64, c, h, 0:64], rhs=UV_[0:64, c, h, :], start=False, stop=True), reads=[AT_, UV_], writes=[psY])
                    yield
                    kb.op("dve", lambda e: e.tensor_scalar(out=nZ[:, :, :], in0=psY[:, :, :], scalar1=-1.0, scalar2=None, op0=ALU.mult), reads=[psY], writes=[nZ])
                    yield
                    for h in range(2):
                        kb.op("pe", lambda e: e.matmul(psU[:, h, :], lhsT=TTp_[:, 2 * c + h, :], rhs=nZ[:, h, :], start=True, stop=True), reads=[TTp_, nZ], writes=[psU])
                    yield
                    kb.op("act", lambda e: e.activation(out=UV_[64:128, c, :, :], in_=psU[64:128, :, :], func=AF.Copy), reads=[psU], writes=[UV_])
                    yield
                    for h in range(2):
                        hs = slice(h * 64, (h + 1) * 64)
                        kb.op("pe", lambda e: e.matmul(psY[:, h, :], lhsT=STp[h][:, :], rhs=QC_[:, c, 64:128], start=True, stop=False), reads=[QC_, STp[h]], writes=[psY])
                        kb.op("pe", lambda e: e.matmul(psY[:, h, :], lhsT=UV_[:, c, h, :], rhs=AT_[:, c, h, 64:128], start=False, stop=True), reads=[AT_, UV_], writes=[psY])
                    kb.op("pe", lambda e: e.matmul(psU[:, 0, :], lhsT=KTT_[:, c, :], rhs=UV_[:, c, 1, :], start=True, stop=True), reads=[KTT_, UV_], writes=[psU])
                    kb.op("pe", lambda e: e.matmul(psU[0:64, 0, :], lhsT=KTT_[:, c, 0:64], rhs=UV_[:, c, 0, :], start=True, stop=True), reads=[KTT_, UV_], writes=[psU])
                    yield
                    wcol = p0 + 63 if d == 0 else p0
                    kb.op("dve", lambda e: e.tensor_tensor(out=stmp[:, :], in0=psU[:, 0, :], in1=ST32[:, :], op=ALU.add), reads=[psU, ST32], writes=[stmp])
                    kb.op("act", lambda e: e.activation(out=yt[:, :, p0:p0 + 64], in_=psY[:, :, :], func=AF.Copy), reads=[psY], writes=[yt])
                    kb.op("dve", lambda e: e.tensor_scalar(out=ST32[:, :], in0=stmp[:, :], scalar1=E1_[:, wcol:wcol + 1], scalar2=None, op0=ALU.mult), reads=[stmp, E1_], writes=[ST32])
                    yield
                    kb.op("act", lambda e: e.activation(out=STp[0][0:64, :], in_=ST32[0:64, :], func=AF.Copy), reads=[ST32], writes=[STp[0]])
                    kb.op("dve", lambda e: e.tensor_copy(out=STp[1][64:128, :], in_=ST32[64:128, :]), reads=[ST32], writes=[STp[1]])
                    yield
                kb.dma("pool", S["YT%d" % d][rows, cols].rearrange("(h v) t -> v h t", h=2), yt[:, :, :], reads=[yt])

            def drain(g):
                for _ in g:
                    pass

            seqb = []
            for cc in range(8):
                for b in range(NB):
                    for d in range(2):
                        bl = list(range(nblk)) if d == 0 else list(range(cblk - 1, -1, -1)) + list(range(nblk - 1, cblk - 1, -1))
                        for n_, blk in enumerate(bl):
                            seqb.append((cc, b, d, blk, n_ == 0))
            drain(gen1(*seqb[0][:4], 0))
            for k, (cc, b, d, blk, first) in enumerate(seqb):
                if first:
                    kb.op("dve", lambda e: e.memset(ST32[:, :], 0.0), writes=[ST32])
                    for h_ in range(2):
                        kb.op("dve", lambda e: e.memset(STp[h_][:, :], 0.0), writes=[STp[h_]])
                gens = [gen2(cc, b, d, blk, k % 2)]
                if k + 1 < len(seqb):
                    if self.cfg.get("rw_interleave", True):
                        gens.append(gen1(*seqb[k + 1][:4], (k + 1) % 2))
                    else:
                        drain(gen1(*seqb[k + 1][:4], (k + 1) % 2))
                while gens:
                    for g in list(gens):
                        try:
                            next(g)
                        except StopIteration:
                            gens.remove(g)
        self.st_rw_post(l)

    def st_rw_post(self, l):
        kb = self.kb
        S = self.scr
        with kb.stage():
            bones = kb.tile([128, 128], F32)
            kb.dma("sp", bones[:, :], self.inp["bones_f"], writes=[bones])
            prm = {}
            for nm in ("rw_k_aT", "rw_r_kT", "rw_ln_wT", "rw_ln_bT"):
                prm[nm] = kb.tile([128, 8], F32)
                kb.dma("sp", prm[nm][:, :], self.inp[nm][l], writes=[prm[nm]])
            omk2 = kb.tile([128, 8], F32)
            kb.op("dve", lambda e: e.tensor_scalar(out=omk2[:, :], in0=prm["rw_k_aT"][:, :], scalar1=-2.0, scalar2=2.0, op0=ALU.mult, op1=ALU.add), reads=[prm["rw_k_aT"]], writes=[omk2])
            lneps = kb.tile([128, 1], F32)
            kb.op("dve", lambda e: e.memset(lneps[:, :], RW_LN_EPS), writes=[lneps])
            psB = [kb.psum([128, 512], F32) for _ in range(3)]
            PB = 512
            mk2 = lambda dt=F32: [kb.tile([128, PB], dt) for _ in range(2)]
            r2, k2, v2, g2, a0, a1, y0, y1, tq, uq = mk2(), mk2(), mk2(), mk2(), mk2(), mk2(), mk2(), mk2(), mk2(), mk2()
            yo = mk2(BF16)
            it = 0
            for cc in range(8):
                rows = slice(cc * 128, (cc + 1) * 128)
                for t0 in range(0, self.T, PB):
                    tsz = min(PB, self.T - t0)
                    i = it % 2
                    it += 1
                    cs_ = slice(t0, t0 + tsz)
                    ld = [(r2[i], S["RS"][rows, cs_]), (k2[i], S["RS"][D + cc * 128:D + (cc + 1) * 128, cs_]), (v2[i], S["RS"][2 * D + cc * 128:2 * D + (cc + 1) * 128, cs_]),
                          (g2[i], S["GG"][rows, cs_]), (a0[i], S["AA0"][rows, cs_]), (a1[i], S["AA1"][rows, cs_]), (y0[i], S["YT0"][rows, cs_]), (y1[i], S["YT1"][rows, cs_])]
                    for tl, src in ld:
                        kb.dma("sp", tl[:, 0:tsz], src, writes=[tl])
                    r_, k_, v_, g_, a0_, a1_, y_, y1_, tq_, uq_, yo_ = r2[i], k2[i], v2[i], g2[i], a0[i], a1[i], y0[i], y1[i], tq[i], uq[i], yo[i]
                    sl = slice(0, tsz)
                    p1, p2, p3 = psB
                    kb.op("dve", lambda e: e.tensor_tensor(out=a0_[:, sl], in0=a0_[:, sl], in1=a1_[:, sl], op=ALU.add), reads=[a0_, a1_], writes=[a0_])
                    kb.op("dve", lambda e: e.tensor_scalar(out=a0_[:, sl], in0=a0_[:, sl], scalar1=prm["rw_k_aT"][:, cc:cc + 1], scalar2=omk2[:, cc:cc + 1], op0=ALU.mult, op1=ALU.add), reads=[a0_, prm["rw_k_aT"], omk2], writes=[a0_])
                    kb.op("dve", lambda e: e.tensor_tensor(out=a0_[:, sl], in0=a0_[:, sl], in1=k_[:, sl], op=ALU.mult), reads=[a0_, k_], writes=[a0_])
                    kb.op("dve", lambda e: e.scalar_tensor_tensor(out=a0_[:, sl], in0=a0_[:, sl], scalar=prm["rw_r_kT"][:, cc:cc + 1], in1=r_[:, sl], op0=ALU.mult, op1=ALU.mult), reads=[a0_, prm["rw_r_kT"], r_], writes=[a0_])
                    kb.op("pe", lambda e: e.matmul(p1[:, sl], lhsT=bones[:, :], rhs=a0_[:, sl], start=True, stop=True), reads=[bones, a0_], writes=[p1])
                    kb.op("dve", lambda e: e.tensor_tensor(out=y_[:, sl], in0=y_[:, sl], in1=y1_[:, sl], op=ALU.add), reads=[y_, y1_], writes=[y_])
                    kb.op("pe", lambda e: e.matmul(p2[:, sl], lhsT=bones[:, :], rhs=y_[:, sl], start=True, stop=True), reads=[bones, y_], writes=[p2])
                    kb.op("dve", lambda e: e.tensor_tensor(out=a1_[:, sl], in0=p1[:, sl], in1=v_[:, sl], op=ALU.mult), reads=[p1, v_], writes=[a1_])
                    kb.op("dve", lambda e: e.scalar_tensor_tensor(out=tq_[:, sl], in0=p2[:, sl], scalar=-1.0 / 64, in1=y_[:, sl], op0=ALU.mult, op1=ALU.add), reads=[p2, y_], writes=[tq_])
                    kb.op("act", lambda e: e.activation(out=uq_[:, sl], in_=tq_[:, sl], func=AF.Square), reads=[tq_], writes=[uq_])
                    kb.op("pe", lambda e: e.matmul(p3[:, sl], lhsT=bones[:, :], rhs=uq_[:, sl], start=True, stop=True), reads=[bones, uq_], writes=[p3])
                    kb.op("act", lambda e: e.activation(out=uq_[:, sl], in_=p3[:, sl], func=AF.Sqrt, scale=1.0 / 64, bias=lneps[:, 0:1]), reads=[p3, lneps], writes=[uq_])
                    kb.op("dve", lambda e: e.reciprocal(out=uq_[:, sl], in_=uq_[:, sl]), reads=[uq_], writes=[uq_])
                    kb.op("dve", lambda e: e.tensor_tensor(out=tq_[:, sl], in0=tq_[:, sl], in1=uq_[:, sl], op=ALU.mult), reads=[tq_, uq_], writes=[tq_])
                    kb.op("act", lambda e: e.activation(out=tq_[:, sl], in_=tq_[:, sl], func=AF.Identity, scale=prm["rw_ln_wT"][:, cc:cc + 1], bias=prm["rw_ln_bT"][:, cc:cc + 1]), reads=[tq_, prm["rw_ln_wT"], prm["rw_ln_bT"]], writes=[tq_])
                    kb.op("dve", lambda e: e.tensor_tensor(out=tq_[:, sl], in0=tq_[:, sl], in1=a1_[:, sl], op=ALU.add), reads=[tq_, a1_], writes=[tq_])
                    kb.op("dve", lambda e: e.tensor_tensor(out=yo_[:, sl], in0=tq_[:, sl], in1=g_[:, sl], op=ALU.mult), reads=[tq_, g_], writes=[yo_])
                    kb.dma("pool", S["YRW"][rows, cs_], yo_[:, sl], reads=[yo_])

    def build(self):
        nc, kb = self.nc, self.kb
        NB, CTX, LAT, L, T, TS = self.NB, self.CTX, self.LAT, self.L, self.T, self.TS
        self.din("x", [NB, LAT, D])
        self.din("ctx", [NB, CTX, D])
        self.din("condT", [128, 8, 3])
        self.din("w_mod", [L, D, 6 * D])
        self.din("b_mod", [L, 6 * D])
        self.din("norm1_w", [L, D])
        self.din("norm2_w", [L, D])
        self.din("w_in", [L, D, INW])
        self.din("ml_ig_b", [L, 2, 4])
        self.din("ml_fg_b", [L, 2, 4])
        self.din("ml_norm_w", [L, D])
        self.din("lru_conv_wT", [L, 128, 8, 4])
        self.din("lru_conv_bT", [L, 128, 8])
        self.din("lru_gr_w", [L, 2, 16, 64, 64])
        self.din("lru_gr_bT", [L, 128, 2, 8])
        self.din("lru_gi_w", [L, 2, 16, 64, 64])
        self.din("lru_gi_bT", [L, 128, 2, 8])
        self.din("lru_lambdaT", [L, 128, 2, 8])
        self.din("rw_muT", [L, 128, 27])
        self.din("rw_decay0T", [L, 128, 2, 8])
        self.din("rw_iclr0T", [L, 128, 2, 8])
        self.din("rw_decay_up", [L, 2, 64, D])
        self.din("rw_iclr_up", [L, 2, 64, D])
        self.din("rw_gate_up", [L, 128, D])
        for nm in ("rw_k_kT", "rw_k_aT", "rw_r_kT", "rw_ln_wT", "rw_ln_bT"):
            self.din(nm, [L, 128, 8])
        self.din("bones_f", [128, 128])
        self.din("maskA", [2, 128, 128])
        self.din("maskN", [2, 64, 64])
        self.din("rmask", [2, 128, TS])
        self.din("out_ml", [L, D, D])
        self.din("out_lru", [L, D, D])
        self.din("out_rw", [L, D, D])
        self.din("w_out", [L, D, D])
        self.din("w_ffn_in", [L, D, 2 * FFN])
        self.din("w_ffn_out", [L, FFN, D])
        self.din("final_norm_w", [D])
        self.din("ident_bf", [128, 128], BF16)
        self.din("ident_f", [128, 128])
        self.din("ones_f", [128, 128])
        self.din("tri", [2, 128, 128])
        self.din("negm", [2, 128, 128])
        self.out_ap = self.nc.dram_tensor("out", [NB, LAT, D], F32, kind="ExternalOutput").ap()
        self.dscr("XR", [NB, TS, D], F32)
        self.dscr("MOD", [L, 3, 6 * D], F32)
        self.dscr("HT", [D, T], BF16)
        self.dscr("QT", [D, T], BF16)
        self.dscr("KT", [D, T], BF16)
        self.dscr("KTM", [T, D], BF16)
        self.dscr("VTM", [T, D], BF16)
        self.dscr("OG", [T, D], BF16)
        self.dscr("IFG", [T, 16], F32)
        self.dscr("LX", [D, T], F32)
        self.dscr("LG", [D, T], BF16)
        self.dscr("RW", [RWSEG, T], F32)
        self.dscr("MG", [3 * D, T], BF16)
        self.dscr("HML0", [T, D], F32)
        self.dscr("HML1", [T, D], F32)
        self.dscr("YML", [D, T], BF16)
        self.dscr("YLRU", [D, T], BF16)
        self.dscr("YRW", [D, T], BF16)
        self.dscr("YACC", [D, T], F32)
        self.dscr("YM", [D, T], BF16)
        self.dscr("RS", [3 * D, T], F32)
        self.dscr("TD", [128, T], BF16)
        self.dscr("AD", [128, T], BF16)
        self.dscr("GD", [128, T], BF16)
        for d in range(2):
            self.dscr("SG%d" % d, [D, T], F32)
            self.dscr("AA%d" % d, [D, T], F32)
            self.dscr("YT%d" % d, [D, T], F32)
        self.dscr("GG", [D, T], F32)
        self.dscr("AG", [FFN, T], BF16)
        self.dscr("AT", [FFN, T], BF16)
        self.eps_t = kb.tile([128, 1], F32, "eps")
        self.one_t = kb.tile([128, 1], F32, "one")
        kb.op("dve", lambda e: e.memset(self.eps_t[:, :], EPS), writes=[self.eps_t])
        kb.op("dve", lambda e: e.memset(self.one_t[:, :], 1.0), writes=[self.one_t])
        upto = self.cfg.get("upto", None)
        seq = [("init", lambda: self.st_init()), ("adaln", lambda: self.st_adaln())]
        for l in range(L):
            seq += [("norm1", lambda l=l: self.st_norm(l, 1)), ("win", lambda l=l: self.st_win(l)),
                    ("mlstm", lambda l=l: self.st_mlstm(l)), ("mlpost", lambda l=l: self.st_mlstm_post(l)),
                    ("lru", lambda l=l: self.st_lru(l)), ("rwkv", lambda l=l: self.st_rwkv(l)),
                    ("merge", lambda l=l: self.st_merge(l)), ("wout", lambda l=l: self.st_wout(l)),
                    ("ffn", lambda l=l: self.st_ffn(l))]
        seq += [("final", lambda: self.st_norm(0, 1, final=True))]
        skip = self.cfg.get("skip", ())
        for name, fn in seq:
            if name not in skip:
                fn()
            if upto is not None and name == upto:
                break
        kb.finish()


def host_consts():
    c = {}
    c["ident_bf"] = np.eye(128, dtype=np.float32).astype(ml_dtypes.bfloat16)
    c["ident_f"] = np.eye(128, dtype=np.float32)
    c["ones_f"] = np.ones((128, 128), np.float32)
    s = np.arange(128)[:, None]
    t = np.arange(128)[None, :]
    tri = np.stack([(s <= t), (s >= t)]).astype(np.float32)
    c["tri"] = tri
    c["negm"] = ((1.0 - tri) * -30000.0).astype(np.float32)
    bo = np.zeros((128, 128), np.float32)
    bo[:64, :64] = 1.0
    bo[64:, 64:] = 1.0
    c["bones_f"] = bo
    j = np.arange(64)[:, None]
    t = np.arange(64)[None, :]
    mA = []
    mN = []
    for d in range(2):
        strict = (j < t) if d == 0 else (j > t)
        incl = (j <= t) if d == 0 else (j >= t)
        blk = np.concatenate([strict, incl], 1).astype(np.float32)
        mA.append(np.concatenate([blk, blk], 0))
        mN.append(strict.T.astype(np.float32))
    c["maskA"] = np.stack(mA)
    c["maskN"] = np.stack(mN)
    return c


def chanT(a):
    a = np.asarray(a, np.float32)
    lead = a.shape[:-1]
    a = a.reshape(lead + (8, 128))
    return np.ascontiguousarray(np.moveaxis(a, -1, 0))


def make_in_maps(inputs, cfg, ncores):
    NB, L = cfg["NB"], cfg["DEPTH"]
    consts = host_consts()
    f = lambda k: np.ascontiguousarray(np.asarray(inputs[k], np.float32)[:L])
    shared = {
        "w_mod": f("w_mod"), "b_mod": f("b_mod"), "norm1_w": f("norm1_w"), "norm2_w": f("norm2_w"),
        "w_in": f("w_in"), "ml_ig_b": f("ml_ig_b"), "ml_fg_b": f("ml_fg_b"), "ml_norm_w": f("ml_norm_w"),
        "lru_gr_w": f("lru_gr_w"), "lru_gi_w": f("lru_gi_w"),
        "out_ml": f("out_ml"), "out_lru": f("out_lru"), "out_rw": f("out_rw"), "w_out": f("w_out"),
        "w_ffn_in": f("w_ffn_in"), "w_ffn_out": f("w_ffn_out"),
        "final_norm_w": np.asarray(inputs["final_norm_w"], np.float32),
    }
    cw = np.asarray(inputs["lru_conv_w"], np.float32)[:L]
    shared["lru_conv_wT"] = np.ascontiguousarray(np.stack([np.moveaxis(chanT(cw[l]), 1, 2) for l in range(L)]))
    shared["lru_conv_bT"] = np.ascontiguousarray(np.stack([chanT(np.asarray(inputs["lru_conv_b"], np.float32)[l]) for l in range(L)]))
    for k in ("lru_gr_b", "lru_gi_b", "lru_lambda"):
        a = np.asarray(inputs[k], np.float32)[:L]
        shared[k + "T"] = np.ascontiguousarray(np.stack([chanT(a[l]) for l in range(L)]))
    TS = cfg["CTX"] + cfg["LAT"]
    tt = np.arange(TS)
    rm = np.stack([(tt % 64 != 0), (tt % 64 != 63)]).astype(np.float32)
    shared["rmask"] = np.ascontiguousarray(np.broadcast_to(rm[:, None, :], (2, 128, TS)))
    mu = np.asarray(inputs["rw_mu"], np.float32)[:L]
    shared["rw_muT"] = np.ascontiguousarray(mu.reshape(L, 27, 128).transpose(0, 2, 1))
    for k in ("rw_decay0", "rw_iclr0"):
        a = np.asarray(inputs[k], np.float32)[:L]
        shared[k + "T"] = np.ascontiguousarray(np.stack([chanT(a[l]) for l in range(L)]))
    for k in ("rw_k_k", "rw_k_a", "rw_r_k", "rw_ln_w", "rw_ln_b"):
        a = np.asarray(inputs[k], np.float32)[:L]
        shared[k + "T"] = np.ascontiguousarray(np.stack([chanT(a[l]) for l in range(L)]))
    for k in ("rw_decay_up", "rw_iclr_up", "rw_gate_up"):
        shared[k] = f(k)
    shared.update(consts)
    x = np.asarray(inputs["x"], np.float32)
    ctx = np.asarray(inputs["ctx"], np.float32)
    c = np.asarray(inputs["c"], np.float32)
    cc = np.asarray(inputs["c_ctx"], np.float32)
    maps = []
    for i in range(ncores):
        m = dict(shared)
        m["x"] = np.ascontiguousarray(x[i * NB:(i + 1) * NB])
        m["ctx"] = np.ascontiguousarray(ctx[i * NB:(i + 1) * NB])
        cond = np.concatenate([c[i * NB:(i + 1) * NB], cc[None, :]], 0)
        m["condT"] = np.ascontiguousarray(cond.T.reshape(8, 128, 3).transpose(1, 0, 2))
        maps.append(m)
    return maps


def kernel(**inputs):
    cfg = {"NB": 2, "CTX": 256, "LAT": 4096, "DEPTH": 4}
    nc = bass.Bass("TRN2", target_bir_lowering=False)
    p = Prog(nc, cfg)
    p.build()
    maps = make_in_maps(inputs, cfg, NCORES)
    res = run_bass_kernel_spmd(nc, maps, core_ids=list(range(NCORES)))
    return np.concatenate([np.asarray(r["out"], np.float32) for r in res.results], axis=0)
```

```python
import math
from contextlib import ExitStack, contextmanager
import numpy as np
import ml_dtypes
import concourse.bass as bass
import concourse.mybir as mybir
from concourse.bass_utils import run_bass_kernel_spmd

F32 = mybir.dt.float32
BF16 = mybir.dt.bfloat16
AF = mybir.ActivationFunctionType
ALU = mybir.AluOpType
AX = mybir.AxisListType

D = 1024
NCORES = 8
FFN = 2816
INW = 12688
RWSEG = 3456
EPS = 1e-6
RW_LN_EPS = 64e-5
RW_DECAY_SCALE = math.exp(-0.5)


class Buf:
    __slots__ = ("w", "r", "prow")

    def __init__(self):
        self.w = {}
        self.r = {}
        self.prow = None


class Tile:
    def __init__(self, h):
        self.h = h
        self.b = Buf()

    def __getitem__(self, k):
        return self.h[k]


class KB:
    def __init__(self, nc, n_dma=56):
        self.nc = nc
        self.E = {"pe": nc.tensor, "act": nc.scalar, "dve": nc.vector, "pool": nc.gpsimd, "sp": nc.sync}
        self.sem = {}
        self.cnt = {}
        for e in ("pe", "act", "dve", "pool"):
            self.sem[e] = nc.alloc_semaphore("c_" + e)
            self.cnt[e] = 0
        self.dsem = [nc.alloc_semaphore("d%d" % i) for i in range(n_dma)]
        self.dval = [0] * n_dma
        self.drr = 0
        self.drr_sw = 0
        self.seen = {}
        self.n_inst = 0
        self.es = None
        self.uid = 0

    def tile(self, shape, dt, name="t"):
        self.uid += 1
        nm = "%s_%d" % (name, self.uid)
        if self.es is not None:
            return Tile(self.es.enter_context(self.nc.sbuf_tensor(nm, list(shape), dt)))
        return Tile(self.nc.alloc_sbuf_tensor(nm, list(shape), dt))

    def psum(self, shape, dt, name="p"):
        self.uid += 1
        nm = "%s_%d" % (name, self.uid)
        if self.es is not None:
            return Tile(self.es.enter_context(self.nc.psum_tensor(nm, list(shape), dt)))
        return Tile(self.nc.alloc_psum_tensor(nm, list(shape), dt))

    @contextmanager
    def stage(self):
        es = ExitStack()
        self.es = es
        try:
            yield
            self.barrier()
        finally:
            self.es = None
            es.close()

    def _semh(self, key):
        return self.sem[key] if isinstance(key, str) else self.dsem[key]

    def _wait(self, e, key, val):
        if e == "pe" and key == "pe":
            return
        if self.seen.get((e, key), 0) >= val:
            return
        self.E[e].wait_ge(self._semh(key), val)
        self.seen[(e, key)] = val
        self.n_inst += 1

    @staticmethod
    def _bufs(lst):
        return [x.b if isinstance(x, Tile) else x for x in lst]

    def _deps(self, e, reads, writes):
        deps = {}
        for b in reads:
            for k, v in b.w.items():
                if deps.get(k, 0) < v:
                    deps[k] = v
        for b in writes:
            for k, v in b.w.items():
                if deps.get(k, 0) < v:
                    deps[k] = v
            for k, v in b.r.items():
                if deps.get(k, 0) < v:
                    deps[k] = v
        for k, v in deps.items():
            self._wait(e, k, v)

    def _mark(self, ticket, reads, writes):
        k, v = ticket
        for b in reads:
            b.r[k] = v
        for b in writes:
            b.w = {k: v}
            b.r = {}

    def op(self, e, fn, reads=(), writes=(), row=None):
        reads = self._bufs(reads)
        writes = self._bufs(writes)
        if e == "pe":
            if row is not None and any(b.prow is not None and b.prow != row for b in writes):
                self.pe_fence()
            for b in writes:
                b.prow = row
        self._deps(e, reads, writes)
        ins = fn(self.E[e])
        self.cnt[e] += 1
        ins.then_inc(self.sem[e], 1)
        self.n_inst += 1
        self._mark((e, self.cnt[e]), reads, writes)
        return ins

    def dma(self, e, out, in_, reads=(), writes=(), **kw):
        reads = self._bufs(reads)
        writes = self._bufs(writes)
        self._deps(e, reads, writes)
        half = len(self.dsem) // 2
        if e == "pool":
            s = half + self.drr_sw
            self.drr_sw = (self.drr_sw + 1) % (len(self.dsem) - half)
        else:
            s = self.drr
            self.drr = (self.drr + 1) % half
        if self.dval[s] > 0:
            self._wait(e, s, self.dval[s])
        ins = self.E[e].dma_start(out=out, in_=in_, **kw)
        self.dval[s] += 16
        ins.then_inc(self.dsem[s], 16)
        self.n_inst += 1
        self._mark((s, self.dval[s]), reads, writes)

    def pe_fence(self):
        if self.cnt["pe"] > 0:
            self.E["pe"].wait_ge(self.sem["pe"], self.cnt["pe"])
            self.n_inst += 1

    def barrier(self, engines=("pe", "act", "dve", "pool", "sp")):
        for e in engines:
            for s in range(len(self.dsem)):
                if self.dval[s] > 0:
                    self._wait(e, s, self.dval[s])
            for k in ("pe", "act", "dve", "pool"):
                if self.cnt[k] > 0 and k != e:
                    self._wait(e, k, self.cnt[k])

    def finish(self):
        self.barrier()


class Prog:
    def __init__(self, nc, cfg):
        self.nc = nc
        self.kb = KB(nc)
        self.cfg = cfg
        self.NB = cfg["NB"]
        self.CTX = cfg["CTX"]
        self.LAT = cfg["LAT"]
        self.L = cfg["DEPTH"]
        self.ROWS = self.LAT // 64
        self.TS = self.CTX + self.LAT
        self.T = self.NB * self.TS
        self.dbg = cfg.get("dbg", ())
        self.inp = {}
        self.scr = {}
        self.flip = 0

    def din(self, name, shape, dt=F32):
        self.inp[name] = self.nc.dram_tensor(name, list(shape), dt, kind="ExternalInput").ap()
        return self.inp[name]

    def dscr(self, name, shape, dt):
        kind = "ExternalOutput" if name in self.dbg else "Internal"
        self.scr[name] = self.nc.dram_tensor(name, list(shape), dt, kind=kind).ap()
        return self.scr[name]

    def alt(self):
        self.flip ^= 1
        return "act" if self.flip else "dve"

    def xr_rows(self, layer, b, pos0, n=128):
        XR = self.scr["XR"]
        if pos0 < self.CTX or layer % 2 == 0:
            return [(0, n, XR[b, pos0:pos0 + n, :])]
        lat = XR[b, self.CTX:self.TS, :].rearrange("(r c) d -> c r d", c=64)
        out = []
        m0 = pos0 - self.CTX
        m = m0
        while m < m0 + n:
            c = m // self.ROWS
            r0 = m % self.ROWS
            k = min(self.ROWS - r0, m0 + n - m)
            out.append((m - m0, k, lat[c, r0:r0 + k, :]))
            m += k
        return out

    def st_init(self):
        kb = self.kb
        with kb.stage():
            XR = self.scr["XR"]
            for b in range(self.NB):
                kb.dma("sp", XR[b, 0:self.CTX, :], self.inp["ctx"][b])
                kb.dma("sp", XR[b, self.CTX:self.TS, :], self.inp["x"][b])

    def st_adaln(self):
        kb, nc = self.kb, self.nc
        with kb.stage():
            ct = kb.tile([128, 8, 3], F32)
            sc = kb.tile([128, 8, 3], F32)
            cb = kb.tile([128, 24, 128], F32)
            kb.dma("sp", ct[:, :, :], self.inp["condT"], writes=[ct])
            kb.op("act", lambda e: e.activation(out=sc[:, :, :], in_=ct[:, :, :], func=AF.Silu), reads=[ct], writes=[sc])
            for kc in range(8):
                for r in range(3):
                    kb.op("dve", lambda e: e.tensor_copy(out=cb[:, kc * 3 + r, :], in_=sc[:, kc, r:r + 1].to_broadcast([128, 128])), reads=[sc], writes=[cb])
            wts = [kb.tile([128, 8, 512], F32) for _ in range(2)]
            bts = [kb.tile([128, 512], F32) for _ in range(2)]
            pss = [kb.psum([128, 512], F32) for _ in range(3)]
            ots = [kb.tile([128, 512], F32) for _ in range(3)]
            it = 0
            for l in range(self.L):
                for nch in range(12):
                    wt, bt = wts[it % 2], bts[it % 2]
                    it += 1
                    kb.dma("sp", wt[:, :, :], self.inp["w_mod"][l].rearrange("(kc p) n -> p kc n", p=128)[:, :, nch * 512:(nch + 1) * 512], writes=[wt])
                    kb.dma("sp", bt[:, :], self.inp["b_mod"][l:l + 1, nch * 512:(nch + 1) * 512].to_broadcast([128, 512]), writes=[bt])
                    for r in range(3):
                        ps, ot = pss[r], ots[r]
                        for kc in range(8):
                            kb.op("pe", lambda e: e.matmul(ps[:, :], lhsT=cb[:, kc * 3 + r, :], rhs=wt[:, kc, :], start=(kc == 0), stop=(kc == 7)), reads=[cb, wt], writes=[ps])
                        kb.op("dve", lambda e: e.tensor_tensor(out=ot[:, :], in0=ps[:, :], in1=bt[:, :], op=ALU.add), reads=[ps, bt], writes=[ot])
                        kb.dma("pool", self.scr["MOD"][l, r:r + 1, nch * 512:(nch + 1) * 512], ot[0:1, :], reads=[ot])

    def load_bcast(self, t, src_row):
        n = src_row.shape[-1]
        self.kb.dma("sp", t[:, 0:n], src_row.to_broadcast([128, n]), writes=[t])

    def transpose_store(self, hb, dst, col0, ident, psT, hT):
        kb = self.kb
        for kc in range(8):
            kb.op("pe", lambda e: e.transpose(psT[:, kc, :], hb[:, kc * 128:(kc + 1) * 128], ident[:, :]), reads=[hb, ident], writes=[psT])
        kb.op("act", lambda e: e.activation(out=hT[:, :, :], in_=psT[:, :, :], func=AF.Copy), reads=[psT], writes=[hT])
        kb.dma("act", dst.rearrange("(kc p) t -> p kc t", p=128)[:, :, col0:col0 + 128], hT[:, :, :], reads=[hT])

    def st_norm(self, l, which, final=False):
        kb = self.kb
        with kb.stage():
            nw = kb.tile([128, D], F32)
            G = [kb.tile([128, D], F32) for _ in range(3)]
            SH = [kb.tile([128, D], F32) for _ in range(3)]
            ident = kb.tile([128, 128], BF16)
            kb.dma("sp", ident[:, :], self.inp["ident_bf"], writes=[ident])
            if final:
                self.load_bcast(nw, self.inp["final_norm_w"].rearrange("(o d) -> o d", o=1))
            else:
                nwsrc = self.inp["norm1_w" if which == 1 else "norm2_w"]
                self.load_bcast(nw, nwsrc[l:l + 1, :])
                shi, sci = (0, 1) if which == 1 else (3, 4)
                for r in range(3):
                    self.load_bcast(G[r], self.scr["MOD"][l, r:r + 1, sci * D:(sci + 1) * D])
                    self.load_bcast(SH[r], self.scr["MOD"][l, r:r + 1, shi * D:(shi + 1) * D])
                    kb.op("dve", lambda e: e.scalar_tensor_tensor(out=G[r][:, :], in0=G[r][:, :], scalar=1.0, in1=nw[:, :], op0=ALU.add, op1=ALU.mult), reads=[G[r], nw], writes=[G[r]])
            NBUF = 3
            xt = [kb.tile([128, D], F32) for _ in range(NBUF)]
            junk = [kb.tile([128, D], F32) for _ in range(NBUF)]
            ss = [kb.tile([128, 1], F32) for _ in range(NBUF)]
            rs = [kb.tile([128, 1], F32) for _ in range(NBUF)]
            hb = [kb.tile([128, D], BF16) for _ in range(NBUF)]
            hT = [kb.tile([128, 8, 128], BF16) for _ in range(NBUF)]
            psT = [kb.psum([128, 8, 128], BF16) for _ in range(2)]
            it = 0
            for b in range(self.NB):
                for j in range(self.TS // 128):
                    pos0 = j * 128
                    if final and pos0 < self.CTX:
                        continue
                    i = it % NBUF
                    it += 1
                    r = 2 if pos0 < self.CTX else b
                    x, jk, s_, r_, h_ = xt[i], junk[i], ss[i], rs[i], hb[i]
                    for (p0, n, ap) in self.xr_rows(l if not final else 0, b, pos0):
                        kb.dma("sp", x[p0:p0 + n, :], ap, writes=[x])
                    kb.op("act", lambda e: e.activation(out=jk[:, :], in_=x[:, :], func=AF.Square, accum_out=s_[:, :]), reads=[x], writes=[jk, s_])
                    kb.op("act", lambda e: e.activation(out=r_[:, :], in_=s_[:, :], func=AF.Sqrt, scale=1.0 / D, bias=self.eps_t[:, 0:1]), reads=[s_], writes=[r_])
                    kb.op("dve", lambda e: e.reciprocal(out=r_[:, :], in_=r_[:, :]), reads=[r_], writes=[r_])
                    if final:
                        kb.op("dve", lambda e: e.scalar_tensor_tensor(out=jk[:, :], in0=x[:, :], scalar=r_[:, 0:1], in1=nw[:, :], op0=ALU.mult, op1=ALU.mult), reads=[x, r_, nw], writes=[jk])
                        kb.dma("act", self.out_ap[b, pos0 - self.CTX:pos0 - self.CTX + 128, :], jk[:, :], reads=[jk])
                        continue
                    kb.op("dve", lambda e: e.scalar_tensor_tensor(out=jk[:, :], in0=x[:, :], scalar=r_[:, 0:1], in1=G[r][:, :], op0=ALU.mult, op1=ALU.mult), reads=[x, r_, G[r]], writes=[jk])
                    kb.op("dve", lambda e: e.tensor_tensor(out=h_[:, :], in0=jk[:, :], in1=SH[r][:, :], op=ALU.add), reads=[jk, SH[r]], writes=[h_])
                    self.transpose_store(h_, self.scr["HT"], b * self.TS + pos0, ident, psT[it % 2], hT[i])

    def gemm(self, src, K, W, jobs, ng_max=2048):
        kb = self.kb
        T = self.T
        kp = min(K, 128)
        KC = (K + 127) // 128
        assert K == kp * KC
        if KC > 8:
            ng_max = 512
        with kb.stage():
            wb = [kb.tile([kp, KC, ng_max], BF16) for _ in range(2)]
            hbs = [kb.tile([kp, KC, 512], BF16) for _ in range(2)]
            self.g_ps = [kb.psum([128, 512], F32) for _ in range(4)]
            self.g_stF = [kb.tile([128, 512], F32) for _ in range(4)]
            self.g_stB = [kb.tile([128, 512], BF16) for _ in range(4)]
            self.g_tmp = [kb.tile([128, 512], F32) for _ in range(4)]
            self.g_i = 0
            src3 = src.rearrange("(kc p) t -> p kc t", p=kp)
            gi = 0
            hi = 0
            for (c0, ncols, mode, epi, prep) in jobs:
                if prep is not None:
                    prep()
                for g0 in range(c0, c0 + ncols, ng_max):
                    ng = min(ng_max, c0 + ncols - g0)
                    w = wb[gi % 2]
                    gi += 1
                    for kc in range(KC):
                        kb.dma("pool", w[:, kc, 0:ng], W[kc * kp:(kc + 1) * kp, g0:g0 + ng], writes=[w])
                    for t0 in range(0, T, 512):
                        tsz = min(512, T - t0)
                        h = hbs[hi % 2]
                        hi += 1
                        kb.dma("sp", h[:, :, 0:tsz], src3[:, :, t0:t0 + tsz], writes=[h])
                        if mode == "FM":
                            for n0 in range(0, ng, 128):
                                nsz = min(128, ng - n0)
                                ps = self.g_ps[self.g_i % 4]
                                for kc in range(KC):
                                    kb.op("pe", lambda e: e.matmul(ps[0:nsz, 0:tsz], lhsT=w[:, kc, n0:n0 + nsz], rhs=h[:, kc, 0:tsz], start=(kc == 0), stop=(kc == KC - 1)), reads=[w, h], writes=[ps])
                                epi(ps, g0 + n0 - c0, nsz, t0, tsz)
                                self.g_i += 1
                        else:
                            for ts in range(0, tsz, 128):
                                for n0 in range(0, ng, 512):
                                    nsz = min(512, ng - n0)
                                    ps = self.g_ps[self.g_i % 4]
                                    for kc in range(KC):
                                        kb.op("pe", lambda e: e.matmul(ps[:, 0:nsz], lhsT=h[:, kc, ts:ts + 128], rhs=w[:, kc, n0:n0 + nsz], start=(kc == 0), stop=(kc == KC - 1)), reads=[w, h], writes=[ps])
                                    epi(ps, g0 + n0 - c0, nsz, t0 + ts, 128)
                                    self.g_i += 1

    def epi_fm(self, dst, dt, func=AF.Copy, scale=1.0, bias_t=None):
        kb = self.kb

        def epi(ps, c, nsz, t0, tsz):
            st = (self.g_stF if dt == F32 else self.g_stB)[self.g_i % 4]
            if func == AF.Copy and bias_t is None and self.g_i % 2 == 0:
                kb.op("dve", lambda e: e.tensor_scalar(out=st[0:nsz, 0:tsz], in0=ps[0:nsz, 0:tsz], scalar1=float(scale), scalar2=None, op0=ALU.mult), reads=[ps], writes=[st])
            elif bias_t is None:
                kb.op("act", lambda e: e.activation(out=st[0:nsz, 0:tsz], in_=ps[0:nsz, 0:tsz], func=func, scale=float(scale)), reads=[ps], writes=[st])
            else:
                kb.op("act", lambda e: e.activation(out=st[0:nsz, 0:tsz], in_=ps[0:nsz, 0:tsz], func=func, scale=float(scale), bias=bias_t[0:nsz, c // 128:c // 128 + 1]), reads=[ps, bias_t], writes=[st])
            kb.dma("act", dst[c:c + nsz, t0:t0 + tsz], st[0:nsz, 0:tsz], reads=[st])
        return epi

    def epi_tm(self, dst, dt, func=AF.Copy, scale=1.0):
        kb = self.kb

        def epi(ps, c, nsz, t0, tsz):
            st = (self.g_stF if dt == F32 else self.g_stB)[self.g_i % 4]
            if func == AF.Copy and self.g_i % 2 == 0:
                kb.op("dve", lambda e: e.tensor_scalar(out=st[0:tsz, 0:nsz], in0=ps[0:tsz, 0:nsz], scalar1=float(scale), scalar2=None, op0=ALU.mult), reads=[ps], writes=[st])
            else:
                kb.op("act", lambda e: e.activation(out=st[0:tsz, 0:nsz], in_=ps[0:tsz, 0:nsz], func=func, scale=float(scale)), reads=[ps], writes=[st])
            kb.dma("act", dst[t0:t0 + tsz, c:c + nsz], st[0:tsz, 0:nsz], reads=[st])
        return epi

    def epi_gelu_fm(self, dst):
        kb = self.kb

        def epi(ps, c, nsz, t0, tsz):
            x = self.g_stF[self.g_i % 4]
            u = self.g_tmp[self.g_i % 4]
            st = self.g_stB[self.g_i % 4]
            kb.op("act", lambda e: e.activation(out=x[0:nsz, 0:tsz], in_=ps[0:nsz, 0:tsz], func=AF.Copy), reads=[ps], writes=[x])
            kb.op("dve", lambda e: e.tensor_tensor(out=u[0:nsz, 0:tsz], in0=x[0:nsz, 0:tsz], in1=x[0:nsz, 0:tsz], op=ALU.mult), reads=[x], writes=[u])
            kb.op("dve", lambda e: e.tensor_scalar(out=u[0:nsz, 0:tsz], in0=u[0:nsz, 0:tsz], scalar1=0.044715, scalar2=1.0, op0=ALU.mult, op1=ALU.add), reads=[u], writes=[u])
            kb.op("dve", lambda e: e.tensor_tensor(out=u[0:nsz, 0:tsz], in0=u[0:nsz, 0:tsz], in1=x[0:nsz, 0:tsz], op=ALU.mult), reads=[u, x], writes=[u])
            kb.op("act", lambda e: e.activation(out=u[0:nsz, 0:tsz], in_=u[0:nsz, 0:tsz], func=AF.Sigmoid, scale=1.5957691216057308), reads=[u], writes=[u])
            kb.op("dve", lambda e: e.tensor_tensor(out=st[0:nsz, 0:tsz], in0=u[0:nsz, 0:tsz], in1=x[0:nsz, 0:tsz], op=ALU.mult), reads=[u, x], writes=[st])
            kb.dma("act", dst[c:c + nsz, t0:t0 + tsz], st[0:nsz, 0:tsz], reads=[st])
        return epi

    def st_win(self, l):
        S = self.scr
        W = self.inp["w_in"][l]
        none = None
        jobs = [
            (0, 1024, "FM", self.epi_fm(S["QT"], BF16), none),
            (1024, 1024, "FM", self.epi_fm(S["KT"], BF16, scale=1.0 / 16), none),
            (1024, 1024, "TM", self.epi_tm(S["KTM"], BF16, scale=1.0 / 16), none),
            (2048, 1024, "TM", self.epi_tm(S["VTM"], BF16), none),
            (3072, 1024, "TM", self.epi_tm(S["OG"], BF16, func=AF.Sigmoid), none),
            (4096, 16, "TM", self.epi_tm(S["IFG"], F32), none),
            (4112, 1024, "FM", self.epi_fm(S["LX"], F32), none),
            (5136, 1024, "FM", self.epi_gelu_fm(S["LG"]), none),
            (6160, RWSEG, "FM", self.epi_fm(S["RW"], F32), none),
            (9616, 3072, "FM", self.epi_fm(S["MG"], BF16, func=AF.Sigmoid), none),
        ]
        self.gemm(S["HT"], D, W, jobs)

    def epi_resid(self, l, gate_idx):
        kb = self.kb
        self.r_g = None

        def prep():
            self.r_g = [kb.tile([128, D], F32) for _ in range(3)]
            for r in range(3):
                self.load_bcast(self.r_g[r], self.scr["MOD"][l, r:r + 1, gate_idx * D:(gate_idx + 1) * D])
            self.r_x = [kb.tile([128, 512], F32) for _ in range(4)]

        def epi(ps, c, nsz, t0, tsz):
            b = t0 // self.TS
            pos0 = t0 - b * self.TS
            r = 2 if pos0 < self.CTX else b
            x = self.r_x[self.g_i % 4]
            st = self.g_stF[self.g_i % 4]
            rows = self.xr_rows(l, b, pos0)
            for (p0, n, ap) in rows:
                kb.dma("sp", x[p0:p0 + n, 0:nsz], ap[:, c:c + nsz], writes=[x])
            kb.op("dve", lambda e: e.tensor_tensor(out=st[:, 0:nsz], in0=ps[:, 0:nsz], in1=self.r_g[r][:, c:c + nsz], op=ALU.mult), reads=[ps, self.r_g[r]], writes=[st])
            kb.op("dve", lambda e: e.tensor_tensor(out=st[:, 0:nsz], in0=st[:, 0:nsz], in1=x[:, 0:nsz], op=ALU.add), reads=[st, x], writes=[st])
            for (p0, n, ap) in rows:
                kb.dma("act", ap[:, c:c + nsz], st[p0:p0 + n, 0:nsz], reads=[st])
        return epi, prep

    def st_wout(self, l):
        epi, prep = self.epi_resid(l, 2)
        self.gemm(self.scr["YM"], D, self.inp["w_out"][l], [(0, D, "TM", epi, prep)])

    def st_ffn(self, l):
        kb = self.kb
        S = self.scr
        self.st_norm(l, 2)
        W = self.inp["w_ffn_in"][l]
        self.gemm(S["HT"], D, W, [(0, FFN, "FM", self.epi_fm(S["AG"], BF16, func=AF.Silu), None)])

        def prep():
            self.f_g = [kb.tile([128, 512], BF16) for _ in range(4)]

        def epi_up(ps, c, nsz, t0, tsz):
            g = self.f_g[self.g_i % 4]
            st = self.g_stB[self.g_i % 4]
            kb.dma("sp", g[0:nsz, 0:tsz], S["AG"][c:c + nsz, t0:t0 + tsz], writes=[g])
            kb.op("dve", lambda e: e.tensor_tensor(out=st[0:nsz, 0:tsz], in0=ps[0:nsz, 0:tsz], in1=g[0:nsz, 0:tsz], op=ALU.mult), reads=[ps, g], writes=[st])
            kb.dma("act", S["AT"][c:c + nsz, t0:t0 + tsz], st[0:nsz, 0:tsz], reads=[st])
        self.gemm(S["HT"], D, W[:, FFN:2 * FFN], [(0, FFN, "FM", epi_up, prep)])
        epi, prep2 = self.epi_resid(l, 5)
        self.gemm(S["AT"], FFN, self.inp["w_ffn_out"][l], [(0, D, "TM", epi, prep2)])

    def st_merge(self, l):
        kb = self.kb
        S = self.scr
        srcs = [("YML", "out_ml"), ("YLRU", "out_lru"), ("YRW", "out_rw")]
        for bi, (ys, wn) in enumerate(srcs):
            def prep():
                self.m_g = [kb.tile([128, 512], BF16) for _ in range(4)]
                self.m_a = [kb.tile([128, 512], F32) for _ in range(4)]

            def epi(ps, c, nsz, t0, tsz, bi=bi):
                g = self.m_g[self.g_i % 4]
                a = self.m_a[self.g_i % 4]
                kb.dma("sp", g[0:nsz, 0:tsz], S["MG"][bi * D + c:bi * D + c + nsz, t0:t0 + tsz], writes=[g])
                if bi == 0:
                    st = self.g_stF[self.g_i % 4]
                    kb.op("dve", lambda e: e.tensor_tensor(out=st[0:nsz, 0:tsz], in0=ps[0:nsz, 0:tsz], in1=g[0:nsz, 0:tsz], op=ALU.mult), reads=[ps, g], writes=[st])
                    kb.dma("act", S["YACC"][c:c + nsz, t0:t0 + tsz], st[0:nsz, 0:tsz], reads=[st])
                else:
                    kb.dma("sp", a[0:nsz, 0:tsz], S["YACC"][c:c + nsz, t0:t0 + tsz], writes=[a])
                    st = self.g_stF[self.g_i % 4] if bi == 1 else self.g_stB[self.g_i % 4]
                    tmp = self.g_tmp[self.g_i % 4]
                    kb.op("dve", lambda e: e.tensor_tensor(out=tmp[0:nsz, 0:tsz], in0=ps[0:nsz, 0:tsz], in1=g[0:nsz, 0:tsz], op=ALU.mult), reads=[ps, g], writes=[tmp])
                    kb.op("dve", lambda e: e.tensor_tensor(out=st[0:nsz, 0:tsz], in0=tmp[0:nsz, 0:tsz], in1=a[0:nsz, 0:tsz], op=ALU.add), reads=[tmp, a], writes=[st])
                    dst = S["YACC"] if bi == 1 else S["YM"]
                    kb.dma("act", dst[c:c + nsz, t0:t0 + tsz], st[0:nsz, 0:tsz], reads=[st])
            self.gemm(S[ys], D, self.inp[wn][l], [(0, D, "FM", epi, prep)])

    def st_mlstm(self, l):
        kb = self.kb
        S = self.scr
        TS, NB = self.TS, self.NB
        nck = TS // 128
        ctxc = self.CTX // 128
        with kb.stage():
            identF = kb.tile([128, 128], F32)
            ones = kb.tile([128, 128], F32)
            tri = [kb.tile([128, 128], F32) for _ in range(2)]
            negm = [kb.tile([128, 128], F32) for _ in range(2)]
            GB = kb.tile([128, 16], F32)
            kb.dma("sp", identF[:, :], self.inp["ident_f"], writes=[identF])
            kb.dma("sp", ones[:, :], self.inp["ones_f"], writes=[ones])
            for d in range(2):
                kb.dma("sp", tri[d][:, :], self.inp["tri"][d], writes=[tri[d]])
                kb.dma("sp", negm[d][:, :], self.inp["negm"][d], writes=[negm[d]])
            kb.dma("sp", GB[:, 0:8], self.inp["ml_ig_b"][l].rearrange("(o a) b -> o (a b)", o=1).to_broadcast([128, 8]), writes=[GB])
            kb.dma("sp", GB[:, 8:16], self.inp["ml_fg_b"][l].rearrange("(o a) b -> o (a b)", o=1).to_broadcast([128, 8]), writes=[GB])
            NBUF = 2
            qT = [kb.tile([128, 8, 128], BF16) for _ in range(NBUF)]
            kT = [kb.tile([128, 8, 128], BF16) for _ in range(NBUF)]
            kTM = [kb.tile([128, D], BF16) for _ in range(NBUF)]
            VA = [kb.tile([128, 4, 257], BF16) for _ in range(NBUF)]
            IFt = [kb.tile([128, 16], F32) for _ in range(NBUF)]
            for i in range(NBUF):
                kb.op("dve", lambda e: e.memset(VA[i][:, :, :], 1.0), writes=[VA[i]])
            gx = kb.tile([128, 16], F32)
            lf = kb.tile([128, 4], F32)
            e1 = kb.tile([128, 4], F32)
            lfB = kb.tile([128, 4, 128], F32)
            fc = kb.tile([128, 8], F32)
            cs = kb.tile([128, 4], F32)
            ef = kb.tile([128, 4], F32)
            ev = kb.tile([128, 4], F32)
            eT = kb.tile([128, 4], F32)
            tmp4 = kb.tile([128, 4], F32)
            C32 = [kb.tile([128, 2, 257], F32) for _ in range(4)]
            Cbf = [kb.tile([128, 2, 257], BF16) for _ in range(4)]
            DT = [kb.tile([128, 128], F32) for _ in range(2)]
            AT = [kb.tile([128, 128], BF16) for _ in range(2)]
            tI = [kb.tile([128, 257], F32) for _ in range(2)]
            ND = [kb.tile([128, 257], F32) for _ in range(2)]
            den = [kb.tile([128, 1], F32) for _ in range(2)]
            VS = [kb.tile([128, 257], BF16) for _ in range(2)]
            HO = [kb.tile([128, D], F32) for _ in range(2)]
            ps_g = kb.psum([128, 8], F32)
            psA = kb.psum([128, 128], F32)
            psF = kb.psum([128, 128], F32)
            psI = kb.psum([128, 257], F32)
            psC = kb.psum([128, 257], F32)
            psD = [kb.psum([128, 257], F32) for _ in range(2)]
            it = 0
            hh = 0
            for d in range(2):
                for b in range(NB):
                    for h in range(4):
                        kb.op("dve", lambda e: e.memset(C32[h][:, :, :], 0.0), writes=[C32[h]])
                        kb.op("dve", lambda e: e.memset(Cbf[h][:, :, :], 0.0), writes=[Cbf[h]])
                    cl = list(range(ctxc)) + list(range(ctxc, nck)) if d == 0 else list(range(ctxc - 1, -1, -1)) + list(range(nck - 1, ctxc - 1, -1))
                    for c in cl:
                        i = it % NBUF
                        it += 1
                        col0 = b * TS + c * 128
                        q_, k_, km_, va_, if_ = qT[i], kT[i], kTM[i], VA[i], IFt[i]
                        kb.dma("sp", q_[:, :, :], S["QT"].rearrange("(kc p) t -> p kc t", p=128)[:, :, col0:col0 + 128], writes=[q_])
                        kb.dma("sp", k_[:, :, :], S["KT"].rearrange("(kc p) t -> p kc t", p=128)[:, :, col0:col0 + 128], writes=[k_])
                        kb.dma("sp", km_[:, :], S["KTM"][col0:col0 + 128, :], writes=[km_])
                        kb.dma("sp", va_[:, :, 0:256], S["VTM"][col0:col0 + 128, :].rearrange("t (h e) -> t h e", h=4), writes=[va_])
                        kb.dma("sp", if_[:, :], S["IFG"][col0:col0 + 128, :], writes=[if_])
                        kb.op("dve", lambda e: e.tensor_tensor(out=gx[:, :], in0=if_[:, :], in1=GB[:, :], op=ALU.add), reads=[if_, GB], writes=[gx])
                        i4 = gx[:, d * 4:d * 4 + 4]
                        f4 = gx[:, 8 + d * 4:12 + d * 4]
                        kb.op("act", lambda e: e.activation(out=e1[:, :], in_=f4, func=AF.Exp, scale=-1.0), reads=[gx], writes=[e1])
                        kb.op("act", lambda e: e.activation(out=e1[:, :], in_=e1[:, :], func=AF.Ln, bias=self.one_t[:, 0:1]), reads=[e1], writes=[e1])
                        kb.op("dve", lambda e: e.tensor_scalar(out=lf[:, :], in0=e1[:, :], scalar1=-1.0, scalar2=None, op0=ALU.mult), reads=[e1], writes=[lf])
                        kb.op("dve", lambda e: e.tensor_copy(out=lfB[:, :, :], in_=lf[:, 0:4].unsqueeze(2).to_broadcast([128, 4, 128])), reads=[lf], writes=[lfB])
                        kb.op("pe", lambda e: e.matmul(ps_g[:, 0:4], lhsT=tri[d][:, :], rhs=lf[:, :], start=True, stop=True), reads=[tri[d], lf], writes=[ps_g])
                        kb.op("pe", lambda e: e.matmul(ps_g[:, 4:8], lhsT=ones[:, :], rhs=lf[:, :], start=True, stop=True), reads=[ones, lf], writes=[ps_g])
                        kb.op("dve", lambda e: e.tensor_copy(out=fc[:, :], in_=ps_g[:, :]), reads=[ps_g], writes=[fc])
                        kb.op("dve", lambda e: e.tensor_tensor(out=cs[:, :], in0=i4, in1=fc[:, 0:4], op=ALU.subtract), reads=[gx, fc], writes=[cs])
                        kb.op("act", lambda e: e.activation(out=ef[:, :], in_=fc[:, 0:4], func=AF.Exp), reads=[fc], writes=[ef])
                        kb.op("dve", lambda e: e.tensor_tensor(out=tmp4[:, :], in0=cs[:, :], in1=fc[:, 4:8], op=ALU.add), reads=[cs, fc], writes=[tmp4])
                        kb.op("act", lambda e: e.activation(out=ev[:, :], in_=tmp4[:, :], func=AF.Exp), reads=[tmp4], writes=[ev])
                        kb.op("act", lambda e: e.activation(out=eT[:, :], in_=fc[:, 4:8], func=AF.Exp), reads=[fc], writes=[eT])
                        ho = HO[it % 2]
                        for h in range(4):
                            j2 = hh % 2
                            hh += 1
                            dt_, at_, ti_, nd_, dn_, vs_ = DT[j2], AT[j2], tI[j2], ND[j2], den[j2], VS[j2]
                            for j in range(2):
                                kb.op("pe", lambda e: e.matmul(psA[:, :], lhsT=k_[:, 2 * h + j, :], rhs=q_[:, 2 * h + j, :], start=(j == 0), stop=(j == 1)), reads=[k_, q_], writes=[psA])
                            kb.op("pe", lambda e: e.matmul(psF[:, :], lhsT=lfB[:, h, :], rhs=tri[d][:, :], start=True, stop=False), reads=[lfB, tri[d]], writes=[psF])
                            kb.op("pe", lambda e: e.matmul(psF[:, :], lhsT=identF[:, :], rhs=negm[d][:, :], start=False, stop=True), reads=[identF, negm[d]], writes=[psF])
                            kb.op("act", lambda e: e.activation(out=dt_[:, :], in_=psF[:, :], func=AF.Exp, bias=cs[:, h:h + 1]), reads=[psF, cs], writes=[dt_])
                            kb.op("dve", lambda e: e.tensor_tensor(out=at_[:, :], in0=psA[:, :], in1=dt_[:, :], op=ALU.mult), reads=[psA, dt_], writes=[at_])
                            kb.op("pe", lambda e: e.matmul(psI[:, :], lhsT=at_[:, :], rhs=va_[:, h, :], start=True, stop=True), reads=[at_, va_], writes=[psI])
                            for j in range(2):
                                kb.op("pe", lambda e: e.matmul(psC[:, :], lhsT=q_[:, 2 * h + j, :], rhs=Cbf[h][:, j, :], start=(j == 0), stop=(j == 1)), reads=[q_, Cbf[h]], writes=[psC])
                            kb.op("act", lambda e: e.activation(out=ti_[:, :], in_=psI[:, :], func=AF.Copy), reads=[psI], writes=[ti_])
                            kb.op("dve", lambda e: e.scalar_tensor_tensor(out=nd_[:, :], in0=psC[:, :], scalar=ef[:, h:h + 1], in1=ti_[:, :], op0=ALU.mult, op1=ALU.add), reads=[psC, ef, ti_], writes=[nd_])
                            kb.op("act", lambda e: e.activation(out=dn_[:, :], in_=nd_[:, 256:257], func=AF.Abs), reads=[nd_], writes=[dn_])
                            kb.op("dve", lambda e: e.tensor_scalar(out=dn_[:, :], in0=dn_[:, :], scalar1=1.0, scalar2=None, op0=ALU.max), reads=[dn_], writes=[dn_])
                            kb.op("dve", lambda e: e.reciprocal(out=dn_[:, :], in_=dn_[:, :]), reads=[dn_], writes=[dn_])
                            kb.op("act", lambda e: e.activation(out=ho[:, h * 256:(h + 1) * 256], in_=nd_[:, 0:256], func=AF.Copy, scale=dn_[:, 0:1]), reads=[nd_, dn_], writes=[ho])
                            kb.op("dve", lambda e: e.tensor_scalar(out=vs_[:, :], in0=va_[:, h, :], scalar1=ev[:, h:h + 1], scalar2=None, op0=ALU.mult), reads=[va_, ev], writes=[vs_])
                            for j in range(2):
                                kb.op("pe", lambda e: e.matmul(psD[j][:, :], lhsT=km_[:, h * 256 + j * 128:h * 256 + (j + 1) * 128], rhs=vs_[:, :], start=True, stop=True), reads=[km_, vs_], writes=[psD[j]])
                                kb.op("dve", lambda e: e.scalar_tensor_tensor(out=C32[h][:, j, :], in0=C32[h][:, j, :], scalar=eT[:, h:h + 1], in1=psD[j][:, :], op0=ALU.mult, op1=ALU.add), reads=[C32[h], eT, psD[j]], writes=[C32[h]])
                            kb.op("act", lambda e: e.activation(out=Cbf[h][:, :, :], in_=C32[h][:, :, :], func=AF.Copy), reads=[C32[h]], writes=[Cbf[h]])
                        kb.dma("pool", S["HML%d" % d][col0:col0 + 128, :], ho[:, :], reads=[ho])

    def st_mlstm_post(self, l):
        kb = self.kb
        S = self.scr
        with kb.stage():
            ident = kb.tile([128, 128], BF16)
            kb.dma("sp", ident[:, :], self.inp["ident_bf"], writes=[ident])
            nw = kb.tile([128, D], F32)
            self.load_bcast(nw, self.inp["ml_norm_w"][l:l + 1, :])
            NBUF = 2
            hf = [kb.tile([128, D], F32) for _ in range(NBUF)]
            hbk = [kb.tile([128, D], F32) for _ in range(NBUF)]
            og = [kb.tile([128, D], BF16) for _ in range(NBUF)]
            junk = [kb.tile([128, 256], F32) for _ in range(NBUF)]
            ms = [kb.tile([128, 4], F32) for _ in range(NBUF)]
            yb = [kb.tile([128, D], BF16) for _ in range(NBUF)]
            hT = [kb.tile([128, 8, 128], BF16) for _ in range(NBUF)]
            psT = [kb.psum([128, 8, 128], BF16) for _ in range(2)]
            for tix in range(self.T // 128):
                i = tix % NBUF
                col0 = tix * 128
                a, b_, o_, jk, m_, y_ = hf[i], hbk[i], og[i], junk[i], ms[i], yb[i]
                kb.dma("sp", a[:, :], S["HML0"][col0:col0 + 128, :], writes=[a])
                kb.dma("sp", b_[:, :], S["HML1"][col0:col0 + 128, :], writes=[b_])
                kb.dma("sp", o_[:, :], S["OG"][col0:col0 + 128, :], writes=[o_])
                kb.op("dve", lambda e: e.tensor_tensor(out=a[:, :], in0=a[:, :], in1=b_[:, :], op=ALU.add), reads=[a, b_], writes=[a])
                for h in range(4):
                    kb.op("act", lambda e: e.activation(out=jk[:, :], in_=a[:, h * 256:(h + 1) * 256], func=AF.Square, accum_out=m_[:, h:h + 1]), reads=[a], writes=[jk, m_])
                kb.op("act", lambda e: e.activation(out=m_[:, :], in_=m_[:, :], func=AF.Sqrt, scale=1.0 / 256, bias=self.eps_t[:, 0:1]), reads=[m_], writes=[m_])
                kb.op("dve", lambda e: e.reciprocal(out=m_[:, :], in_=m_[:, :]), reads=[m_], writes=[m_])
                kb.op("dve", lambda e: e.tensor_tensor(out=a[:, :].rearrange("p (h e) -> p h e", h=4), in0=a[:, :].rearrange("p (h e) -> p h e", h=4), in1=m_[:, 0:4].unsqueeze(2).to_broadcast([128, 4, 256]), op=ALU.mult), reads=[a, m_], writes=[a])
                kb.op("dve", lambda e: e.tensor_tensor(out=a[:, :], in0=a[:, :], in1=nw[:, :], op=ALU.mult), reads=[a, nw], writes=[a])
                kb.op("dve", lambda e: e.tensor_tensor(out=y_[:, :], in0=a[:, :], in1=o_[:, :], op=ALU.mult), reads=[a, o_], writes=[y_])
                self.transpose_store(y_, S["YML"], col0, ident, psT[tix % 2], hT[i])

    def st_lru(self, l):
        kb = self.kb
        S = self.scr
        TS, NB, CTX = self.TS, self.NB, self.CTX
        segs = [(0, CTX), (CTX, TS)]
        with kb.stage():
            cw = kb.tile([128, 8, 4], F32)
            cbias = kb.tile([128, 8], F32)
            grb = kb.tile([128, 2, 8], F32)
            gib = kb.tile([128, 2, 8], F32)
            lam = kb.tile([128, 2, 8], F32)
            cc_ = kb.tile([128, 2, 8], F32)
            kb.dma("sp", cw[:, :, :], self.inp["lru_conv_wT"][l], writes=[cw])
            kb.dma("sp", cbias[:, :], self.inp["lru_conv_bT"][l], writes=[cbias])
            kb.dma("sp", grb[:, :, :], self.inp["lru_gr_bT"][l], writes=[grb])
            kb.dma("sp", gib[:, :, :], self.inp["lru_gi_bT"][l], writes=[gib])
            kb.dma("sp", lam[:, :, :], self.inp["lru_lambdaT"][l], writes=[lam])
            kb.op("act", lambda e: e.activation(out=cc_[:, :, :], in_=lam[:, :, :], func=AF.Exp, scale=-1.0), reads=[lam], writes=[cc_])
            kb.op("act", lambda e: e.activation(out=cc_[:, :, :], in_=cc_[:, :, :], func=AF.Ln, bias=self.one_t[:, 0:1]), reads=[cc_], writes=[cc_])
            kb.op("dve", lambda e: e.tensor_scalar(out=cc_[:, :, :], in0=cc_[:, :, :], scalar1=-8.0, scalar2=None, op0=ALU.mult), reads=[cc_], writes=[cc_])
            wbd = [[[kb.tile([128, 128], F32) for _ in range(2)] for _ in range(2)] for _ in range(2)]
            x = [kb.tile([128, TS], F32) for _ in range(2)]
            u = [kb.tile([128, TS], F32) for _ in range(2)]
            lg = [kb.tile([128, TS], BF16) for _ in range(2)]
            aa = kb.tile([128, TS], F32)
            bx = kb.tile([128, TS], F32)
            hf = kb.tile([128, TS], F32)
            hb = kb.tile([128, TS], F32)
            yo = [kb.tile([128, TS], BF16) for _ in range(2)]
            rr = [kb.tile([128, 512], F32) for _ in range(2)]
            ii = [kb.tile([128, 512], F32) for _ in range(2)]
            a2 = [kb.tile([128, 512], F32) for _ in range(2)]
            psr = [kb.psum([128, 512], F32) for _ in range(2)]
            psi = [kb.psum([128, 512], F32) for _ in range(2)]
            it = 0
            for cc in range(8):
                wv = wbd[cc % 2]
                for d in range(2):
                    for g, nm in enumerate(("lru_gr_w", "lru_gi_w")):
                        w = wv[d][g]
                        kb.op("dve", lambda e: e.memset(w[:, :], 0.0), writes=[w])
                        for blk in range(2):
                            kb.dma("sp", w[blk * 64:(blk + 1) * 64, blk * 64:(blk + 1) * 64], self.inp[nm][l, d, 2 * cc + blk], writes=[w])
                for b in range(NB):
                    i = it % 2
                    it += 1
                    x_, u_, lg_, yo_ = x[i], u[i], lg[i], yo[i]
                    kb.dma("sp", x_[:, :], S["LX"][cc * 128:(cc + 1) * 128, b * TS:(b + 1) * TS], writes=[x_])
                    kb.dma("sp", lg_[:, :], S["LG"][cc * 128:(cc + 1) * 128, b * TS:(b + 1) * TS], writes=[lg_])
                    for (s0, s1) in segs:
                        kb.op("dve", lambda e: e.tensor_scalar(out=u_[:, s0:s1], in0=x_[:, s0:s1], scalar1=cw[:, cc, 2:3], scalar2=cbias[:, cc:cc + 1], op0=ALU.mult, op1=ALU.add), reads=[x_, cw, cbias], writes=[u_])
                        kb.op("dve", lambda e: e.scalar_tensor_tensor(out=u_[:, s0 + 2:s1], in0=x_[:, s0:s1 - 2], scalar=cw[:, cc, 0:1], in1=u_[:, s0 + 2:s1], op0=ALU.mult, op1=ALU.add), reads=[x_, cw, u_], writes=[u_])
                        kb.op("dve", lambda e: e.scalar_tensor_tensor(out=u_[:, s0 + 1:s1], in0=x_[:, s0:s1 - 1], scalar=cw[:, cc, 1:2], in1=u_[:, s0 + 1:s1], op0=ALU.mult, op1=ALU.add), reads=[x_, cw, u_], writes=[u_])
                        kb.op("dve", lambda e: e.scalar_tensor_tensor(out=u_[:, s0:s1 - 1], in0=x_[:, s0 + 1:s1], scalar=cw[:, cc, 3:4], in1=u_[:, s0:s1 - 1], op0=ALU.mult, op1=ALU.add), reads=[x_, cw, u_], writes=[u_])
                    for d in range(2):
                        for t0 in range(0, TS, 512):
                            tsz = min(512, TS - t0)
                            j = (t0 // 512) % 2
                            r_, i_, a2_ = rr[j], ii[j], a2[j]
                            kb.op("pe", lambda e: e.matmul(psr[j][:, 0:tsz], lhsT=wv[d][0][:, :], rhs=u_[:, t0:t0 + tsz], start=True, stop=True), reads=[wv[d][0], u_], writes=[psr[j]])
                            kb.op("pe", lambda e: e.matmul(psi[j][:, 0:tsz], lhsT=wv[d][1][:, :], rhs=u_[:, t0:t0 + tsz], start=True, stop=True), reads=[wv[d][1], u_], writes=[psi[j]])
                            kb.op("act", lambda e: e.activation(out=r_[:, 0:tsz], in_=psr[j][:, 0:tsz], func=AF.Sigmoid, bias=grb[:, d, cc:cc + 1]), reads=[psr[j], grb], writes=[r_])
                            kb.op("act", lambda e: e.activation(out=i_[:, 0:tsz], in_=psi[j][:, 0:tsz], func=AF.Sigmoid, bias=gib[:, d, cc:cc + 1]), reads=[psi[j], gib], writes=[i_])
                            kb.op("act", lambda e: e.activation(out=aa[:, t0:t0 + tsz], in_=r_[:, 0:tsz], func=AF.Exp, scale=cc_[:, d, cc:cc + 1]), reads=[r_, cc_], writes=[aa])
                            kb.op("dve", lambda e: e.tensor_tensor(out=a2_[:, 0:tsz], in0=aa[:, t0:t0 + tsz], in1=aa[:, t0:t0 + tsz], op=ALU.mult), reads=[aa], writes=[a2_])
                            kb.op("act", lambda e: e.activation(out=a2_[:, 0:tsz], in_=a2_[:, 0:tsz], func=AF.Sqrt, scale=-1.0, bias=self.one_t[:, 0:1]), reads=[a2_], writes=[a2_])
                            kb.op("dve", lambda e: e.tensor_tensor(out=i_[:, 0:tsz], in0=i_[:, 0:tsz], in1=u_[:, t0:t0 + tsz], op=ALU.mult), reads=[i_, u_], writes=[i_])
                            kb.op("dve", lambda e: e.tensor_tensor(out=bx[:, t0:t0 + tsz], in0=i_[:, 0:tsz], in1=a2_[:, 0:tsz], op=ALU.mult), reads=[i_, a2_], writes=[bx])
                        if d == 0:
                            kb.op("dve", lambda e: e.tensor_tensor_scan(out=hf[:, :], data0=aa[:, :], data1=bx[:, :], initial=0.0, op0=ALU.mult, op1=ALU.add), reads=[aa, bx], writes=[hf])
                        else:
                            kb.op("dve", lambda e: e.tensor_tensor_scan(out=hb[:, 0:CTX][:, ::-1], data0=aa[:, 0:CTX][:, ::-1], data1=bx[:, 0:CTX][:, ::-1], initial=0.0, op0=ALU.mult, op1=ALU.add), reads=[aa, bx], writes=[hb])
                            kb.op("dve", lambda e: e.tensor_tensor_scan(out=hb[:, CTX:TS][:, ::-1], data0=aa[:, CTX:TS][:, ::-1], data1=bx[:, CTX:TS][:, ::-1], initial=hb[:, 0:1], op0=ALU.mult, op1=ALU.add), reads=[aa, bx, hb], writes=[hb])
                    kb.op("dve", lambda e: e.tensor_tensor(out=hf[:, :], in0=hf[:, :], in1=hb[:, :], op=ALU.add), reads=[hf, hb], writes=[hf])
                    kb.op("dve", lambda e: e.tensor_tensor(out=yo_[:, :], in0=hf[:, :], in1=lg_[:, :], op=ALU.mult), reads=[hf, lg_], writes=[yo_])
                    kb.dma("pool", S["YLRU"][cc * 128:(cc + 1) * 128, b * TS:(b + 1) * TS], yo_[:, :], reads=[yo_])

    def st_rwkv(self, l):
        self.st_rw_shift(l)
        self.st_rw_lowrank(l)
        self.st_rw_core(l)

    def st_rw_shift(self, l):
        kb = self.kb
        S = self.scr
        TS, NB, CTX = self.TS, self.NB, self.CTX
        segs = [(0, CTX), (CTX, TS)]
        with kb.stage():
            mu = kb.tile([128, 27], F32)
            kb.dma("sp", mu[:, :], self.inp["rw_muT"][l], writes=[mu])
            x = [kb.tile([128, TS], F32) for _ in range(2)]
            tm = [kb.tile([128, TS], F32) for _ in range(2)]
            ob = [kb.tile([128, TS], BF16) for _ in range(2)]
            it = 0
            for ch in range(27):
                for b in range(NB):
                    i = it % 2
                    it += 1
                    x_, t_, o_ = x[i], tm[i], ob[i]
                    kb.dma("sp", x_[:, :], S["RW"][ch * 128:(ch + 1) * 128, b * TS:(b + 1) * TS], writes=[x_])
                    for (s0, s1) in segs:
                        kb.op("dve", lambda e: e.tensor_tensor(out=t_[:, s0 + 1:s1 - 1], in0=x_[:, s0:s1 - 2], in1=x_[:, s0 + 2:s1], op=ALU.add), reads=[x_], writes=[t_])
                        kb.op("dve", lambda e: e.tensor_copy(out=t_[:, s0:s0 + 1], in_=x_[:, s0 + 1:s0 + 2]), reads=[x_], writes=[t_])
                        kb.op("dve", lambda e: e.tensor_copy(out=t_[:, s1 - 1:s1], in_=x_[:, s1 - 2:s1 - 1]), reads=[x_], writes=[t_])
                    kb.op("dve", lambda e: e.scalar_tensor_tensor(out=t_[:, :], in0=t_[:, :], scalar=0.5, in1=x_[:, :], op0=ALU.mult, op1=ALU.subtract), reads=[t_, x_], writes=[t_])
                    kb.op("dve", lambda e: e.scalar_tensor_tensor(out=t_[:, :], in0=t_[:, :], scalar=mu[:, ch:ch + 1], in1=x_[:, :], op0=ALU.mult, op1=ALU.add), reads=[t_, x_, mu], writes=[t_])
                    cols = slice(b * TS, (b + 1) * TS)
                    if ch < 24:
                        kb.dma("pool", S["RS"][ch * 128:(ch + 1) * 128, cols], t_[:, :], reads=[t_])
                    else:
                        fn = (AF.Tanh, AF.Copy, AF.Sigmoid)[ch - 24]
                        dst = (S["TD"], S["AD"], S["GD"])[ch - 24]
                        kb.op("act", lambda e: e.activation(out=o_[:, :], in_=t_[:, :], func=fn), reads=[t_], writes=[o_])
                        kb.dma("pool", dst[:, cols], o_[:, :], reads=[o_])

    def st_rw_lowrank(self, l):
        kb = self.kb
        S = self.scr
        for d in range(2):
            holder = {}

            def prep(d=d):
                holder["d0"] = kb.tile([128, 8], F32)
                holder["i0"] = kb.tile([128, 8], F32)
                kb.dma("sp", holder["d0"][:, :], self.inp["rw_decay0T"][l][:, d, :], writes=[holder["d0"]])
                kb.dma("sp", holder["i0"][:, :], self.inp["rw_iclr0T"][l][:, d, :], writes=[holder["i0"]])

            def epi_b(dst, key):
                def epi(ps, c, nsz, t0, tsz):
                    st = self.g_stF[self.g_i % 4]
                    bt = holder[key]
                    kb.op("act", lambda e: e.activation(out=st[0:nsz, 0:tsz], in_=ps[0:nsz, 0:tsz], func=AF.Sigmoid, bias=bt[0:nsz, c // 128:c // 128 + 1]), reads=[ps, bt], writes=[st])
                    kb.dma("act", dst[c:c + nsz, t0:t0 + tsz], st[0:nsz, 0:tsz], reads=[st])
                return epi
            self.gemm(S["TD"][d * 64:(d + 1) * 64, :], 64, self.inp["rw_decay_up"][l, d], [(0, D, "FM", epi_b(S["SG%d" % d], "d0"), prep)])
            self.gemm(S["AD"][d * 64:(d + 1) * 64, :], 64, self.inp["rw_iclr_up"][l, d], [(0, D, "FM", epi_b(S["AA%d" % d], "i0"), prep)])
        self.gemm(S["GD"], 128, self.inp["rw_gate_up"][l], [(0, D, "FM", self.epi_fm(S["GG"], F32), None)])

    def st_rw_core(self, l):
        kb = self.kb
        S = self.scr
        TS, NB, CTX = self.TS, self.NB, self.CTX
        TB = min(CTX, 256)
        nblk = TS // TB
        cblk = CTX // TB
        ncb = TB // 64
        DS = RW_DECAY_SCALE
        v3 = lambda t: t[:, :].rearrange("p (c e) -> p c e", e=64)
        h2 = lambda ap: ap.rearrange("p (h c) -> p h c", h=2)
        with kb.stage():
            identF = kb.tile([128, 128], F32)
            identB = kb.tile([128, 128], BF16)
            bones = kb.tile([128, 128], F32)
            maskA = [kb.tile([128, 128], F32) for _ in range(2)]
            maskN = [kb.tile([64, 64], F32) for _ in range(2)]
            rmask = [kb.tile([128, TB], F32) for _ in range(2)]
            kb.dma("sp", identF[:, :], self.inp["ident_f"], writes=[identF])
            kb.dma("sp", identB[:, :], self.inp["ident_bf"], writes=[identB])
            kb.dma("sp", bones[:, :], self.inp["bones_f"], writes=[bones])
            for d in range(2):
                kb.dma("sp", maskA[d][:, :], self.inp["maskA"][d], writes=[maskA[d]])
                kb.dma("sp", maskN[d][:, :], self.inp["maskN"][d], writes=[maskN[d]])
                kb.dma("sp", rmask[d][:, :], self.inp["rmask"][d][:, 0:TB], writes=[rmask[d]])
            prm = {}
            for nm in ("rw_k_kT", "rw_k_aT"):
                prm[nm] = kb.tile([128, 8], F32)
                kb.dma("sp", prm[nm][:, :], self.inp[nm][l], writes=[prm[nm]])
            omk = kb.tile([128, 8], F32)
            kb.op("dve", lambda e: e.tensor_scalar(out=omk[:, :], in0=prm["rw_k_aT"][:, :], scalar1=-1.0, scalar2=1.0, op0=ALU.mult, op1=ALU.add), reads=[prm["rw_k_aT"]], writes=[omk])

            NQ = 2 * ncb
            bq = lambda ap, p: ap.unsqueeze(1).to_broadcast([p, NQ, 64])
            W = {}
            mk = lambda dt=F32: [kb.tile([128, TB], dt) for _ in range(2)]
            for nm in ("rT", "kT", "vT", "sg", "aT", "kap", "w1", "w2", "E1", "sq"):
                W[nm] = mk()
            QC = [kb.tile([128, ncb, 128], BF16) for _ in range(2)]
            KC = [kb.tile([128, ncb, 128], BF16) for _ in range(2)]
            YT = [kb.tile([64, 2, TB], F32) for _ in range(2)]
            AT4 = [kb.tile([128, ncb, 2, 128], BF16) for _ in range(2)]
            UV4 = [kb.tile([128, ncb, 2, 64], BF16) for _ in range(2)]
            KTT4 = [kb.tile([128, ncb, 128], BF16) for _ in range(2)]
            TTp = [kb.tile([64, NQ, 128], BF16) for _ in range(2)]
            for i in range(2):
                kb.op("dve", lambda e: e.memset(TTp[i][:, :, :], 0.0), writes=[TTp[i]])
            P = [kb.tile([64, NQ, 64], F32) for _ in range(2)]
            PT = [kb.tile([64, NQ, 64], F32) for _ in range(2)]
            TT = [kb.tile([64, NQ, 64], F32) for _ in range(2)]
            Pb = [kb.tile([64, NQ, 64], BF16) for _ in range(2)]
            PTb = [kb.tile([64, NQ, 64], BF16) for _ in range(2)]
            TTb = kb.tile([64, NQ, 64], BF16)
            ST32 = kb.tile([128, 64], F32)
            STp = [kb.tile([128, 64], BF16) for _ in range(2)]
            nZ = kb.tile([64, 2, 64], BF16)
            stmp = kb.tile([128, 64], F32)
            psM = [kb.psum([64, NQ, 64], F32) for _ in range(2)]
            psN = kb.psum([64, NQ, 64], F32)
            psA = kb.psum([128, 256], F32)
            psV = kb.psum([64, NQ, 64], F32)
            psK = kb.psum([128, ncb, 128], BF16)
            psY = kb.psum([64, 2, 64], F32)
            psU = kb.psum([128, 2, 64], F32)
            psA3 = h2(psA[:, :])

            def gen1(cc, b, d, blk, i):
                rows = slice(cc * 128, (cc + 1) * 128)
                cols = slice(b * TS + blk * TB, b * TS + (blk + 1) * TB)
                r_, k_, v_, sg_, a_ = W["rT"][i], W["kT"][i], W["vT"][i], W["sg"][i], W["aT"][i]
                kap_, w1_, w2_, E1_, sq_ = W["kap"][i], W["w1"][i], W["w2"][i], W["E1"][i], W["sq"][i]
                QC_, KC_ = QC[i], KC[i]
                kb.dma("sp", r_[:, :], S["RS"][rows, cols], writes=[r_])
                kb.dma("sp", k_[:, :], S["RS"][D + cc * 128:D + (cc + 1) * 128, cols], writes=[k_])
                kb.dma("sp", v_[:, :], S["RS"][2 * D + cc * 128:2 * D + (cc + 1) * 128, cols], writes=[v_])
                kb.dma("sp", sg_[:, :], S["SG%d" % d][rows, cols], writes=[sg_])
                kb.dma("sp", a_[:, :], S["AA%d" % d][rows, cols], writes=[a_])
                yield
                kb.op("dve", lambda e: e.tensor_scalar(out=kap_[:, :], in0=k_[:, :], scalar1=prm["rw_k_kT"][:, cc:cc + 1], scalar2=None, op0=ALU.mult), reads=[k_, prm["rw_k_kT"]], writes=[kap_])
                kb.op("dve", lambda e: e.tensor_tensor(out=sq_[:, :], in0=kap_[:, :], in1=kap_[:, :], op=ALU.mult), reads=[kap_], writes=[sq_])
                kb.op("pe", lambda e: e.matmul(psA[:, 0:TB], lhsT=bones[:, :], rhs=sq_[:, :], start=True, stop=True), reads=[bones, sq_], writes=[psA])
                kb.op("dve", lambda e: e.tensor_scalar(out=w1_[:, :], in0=a_[:, :], scalar1=prm["rw_k_aT"][:, cc:cc + 1], scalar2=omk[:, cc:cc + 1], op0=ALU.mult, op1=ALU.add), reads=[a_, prm["rw_k_aT"], omk], writes=[w1_])
                kb.op("dve", lambda e: e.tensor_tensor(out=w1_[:, :], in0=w1_[:, :], in1=k_[:, :], op=ALU.mult), reads=[w1_, k_], writes=[w1_])
                if d == 0:
                    kb.op("dve", lambda e: e.tensor_tensor_scan(out=w2_[:, :], data0=rmask[0][:, :], data1=sg_[:, :], initial=0.0, op0=ALU.mult, op1=ALU.add), reads=[rmask[0], sg_], writes=[w2_])
                else:
                    kb.op("dve", lambda e: e.tensor_tensor_scan(out=w2_[:, ::-1], data0=rmask[1][:, ::-1], data1=sg_[:, ::-1], initial=0.0, op0=ALU.mult, op1=ALU.add), reads=[rmask[1], sg_], writes=[w2_])
                yield
                kb.op("act", lambda e: e.activation(out=sq_[:, :], in_=psA[:, 0:TB], func=AF.Sqrt), reads=[psA], writes=[sq_])
                kb.op("act", lambda e: e.activation(out=E1_[:, :], in_=w2_[:, :], func=AF.Exp, scale=-DS), reads=[w2_], writes=[E1_])
                kb.op("dve", lambda e: e.tensor_scalar(out=sq_[:, :], in0=sq_[:, :], scalar1=1e-12, scalar2=None, op0=ALU.max), reads=[sq_], writes=[sq_])
                kb.op("dve", lambda e: e.reciprocal(out=sq_[:, :], in_=sq_[:, :]), reads=[sq_], writes=[sq_])
                kb.op("dve", lambda e: e.tensor_tensor(out=kap_[:, :], in0=kap_[:, :], in1=sq_[:, :], op=ALU.mult), reads=[kap_, sq_], writes=[kap_])
                kb.op("dve", lambda e: e.tensor_tensor(out=sg_[:, :], in0=w2_[:, :], in1=sg_[:, :], op=ALU.subtract), reads=[w2_, sg_], writes=[sg_])
                yield
                kb.op("act", lambda e: e.activation(out=sg_[:, :], in_=sg_[:, :], func=AF.Exp, scale=-DS), reads=[sg_], writes=[sg_])
                kb.op("act", lambda e: e.activation(out=w2_[:, :], in_=w2_[:, :], func=AF.Exp, scale=DS), reads=[w2_], writes=[w2_])
                kb.op("dve", lambda e: e.tensor_tensor(out=QC_[:, :, 64:128], in0=v3(r_), in1=v3(E1_), op=ALU.mult), reads=[r_, E1_], writes=[QC_])
                kb.op("dve", lambda e: e.tensor_tensor(out=a_[:, :], in0=a_[:, :], in1=kap_[:, :], op=ALU.mult), reads=[a_, kap_], writes=[a_])
                yield
                kb.op("dve", lambda e: e.tensor_tensor(out=QC_[:, :, 0:64], in0=v3(kap_), in1=v3(sg_), op=ALU.mult), reads=[kap_, sg_], writes=[QC_])
                kb.op("dve", lambda e: e.tensor_tensor(out=KC_[:, :, 0:64], in0=v3(w1_), in1=v3(w2_), op=ALU.mult), reads=[w1_, w2_], writes=[KC_])
                kb.op("dve", lambda e: e.tensor_tensor(out=KC_[:, :, 64:128], in0=v3(a_), in1=v3(w2_), op=ALU.mult), reads=[a_, w2_], writes=[KC_])
                yield
                for h in range(2):
                    hs = slice(h * 64, (h + 1) * 64)
                    for c in range(ncb):
                        q = 2 * c + h
                        kb.op("pe", lambda e: e.matmul(psM[0][:, q, :], lhsT=QC_[hs, c, 0:64], rhs=KC_[hs, c, 64:128], start=True, stop=True), reads=[KC_, QC_], writes=[psM[0]], row=h * 64)
                        kb.op("pe", lambda e: e.matmul(psM[1][:, q, :], lhsT=KC_[hs, c, 64:128], rhs=QC_[hs, c, 0:64], start=True, stop=True), reads=[KC_, QC_], writes=[psM[1]], row=h * 64)
                yield
                BL = self.cfg.get("rw_bl", 1)
                p0_ = Pb[0] if BL <= 1 else P[0]
                kb.op("dve", lambda e: e.scalar_tensor_tensor(out=p0_[:, :, :], in0=psM[0][:, :, :], scalar=-1.0, in1=bq(maskN[d][:, :], 64), op0=ALU.mult, op1=ALU.mult), reads=[psM[0], maskN[d]], writes=[p0_])
                kb.op("dve", lambda e: e.scalar_tensor_tensor(out=PT[0][:, :, :], in0=psM[1][:, :, :], scalar=-1.0, in1=bq(maskA[d][0:64, 0:64], 64), op0=ALU.mult, op1=ALU.mult), reads=[psM[1], maskA[d]], writes=[PT[0]])
                kb.op("dve", lambda e: e.tensor_tensor(out=TT[0][:, :, :], in0=PT[0][:, :, :], in1=bq(identF[0:64, 0:64], 64), op=ALU.add), reads=[PT[0], identF], writes=[TT[0]])
                if BL <= 1:
                    kb.op("pool", lambda e: e.tensor_copy(out=PTb[0][:, :, :], in_=PT[0][:, :, :]), reads=[PT[0]], writes=[PTb[0]])
                    kb.op("pool", lambda e: e.tensor_copy(out=TTb[:, :, :], in_=TT[0][:, :, :]), reads=[TT[0]], writes=[TTb])
                yield
                extra = []
                for c in range(ncb):
                    def ex_a(c=c):
                        for h in range(2):
                            hs = slice(h * 64, (h + 1) * 64)
                            kb.op("pe", lambda e: e.matmul(psA3[:, h, :], lhsT=KC_[hs, c, :], rhs=QC_[hs, c, :], start=True, stop=True), reads=[KC_, QC_], writes=[psA], row=h * 64)
                        kb.op("dve", lambda e: e.tensor_tensor(out=AT4[i][:, c, :, :], in0=psA3, in1=maskA[d][:, :].unsqueeze(1).to_broadcast([128, 2, 128]), op=ALU.mult), reads=[psA, maskA[d]], writes=[AT4[i]])
                    extra.append(ex_a)

                def ex_v():
                    for h in range(2):
                        hs = slice(h * 64, (h + 1) * 64)
                        for c in range(ncb):
                            kb.op("pe", lambda e: e.transpose(psV[:, 2 * c + h, :], v_[hs, c * 64:(c + 1) * 64], identF[hs, hs]), reads=[v_, identF], writes=[psV], row=h * 64)
                    kb.op("act", lambda e: e.activation(out=UV4[i][0:64, :, :, :].rearrange("p c h e -> p (c h) e"), in_=psV[:, :, :], func=AF.Copy), reads=[psV], writes=[UV4[i]])
                extra.append(ex_v)

                def ex_k():
                    for c in range(ncb):
                        kb.op("pe", lambda e: e.transpose(psK[:, c, :], KC_[:, c, :], identB[:, :]), reads=[KC_, identB], writes=[psK])
                    kb.op("act", lambda e: e.activation(out=KTT4[i][:, :, :], in_=psK[:, :, :], func=AF.Copy), reads=[psK], writes=[KTT4[i]])
                extra.append(ex_k)
                for m in range(1, 6):
                    lo = m >= BL
                    if lo:
                        pc_, pn_ = Pb[(m - 1) % 2], Pb[m % 2]
                        tc_, tn_ = PTb[(m - 1) % 2], PTb[m % 2]
                    else:
                        pc_, pn_ = P[(m - 1) % 2], P[m % 2]
                        tc_, tn_ = PT[(m - 1) % 2], PT[m % 2]
                    if m + 1 == BL:
                        tn_ = PTb[m % 2]
                    for q in range(NQ):
                        kb.op("pe", lambda e: e.matmul(psM[0][:, q, :], lhsT=tc_[:, q, :], rhs=pc_[:, q, :], start=True, stop=True), reads=[tc_, pc_], writes=[psM[0]], row=0)
                    if m < 5:
                        for q in range(NQ):
                            kb.op("pe", lambda e: e.matmul(psM[1][:, q, :], lhsT=pc_[:, q, :], rhs=tc_[:, q, :], start=True, stop=True), reads=[tc_, pc_], writes=[psM[1]], row=0)
                    if extra:
                        extra.pop(0)()
                    yield
                    kb.op("act", lambda e: e.activation(out=pn_[:, :, :], in_=psM[0][:, :, :], func=AF.Copy), reads=[psM[0]], writes=[pn_])
                    if m < 5:
                        kb.op("dve", lambda e: e.tensor_copy(out=tn_[:, :, :], in_=psM[1][:, :, :]), reads=[psM[1]], writes=[tn_])
                    if m + 1 == BL:
                        pb_ = Pb[m % 2]
                        kb.op("pool", lambda e: e.tensor_copy(out=pb_[:, :, :], in_=pn_[:, :, :]), reads=[pn_], writes=[pb_])
                    yield
                    tt_c, tt_n = TT[(m - 1) % 2], TT[m % 2]
                    tt_r = TTb if lo else tt_c
                    for q in range(NQ):
                        kb.op("pe", lambda e: e.matmul(psN[:, q, :], lhsT=pn_[:, q, :], rhs=tt_r[:, q, :], start=True, stop=True), reads=[pn_, tt_r], writes=[psN], row=0)
                    if extra:
                        extra.pop(0)()
                    yield
                    if m < 5:
                        kb.op("dve", lambda e: e.tensor_tensor(out=tt_n[:, :, :], in0=psN[:, :, :], in1=tt_c[:, :, :], op=ALU.add), reads=[psN, tt_c], writes=[tt_n])
                        if m + 1 >= BL:
                            kb.op("pool", lambda e: e.tensor_copy(out=TTb[:, :, :], in_=tt_n[:, :, :]), reads=[tt_n], writes=[TTb])
                    else:
                        kb.op("dve", lambda e: e.tensor_tensor(out=TTp[i][:, :, 64:128], in0=psN[:, :, :], in1=tt_c[:, :, :], op=ALU.add), reads=[psN, tt_c], writes=[TTp[i]])
                    yield
                while extra:
                    extra.pop(0)()
                    yield

            def gen2(cc, b, d, blk, i):
                rows = slice(cc * 128, (cc + 1) * 128)
                cols = slice(b * TS + blk * TB, b * TS + (blk + 1) * TB)
                QC_, KC_, yt, E1_ = QC[i], KC[i], YT[i], W["E1"][i]
                AT_, UV_, KTT_, TTp_ = AT4[i], UV4[i], KTT4[i], TTp[i]
                for c in (range(ncb) if d == 0 else range(ncb - 1, -1, -1)):
                    p0 = c * 64
                    for h in range(2):
                        hs = slice(h * 64, (h + 1) * 64)
                        kb.op("pe", lambda e: e.matmul(psY[:, h, :], lhsT=QC_[:, c, 0:64], rhs=STp[h][:, :], start=True, stop=False), reads=[QC_, STp[h]], writes=[psY])
                        kb.op("pe", lambda e: e.matmul(psY[:, h, :], lhsT=AT_[0:64, c, h, 0:64], rhs=UV_[0:64, c, h, :], start=False, stop=True), reads=[AT_, UV_], writes=[psY])
                    yield
                    kb.op("dve", lambda e: e.tensor_scalar(out=nZ[:, :, :], in0=psY[:, :, :], scalar1=-1.0, scalar2=None, op0=ALU.mult), reads=[psY], writes=[nZ])
                    yield
                    for h in range(2):
                        kb.op("pe", lambda e: e.matmul(psU[:, h, :], lhsT=TTp_[:, 2 * c + h, :], rhs=nZ[:, h, :], start=True, stop=True), reads=[TTp_, nZ], writes=[psU])
                    yield
                    kb.op("act", lambda e: e.activation(out=UV_[64:128, c, :, :], in_=psU[64:128, :, :], func=AF.Copy), reads=[psU], writes=[UV_])
                    yield
                    for h in range(2):
                        hs = slice(h * 64, (h + 1) * 64)
                        kb.op("pe", lambda e: e.matmul(psY[:, h, :], lhsT=STp[h][:, :], rhs=QC_[:, c, 64:128], start=True, stop=False), reads=[QC_, STp[h]], writes=[psY])
                        kb.op("pe", lambda e: e.matmul(psY[:, h, :], lhsT=UV_[:, c, h, :], rhs=AT_[:, c, h, 64:128], start=False, stop=True), reads=[AT_, UV_], writes=[psY])
                    kb.op("pe", lambda e: e.matmul(psU[:, 0, :], lhsT=KTT_[:, c, :], rhs=UV_[:, c, 1, :], start=True, stop=True), reads=[KTT_, UV_], writes=[psU])
                    kb.op("pe", lambda e: e.matmul(psU[0:64, 0, :], lhsT=KTT_[:, c, 0:64], rhs=UV_[:, c, 0, :], start=True, stop=True), reads=[KTT_, UV_], writes=[psU])
                    yield
                    wcol = p0 + 63 if d == 0 else p0
                    kb.op("dve", lambda e: e.tensor_tensor(out=stmp[:, :], in0=psU[:, 0, :], in1=ST32[:, :], op=ALU.add), reads=[psU, ST32], writes=[stmp])
                    kb.op("act", lambda e: e.activation(out=yt[:, :, p0:p0 + 64], in_=psY[:, :, :], func=AF.Copy), reads=[psY], writes=[yt])
                    kb.op("dve", lambda e: e.tensor_scalar(out=ST32[:, :], in0=stmp[:, :], scalar1=E1_[:, wcol:wcol + 1], scalar2=None, op0=ALU.mult), reads=[stmp, E1_], writes=[ST32])
                    yield
                    kb.op("act", lambda e: e.activation(out=STp[0][0:64, :], in_=ST32[0:64, :], func=AF.Copy), reads=[ST32], writes=[STp[0]])
                    kb.op("dve", lambda e: e.tensor_copy(out=STp[1][64:128, :], in_=ST32[64:128, :]), reads=[ST32], writes=[STp[1]])
                    yield
                kb.dma("pool", S["YT%d" % d][rows, cols].rearrange("(h v) t -> v h t", h=2), yt[:, :, :], reads=[yt])

            def drain(g):
                for _ in g:
                    pass

            seqb = []
            for cc in range(8):
                for b in range(NB):
                    for d in range(2):
                        bl = list(range(nblk)) if d == 0 else list(range(cblk - 1, -1, -1)) + list(range(nblk - 1, cblk - 1, -1))
                        for n_, blk in enumerate(bl):
                            seqb.append((cc, b, d, blk, n_ == 0))
            drain(gen1(*seqb[0][:4], 0))
            for k, (cc, b, d, blk, first) in enumerate(seqb):
                if first:
                    kb.op("dve", lambda e: e.memset(ST32[:, :], 0.0), writes=[ST32])
                    for h_ in range(2):
                        kb.op("dve", lambda e: e.memset(STp[h_][:, :], 0.0), writes=[STp[h_]])
                gens = [gen2(cc, b, d, blk, k % 2)]
                if k + 1 < len(seqb):
                    if self.cfg.get("rw_interleave", True):
                        gens.append(gen1(*seqb[k + 1][:4], (k + 1) % 2))
                    else:
                        drain(gen1(*seqb[k + 1][:4], (k + 1) % 2))
                while gens:
                    for g in list(gens):
                        try:
                            next(g)
                        except StopIteration:
                            gens.remove(g)
        self.st_rw_post(l)

    def st_rw_post(self, l):
        kb = self.kb
        S = self.scr
        with kb.stage():
            bones = kb.tile([128, 128], F32)
            kb.dma("sp", bones[:, :], self.inp["bones_f"], writes=[bones])
            prm = {}
            for nm in ("rw_k_aT", "rw_r_kT", "rw_ln_wT", "rw_ln_bT"):
                prm[nm] = kb.tile([128, 8], F32)
                kb.dma("sp", prm[nm][:, :], self.inp[nm][l], writes=[prm[nm]])
            omk2 = kb.tile([128, 8], F32)
            kb.op("dve", lambda e: e.tensor_scalar(out=omk2[:, :], in0=prm["rw_k_aT"][:, :], scalar1=-2.0, scalar2=2.0, op0=ALU.mult, op1=ALU.add), reads=[prm["rw_k_aT"]], writes=[omk2])
            lneps = kb.tile([128, 1], F32)
            kb.op("dve", lambda e: e.memset(lneps[:, :], RW_LN_EPS), writes=[lneps])
            psB = [kb.psum([128, 512], F32) for _ in range(3)]
            PB = 512
            mk2 = lambda dt=F32: [kb.tile([128, PB], dt) for _ in range(2)]
            r2, k2, v2, g2, a0, a1, y0, y1, tq, uq = mk2(), mk2(), mk2(), mk2(), mk2(), mk2(), mk2(), mk2(), mk2(), mk2()
            yo = mk2(BF16)
            it = 0
            for cc in range(8):
                rows = slice(cc * 128, (cc + 1) * 128)
                for t0 in range(0, self.T, PB):
                    tsz = min(PB, self.T - t0)
                    i = it % 2
                    it += 1
                    cs_ = slice(t0, t0 + tsz)
                    ld = [(r2[i], S["RS"][rows, cs_]), (k2[i], S["RS"][D + cc * 128:D + (cc + 1) * 128, cs_]), (v2[i], S["RS"][2 * D + cc * 128:2 * D + (cc + 1) * 128, cs_]),
                          (g2[i], S["GG"][rows, cs_]), (a0[i], S["AA0"][rows, cs_]), (a1[i], S["AA1"][rows, cs_]), (y0[i], S["YT0"][rows, cs_]), (y1[i], S["YT1"][rows, cs_])]
                    for tl, src in ld:
                        kb.dma("sp", tl[:, 0:tsz], src, writes=[tl])
                    r_, k_, v_, g_, a0_, a1_, y_, y1_, tq_, uq_, yo_ = r2[i], k2[i], v2[i], g2[i], a0[i], a1[i], y0[i], y1[i], tq[i], uq[i], yo[i]
                    sl = slice(0, tsz)
                    p1, p2, p3 = psB
                    kb.op("dve", lambda e: e.tensor_tensor(out=a0_[:, sl], in0=a0_[:, sl], in1=a1_[:, sl], op=ALU.add), reads=[a0_, a1_], writes=[a0_])
                    kb.op("dve", lambda e: e.tensor_scalar(out=a0_[:, sl], in0=a0_[:, sl], scalar1=prm["rw_k_aT"][:, cc:cc + 1], scalar2=omk2[:, cc:cc + 1], op0=ALU.mult, op1=ALU.add), reads=[a0_, prm["rw_k_aT"], omk2], writes=[a0_])
                    kb.op("dve", lambda e: e.tensor_tensor(out=a0_[:, sl], in0=a0_[:, sl], in1=k_[:, sl], op=ALU.mult), reads=[a0_, k_], writes=[a0_])
                    kb.op("dve", lambda e: e.scalar_tensor_tensor(out=a0_[:, sl], in0=a0_[:, sl], scalar=prm["rw_r_kT"][:, cc:cc + 1], in1=r_[:, sl], op0=ALU.mult, op1=ALU.mult), reads=[a0_, prm["rw_r_kT"], r_], writes=[a0_])
                    kb.op("pe", lambda e: e.matmul(p1[:, sl], lhsT=bones[:, :], rhs=a0_[:, sl], start=True, stop=True), reads=[bones, a0_], writes=[p1])
                    kb.op("dve", lambda e: e.tensor_tensor(out=y_[:, sl], in0=y_[:, sl], in1=y1_[:, sl], op=ALU.add), reads=[y_, y1_], writes=[y_])
                    kb.op("pe", lambda e: e.matmul(p2[:, sl], lhsT=bones[:, :], rhs=y_[:, sl], start=True, stop=True), reads=[bones, y_], writes=[p2])
                    kb.op("dve", lambda e: e.tensor_tensor(out=a1_[:, sl], in0=p1[:, sl], in1=v_[:, sl], op=ALU.mult), reads=[p1, v_], writes=[a1_])
                    kb.op("dve", lambda e: e.scalar_tensor_tensor(out=tq_[:, sl], in0=p2[:, sl], scalar=-1.0 / 64, in1=y_[:, sl], op0=ALU.mult, op1=ALU.add), reads=[p2, y_], writes=[tq_])
                    kb.op("act", lambda e: e.activation(out=uq_[:, sl], in_=tq_[:, sl], func=AF.Square), reads=[tq_], writes=[uq_])
                    kb.op("pe", lambda e: e.matmul(p3[:, sl], lhsT=bones[:, :], rhs=uq_[:, sl], start=True, stop=True), reads=[bones, uq_], writes=[p3])
                    kb.op("act", lambda e: e.activation(out=uq_[:, sl], in_=p3[:, sl], func=AF.Sqrt, scale=1.0 / 64, bias=lneps[:, 0:1]), reads=[p3, lneps], writes=[uq_])
                    kb.op("dve", lambda e: e.reciprocal(out=uq_[:, sl], in_=uq_[:, sl]), reads=[uq_], writes=[uq_])
                    kb.op("dve", lambda e: e.tensor_tensor(out=tq_[:, sl], in0=tq_[:, sl], in1=uq_[:, sl], op=ALU.mult), reads=[tq_, uq_], writes=[tq_])
                    kb.op("act", lambda e: e.activation(out=tq_[:, sl], in_=tq_[:, sl], func=AF.Identity, scale=prm["rw_ln_wT"][:, cc:cc + 1], bias=prm["rw_ln_bT"][:, cc:cc + 1]), reads=[tq_, prm["rw_ln_wT"], prm["rw_ln_bT"]], writes=[tq_])
                    kb.op("dve", lambda e: e.tensor_tensor(out=tq_[:, sl], in0=tq_[:, sl], in1=a1_[:, sl], op=ALU.add), reads=[tq_, a1_], writes=[tq_])
                    kb.op("dve", lambda e: e.tensor_tensor(out=yo_[:, sl], in0=tq_[:, sl], in1=g_[:, sl], op=ALU.mult), reads=[tq_, g_], writes=[yo_])
                    kb.dma("pool", S["YRW"][rows, cs_], yo_[:, sl], reads=[yo_])

    def build(self):
        nc, kb = self.nc, self.kb
        NB, CTX, LAT, L, T, TS = self.NB, self.CTX, self.LAT, self.L, self.T, self.TS
        self.din("x", [NB, LAT, D])
        self.din("ctx", [NB, CTX, D])
        self.din("condT", [128, 8, 3])
        self.din("w_mod", [L, D, 6 * D])
        self.din("b_mod", [L, 6 * D])
        self.din("norm1_w", [L, D])
        self.din("norm2_w", [L, D])
        self.din("w_in", [L, D, INW])
        self.din("ml_ig_b", [L, 2, 4])
        self.din("ml_fg_b", [L, 2, 4])
        self.din("ml_norm_w", [L, D])
        self.din("lru_conv_wT", [L, 128, 8, 4])
        self.din("lru_conv_bT", [L, 128, 8])
        self.din("lru_gr_w", [L, 2, 16, 64, 64])
        self.din("lru_gr_bT", [L, 128, 2, 8])
        self.din("lru_gi_w", [L, 2, 16, 64, 64])
        self.din("lru_gi_bT", [L, 128, 2, 8])
        self.din("lru_lambdaT", [L, 128, 2, 8])
        self.din("rw_muT", [L, 128, 27])
        self.din("rw_decay0T", [L, 128, 2, 8])
        self.din("rw_iclr0T", [L, 128, 2, 8])
        self.din("rw_decay_up", [L, 2, 64, D])
        self.din("rw_iclr_up", [L, 2, 64, D])
        self.din("rw_gate_up", [L, 128, D])
        for nm in ("rw_k_kT", "rw_k_aT", "rw_r_kT", "rw_ln_wT", "rw_ln_bT"):
            self.din(nm, [L, 128, 8])
        self.din("bones_f", [128, 128])
        self.din("maskA", [2, 128, 128])
        self.din("maskN", [2, 64, 64])
        self.din("rmask", [2, 128, TS])
        self.din("out_ml", [L, D, D])
        self.din("out_lru", [L, D, D])
        self.din("out_rw", [L, D, D])
        self.din("w_out", [L, D, D])
        self.din("w_ffn_in", [L, D, 2 * FFN])
        self.din("w_ffn_out", [L, FFN, D])
        self.din("final_norm_w", [D])
        self.din("ident_bf", [128, 128], BF16)
        self.din("ident_f", [128, 128])
        self.din("ones_f", [128, 128])
        self.din("tri", [2, 128, 128])
        self.din("negm", [2, 128, 128])
        self.out_ap = self.nc.dram_tensor("out", [NB, LAT, D], F32, kind="ExternalOutput").ap()
        self.dscr("XR", [NB, TS, D], F32)
        self.dscr("MOD", [L, 3, 6 * D], F32)
        self.dscr("HT", [D, T], BF16)
        self.dscr("QT", [D, T], BF16)
        self.dscr("KT", [D, T], BF16)
        self.dscr("KTM", [T, D], BF16)
        self.dscr("VTM", [T, D], BF16)
        self.dscr("OG", [T, D], BF16)
        self.dscr("IFG", [T, 16], F32)
        self.dscr("LX", [D, T], F32)
        self.dscr("LG", [D, T], BF16)
        self.dscr("RW", [RWSEG, T], F32)
        self.dscr("MG", [3 * D, T], BF16)
        self.dscr("HML0", [T, D], F32)
        self.dscr("HML1", [T, D], F32)
        self.dscr("YML", [D, T], BF16)
        self.dscr("YLRU", [D, T], BF16)
        self.dscr("YRW", [D, T], BF16)
        self.dscr("YACC", [D, T], F32)
        self.dscr("YM", [D, T], BF16)
        self.dscr("RS", [3 * D, T], F32)
        self.dscr("TD", [128, T], BF16)
        self.dscr("AD", [128, T], BF16)
        self.dscr("GD", [128, T], BF16)
        for d in range(2):
            self.dscr("SG%d" % d, [D, T], F32)
            self.dscr("AA%d" % d, [D, T], F32)
            self.dscr("YT%d" % d, [D, T], F32)
        self.dscr("GG", [D, T], F32)
        self.dscr("AG", [FFN, T], BF16)
        self.dscr("AT", [FFN, T], BF16)
        self.eps_t = kb.tile([128, 1], F32, "eps")
        self.one_t = kb.tile([128, 1], F32, "one")
        kb.op("dve", lambda e: e.memset(self.eps_t[:, :], EPS), writes=[self.eps_t])
        kb.op("dve", lambda e: e.memset(self.one_t[:, :], 1.0), writes=[self.one_t])
        upto = self.cfg.get("upto", None)
        seq = [("init", lambda: self.st_init()), ("adaln", lambda: self.st_adaln())]
        for l in range(L):
            seq += [("norm1", lambda l=l: self.st_norm(l, 1)), ("win", lambda l=l: self.st_win(l)),
                    ("mlstm", lambda l=l: self.st_mlstm(l)), ("mlpost", lambda l=l: self.st_mlstm_post(l)),
                    ("lru", lambda l=l: self.st_lru(l)), ("rwkv", lambda l=l: self.st_rwkv(l)),
                    ("merge", lambda l=l: self.st_merge(l)), ("wout", lambda l=l: self.st_wout(l)),
                    ("ffn", lambda l=l: self.st_ffn(l))]
        seq += [("final", lambda: self.st_norm(0, 1, final=True))]
        skip = self.cfg.get("skip", ())
        for name, fn in seq:
            if name not in skip:
                fn()
            if upto is not None and name == upto:
                break
        kb.finish()


def host_consts():
    c = {}
    c["ident_bf"] = np.eye(128, dtype=np.float32).astype(ml_dtypes.bfloat16)
    c["ident_f"] = np.eye(128, dtype=np.float32)
    c["ones_f"] = np.ones((128, 128), np.float32)
    s = np.arange(128)[:, None]
    t = np.arange(128)[None, :]
    tri = np.stack([(s <= t), (s >= t)]).astype(np.float32)
    c["tri"] = tri
    c["negm"] = ((1.0 - tri) * -30000.0).astype(np.float32)
    bo = np.zeros((128, 128), np.float32)
    bo[:64, :64] = 1.0
    bo[64:, 64:] = 1.0
    c["bones_f"] = bo
    j = np.arange(64)[:, None]
    t = np.arange(64)[None, :]
    mA = []
    mN = []
    for d in range(2):
        strict = (j < t) if d == 0 else (j > t)
        incl = (j <= t) if d == 0 else (j >= t)
        blk = np.concatenate([strict, incl], 1).astype(np.float32)
        mA.append(np.concatenate([blk, blk], 0))
        mN.append(strict.T.astype(np.float32))
    c["maskA"] = np.stack(mA)
    c["maskN"] = np.stack(mN)
    return c


def chanT(a):
    a = np.asarray(a, np.float32)
    lead = a.shape[:-1]
    a = a.reshape(lead + (8, 128))
    return np.ascontiguousarray(np.moveaxis(a, -1, 0))


def make_in_maps(inputs, cfg, ncores):
    NB, L = cfg["NB"], cfg["DEPTH"]
    consts = host_consts()
    f = lambda k: np.ascontiguousarray(np.asarray(inputs[k], np.float32)[:L])
    shared = {
        "w_mod": f("w_mod"), "b_mod": f("b_mod"), "norm1_w": f("norm1_w"), "norm2_w": f("norm2_w"),
        "w_in": f("w_in"), "ml_ig_b": f("ml_ig_b"), "ml_fg_b": f("ml_fg_b"), "ml_norm_w": f("ml_norm_w"),
        "lru_gr_w": f("lru_gr_w"), "lru_gi_w": f("lru_gi_w"),
        "out_ml": f("out_ml"), "out_lru": f("out_lru"), "out_rw": f("out_rw"), "w_out": f("w_out"),
        "w_ffn_in": f("w_ffn_in"), "w_ffn_out": f("w_ffn_out"),
        "final_norm_w": np.asarray(inputs["final_norm_w"], np.float32),
    }
    cw = np.asarray(inputs["lru_conv_w"], np.float32)[:L]
    shared["lru_conv_wT"] = np.ascontiguousarray(np.stack([np.moveaxis(chanT(cw[l]), 1, 2) for l in range(L)]))
    shared["lru_conv_bT"] = np.ascontiguousarray(np.stack([chanT(np.asarray(inputs["lru_conv_b"], np.float32)[l]) for l in range(L)]))
    for k in ("lru_gr_b", "lru_gi_b", "lru_lambda"):
        a = np.asarray(inputs[k], np.float32)[:L]
        shared[k + "T"] = np.ascontiguousarray(np.stack([chanT(a[l]) for l in range(L)]))
    TS = cfg["CTX"] + cfg["LAT"]
    tt = np.arange(TS)
    rm = np.stack([(tt % 64 != 0), (tt % 64 != 63)]).astype(np.float32)
    shared["rmask"] = np.ascontiguousarray(np.broadcast_to(rm[:, None, :], (2, 128, TS)))
    mu = np.asarray(inputs["rw_mu"], np.float32)[:L]
    shared["rw_muT"] = np.ascontiguousarray(mu.reshape(L, 27, 128).transpose(0, 2, 1))
    for k in ("rw_decay0", "rw_iclr0"):
        a = np.asarray(inputs[k], np.float32)[:L]
        shared[k + "T"] = np.ascontiguousarray(np.stack([chanT(a[l]) for l in range(L)]))
    for k in ("rw_k_k", "rw_k_a", "rw_r_k", "rw_ln_w", "rw_ln_b"):
        a = np.asarray(inputs[k], np.float32)[:L]
        shared[k + "T"] = np.ascontiguousarray(np.stack([chanT(a[l]) for l in range(L)]))
    for k in ("rw_decay_up", "rw_iclr_up", "rw_gate_up"):
        shared[k] = f(k)
    shared.update(consts)
    x = np.asarray(inputs["x"], np.float32)
    ctx = np.asarray(inputs["ctx"], np.float32)
    c = np.asarray(inputs["c"], np.float32)
    cc = np.asarray(inputs["c_ctx"], np.float32)
    maps = []
    for i in range(ncores):
        m = dict(shared)
        m["x"] = np.ascontiguousarray(x[i * NB:(i + 1) * NB])
        m["ctx"] = np.ascontiguousarray(ctx[i * NB:(i + 1) * NB])
        cond = np.concatenate([c[i * NB:(i + 1) * NB], cc[None, :]], 0)
        m["condT"] = np.ascontiguousarray(cond.T.reshape(8, 128, 3).transpose(1, 0, 2))
        maps.append(m)
    return maps


def kernel(**inputs):
    cfg = {"NB": 2, "CTX": 256, "LAT": 4096, "DEPTH": 4}
    nc = bass.Bass("TRN2", target_bir_lowering=False)
    p = Prog(nc, cfg)
    p.build()
    maps = make_in_maps(inputs, cfg, NCORES)
    res = run_bass_kernel_spmd(nc, maps, core_ids=list(range(NCORES)))
    return np.concatenate([np.asarray(r["out"], np.float32) for r in res.results], axis=0)
```

```python
import math
from contextlib import ExitStack, contextmanager
import numpy as np
import ml_dtypes
import concourse.bass as bass
import concourse.mybir as mybir
from concourse.bass_utils import run_bass_kernel_spmd

F32 = mybir.dt.float32
BF16 = mybir.dt.bfloat16
AF = mybir.ActivationFunctionType
ALU = mybir.AluOpType
AX = mybir.AxisListType

D = 1024
NCORES = 8
FFN = 2816
INW = 12688
RWSEG = 3456
EPS = 1e-6
RW_LN_EPS = 64e-5
RW_DECAY_SCALE = math.exp(-0.5)


class Buf:
    __slots__ = ("w", "r", "prow")

    def __init__(self):
        self.w = {}
        self.r = {}
        self.prow = None


class Tile:
    def __init__(self, h):
        self.h = h
        self.b = Buf()

    def __getitem__(self, k):
        return self.h[k]


class KB:
    def __init__(self, nc, n_dma=56):
        self.nc = nc
        self.E = {"pe": nc.tensor, "act": nc.scalar, "dve": nc.vector, "pool": nc.gpsimd, "sp": nc.sync}
        self.sem = {}
        self.cnt = {}
        for e in ("pe", "act", "dve", "pool"):
            self.sem[e] = nc.alloc_semaphore("c_" + e)
            self.cnt[e] = 0
        self.dsem = [nc.alloc_semaphore("d%d" % i) for i in range(n_dma)]
        self.dval = [0] * n_dma
        self.drr = 0
        self.drr_sw = 0
        self.seen = {}
        self.n_inst = 0
        self.es = None
        self.uid = 0

    def tile(self, shape, dt, name="t"):
        self.uid += 1
        nm = "%s_%d" % (name, self.uid)
        if self.es is not None:
            return Tile(self.es.enter_context(self.nc.sbuf_tensor(nm, list(shape), dt)))
        return Tile(self.nc.alloc_sbuf_tensor(nm, list(shape), dt))

    def psum(self, shape, dt, name="p"):
        self.uid += 1
        nm = "%s_%d" % (name, self.uid)
        if self.es is not None:
            return Tile(self.es.enter_context(self.nc.psum_tensor(nm, list(shape), dt)))
        return Tile(self.nc.alloc_psum_tensor(nm, list(shape), dt))

    @contextmanager
    def stage(self):
        es = ExitStack()
        self.es = es
        try:
            yield
            self.barrier()
        finally:
            self.es = None
            es.close()

    def _semh(self, key):
        return self.sem[key] if isinstance(key, str) else self.dsem[key]

    def _wait(self, e, key, val):
        if e == "pe" and key == "pe":
            return
        if self.seen.get((e, key), 0) >= val:
            return
        self.E[e].wait_ge(self._semh(key), val)
        self.seen[(e, key)] = val
        self.n_inst += 1

    @staticmethod
    def _bufs(lst):
        return [x.b if isinstance(x, Tile) else x for x in lst]

    def _deps(self, e, reads, writes):
        deps = {}
        for b in reads:
            for k, v in b.w.items():
                if deps.get(k, 0) < v:
                    deps[k] = v
        for b in writes:
            for k, v in b.w.items():
                if deps.get(k, 0) < v:
                    deps[k] = v
            for k, v in b.r.items():
                if deps.get(k, 0) < v:
                    deps[k] = v
        for k, v in deps.items():
            self._wait(e, k, v)

    def _mark(self, ticket, reads, writes):
        k, v = ticket
        for b in reads:
            b.r[k] = v
        for b in writes:
            b.w = {k: v}
            b.r = {}

    def op(self, e, fn, reads=(), writes=(), row=None):
        reads = self._bufs(reads)
        writes = self._bufs(writes)
        if e == "pe":
            if row is not None and any(b.prow is not None and b.prow != row for b in writes):
                self.pe_fence()
            for b in writes:
                b.prow = row
        self._deps(e, reads, writes)
        ins = fn(self.E[e])
        self.cnt[e] += 1
        ins.then_inc(self.sem[e], 1)
        self.n_inst += 1
        self._mark((e, self.cnt[e]), reads, writes)
        return ins

    def dma(self, e, out, in_, reads=(), writes=(), **kw):
        reads = self._bufs(reads)
        writes = self._bufs(writes)
        self._deps(e, reads, writes)
        half = len(self.dsem) // 2
        if e == "pool":
            s = half + self.drr_sw
            self.drr_sw = (self.drr_sw + 1) % (len(self.dsem) - half)
        else:
            s = self.drr
            self.drr = (self.drr + 1) % half
        if self.dval[s] > 0:
            self._wait(e, s, self.dval[s])
        ins = self.E[e].dma_start(out=out, in_=in_, **kw)
        self.dval[s] += 16
        ins.then_inc(self.dsem[s], 16)
        self.n_inst += 1
        self._mark((s, self.dval[s]), reads, writes)

    def pe_fence(self):
        if self.cnt["pe"] > 0:
            self.E["pe"].wait_ge(self.sem["pe"], self.cnt["pe"])
            self.n_inst += 1

    def barrier(self, engines=("pe", "act", "dve", "pool", "sp")):
        for e in engines:
            for s in range(len(self.dsem)):
                if self.dval[s] > 0:
                    self._wait(e, s, self.dval[s])
            for k in ("pe", "act", "dve", "pool"):
                if self.cnt[k] > 0 and k != e:
                    self._wait(e, k, self.cnt[k])

    def finish(self):
        self.barrier()


class Prog:
    def __init__(self, nc, cfg):
        self.nc = nc
        self.kb = KB(nc)
        self.cfg = cfg
        self.NB = cfg["NB"]
        self.CTX = cfg["CTX"]
        self.LAT = cfg["LAT"]
        self.L = cfg["DEPTH"]
        self.ROWS = self.LAT // 64
        self.TS = self.CTX + self.LAT
        self.T = self.NB * self.TS
        self.dbg = cfg.get("dbg", ())
        self.inp = {}
        self.scr = {}
        self.flip = 0

    def din(self, name, shape, dt=F32):
        self.inp[name] = self.nc.dram_tensor(name, list(shape), dt, kind="ExternalInput").ap()
        return self.inp[name]

    def dscr(self, name, shape, dt):
        kind = "ExternalOutput" if name in self.dbg else "Internal"
        self.scr[name] = self.nc.dram_tensor(name, list(shape), dt, kind=kind).ap()
        return self.scr[name]

    def alt(self):
        self.flip ^= 1
        return "act" if self.flip else "dve"

    def xr_rows(self, layer, b, pos0, n=128):
        XR = self.scr["XR"]
        if pos0 < self.CTX or layer % 2 == 0:
            return [(0, n, XR[b, pos0:pos0 + n, :])]
        lat = XR[b, self.CTX:self.TS, :].rearrange("(r c) d -> c r d", c=64)
        out = []
        m0 = pos0 - self.CTX
        m = m0
        while m < m0 + n:
            c = m // self.ROWS
            r0 = m % self.ROWS
            k = min(self.ROWS - r0, m0 + n - m)
            out.append((m - m0, k, lat[c, r0:r0 + k, :]))
            m += k
        return out

    def st_init(self):
        kb = self.kb
        with kb.stage():
            XR = self.scr["XR"]
            for b in range(self.NB):
                kb.dma("sp", XR[b, 0:self.CTX, :], self.inp["ctx"][b])
                kb.dma("sp", XR[b, self.CTX:self.TS, :], self.inp["x"][b])

    def st_adaln(self):
        kb, nc = self.kb, self.nc
        with kb.stage():
            ct = kb.tile([128, 8, 3], F32)
            sc = kb.tile([128, 8, 3], F32)
            cb = kb.tile([128, 24, 128], F32)
            kb.dma("sp", ct[:, :, :], self.inp["condT"], writes=[ct])
            kb.op("act", lambda e: e.activation(out=sc[:, :, :], in_=ct[:, :, :], func=AF.Silu), reads=[ct], writes=[sc])
            for kc in range(8):
                for r in range(3):
                    kb.op("dve", lambda e: e.tensor_copy(out=cb[:, kc * 3 + r, :], in_=sc[:, kc, r:r + 1].to_broadcast([128, 128])), reads=[sc], writes=[cb])
            wts = [kb.tile([128, 8, 512], F32) for _ in range(2)]
            bts = [kb.tile([128, 512], F32) for _ in range(2)]
            pss = [kb.psum([128, 512], F32) for _ in range(3)]
            ots = [kb.tile([128, 512], F32) for _ in range(3)]
            it = 0
            for l in range(self.L):
                for nch in range(12):
                    wt, bt = wts[it % 2], bts[it % 2]
                    it += 1
                    kb.dma("sp", wt[:, :, :], self.inp["w_mod"][l].rearrange("(kc p) n -> p kc n", p=128)[:, :, nch * 512:(nch + 1) * 512], writes=[wt])
                    kb.dma("sp", bt[:, :], self.inp["b_mod"][l:l + 1, nch * 512:(nch + 1) * 512].to_broadcast([128, 512]), writes=[bt])
                    for r in range(3):
                        ps, ot = pss[r], ots[r]
                        for kc in range(8):
                            kb.op("pe", lambda e: e.matmul(ps[:, :], lhsT=cb[:, kc * 3 + r, :], rhs=wt[:, kc, :], start=(kc == 0), stop=(kc == 7)), reads=[cb, wt], writes=[ps])
                        kb.op("dve", lambda e: e.tensor_tensor(out=ot[:, :], in0=ps[:, :], in1=bt[:, :], op=ALU.add), reads=[ps, bt], writes=[ot])
                        kb.dma("pool", self.scr["MOD"][l, r:r + 1, nch * 512:(nch + 1) * 512], ot[0:1, :], reads=[ot])

    def load_bcast(self, t, src_row):
        n = src_row.shape[-1]
        self.kb.dma("sp", t[:, 0:n], src_row.to_broadcast([128, n]), writes=[t])

    def transpose_store(self, hb, dst, col0, ident, psT, hT):
        kb = self.kb
        for kc in range(8):
            kb.op("pe", lambda e: e.transpose(psT[:, kc, :], hb[:, kc * 128:(kc + 1) * 128], ident[:, :]), reads=[hb, ident], writes=[psT])
        kb.op("act", lambda e: e.activation(out=hT[:, :, :], in_=psT[:, :, :], func=AF.Copy), reads=[psT], writes=[hT])
        kb.dma("act", dst.rearrange("(kc p) t -> p kc t", p=128)[:, :, col0:col0 + 128], hT[:, :, :], reads=[hT])

    def st_norm(self, l, which, final=False):
        kb = self.kb
        with kb.stage():
            nw = kb.tile([128, D], F32)
            G = [kb.tile([128, D], F32) for _ in range(3)]
            SH = [kb.tile([128, D], F32) for _ in range(3)]
            ident = kb.tile([128, 128], BF16)
            kb.dma("sp", ident[:, :], self.inp["ident_bf"], writes=[ident])
            if final:
                self.load_bcast(nw, self.inp["final_norm_w"].rearrange("(o d) -> o d", o=1))
            else:
                nwsrc = self.inp["norm1_w" if which == 1 else "norm2_w"]
                self.load_bcast(nw, nwsrc[l:l + 1, :])
                shi, sci = (0, 1) if which == 1 else (3, 4)
                for r in range(3):
                    self.load_bcast(G[r], self.scr["MOD"][l, r:r + 1, sci * D:(sci + 1) * D])
                    self.load_bcast(SH[r], self.scr["MOD"][l, r:r + 1, shi * D:(shi + 1) * D])
                    kb.op("dve", lambda e: e.scalar_tensor_tensor(out=G[r][:, :], in0=G[r][:, :], scalar=1.0, in1=nw[:, :], op0=ALU.add, op1=ALU.mult), reads=[G[r], nw], writes=[G[r]])
            NBUF = 3
            xt = [kb.tile([128, D], F32) for _ in range(NBUF)]
            junk = [kb.tile([128, D], F32) for _ in range(NBUF)]
            ss = [kb.tile([128, 1], F32) for _ in range(NBUF)]
            rs = [kb.tile([128, 1], F32) for _ in range(NBUF)]
            hb = [kb.tile([128, D], BF16) for _ in range(NBUF)]
            hT = [kb.tile([128, 8, 128], BF16) for _ in range(NBUF)]
            psT = [kb.psum([128, 8, 128], BF16) for _ in range(2)]
            it = 0
            for b in range(self.NB):
                for j in range(self.TS // 128):
                    pos0 = j * 128
                    if final and pos0 < self.CTX:
                        continue
                    i = it % NBUF
                    it += 1
                    r = 2 if pos0 < self.CTX else b
                    x, jk, s_, r_, h_ = xt[i], junk[i], ss[i], rs[i], hb[i]
                    for (p0, n, ap) in self.xr_rows(l if not final else 0, b, pos0):
                        kb.dma("sp", x[p0:p0 + n, :], ap, writes=[x])
                    kb.op("act", lambda e: e.activation(out=jk[:, :], in_=x[:, :], func=AF.Square, accum_out=s_[:, :]), reads=[x], writes=[jk, s_])
                    kb.op("act", lambda e: e.activation(out=r_[:, :], in_=s_[:, :], func=AF.Sqrt, scale=1.0 / D, bias=self.eps_t[:, 0:1]), reads=[s_], writes=[r_])
                    kb.op("dve", lambda e: e.reciprocal(out=r_[:, :], in_=r_[:, :]), reads=[r_], writes=[r_])
                    if final:
                        kb.op("dve", lambda e: e.scalar_tensor_tensor(out=jk[:, :], in0=x[:, :], scalar=r_[:, 0:1], in1=nw[:, :], op0=ALU.mult, op1=ALU.mult), reads=[x, r_, nw], writes=[jk])
                        kb.dma("act", self.out_ap[b, pos0 - self.CTX:pos0 - self.CTX + 128, :], jk[:, :], reads=[jk])
                        continue
                    kb.op("dve", lambda e: e.scalar_tensor_tensor(out=jk[:, :], in0=x[:, :], scalar=r_[:, 0:1], in1=G[r][:, :], op0=ALU.mult, op1=ALU.mult), reads=[x, r_, G[r]], writes=[jk])
                    kb.op("dve", lambda e: e.tensor_tensor(out=h_[:, :], in0=jk[:, :], in1=SH[r][:, :], op=ALU.add), reads=[jk, SH[r]], writes=[h_])
                    self.transpose_store(h_, self.scr["HT"], b * self.TS + pos0, ident, psT[it % 2], hT[i])

    def gemm(self, src, K, W, jobs, ng_max=2048):
        kb = self.kb
        T = self.T
        kp = min(K, 128)
        KC = (K + 127) // 128
        assert K == kp * KC
        if KC > 8:
            ng_max = 512
        with kb.stage():
            wb = [kb.tile([kp, KC, ng_max], BF16) for _ in range(2)]
            hbs = [kb.tile([kp, KC, 512], BF16) for _ in range(2)]
            self.g_ps = [kb.psum([128, 512], F32) for _ in range(4)]
            self.g_stF = [kb.tile([128, 512], F32) for _ in range(4)]
            self.g_stB = [kb.tile([128, 512], BF16) for _ in range(4)]
            self.g_tmp = [kb.tile([128, 512], F32) for _ in range(4)]
            self.g_i = 0
            src3 = src.rearrange("(kc p) t -> p kc t", p=kp)
            gi = 0
            hi = 0
            for (c0, ncols, mode, epi, prep) in jobs:
                if prep is not None:
                    prep()
                for g0 in range(c0, c0 + ncols, ng_max):
                    ng = min(ng_max, c0 + ncols - g0)
                    w = wb[gi % 2]
                    gi += 1
                    for kc in range(KC):
                        kb.dma("pool", w[:, kc, 0:ng], W[kc * kp:(kc + 1) * kp, g0:g0 + ng], writes=[w])
                    for t0 in range(0, T, 512):
                        tsz = min(512, T - t0)
                        h = hbs[hi % 2]
                        hi += 1
                        kb.dma("sp", h[:, :, 0:tsz], src3[:, :, t0:t0 + tsz], writes=[h])
                        if mode == "FM":
                            for n0 in range(0, ng, 128):
                                nsz = min(128, ng - n0)
                                ps = self.g_ps[self.g_i % 4]
                                for kc in range(KC):
                                    kb.op("pe", lambda e: e.matmul(ps[0:nsz, 0:tsz], lhsT=w[:, kc, n0:n0 + nsz], rhs=h[:, kc, 0:tsz], start=(kc == 0), stop=(kc == KC - 1)), reads=[w, h], writes=[ps])
                                epi(ps, g0 + n0 - c0, nsz, t0, tsz)
                                self.g_i += 1
                        else:
                            for ts in range(0, tsz, 128):
                                for n0 in range(0, ng, 512):
                                    nsz = min(512, ng - n0)
                                    ps = self.g_ps[self.g_i % 4]
                                    for kc in range(KC):
                                        kb.op("pe", lambda e: e.matmul(ps[:, 0:nsz], lhsT=h[:, kc, ts:ts + 128], rhs=w[:, kc, n0:n0 + nsz], start=(kc == 0), stop=(kc == KC - 1)), reads=[w, h], writes=[ps])
                                    epi(ps, g0 + n0 - c0, nsz, t0 + ts, 128)
                                    self.g_i += 1

    def epi_fm(self, dst, dt, func=AF.Copy, scale=1.0, bias_t=None):
        kb = self.kb

        def epi(ps, c, nsz, t0, tsz):
            st = (self.g_stF if dt == F32 else self.g_stB)[self.g_i % 4]
            if func == AF.Copy and bias_t is None and self.g_i % 2 == 0:
                kb.op("dve", lambda e: e.tensor_scalar(out=st[0:nsz, 0:tsz], in0=ps[0:nsz, 0:tsz], scalar1=float(scale), scalar2=None, op0=ALU.mult), reads=[ps], writes=[st])
            elif bias_t is None:
                kb.op("act", lambda e: e.activation(out=st[0:nsz, 0:tsz], in_=ps[0:nsz, 0:tsz], func=func, scale=float(scale)), reads=[ps], writes=[st])
            else:
                kb.op("act", lambda e: e.activation(out=st[0:nsz, 0:tsz], in_=ps[0:nsz, 0:tsz], func=func, scale=float(scale), bias=bias_t[0:nsz, c // 128:c // 128 + 1]), reads=[ps, bias_t], writes=[st])
            kb.dma("act", dst[c:c + nsz, t0:t0 + tsz], st[0:nsz, 0:tsz], reads=[st])
        return epi

    def epi_tm(self, dst, dt, func=AF.Copy, scale=1.0):
        kb = self.kb

        def epi(ps, c, nsz, t0, tsz):
            st = (self.g_stF if dt == F32 else self.g_stB)[self.g_i % 4]
            if func == AF.Copy and self.g_i % 2 == 0:
                kb.op("dve", lambda e: e.tensor_scalar(out=st[0:tsz, 0:nsz], in0=ps[0:tsz, 0:nsz], scalar1=float(scale), scalar2=None, op0=ALU.mult), reads=[ps], writes=[st])
            else:
                kb.op("act", lambda e: e.activation(out=st[0:tsz, 0:nsz], in_=ps[0:tsz, 0:nsz], func=func, scale=float(scale)), reads=[ps], writes=[st])
            kb.dma("act", dst[t0:t0 + tsz, c:c + nsz], st[0:tsz, 0:nsz], reads=[st])
        return epi

    def epi_gelu_fm(self, dst):
        kb = self.kb

        def epi(ps, c, nsz, t0, tsz):
            x = self.g_stF[self.g_i % 4]
            u = self.g_tmp[self.g_i % 4]
            st = self.g_stB[self.g_i % 4]
            kb.op("act", lambda e: e.activation(out=x[0:nsz, 0:tsz], in_=ps[0:nsz, 0:tsz], func=AF.Copy), reads=[ps], writes=[x])
            kb.op("dve", lambda e: e.tensor_tensor(out=u[0:nsz, 0:tsz], in0=x[0:nsz, 0:tsz], in1=x[0:nsz, 0:tsz], op=ALU.mult), reads=[x], writes=[u])
            kb.op("dve", lambda e: e.tensor_scalar(out=u[0:nsz, 0:tsz], in0=u[0:nsz, 0:tsz], scalar1=0.044715, scalar2=1.0, op0=ALU.mult, op1=ALU.add), reads=[u], writes=[u])
            kb.op("dve", lambda e: e.tensor_tensor(out=u[0:nsz, 0:tsz], in0=u[0:nsz, 0:tsz], in1=x[0:nsz, 0:tsz], op=ALU.mult), reads=[u, x], writes=[u])
            kb.op("act", lambda e: e.activation(out=u[0:nsz, 0:tsz], in_=u[0:nsz, 0:tsz], func=AF.Sigmoid, scale=1.5957691216057308), reads=[u], writes=[u])
            kb.op("dve", lambda e: e.tensor_tensor(out=st[0:nsz, 0:tsz], in0=u[0:nsz, 0:tsz], in1=x[0:nsz, 0:tsz], op=ALU.mult), reads=[u, x], writes=[st])
            kb.dma("act", dst[c:c + nsz, t0:t0 + tsz], st[0:nsz, 0:tsz], reads=[st])
        return epi

    def st_win(self, l):
        S = self.scr
        W = self.inp["w_in"][l]
        none = None
        jobs = [
            (0, 1024, "FM", self.epi_fm(S["QT"], BF16), none),
            (1024, 1024, "FM", self.epi_fm(S["KT"], BF16, scale=1.0 / 16), none),
            (1024, 1024, "TM", self.epi_tm(S["KTM"], BF16, scale=1.0 / 16), none),
            (2048, 1024, "TM", self.epi_tm(S["VTM"], BF16), none),
            (3072, 1024, "TM", self.epi_tm(S["OG"], BF16, func=AF.Sigmoid), none),
            (4096, 16, "TM", self.epi_tm(S["IFG"], F32), none),
            (4112, 1024, "FM", self.epi_fm(S["LX"], F32), none),
            (5136, 1024, "FM", self.epi_gelu_fm(S["LG"]), none),
            (6160, RWSEG, "FM", self.epi_fm(S["RW"], F32), none),
            (9616, 3072, "FM", self.epi_fm(S["MG"], BF16, func=AF.Sigmoid), none),
        ]
        self.gemm(S["HT"], D, W, jobs)

    def epi_resid(self, l, gate_idx):
        kb = self.kb
        self.r_g = None

        def prep():
            self.r_g = [kb.tile([128, D], F32) for _ in range(3)]
            for r in range(3):
                self.load_bcast(self.r_g[r], self.scr["MOD"][l, r:r + 1, gate_idx * D:(gate_idx + 1) * D])
            self.r_x = [kb.tile([128, 512], F32) for _ in range(4)]

        def epi(ps, c, nsz, t0, tsz):
            b = t0 // self.TS
            pos0 = t0 - b * self.TS
            r = 2 if pos0 < self.CTX else b
            x = self.r_x[self.g_i % 4]
            st = self.g_stF[self.g_i % 4]
            rows = self.xr_rows(l, b, pos0)
            for (p0, n, ap) in rows:
                kb.dma("sp", x[p0:p0 + n, 0:nsz], ap[:, c:c + nsz], writes=[x])
            kb.op("dve", lambda e: e.tensor_tensor(out=st[:, 0:nsz], in0=ps[:, 0:nsz], in1=self.r_g[r][:, c:c + nsz], op=ALU.mult), reads=[ps, self.r_g[r]], writes=[st])
            kb.op("dve", lambda e: e.tensor_tensor(out=st[:, 0:nsz], in0=st[:, 0:nsz], in1=x[:, 0:nsz], op=ALU.add), reads=[st, x], writes=[st])
            for (p0, n, ap) in rows:
                kb.dma("act", ap[:, c:c + nsz], st[p0:p0 + n, 0:nsz], reads=[st])
        return epi, prep

    def st_wout(self, l):
        epi, prep = self.epi_resid(l, 2)
        self.gemm(self.scr["YM"], D, self.inp["w_out"][l], [(0, D, "TM", epi, prep)])

    def st_ffn(self, l):
        kb = self.kb
        S = self.scr
        self.st_norm(l, 2)
        W = self.inp["w_ffn_in"][l]
        self.gemm(S["HT"], D, W, [(0, FFN, "FM", self.epi_fm(S["AG"], BF16, func=AF.Silu), None)])

        def prep():
            self.f_g = [kb.tile([128, 512], BF16) for _ in range(4)]

        def epi_up(ps, c, nsz, t0, tsz):
            g = self.f_g[self.g_i % 4]
            st = self.g_stB[self.g_i % 4]
            kb.dma("sp", g[0:nsz, 0:tsz], S["AG"][c:c + nsz, t0:t0 + tsz], writes=[g])
            kb.op("dve", lambda e: e.tensor_tensor(out=st[0:nsz, 0:tsz], in0=ps[0:nsz, 0:tsz], in1=g[0:nsz, 0:tsz], op=ALU.mult), reads=[ps, g], writes=[st])
            kb.dma("act", S["AT"][c:c + nsz, t0:t0 + tsz], st[0:nsz, 0:tsz], reads=[st])
        self.gemm(S["HT"], D, W[:, FFN:2 * FFN], [(0, FFN, "FM", epi_up, prep)])
        epi, prep2 = self.epi_resid(l, 5)
        self.gemm(S["AT"], FFN, self.inp["w_ffn_out"][l], [(0, D, "TM", epi, prep2)])

    def st_merge(self, l):
        kb = self.kb
        S = self.scr
        srcs = [("YML", "out_ml"), ("YLRU", "out_lru"), ("YRW", "out_rw")]
        for bi, (ys, wn) in enumerate(srcs):
            def prep():
                self.m_g = [kb.tile([128, 512], BF16) for _ in range(4)]
                self.m_a = [kb.tile([128, 512], F32) for _ in range(4)]

            def epi(ps, c, nsz, t0, tsz, bi=bi):
                g = self.m_g[self.g_i % 4]
                a = self.m_a[self.g_i % 4]
                kb.dma("sp", g[0:nsz, 0:tsz], S["MG"][bi * D + c:bi * D + c + nsz, t0:t0 + tsz], writes=[g])
                if bi == 0:
                    st = self.g_stF[self.g_i % 4]
                    kb.op("dve", lambda e: e.tensor_tensor(out=st[0:nsz, 0:tsz], in0=ps[0:nsz, 0:tsz], in1=g[0:nsz, 0:tsz], op=ALU.mult), reads=[ps, g], writes=[st])
                    kb.dma("act", S["YACC"][c:c + nsz, t0:t0 + tsz], st[0:nsz, 0:tsz], reads=[st])
                else:
                    kb.dma("sp", a[0:nsz, 0:tsz], S["YACC"][c:c + nsz, t0:t0 + tsz], writes=[a])
                    st = self.g_stF[self.g_i % 4] if bi == 1 else self.g_stB[self.g_i % 4]
                    tmp = self.g_tmp[self.g_i % 4]
                    kb.op("dve", lambda e: e.tensor_tensor(out=tmp[0:nsz, 0:tsz], in0=ps[0:nsz, 0:tsz], in1=g[0:nsz, 0:tsz], op=ALU.mult), reads=[ps, g], writes=[tmp])
                    kb.op("dve", lambda e: e.tensor_tensor(out=st[0:nsz, 0:tsz], in0=tmp[0:nsz, 0:tsz], in1=a[0:nsz, 0:tsz], op=ALU.add), reads=[tmp, a], writes=[st])
                    dst = S["YACC"] if bi == 1 else S["YM"]
                    kb.dma("act", dst[c:c + nsz, t0:t0 + tsz], st[0:nsz, 0:tsz], reads=[st])
            self.gemm(S[ys], D, self.inp[wn][l], [(0, D, "FM", epi, prep)])

    def st_mlstm(self, l):
        kb = self.kb
        S = self.scr
        TS, NB = self.TS, self.NB
        nck = TS // 128
        ctxc = self.CTX // 128
        with kb.stage():
            identF = kb.tile([128, 128], F32)
            ones = kb.tile([128, 128], F32)
            tri = [kb.tile([128, 128], F32) for _ in range(2)]
            negm = [kb.tile([128, 128], F32) for _ in range(2)]
            GB = kb.tile([128, 16], F32)
            kb.dma("sp", identF[:, :], self.inp["ident_f"], writes=[identF])
            kb.dma("sp", ones[:, :], self.inp["ones_f"], writes=[ones])
            for d in range(2):
                kb.dma("sp", tri[d][:, :], self.inp["tri"][d], writes=[tri[d]])
                kb.dma("sp", negm[d][:, :], self.inp["negm"][d], writes=[negm[d]])
            kb.dma("sp", GB[:, 0:8], self.inp["ml_ig_b"][l].rearrange("(o a) b -> o (a b)", o=1).to_broadcast([128, 8]), writes=[GB])
            kb.dma("sp", GB[:, 8:16], self.inp["ml_fg_b"][l].rearrange("(o a) b -> o (a b)", o=1).to_broadcast([128, 8]), writes=[GB])
            NBUF = 2
            qT = [kb.tile([128, 8, 128], BF16) for _ in range(NBUF)]
            kT = [kb.tile([128, 8, 128], BF16) for _ in range(NBUF)]
            kTM = [kb.tile([128, D], BF16) for _ in range(NBUF)]
            VA = [kb.tile([128, 4, 257], BF16) for _ in range(NBUF)]
            IFt = [kb.tile([128, 16], F32) for _ in range(NBUF)]
            for i in range(NBUF):
                kb.op("dve", lambda e: e.memset(VA[i][:, :, :], 1.0), writes=[VA[i]])
            gx = kb.tile([128, 16], F32)
            lf = kb.tile([128, 4], F32)
            e1 = kb.tile([128, 4], F32)
            lfB = kb.tile([128, 4, 128], F32)
            fc = kb.tile([128, 8], F32)
            cs = kb.tile([128, 4], F32)
            ef = kb.tile([128, 4], F32)
            ev = kb.tile([128, 4], F32)
            eT = kb.tile([128, 4], F32)
            tmp4 = kb.tile([128, 4], F32)
            C32 = [kb.tile([128, 2, 257], F32) for _ in range(4)]
            Cbf = [kb.tile([128, 2, 257], BF16) for _ in range(4)]
            DT = [kb.tile([128, 128], F32) for _ in range(2)]
            AT = [kb.tile([128, 128], BF16) for _ in range(2)]
            tI = [kb.tile([128, 257], F32) for _ in range(2)]
            ND = [kb.tile([128, 257], F32) for _ in range(2)]
            den = [kb.tile([128, 1], F32) for _ in range(2)]
            VS = [kb.tile([128, 257], BF16) for _ in range(2)]
            HO = [kb.tile([128, D], F32) for _ in range(2)]
            ps_g = kb.psum([128, 8], F32)
            psA = kb.psum([128, 128], F32)
            psF = kb.psum([128, 128], F32)
            psI = kb.psum([128, 257], F32)
            psC = kb.psum([128, 257], F32)
            psD = [kb.psum([128, 257], F32) for _ in range(2)]
            it = 0
            hh = 0
            for d in range(2):
                for b in range(NB):
                    for h in range(4):
                        kb.op("dve", lambda e: e.memset(C32[h][:, :, :], 0.0), writes=[C32[h]])
                        kb.op("dve", lambda e: e.memset(Cbf[h][:, :, :], 0.0), writes=[Cbf[h]])
                    cl = list(range(ctxc)) + list(range(ctxc, nck)) if d == 0 else list(range(ctxc - 1, -1, -1)) + list(range(nck - 1, ctxc - 1, -1))
                    for c in cl:
                        i = it % NBUF
                        it += 1
                        col0 = b * TS + c * 128
                        q_, k_, km_, va_, if_ = qT[i], kT[i], kTM[i], VA[i], IFt[i]
                        kb.dma("sp", q_[:, :, :], S["QT"].rearrange("(kc p) t -> p kc t", p=128)[:, :, col0:col0 + 128], writes=[q_])
                        kb.dma("sp", k_[:, :, :], S["KT"].rearrange("(kc p) t -> p kc t", p=128)[:, :, col0:col0 + 128], writes=[k_])
                        kb.dma("sp", km_[:, :], S["KTM"][col0:col0 + 128, :], writes=[km_])
                        kb.dma("sp", va_[:, :, 0:256], S["VTM"][col0:col0 + 128, :].rearrange("t (h e) -> t h e", h=4), writes=[va_])
                        kb.dma("sp", if_[:, :], S["IFG"][col0:col0 + 128, :], writes=[if_])
                        kb.op("dve", lambda e: e.tensor_tensor(out=gx[:, :], in0=if_[:, :], in1=GB[:, :], op=ALU.add), reads=[if_, GB], writes=[gx])
                        i4 = gx[:, d * 4:d * 4 + 4]
                        f4 = gx[:, 8 + d * 4:12 + d * 4]
                        kb.op("act", lambda e: e.activation(out=e1[:, :], in_=f4, func=AF.Exp, scale=-1.0), reads=[gx], writes=[e1])
                        kb.op("act", lambda e: e.activation(out=e1[:, :], in_=e1[:, :], func=AF.Ln, bias=self.one_t[:, 0:1]), reads=[e1], writes=[e1])
                        kb.op("dve", lambda e: e.tensor_scalar(out=lf[:, :], in0=e1[:, :], scalar1=-1.0, scalar2=None, op0=ALU.mult), reads=[e1], writes=[lf])
                        kb.op("dve", lambda e: e.tensor_copy(out=lfB[:, :, :], in_=lf[:, 0:4].unsqueeze(2).to_broadcast([128, 4, 128])), reads=[lf], writes=[lfB])
                        kb.op("pe", lambda e: e.matmul(ps_g[:, 0:4], lhsT=tri[d][:, :], rhs=lf[:, :], start=True, stop=True), reads=[tri[d], lf], writes=[ps_g])
                        kb.op("pe", lambda e: e.matmul(ps_g[:, 4:8], lhsT=ones[:, :], rhs=lf[:, :], start=True, stop=True), reads=[ones, lf], writes=[ps_g])
                        kb.op("dve", lambda e: e.tensor_copy(out=fc[:, :], in_=ps_g[:, :]), reads=[ps_g], writes=[fc])
                        kb.op("dve", lambda e: e.tensor_tensor(out=cs[:, :], in0=i4, in1=fc[:, 0:4], op=ALU.subtract), reads=[gx, fc], writes=[cs])
                        kb.op("act", lambda e: e.activation(out=ef[:, :], in_=fc[:, 0:4], func=AF.Exp), reads=[fc], writes=[ef])
                        kb.op("dve", lambda e: e.tensor_tensor(out=tmp4[:, :], in0=cs[:, :], in1=fc[:, 4:8], op=ALU.add), reads=[cs, fc], writes=[tmp4])
                        kb.op("act", lambda e: e.activation(out=ev[:, :], in_=tmp4[:, :], func=AF.Exp), reads=[tmp4], writes=[ev])
                        kb.op("act", lambda e: e.activation(out=eT[:, :], in_=fc[:, 4:8], func=AF.Exp), reads=[fc], writes=[eT])
                        ho = HO[it % 2]
                        for h in range(4):
                            j2 = hh % 2
                            hh += 1
                            dt_, at_, ti_, nd_, dn_, vs_ = DT[j2], AT[j2], tI[j2], ND[j2], den[j2], VS[j2]
                            for j in range(2):
                                kb.op("pe", lambda e: e.matmul(psA[:, :], lhsT=k_[:, 2 * h + j, :], rhs=q_[:, 2 * h + j, :], start=(j == 0), stop=(j == 1)), reads=[k_, q_], writes=[psA])
                            kb.op("pe", lambda e: e.matmul(psF[:, :], lhsT=lfB[:, h, :], rhs=tri[d][:, :], start=True, stop=False), reads=[lfB, tri[d]], writes=[psF])
                            kb.op("pe", lambda e: e.matmul(psF[:, :], lhsT=identF[:, :], rhs=negm[d][:, :], start=False, stop=True), reads=[identF, negm[d]], writes=[psF])
                            kb.op("act", lambda e: e.activation(out=dt_[:, :], in_=psF[:, :], func=AF.Exp, bias=cs[:, h:h + 1]), reads=[psF, cs], writes=[dt_])
                            kb.op("dve", lambda e: e.tensor_tensor(out=at_[:, :], in0=psA[:, :], in1=dt_[:, :], op=ALU.mult), reads=[psA, dt_], writes=[at_])
                            kb.op("pe", lambda e: e.matmul(psI[:, :], lhsT=at_[:, :], rhs=va_[:, h, :], start=True, stop=True), reads=[at_, va_], writes=[psI])
                            for j in range(2):
                                kb.op("pe", lambda e: e.matmul(psC[:, :], lhsT=q_[:, 2 * h + j, :], rhs=Cbf[h][:, j, :], start=(j == 0), stop=(j == 1)), reads=[q_, Cbf[h]], writes=[psC])
                            kb.op("act", lambda e: e.activation(out=ti_[:, :], in_=psI[:, :], func=AF.Copy), reads=[psI], writes=[ti_])
                            kb.op("dve", lambda e: e.scalar_tensor_tensor(out=nd_[:, :], in0=psC[:, :], scalar=ef[:, h:h + 1], in1=ti_[:, :], op0=ALU.mult, op1=ALU.add), reads=[psC, ef, ti_], writes=[nd_])
                            kb.op("act", lambda e: e.activation(out=dn_[:, :], in_=nd_[:, 256:257], func=AF.Abs), reads=[nd_], writes=[dn_])
                            kb.op("dve", lambda e: e.tensor_scalar(out=dn_[:, :], in0=dn_[:, :], scalar1=1.0, scalar2=None, op0=ALU.max), reads=[dn_], writes=[dn_])
                            kb.op("dve", lambda e: e.reciprocal(out=dn_[:, :], in_=dn_[:, :]), reads=[dn_], writes=[dn_])
                            kb.op("act", lambda e: e.activation(out=ho[:, h * 256:(h + 1) * 256], in_=nd_[:, 0:256], func=AF.Copy, scale=dn_[:, 0:1]), reads=[nd_, dn_], writes=[ho])
                            kb.op("dve", lambda e: e.tensor_scalar(out=vs_[:, :], in0=va_[:, h, :], scalar1=ev[:, h:h + 1], scalar2=None, op0=ALU.mult), reads=[va_, ev], writes=[vs_])
                            for j in range(2):
                                kb.op("pe", lambda e: e.matmul(psD[j][:, :], lhsT=km_[:, h * 256 + j * 128:h * 256 + (j + 1) * 128], rhs=vs_[:, :], start=True, stop=True), reads=[km_, vs_], writes=[psD[j]])
                                kb.op("dve", lambda e: e.scalar_tensor_tensor(out=C32[h][:, j, :], in0=C32[h][:, j, :], scalar=eT[:, h:h + 1], in1=psD[j][:, :], op0=ALU.mult, op1=ALU.add), reads=[C32[h], eT, psD[j]], writes=[C32[h]])
                            kb.op("act", lambda e: e.activation(out=Cbf[h][:, :, :], in_=C32[h][:, :, :], func=AF.Copy), reads=[C32[h]], writes=[Cbf[h]])
                        kb.dma("pool", S["HML%d" % d][col0:col0 + 128, :], ho[:, :], reads=[ho])

    def st_mlstm_post(self, l):
        kb = self.kb
        S = self.scr
        with kb.stage():
            ident = kb.tile([128, 128], BF16)
            kb.dma("sp", ident[:, :], self.inp["ident_bf"], writes=[ident])
            nw = kb.tile([128, D], F32)
            self.load_bcast(nw, self.inp["ml_norm_w"][l:l + 1, :])
            NBUF = 2
            hf = [kb.tile([128, D], F32) for _ in range(NBUF)]
            hbk = [kb.tile([128, D], F32) for _ in range(NBUF)]
            og = [kb.tile([128, D], BF16) for _ in range(NBUF)]
            junk = [kb.tile([128, 256], F32) for _ in range(NBUF)]
            ms = [kb.tile([128, 4], F32) for _ in range(NBUF)]
            yb = [kb.tile([128, D], BF16) for _ in range(NBUF)]
            hT = [kb.tile([128, 8, 128], BF16) for _ in range(NBUF)]
            psT = [kb.psum([128, 8, 128], BF16) for _ in range(2)]
            for tix in range(self.T // 128):
                i = tix % NBUF
                col0 = tix * 128
                a, b_, o_, jk, m_, y_ = hf[i], hbk[i], og[i], junk[i], ms[i], yb[i]
                kb.dma("sp", a[:, :], S["HML0"][col0:col0 + 128, :], writes=[a])
                kb.dma("sp", b_[:, :], S["HML1"][col0:col0 + 128, :], writes=[b_])
                kb.dma("sp", o_[:, :], S["OG"][col0:col0 + 128, :], writes=[o_])
                kb.op("dve", lambda e: e.tensor_tensor(out=a[:, :], in0=a[:, :], in1=b_[:, :], op=ALU.add), reads=[a, b_], writes=[a])
                for h in range(4):
                    kb.op("act", lambda e: e.activation(out=jk[:, :], in_=a[:, h * 256:(h + 1) * 256], func=AF.Square, accum_out=m_[:, h:h + 1]), reads=[a], writes=[jk, m_])
                kb.op("act", lambda e: e.activation(out=m_[:, :], in_=m_[:, :], func=AF.Sqrt, scale=1.0 / 256, bias=self.eps_t[:, 0:1]), reads=[m_], writes=[m_])
                kb.op("dve", lambda e: e.reciprocal(out=m_[:, :], in_=m_[:, :]), reads=[m_], writes=[m_])
                kb.op("dve", lambda e: e.tensor_tensor(out=a[:, :].rearrange("p (h e) -> p h e", h=4), in0=a[:, :].rearrange("p (h e) -> p h e", h=4), in1=m_[:, 0:4].unsqueeze(2).to_broadcast([128, 4, 256]), op=ALU.mult), reads=[a, m_], writes=[a])
                kb.op("dve", lambda e: e.tensor_tensor(out=a[:, :], in0=a[:, :], in1=nw[:, :], op=ALU.mult), reads=[a, nw], writes=[a])
                kb.op("dve", lambda e: e.tensor_tensor(out=y_[:, :], in0=a[:, :], in1=o_[:, :], op=ALU.mult), reads=[a, o_], writes=[y_])
                self.transpose_store(y_, S["YML"], col0, ident, psT[tix % 2], hT[i])

    def st_lru(self, l):
        kb = self.kb
        S = self.scr
        TS, NB, CTX = self.TS, self.NB, self.CTX
        segs = [(0, CTX), (CTX, TS)]
        with kb.stage():
            cw = kb.tile([128, 8, 4], F32)
            cbias = kb.tile([128, 8], F32)
            grb = kb.tile([128, 2, 8], F32)
            gib = kb.tile([128, 2, 8], F32)
            lam = kb.tile([128, 2, 8], F32)
            cc_ = kb.tile([128, 2, 8], F32)
            kb.dma("sp", cw[:, :, :], self.inp["lru_conv_wT"][l], writes=[cw])
            kb.dma("sp", cbias[:, :], self.inp["lru_conv_bT"][l], writes=[cbias])
            kb.dma("sp", grb[:, :, :], self.inp["lru_gr_bT"][l], writes=[grb])
            kb.dma("sp", gib[:, :, :], self.inp["lru_gi_bT"][l], writes=[gib])
            kb.dma("sp", lam[:, :, :], self.inp["lru_lambdaT"][l], writes=[lam])
            kb.op("act", lambda e: e.activation(out=cc_[:, :, :], in_=lam[:, :, :], func=AF.Exp, scale=-1.0), reads=[lam], writes=[cc_])
            kb.op("act", lambda e: e.activation(out=cc_[:, :, :], in_=cc_[:, :, :], func=AF.Ln, bias=self.one_t[:, 0:1]), reads=[cc_], writes=[cc_])
            kb.op("dve", lambda e: e.tensor_scalar(out=cc_[:, :, :], in0=cc_[:, :, :], scalar1=-8.0, scalar2=None, op0=ALU.mult), reads=[cc_], writes=[cc_])
            wbd = [[[kb.tile([128, 128], F32) for _ in range(2)] for _ in range(2)] for _ in range(2)]
            x = [kb.tile([128, TS], F32) for _ in range(2)]
            u = [kb.tile([128, TS], F32) for _ in range(2)]
            lg = [kb.tile([128, TS], BF16) for _ in range(2)]
            aa = kb.tile([128, TS], F32)
            bx = kb.tile([128, TS], F32)
            hf = kb.tile([128, TS], F32)
            hb = kb.tile([128, TS], F32)
            yo = [kb.tile([128, TS], BF16) for _ in range(2)]
            rr = [kb.tile([128, 512], F32) for _ in range(2)]
            ii = [kb.tile([128, 512], F32) for _ in range(2)]
            a2 = [kb.tile([128, 512], F32) for _ in range(2)]
            psr = [kb.psum([128, 512], F32) for _ in range(2)]
            psi = [kb.psum([128, 512], F32) for _ in range(2)]
            it = 0
            for cc in range(8):
                wv = wbd[cc % 2]
                for d in range(2):
                    for g, nm in enumerate(("lru_gr_w", "lru_gi_w")):
                        w = wv[d][g]
                        kb.op("dve", lambda e: e.memset(w[:, :], 0.0), writes=[w])
                        for blk in range(2):
                            kb.dma("sp", w[blk * 64:(blk + 1) * 64, blk * 64:(blk + 1) * 64], self.inp[nm][l, d, 2 * cc + blk], writes=[w])
                for b in range(NB):
                    i = it % 2
                    it += 1
                    x_, u_, lg_, yo_ = x[i], u[i], lg[i], yo[i]
                    kb.dma("sp", x_[:, :], S["LX"][cc * 128:(cc + 1) * 128, b * TS:(b + 1) * TS], writes=[x_])
                    kb.dma("sp", lg_[:, :], S["LG"][cc * 128:(cc + 1) * 128, b * TS:(b + 1) * TS], writes=[lg_])
                    for (s0, s1) in segs:
                        kb.op("dve", lambda e: e.tensor_scalar(out=u_[:, s0:s1], in0=x_[:, s0:s1], scalar1=cw[:, cc, 2:3], scalar2=cbias[:, cc:cc + 1], op0=ALU.mult, op1=ALU.add), reads=[x_, cw, cbias], writes=[u_])
                        kb.op("dve", lambda e: e.scalar_tensor_tensor(out=u_[:, s0 + 2:s1], in0=x_[:, s0:s1 - 2], scalar=cw[:, cc, 0:1], in1=u_[:, s0 + 2:s1], op0=ALU.mult, op1=ALU.add), reads=[x_, cw, u_], writes=[u_])
                        kb.op("dve", lambda e: e.scalar_tensor_tensor(out=u_[:, s0 + 1:s1], in0=x_[:, s0:s1 - 1], scalar=cw[:, cc, 1:2], in1=u_[:, s0 + 1:s1], op0=ALU.mult, op1=ALU.add), reads=[x_, cw, u_], writes=[u_])
                        kb.op("dve", lambda e: e.scalar_tensor_tensor(out=u_[:, s0:s1 - 1], in0=x_[:, s0 + 1:s1], scalar=cw[:, cc, 3:4], in1=u_[:, s0:s1 - 1], op0=ALU.mult, op1=ALU.add), reads=[x_, cw, u_], writes=[u_])
                    for d in range(2):
                        for t0 in range(0, TS, 512):
                            tsz = min(512, TS - t0)
                            j = (t0 // 512) % 2
                            r_, i_, a2_ = rr[j], ii[j], a2[j]
                            kb.op("pe", lambda e: e.matmul(psr[j][:, 0:tsz], lhsT=wv[d][0][:, :], rhs=u_[:, t0:t0 + tsz], start=True, stop=True), reads=[wv[d][0], u_], writes=[psr[j]])
                            kb.op("pe", lambda e: e.matmul(psi[j][:, 0:tsz], lhsT=wv[d][1][:, :], rhs=u_[:, t0:t0 + tsz], start=True, stop=True), reads=[wv[d][1], u_], writes=[psi[j]])
                            kb.op("act", lambda e: e.activation(out=r_[:, 0:tsz], in_=psr[j][:, 0:tsz], func=AF.Sigmoid, bias=grb[:, d, cc:cc + 1]), reads=[psr[j], grb], writes=[r_])
                            kb.op("act", lambda e: e.activation(out=i_[:, 0:tsz], in_=psi[j][:, 0:tsz], func=AF.Sigmoid, bias=gib[:, d, cc:cc + 1]), reads=[psi[j], gib], writes=[i_])
                            kb.op("act", lambda e: e.activation(out=aa[:, t0:t0 + tsz], in_=r_[:, 0:tsz], func=AF.Exp, scale=cc_[:, d, cc:cc + 1]), reads=[r_, cc_], writes=[aa])
                            kb.op("dve", lambda e: e.tensor_tensor(out=a2_[:, 0:tsz], in0=aa[:, t0:t0 + tsz], in1=aa[:, t0:t0 + tsz], op=ALU.mult), reads=[aa], writes=[a2_])
                            kb.op("act", lambda e: e.activation(out=a2_[:, 0:tsz], in_=a2_[:, 0:tsz], func=AF.Sqrt, scale=-1.0, bias=self.one_t[:, 0:1]), reads=[a2_], writes=[a2_])
                            kb.op("dve", lambda e: e.tensor_tensor(out=i_[:, 0:tsz], in0=i_[:, 0:tsz], in1=u_[:, t0:t0 + tsz], op=ALU.mult), reads=[i_, u_], writes=[i_])
                            kb.op("dve", lambda e: e.tensor_tensor(out=bx[:, t0:t0 + tsz], in0=i_[:, 0:tsz], in1=a2_[:, 0:tsz], op=ALU.mult), reads=[i_, a2_], writes=[bx])
                        if d == 0:
                            kb.op("dve", lambda e: e.tensor_tensor_scan(out=hf[:, :], data0=aa[:, :], data1=bx[:, :], initial=0.0, op0=ALU.mult, op1=ALU.add), reads=[aa, bx], writes=[hf])
                        else:
                            kb.op("dve", lambda e: e.tensor_tensor_scan(out=hb[:, 0:CTX][:, ::-1], data0=aa[:, 0:CTX][:, ::-1], data1=bx[:, 0:CTX][:, ::-1], initial=0.0, op0=ALU.mult, op1=ALU.add), reads=[aa, bx], writes=[hb])
                            kb.op("dve", lambda e: e.tensor_tensor_scan(out=hb[:, CTX:TS][:, ::-1], data0=aa[:, CTX:TS][:, ::-1], data1=bx[:, CTX:TS][:, ::-1], initial=hb[:, 0:1], op0=ALU.mult, op1=ALU.add), reads=[aa, bx, hb], writes=[hb])
                    kb.op("dve", lambda e: e.tensor_tensor(out=hf[:, :], in0=hf[:, :], in1=hb[:, :], op=ALU.add), reads=[hf, hb], writes=[hf])
                    kb.op("dve", lambda e: e.tensor_tensor(out=yo_[:, :], in0=hf[:, :], in1=lg_[:, :], op=ALU.mult), reads=[hf, lg_], writes=[yo_])
                    kb.dma("pool", S["YLRU"][cc * 128:(cc + 1) * 128, b * TS:(b + 1) * TS], yo_[:, :], reads=[yo_])

    def st_rwkv(self, l):
        self.st_rw_shift(l)
        self.st_rw_lowrank(l)
        self.st_rw_core(l)

    def st_rw_shift(self, l):
        kb = self.kb
        S = self.scr
        TS, NB, CTX = self.TS, self.NB, self.CTX
        segs = [(0, CTX), (CTX, TS)]
        with kb.stage():
            mu = kb.tile([128, 27], F32)
            kb.dma("sp", mu[:, :], self.inp["rw_muT"][l], writes=[mu])
            x = [kb.tile([128, TS], F32) for _ in range(2)]
            tm = [kb.tile([128, TS], F32) for _ in range(2)]
            ob = [kb.tile([128, TS], BF16) for _ in range(2)]
            it = 0
            for ch in range(27):
                for b in range(NB):
                    i = it % 2
                    it += 1
                    x_, t_, o_ = x[i], tm[i], ob[i]
                    kb.dma("sp", x_[:, :], S["RW"][ch * 128:(ch + 1) * 128, b * TS:(b + 1) * TS], writes=[x_])
                    for (s0, s1) in segs:
                        kb.op("dve", lambda e: e.tensor_tensor(out=t_[:, s0 + 1:s1 - 1], in0=x_[:, s0:s1 - 2], in1=x_[:, s0 + 2:s1], op=ALU.add), reads=[x_], writes=[t_])
                        kb.op("dve", lambda e: e.tensor_copy(out=t_[:, s0:s0 + 1], in_=x_[:, s0 + 1:s0 + 2]), reads=[x_], writes=[t_])
                        kb.op("dve", lambda e: e.tensor_copy(out=t_[:, s1 - 1:s1], in_=x_[:, s1 - 2:s1 - 1]), reads=[x_], writes=[t_])
                    kb.op("dve", lambda e: e.scalar_tensor_tensor(out=t_[:, :], in0=t_[:, :], scalar=0.5, in1=x_[:, :], op0=ALU.mult, op1=ALU.subtract), reads=[t_, x_], writes=[t_])
                    kb.op("dve", lambda e: e.scalar_tensor_tensor(out=t_[:, :], in0=t_[:, :], scalar=mu[:, ch:ch + 1], in1=x_[:, :], op0=ALU.mult, op1=ALU.add), reads=[t_, x_, mu], writes=[t_])
                    cols = slice(b * TS, (b + 1) * TS)
                    if ch < 24:
                        kb.dma("pool", S["RS"][ch * 128:(ch + 1) * 128, cols], t_[:, :], reads=[t_])
                    else:
                        fn = (AF.Tanh, AF.Copy, AF.Sigmoid)[ch - 24]
                        dst = (S["TD"], S["AD"], S["GD"])[ch - 24]
                        kb.op("act", lambda e: e.activation(out=o_[:, :], in_=t_[:, :], func=fn), reads=[t_], writes=[o_])
                        kb.dma("pool", dst[:, cols], o_[:, :], reads=[o_])

    def st_rw_lowrank(self, l):
        kb = self.kb
        S = self.scr
        for d in range(2):
            holder = {}

            def prep(d=d):
                holder["d0"] = kb.tile([128, 8], F32)
                holder["i0"] = kb.tile([128, 8], F32)
                kb.dma("sp", holder["d0"][:, :], self.inp["rw_decay0T"][l][:, d, :], writes=[holder["d0"]])
                kb.dma("sp", holder["i0"][:, :], self.inp["rw_iclr0T"][l][:, d, :], writes=[holder["i0"]])

            def epi_b(dst, key):
                def epi(ps, c, nsz, t0, tsz):
                    st = self.g_stF[self.g_i % 4]
                    bt = holder[key]
                    kb.op("act", lambda e: e.activation(out=st[0:nsz, 0:tsz], in_=ps[0:nsz, 0:tsz], func=AF.Sigmoid, bias=bt[0:nsz, c // 128:c // 128 + 1]), reads=[ps, bt], writes=[st])
                    kb.dma("act", dst[c:c + nsz, t0:t0 + tsz], st[0:nsz, 0:tsz], reads=[st])
                return epi
            self.gemm(S["TD"][d * 64:(d + 1) * 64, :], 64, self.inp["rw_decay_up"][l, d], [(0, D, "FM", epi_b(S["SG%d" % d], "d0"), prep)])
            self.gemm(S["AD"][d * 64:(d + 1) * 64, :], 64, self.inp["rw_iclr_up"][l, d], [(0, D, "FM", epi_b(S["AA%d" % d], "i0"), prep)])
        self.gemm(S["GD"], 128, self.inp["rw_gate_up"][l], [(0, D, "FM", self.epi_fm(S["GG"], F32), None)])

    def st_rw_core(self, l):
        kb = self.kb
        S = self.scr
        TS, NB, CTX = self.TS, self.NB, self.CTX
        TB = min(CTX, 256)
        nblk = TS // TB
        cblk = CTX // TB
        ncb = TB // 64
        DS = RW_DECAY_SCALE
        v3 = lambda t: t[:, :].rearrange("p (c e) -> p c e", e=64)
        h2 = lambda ap: ap.rearrange("p (h c) -> p h c", h=2)
        with kb.stage():
            identF = kb.tile([128, 128], F32)
            identB = kb.tile([128, 128], BF16)
            bones = kb.tile([128, 128], F32)
            maskA = [kb.tile([128, 128], F32) for _ in range(2)]
            maskN = [kb.tile([64, 64], F32) for _ in range(2)]
            rmask = [kb.tile([128, TB], F32) for _ in range(2)]
            kb.dma("sp", identF[:, :], self.inp["ident_f"], writes=[identF])
            kb.dma("sp", identB[:, :], self.inp["ident_bf"], writes=[identB])
            kb.dma("sp", bones[:, :], self.inp["bones_f"], writes=[bones])
            for d in range(2):
                kb.dma("sp", maskA[d][:, :], self.inp["maskA"][d], writes=[maskA[d]])
                kb.dma("sp", maskN[d][:, :], self.inp["maskN"][d], writes=[maskN[d]])
                kb.dma("sp", rmask[d][:, :], self.inp["rmask"][d][:, 0:TB], writes=[rmask[d]])
            prm = {}
            for nm in ("rw_k_kT", "rw_k_aT"):
                prm[nm] = kb.tile([128, 8], F32)
                kb.dma("sp", prm[nm][:, :], self.inp[nm][l], writes=[prm[nm]])
            omk = kb.tile([128, 8], F32)
            kb.op("dve", lambda e: e.tensor_scalar(out=omk[:, :], in0=prm["rw_k_aT"][:, :], scalar1=-1.0, scalar2=1.0, op0=ALU.mult, op1=ALU.add), reads=[prm["rw_k_aT"]], writes=[omk])

            NQ = 2 * ncb
            bq = lambda ap, p: ap.unsqueeze(1).to_broadcast([p, NQ, 64])
            W = {}
            mk = lambda dt=F32: [kb.tile([128, TB], dt) for _ in range(3)]
            for nm in ("rT", "kT", "vT", "sg", "aT", "kap", "w1", "w2", "E1", "sq"):
                W[nm] = mk()
            QC = [kb.tile([128, ncb, 128], BF16) for _ in range(3)]
            KC = [kb.tile([128, ncb, 128], BF16) for _ in range(3)]
            YT = [kb.tile([64, 2, TB], F32) for _ in range(3)]
            AT4 = [kb.tile([128, ncb, 2, 128], BF16) for _ in range(3)]
            UV4 = [kb.tile([128, ncb, 2, 64], BF16) for _ in range(3)]
            KTT4 = [kb.tile([128, ncb, 128], BF16) for _ in range(3)]
            TTp = [kb.tile([64, NQ, 128], BF16) for _ in range(3)]
            for i in range(3):
                kb.op("dve", lambda e: e.memset(TTp[i][:, :, :], 0.0), writes=[TTp[i]])
            P = [kb.tile([64, NQ, 64], F32) for _ in range(2)]
            PT = [kb.tile([64, NQ, 64], F32) for _ in range(2)]
            TT = [kb.tile([64, NQ, 64], F32) for _ in range(2)]
            Pb = [kb.tile([64, NQ, 64], BF16) for _ in range(2)]
            PTb = [kb.tile([64, NQ, 64], BF16) for _ in range(2)]
            TTb = kb.tile([64, NQ, 64], BF16)
            ST32 = kb.tile([128, 64], F32)
            STp = [kb.tile([128, 64], BF16) for _ in range(2)]
            nZ = kb.tile([64, 2, 64], BF16)
            stmp = kb.tile([128, 64], F32)
            psM = [kb.psum([64, NQ, 64], F32) for _ in range(2)]
            psN = kb.psum([64, NQ, 64], F32)
            psA = kb.psum([128, 256], F32)
            psV = kb.psum([64, NQ, 64], F32)
            psK = kb.psum([128, ncb, 128], BF16)
            psY = kb.psum([64, 2, 64], F32)
            psU = kb.psum([128, 2, 64], F32)
            psA3 = h2(psA[:, :])

            def gen1a(cc, b, d, blk, i):
                rows = slice(cc * 128, (cc + 1) * 128)
                cols = slice(b * TS + blk * TB, b * TS + (blk + 1) * TB)
                r_, k_, v_, sg_, a_ = W["rT"][i], W["kT"][i], W["vT"][i], W["sg"][i], W["aT"][i]
                kap_, w1_, w2_, E1_, sq_ = W["kap"][i], W["w1"][i], W["w2"][i], W["E1"][i], W["sq"][i]
                QC_, KC_ = QC[i], KC[i]
                kb.dma("sp", r_[:, :], S["RS"][rows, cols], writes=[r_])
                kb.dma("sp", k_[:, :], S["RS"][D + cc * 128:D + (cc + 1) * 128, cols], writes=[k_])
                kb.dma("sp", v_[:, :], S["RS"][2 * D + cc * 128:2 * D + (cc + 1) * 128, cols], writes=[v_])
                kb.dma("sp", sg_[:, :], S["SG%d" % d][rows, cols], writes=[sg_])
                kb.dma("sp", a_[:, :], S["AA%d" % d][rows, cols], writes=[a_])
                yield
                kb.op("dve", lambda e: e.tensor_scalar(out=kap_[:, :], in0=k_[:, :], scalar1=prm["rw_k_kT"][:, cc:cc + 1], scalar2=None, op0=ALU.mult), reads=[k_, prm["rw_k_kT"]], writes=[kap_])
                kb.op("dve", lambda e: e.tensor_tensor(out=sq_[:, :], in0=kap_[:, :], in1=kap_[:, :], op=ALU.mult), reads=[kap_], writes=[sq_])
                kb.op("pe", lambda e: e.matmul(psA[:, 0:TB], lhsT=bones[:, :], rhs=sq_[:, :], start=True, stop=True), reads=[bones, sq_], writes=[psA])
                kb.op("dve", lambda e: e.tensor_scalar(out=w1_[:, :], in0=a_[:, :], scalar1=prm["rw_k_aT"][:, cc:cc + 1], scalar2=omk[:, cc:cc + 1], op0=ALU.mult, op1=ALU.add), reads=[a_, prm["rw_k_aT"], omk], writes=[w1_])
                kb.op("dve", lambda e: e.tensor_tensor(out=w1_[:, :], in0=w1_[:, :], in1=k_[:, :], op=ALU.mult), reads=[w1_, k_], writes=[w1_])
                if d == 0:
                    kb.op("dve", lambda e: e.tensor_tensor_scan(out=w2_[:, :], data0=rmask[0][:, :], data1=sg_[:, :], initial=0.0, op0=ALU.mult, op1=ALU.add), reads=[rmask[0], sg_], writes=[w2_])
                else:
                    kb.op("dve", lambda e: e.tensor_tensor_scan(out=w2_[:, ::-1], data0=rmask[1][:, ::-1], data1=sg_[:, ::-1], initial=0.0, op0=ALU.mult, op1=ALU.add), reads=[rmask[1], sg_], writes=[w2_])
                kb.op("act", lambda e: e.activation(out=sq_[:, :], in_=psA[:, 0:TB], func=AF.Sqrt), reads=[psA], writes=[sq_])
                yield
                kb.op("act", lambda e: e.activation(out=E1_[:, :], in_=w2_[:, :], func=AF.Exp, scale=-DS), reads=[w2_], writes=[E1_])
                kb.op("dve", lambda e: e.tensor_scalar(out=sq_[:, :], in0=sq_[:, :], scalar1=1e-12, scalar2=None, op0=ALU.max), reads=[sq_], writes=[sq_])
                kb.op("dve", lambda e: e.reciprocal(out=sq_[:, :], in_=sq_[:, :]), reads=[sq_], writes=[sq_])
                kb.op("dve", lambda e: e.tensor_tensor(out=kap_[:, :], in0=kap_[:, :], in1=sq_[:, :], op=ALU.mult), reads=[kap_, sq_], writes=[kap_])
                kb.op("dve", lambda e: e.tensor_tensor(out=sg_[:, :], in0=w2_[:, :], in1=sg_[:, :], op=ALU.subtract), reads=[w2_, sg_], writes=[sg_])
                yield
                kb.op("act", lambda e: e.activation(out=sg_[:, :], in_=sg_[:, :], func=AF.Exp, scale=-DS), reads=[sg_], writes=[sg_])
                kb.op("act", lambda e: e.activation(out=w2_[:, :], in_=w2_[:, :], func=AF.Exp, scale=DS), reads=[w2_], writes=[w2_])
                kb.op("dve", lambda e: e.tensor_tensor(out=QC_[:, :, 64:128], in0=v3(r_), in1=v3(E1_), op=ALU.mult), reads=[r_, E1_], writes=[QC_])
                kb.op("dve", lambda e: e.tensor_tensor(out=a_[:, :], in0=a_[:, :], in1=kap_[:, :], op=ALU.mult), reads=[a_, kap_], writes=[a_])
                yield
                kb.op("dve", lambda e: e.tensor_tensor(out=QC_[:, :, 0:64], in0=v3(kap_), in1=v3(sg_), op=ALU.mult), reads=[kap_, sg_], writes=[QC_])
                kb.op("dve", lambda e: e.tensor_tensor(out=KC_[:, :, 0:64], in0=v3(w1_), in1=v3(w2_), op=ALU.mult), reads=[w1_, w2_], writes=[KC_])
                kb.op("dve", lambda e: e.tensor_tensor(out=KC_[:, :, 64:128], in0=v3(a_), in1=v3(w2_), op=ALU.mult), reads=[a_, w2_], writes=[KC_])
                yield

            def gen1b(cc, b, d, blk, i):
                v_ = W["vT"][i]
                QC_, KC_ = QC[i], KC[i]
                for h in range(2):
                    hs = slice(h * 64, (h + 1) * 64)
                    for c in range(ncb):
                        q = 2 * c + h
                        kb.op("pe", lambda e: e.matmul(psM[0][:, q, :], lhsT=QC_[hs, c, 0:64], rhs=KC_[hs, c, 64:128], start=True, stop=True), reads=[KC_, QC_], writes=[psM[0]], row=h * 64)
                        kb.op("pe", lambda e: e.matmul(psM[1][:, q, :], lhsT=KC_[hs, c, 64:128], rhs=QC_[hs, c, 0:64], start=True, stop=True), reads=[KC_, QC_], writes=[psM[1]], row=h * 64)
                yield
                BL = self.cfg.get("rw_bl", 1)
                p0_ = Pb[0] if BL <= 1 else P[0]
                kb.op("dve", lambda e: e.scalar_tensor_tensor(out=p0_[:, :, :], in0=psM[0][:, :, :], scalar=-1.0, in1=bq(maskN[d][:, :], 64), op0=ALU.mult, op1=ALU.mult), reads=[psM[0], maskN[d]], writes=[p0_])
                kb.op("dve", lambda e: e.scalar_tensor_tensor(out=PT[0][:, :, :], in0=psM[1][:, :, :], scalar=-1.0, in1=bq(maskA[d][0:64, 0:64], 64), op0=ALU.mult, op1=ALU.mult), reads=[psM[1], maskA[d]], writes=[PT[0]])
                kb.op("dve", lambda e: e.tensor_tensor(out=TT[0][:, :, :], in0=PT[0][:, :, :], in1=bq(identF[0:64, 0:64], 64), op=ALU.add), reads=[PT[0], identF], writes=[TT[0]])
                if BL <= 1:
                    kb.op("pool", lambda e: e.tensor_copy(out=PTb[0][:, :, :], in_=PT[0][:, :, :]), reads=[PT[0]], writes=[PTb[0]])
                    kb.op("pool", lambda e: e.tensor_copy(out=TTb[:, :, :], in_=TT[0][:, :, :]), reads=[TT[0]], writes=[TTb])
                yield
                extra = []
                for c in range(ncb):
                    def ex_a(c=c):
                        for h in range(2):
                            hs = slice(h * 64, (h + 1) * 64)
                            kb.op("pe", lambda e: e.matmul(psA3[:, h, :], lhsT=KC_[hs, c, :], rhs=QC_[hs, c, :], start=True, stop=True), reads=[KC_, QC_], writes=[psA], row=h * 64)
                        kb.op("dve", lambda e: e.tensor_tensor(out=AT4[i][:, c, :, :], in0=psA3, in1=maskA[d][:, :].unsqueeze(1).to_broadcast([128, 2, 128]), op=ALU.mult), reads=[psA, maskA[d]], writes=[AT4[i]])
                    extra.append(ex_a)

                def ex_v():
                    for h in range(2):
                        hs = slice(h * 64, (h + 1) * 64)
                        for c in range(ncb):
                            kb.op("pe", lambda e: e.transpose(psV[:, 2 * c + h, :], v_[hs, c * 64:(c + 1) * 64], identF[hs, hs]), reads=[v_, identF], writes=[psV], row=h * 64)
                    kb.op("act", lambda e: e.activation(out=UV4[i][0:64, :, :, :].rearrange("p c h e -> p (c h) e"), in_=psV[:, :, :], func=AF.Copy), reads=[psV], writes=[UV4[i]])
                extra.append(ex_v)

                def ex_k():
                    for c in range(ncb):
                        kb.op("pe", lambda e: e.transpose(psK[:, c, :], KC_[:, c, :], identB[:, :]), reads=[KC_, identB], writes=[psK])
                    kb.op("act", lambda e: e.activation(out=KTT4[i][:, :, :], in_=psK[:, :, :], func=AF.Copy), reads=[psK], writes=[KTT4[i]])
                extra.append(ex_k)
                for m in range(1, 6):
                    lo = m >= BL
                    if lo:
                        pc_, pn_ = Pb[(m - 1) % 2], Pb[m % 2]
                        tc_, tn_ = PTb[(m - 1) % 2], PTb[m % 2]
                    else:
                        pc_, pn_ = P[(m - 1) % 2], P[m % 2]
                        tc_, tn_ = PT[(m - 1) % 2], PT[m % 2]
                    if m + 1 == BL:
                        tn_ = PTb[m % 2]
                    for q in range(NQ):
                        kb.op("pe", lambda e: e.matmul(psM[0][:, q, :], lhsT=tc_[:, q, :], rhs=pc_[:, q, :], start=True, stop=True), reads=[tc_, pc_], writes=[psM[0]], row=0)
                    if m < 5:
                        for q in range(NQ):
                            kb.op("pe", lambda e: e.matmul(psM[1][:, q, :], lhsT=pc_[:, q, :], rhs=tc_[:, q, :], start=True, stop=True), reads=[tc_, pc_], writes=[psM[1]], row=0)
                    if extra:
                        extra.pop(0)()
                    yield
                    kb.op("act", lambda e: e.activation(out=pn_[:, :, :], in_=psM[0][:, :, :], func=AF.Copy), reads=[psM[0]], writes=[pn_])
                    if m < 5:
                        kb.op("dve", lambda e: e.tensor_copy(out=tn_[:, :, :], in_=psM[1][:, :, :]), reads=[psM[1]], writes=[tn_])
                    if m + 1 == BL:
                        pb_ = Pb[m % 2]
                        kb.op("pool", lambda e: e.tensor_copy(out=pb_[:, :, :], in_=pn_[:, :, :]), reads=[pn_], writes=[pb_])
                    yield
                    tt_c, tt_n = TT[(m - 1) % 2], TT[m % 2]
                    tt_r = TTb if lo else tt_c
                    for q in range(NQ):
                        kb.op("pe", lambda e: e.matmul(psN[:, q, :], lhsT=pn_[:, q, :], rhs=tt_r[:, q, :], start=True, stop=True), reads=[pn_, tt_r], writes=[psN], row=0)
                    if extra:
                        extra.pop(0)()
                    yield
                    if m < 5:
                        kb.op("dve", lambda e: e.tensor_tensor(out=tt_n[:, :, :], in0=psN[:, :, :], in1=tt_c[:, :, :], op=ALU.add), reads=[psN, tt_c], writes=[tt_n])
                        if m + 1 >= BL:
                            kb.op("pool", lambda e: e.tensor_copy(out=TTb[:, :, :], in_=tt_n[:, :, :]), reads=[tt_n], writes=[TTb])
                    else:
                        kb.op("dve", lambda e: e.tensor_tensor(out=TTp[i][:, :, 64:128], in0=psN[:, :, :], in1=tt_c[:, :, :], op=ALU.add), reads=[psN, tt_c], writes=[TTp[i]])
                    yield
                while extra:
                    extra.pop(0)()
                    yield

            def gen2(cc, b, d, blk, i):
                rows = slice(cc * 128, (cc + 1) * 128)
                cols = slice(b * TS + blk * TB, b * TS + (blk + 1) * TB)
                QC_, KC_, yt, E1_ = QC[i], KC[i], YT[i], W["E1"][i]
                AT_, UV_, KTT_, TTp_ = AT4[i], UV4[i], KTT4[i], TTp[i]
                for c in (range(ncb) if d == 0 else range(ncb - 1, -1, -1)):
                    p0 = c * 64
                    for h in range(2):
                        hs = slice(h * 64, (h + 1) * 64)
                        kb.op("pe", lambda e: e.matmul(psY[:, h, :], lhsT=QC_[:, c, 0:64], rhs=STp[h][:, :], start=True, stop=False), reads=[QC_, STp[h]], writes=[psY])
                        kb.op("pe", lambda e: e.matmul(psY[:, h, :], lhsT=AT_[0:64, c, h, 0:64], rhs=UV_[0:64, c, h, :], start=False, stop=True), reads=[AT_, UV_], writes=[psY])
                    yield
                    kb.op("dve", lambda e: e.tensor_scalar(out=nZ[:, :, :], in0=psY[:, :, :], scalar1=-1.0, scalar2=None, op0=ALU.mult), reads=[psY], writes=[nZ])
                    yield
                    for h in range(2):
                        kb.op("pe", lambda e: e.matmul(psU[:, h, :], lhsT=TTp_[:, 2 * c + h, :], rhs=nZ[:, h, :], start=True, stop=True), reads=[TTp_, nZ], writes=[psU])
                    yield
                    kb.op("act", lambda e: e.activation(out=UV_[64:128, c, :, :], in_=psU[64:128, :, :], func=AF.Copy), reads=[psU], writes=[UV_])
                    yield
                    for h in range(2):
                        hs = slice(h * 64, (h + 1) * 64)
                        kb.op("pe", lambda e: e.matmul(psY[:, h, :], lhsT=STp[h][:, :], rhs=QC_[:, c, 64:128], start=True, stop=False), reads=[QC_, STp[h]], writes=[psY])
                        kb.op("pe", lambda e: e.matmul(psY[:, h, :], lhsT=UV_[:, c, h, :], rhs=AT_[:, c, h, 64:128], start=False, stop=True), reads=[AT_, UV_], writes=[psY])
                    kb.op("pe", lambda e: e.matmul(psU[:, 0, :], lhsT=KTT_[:, c, :], rhs=UV_[:, c, 1, :], start=True, stop=True), reads=[KTT_, UV_], writes=[psU])
                    kb.op("pe", lambda e: e.matmul(psU[0:64, 0, :], lhsT=KTT_[:, c, 0:64], rhs=UV_[:, c, 0, :], start=True, stop=True), reads=[KTT_, UV_], writes=[psU])
                    yield
                    wcol = p0 + 63 if d == 0 else p0
                    kb.op("dve", lambda e: e.tensor_tensor(out=stmp[:, :], in0=psU[:, 0, :], in1=ST32[:, :], op=ALU.add), reads=[psU, ST32], writes=[stmp])
                    kb.op("act", lambda e: e.activation(out=yt[:, :, p0:p0 + 64], in_=psY[:, :, :], func=AF.Copy), reads=[psY], writes=[yt])
                    kb.op("dve", lambda e: e.tensor_scalar(out=ST32[:, :], in0=stmp[:, :], scalar1=E1_[:, wcol:wcol + 1], scalar2=None, op0=ALU.mult), reads=[stmp, E1_], writes=[ST32])
                    yield
                    kb.op("act", lambda e: e.activation(out=STp[0][0:64, :], in_=ST32[0:64, :], func=AF.Copy), reads=[ST32], writes=[STp[0]])
                    kb.op("dve", lambda e: e.tensor_copy(out=STp[1][64:128, :], in_=ST32[64:128, :]), reads=[ST32], writes=[STp[1]])
                    yield
                kb.dma("pool", S["YT%d" % d][rows, cols].rearrange("(h v) t -> v h t", h=2), yt[:, :, :], reads=[yt])

            def drain(g):
                for _ in g:
                    pass

            seqb = []
            for cc in range(8):
                for b in range(NB):
                    for d in range(2):
                        bl = list(range(nblk)) if d == 0 else list(range(cblk - 1, -1, -1)) + list(range(nblk - 1, cblk - 1, -1))
                        for n_, blk in enumerate(bl):
                            seqb.append((cc, b, d, blk, n_ == 0))
            drain(gen1a(*seqb[0][:4], 0))
            drain(gen1b(*seqb[0][:4], 0))
            if len(seqb) > 1:
                drain(gen1a(*seqb[1][:4], 1))
            for k, (cc, b, d, blk, first) in enumerate(seqb):
                if first:
                    kb.op("dve", lambda e: e.memset(ST32[:, :], 0.0), writes=[ST32])
                    for h_ in range(2):
                        kb.op("dve", lambda e: e.memset(STp[h_][:, :], 0.0), writes=[STp[h_]])
                gens = [gen2(cc, b, d, blk, k % 3)]
                if k + 1 < len(seqb):
                    gens.append(gen1b(*seqb[k + 1][:4], (k + 1) % 3))
                if k + 2 < len(seqb):
                    gens.append(gen1a(*seqb[k + 2][:4], (k + 2) % 3))
                while gens:
                    for g in list(gens):
                        try:
                            next(g)
                        except StopIteration:
                            gens.remove(g)
        self.st_rw_post(l)

    def st_rw_post(self, l):
        kb = self.kb
        S = self.scr
        with kb.stage():
            bones = kb.tile([128, 128], F32)
            kb.dma("sp", bones[:, :], self.inp["bones_f"], writes=[bones])
            prm = {}
            for nm in ("rw_k_aT", "rw_r_kT", "rw_ln_wT", "rw_ln_bT"):
                prm[nm] = kb.tile([128, 8], F32)
                kb.dma("sp", prm[nm][:, :], self.inp[nm][l], writes=[prm[nm]])
            omk2 = kb.tile([128, 8], F32)
            kb.op("dve", lambda e: e.tensor_scalar(out=omk2[:, :], in0=prm["rw_k_aT"][:, :], scalar1=-2.0, scalar2=2.0, op0=ALU.mult, op1=ALU.add), reads=[prm["rw_k_aT"]], writes=[omk2])
            lneps = kb.tile([128, 1], F32)
            kb.op("dve", lambda e: e.memset(lneps[:, :], RW_LN_EPS), writes=[lneps])
            psB = [kb.psum([128, 512], F32) for _ in range(3)]
            PB = 512
            mk2 = lambda dt=F32: [kb.tile([128, PB], dt) for _ in range(2)]
            r2, k2, v2, g2, a0, a1, y0, y1, tq, uq = mk2(), mk2(), mk2(), mk2(), mk2(), mk2(), mk2(), mk2(), mk2(), mk2()
            yo = mk2(BF16)
            it = 0
            for cc in range(8):
                rows = slice(cc * 128, (cc + 1) * 128)
                for t0 in range(0, self.T, PB):
                    tsz = min(PB, self.T - t0)
                    i = it % 2
                    it += 1
                    cs_ = slice(t0, t0 + tsz)
                    ld = [(r2[i], S["RS"][rows, cs_]), (k2[i], S["RS"][D + cc * 128:D + (cc + 1) * 128, cs_]), (v2[i], S["RS"][2 * D + cc * 128:2 * D + (cc + 1) * 128, cs_]),
                          (g2[i], S["GG"][rows, cs_]), (a0[i], S["AA0"][rows, cs_]), (a1[i], S["AA1"][rows, cs_]), (y0[i], S["YT0"][rows, cs_]), (y1[i], S["YT1"][rows, cs_])]
                    for tl, src in ld:
                        kb.dma("sp", tl[:, 0:tsz], src, writes=[tl])
                    r_, k_, v_, g_, a0_, a1_, y_, y1_, tq_, uq_, yo_ = r2[i], k2[i], v2[i], g2[i], a0[i], a1[i], y0[i], y1[i], tq[i], uq[i], yo[i]
                    sl = slice(0, tsz)
                    p1, p2, p3 = psB
                    kb.op("dve", lambda e: e.tensor_tensor(out=a0_[:, sl], in0=a0_[:, sl], in1=a1_[:, sl], op=ALU.add), reads=[a0_, a1_], writes=[a0_])
                    kb.op("dve", lambda e: e.tensor_scalar(out=a0_[:, sl], in0=a0_[:, sl], scalar1=prm["rw_k_aT"][:, cc:cc + 1], scalar2=omk2[:, cc:cc + 1], op0=ALU.mult, op1=ALU.add), reads=[a0_, prm["rw_k_aT"], omk2], writes=[a0_])
                    kb.op("dve", lambda e: e.tensor_tensor(out=a0_[:, sl], in0=a0_[:, sl], in1=k_[:, sl], op=ALU.mult), reads=[a0_, k_], writes=[a0_])
                    kb.op("dve", lambda e: e.scalar_tensor_tensor(out=a0_[:, sl], in0=a0_[:, sl], scalar=prm["rw_r_kT"][:, cc:cc + 1], in1=r_[:, sl], op0=ALU.mult, op1=ALU.mult), reads=[a0_, prm["rw_r_kT"], r_], writes=[a0_])
                    kb.op("pe", lambda e: e.matmul(p1[:, sl], lhsT=bones[:, :], rhs=a0_[:, sl], start=True, stop=True), reads=[bones, a0_], writes=[p1])
                    kb.op("dve", lambda e: e.tensor_tensor(out=y_[:, sl], in0=y_[:, sl], in1=y1_[:, sl], op=ALU.add), reads=[y_, y1_], writes=[y_])
                    kb.op("pe", lambda e: e.matmul(p2[:, sl], lhsT=bones[:, :], rhs=y_[:, sl], start=True, stop=True), reads=[bones, y_], writes=[p2])
                    kb.op("dve", lambda e: e.tensor_tensor(out=a1_[:, sl], in0=p1[:, sl], in1=v_[:, sl], op=ALU.mult), reads=[p1, v_], writes=[a1_])
                    kb.op("dve", lambda e: e.scalar_tensor_tensor(out=tq_[:, sl], in0=p2[:, sl], scalar=-1.0 / 64, in1=y_[:, sl], op0=ALU.mult, op1=ALU.add), reads=[p2, y_], writes=[tq_])
                    kb.op("act", lambda e: e.activation(out=uq_[:, sl], in_=tq_[:, sl], func=AF.Square), reads=[tq_], writes=[uq_])
                    kb.op("pe", lambda e: e.matmul(p3[:, sl], lhsT=bones[:, :], rhs=uq_[:, sl], start=True, stop=True), reads=[bones, uq_], writes=[p3])
                    kb.op("act", lambda e: e.activation(out=uq_[:, sl], in_=p3[:, sl], func=AF.Sqrt, scale=1.0 / 64, bias=lneps[:, 0:1]), reads=[p3, lneps], writes=[uq_])
                    kb.op("dve", lambda e: e.reciprocal(out=uq_[:, sl], in_=uq_[:, sl]), reads=[uq_], writes=[uq_])
                    kb.op("dve", lambda e: e.tensor_tensor(out=tq_[:, sl], in0=tq_[:, sl], in1=uq_[:, sl], op=ALU.mult), reads=[tq_, uq_], writes=[tq_])
                    kb.op("act", lambda e: e.activation(out=tq_[:, sl], in_=tq_[:, sl], func=AF.Identity, scale=prm["rw_ln_wT"][:, cc:cc + 1], bias=prm["rw_ln_bT"][:, cc:cc + 1]), reads=[tq_, prm["rw_ln_wT"], prm["rw_ln_bT"]], writes=[tq_])
                    kb.op("dve", lambda e: e.tensor_tensor(out=tq_[:, sl], in0=tq_[:, sl], in1=a1_[:, sl], op=ALU.add), reads=[tq_, a1_], writes=[tq_])
                    kb.op("dve", lambda e: e.tensor_tensor(out=yo_[:, sl], in0=tq_[:, sl], in1=g_[:, sl], op=ALU.mult), reads=[tq_, g_], writes=[yo_])
                    kb.dma("pool", S["YRW"][rows, cs_], yo_[:, sl], reads=[yo_])

    def build(self):
        nc, kb = self.nc, self.kb
        NB, CTX, LAT, L, T, TS = self.NB, self.CTX, self.LAT, self.L, self.T, self.TS
        self.din("x", [NB, LAT, D])
        self.din("ctx", [NB, CTX, D])
        self.din("condT", [128, 8, 3])
        self.din("w_mod", [L, D, 6 * D])
        self.din("b_mod", [L, 6 * D])
        self.din("norm1_w", [L, D])
        self.din("norm2_w", [L, D])
        self.din("w_in", [L, D, INW])
        self.din("ml_ig_b", [L, 2, 4])
        self.din("ml_fg_b", [L, 2, 4])
        self.din("ml_norm_w", [L, D])
        self.din("lru_conv_wT", [L, 128, 8, 4])
        self.din("lru_conv_bT", [L, 128, 8])
        self.din("lru_gr_w", [L, 2, 16, 64, 64])
        self.din("lru_gr_bT", [L, 128, 2, 8])
        self.din("lru_gi_w", [L, 2, 16, 64, 64])
        self.din("lru_gi_bT", [L, 128, 2, 8])
        self.din("lru_lambdaT", [L, 128, 2, 8])
        self.din("rw_muT", [L, 128, 27])
        self.din("rw_decay0T", [L, 128, 2, 8])
        self.din("rw_iclr0T", [L, 128, 2, 8])
        self.din("rw_decay_up", [L, 2, 64, D])
        self.din("rw_iclr_up", [L, 2, 64, D])
        self.din("rw_gate_up", [L, 128, D])
        for nm in ("rw_k_kT", "rw_k_aT", "rw_r_kT", "rw_ln_wT", "rw_ln_bT"):
            self.din(nm, [L, 128, 8])
        self.din("bones_f", [128, 128])
        self.din("maskA", [2, 128, 128])
        self.din("maskN", [2, 64, 64])
        self.din("rmask", [2, 128, TS])
        self.din("out_ml", [L, D, D])
        self.din("out_lru", [L, D, D])
        self.din("out_rw", [L, D, D])
        self.din("w_out", [L, D, D])
        self.din("w_ffn_in", [L, D, 2 * FFN])
        self.din("w_ffn_out", [L, FFN, D])
        self.din("final_norm_w", [D])
        self.din("ident_bf", [128, 128], BF16)
        self.din("ident_f", [128, 128])
        self.din("ones_f", [128, 128])
        self.din("tri", [2, 128, 128])
        self.din("negm", [2, 128, 128])
        self.out_ap = self.nc.dram_tensor("out", [NB, LAT, D], F32, kind="ExternalOutput").ap()
        self.dscr("XR", [NB, TS, D], F32)
        self.dscr("MOD", [L, 3, 6 * D], F32)
        self.dscr("HT", [D, T], BF16)
        self.dscr("QT", [D, T], BF16)
        self.dscr("KT", [D, T], BF16)
        self.dscr("KTM", [T, D], BF16)
        self.dscr("VTM", [T, D], BF16)
        self.dscr("OG", [T, D], BF16)
        self.dscr("IFG", [T, 16], F32)
        self.dscr("LX", [D, T], F32)
        self.dscr("LG", [D, T], BF16)
        self.dscr("RW", [RWSEG, T], F32)
        self.dscr("MG", [3 * D, T], BF16)
        self.dscr("HML0", [T, D], F32)
        self.dscr("HML1", [T, D], F32)
        self.dscr("YML", [D, T], BF16)
        self.dscr("YLRU", [D, T], BF16)
        self.dscr("YRW", [D, T], BF16)
        self.dscr("YACC", [D, T], F32)
        self.dscr("YM", [D, T], BF16)
        self.dscr("RS", [3 * D, T], F32)
        self.dscr("TD", [128, T], BF16)
        self.dscr("AD", [128, T], BF16)
        self.dscr("GD", [128, T], BF16)
        for d in range(2):
            self.dscr("SG%d" % d, [D, T], F32)
            self.dscr("AA%d" % d, [D, T], F32)
            self.dscr("YT%d" % d, [D, T], F32)
        self.dscr("GG", [D, T], F32)
        self.dscr("AG", [FFN, T], BF16)
        self.dscr("AT", [FFN, T], BF16)
        self.eps_t = kb.tile([128, 1], F32, "eps")
        self.one_t = kb.tile([128, 1], F32, "one")
        kb.op("dve", lambda e: e.memset(self.eps_t[:, :], EPS), writes=[self.eps_t])
        kb.op("dve", lambda e: e.memset(self.one_t[:, :], 1.0), writes=[self.one_t])
        upto = self.cfg.get("upto", None)
        seq = [("init", lambda: self.st_init()), ("adaln", lambda: self.st_adaln())]
        for l in range(L):
            seq += [("norm1", lambda l=l: self.st_norm(l, 1)), ("win", lambda l=l: self.st_win(l)),
                    ("mlstm", lambda l=l: self.st_mlstm(l)), ("mlpost", lambda l=l: self.st_mlstm_post(l)),
                    ("lru", lambda l=l: self.st_lru(l)), ("rwkv", lambda l=l: self.st_rwkv(l)),
                    ("merge", lambda l=l: self.st_merge(l)), ("wout", lambda l=l: self.st_wout(l)),
                    ("ffn", lambda l=l: self.st_ffn(l))]
        seq += [("final", lambda: self.st_norm(0, 1, final=True))]
        skip = self.cfg.get("skip", ())
        for name, fn in seq:
            if name not in skip:
                fn()
            if upto is not None and name == upto:
                break
        kb.finish()


def host_consts():
    c = {}
    c["ident_bf"] = np.eye(128, dtype=np.float32).astype(ml_dtypes.bfloat16)
    c["ident_f"] = np.eye(128, dtype=np.float32)
    c["ones_f"] = np.ones((128, 128), np.float32)
    s = np.arange(128)[:, None]
    t = np.arange(128)[None, :]
    tri = np.stack([(s <= t), (s >= t)]).astype(np.float32)
    c["tri"] = tri
    c["negm"] = ((1.0 - tri) * -30000.0).astype(np.float32)
    bo = np.zeros((128, 128), np.float32)
    bo[:64, :64] = 1.0
    bo[64:, 64:] = 1.0
    c["bones_f"] = bo
    j = np.arange(64)[:, None]
    t = np.arange(64)[None, :]
    mA = []
    mN = []
    for d in range(2):
        strict = (j < t) if d == 0 else (j > t)
        incl = (j <= t) if d == 0 else (j >= t)
        blk = np.concatenate([strict, incl], 1).astype(np.float32)
        mA.append(np.concatenate([blk, blk], 0))
        mN.append(strict.T.astype(np.float32))
    c["maskA"] = np.stack(mA)
    c["maskN"] = np.stack(mN)
    return c


def chanT(a):
    a = np.asarray(a, np.float32)
    lead = a.shape[:-1]
    a = a.reshape(lead + (8, 128))
    return np.ascontiguousarray(np.moveaxis(a, -1, 0))


def make_in_maps(inputs, cfg, ncores):
    NB, L = cfg["NB"], cfg["DEPTH"]
    consts = host_consts()
    f = lambda k: np.ascontiguousarray(np.asarray(inputs[k], np.float32)[:L])
    shared = {
        "w_mod": f("w_mod"), "b_mod": f("b_mod"), "norm1_w": f("norm1_w"), "norm2_w": f("norm2_w"),
        "w_in": f("w_in"), "ml_ig_b": f("ml_ig_b"), "ml_fg_b": f("ml_fg_b"), "ml_norm_w": f("ml_norm_w"),
        "lru_gr_w": f("lru_gr_w"), "lru_gi_w": f("lru_gi_w"),
        "out_ml": f("out_ml"), "out_lru": f("out_lru"), "out_rw": f("out_rw"), "w_out": f("w_out"),
        "w_ffn_in": f("w_ffn_in"), "w_ffn_out": f("w_ffn_out"),
        "final_norm_w": np.asarray(inputs["final_norm_w"], np.float32),
    }
    cw = np.asarray(inputs["lru_conv_w"], np.float32)[:L]
    shared["lru_conv_wT"] = np.ascontiguousarray(np.stack([np.moveaxis(chanT(cw[l]), 1, 2) for l in range(L)]))
    shared["lru_conv_bT"] = np.ascontiguousarray(np.stack([chanT(np.asarray(inputs["lru_conv_b"], np.float32)[l]) for l in range(L)]))
    for k in ("lru_gr_b", "lru_gi_b", "lru_lambda"):
        a = np.asarray(inputs[k], np.float32)[:L]
        shared[k + "T"] = np.ascontiguousarray(np.stack([chanT(a[l]) for l in range(L)]))
    TS = cfg["CTX"] + cfg["LAT"]
    tt = np.arange(TS)
    rm = np.stack([(tt % 64 != 0), (tt % 64 != 63)]).astype(np.float32)
    shared["rmask"] = np.ascontiguousarray(np.broadcast_to(rm[:, None, :], (2, 128, TS)))
    mu = np.asarray(inputs["rw_mu"], np.float32)[:L]
    shared["rw_muT"] = np.ascontiguousarray(mu.reshape(L, 27, 128).transpose(0, 2, 1))
    for k in ("rw_decay0", "rw_iclr0"):
        a = np.asarray(inputs[k], np.float32)[:L]
        shared[k + "T"] = np.ascontiguousarray(np.stack([chanT(a[l]) for l in range(L)]))
    for k in ("rw_k_k", "rw_k_a", "rw_r_k", "rw_ln_w", "rw_ln_b"):
        a = np.asarray(inputs[k], np.float32)[:L]
        shared[k + "T"] = np.ascontiguousarray(np.stack([chanT(a[l]) for l in range(L)]))
    for k in ("rw_decay_up", "rw_iclr_up", "rw_gate_up"):
        shared[k] = f(k)
    shared.update(consts)
    x = np.asarray(inputs["x"], np.float32)
    ctx = np.asarray(inputs["ctx"], np.float32)
    c = np.asarray(inputs["c"], np.float32)
    cc = np.asarray(inputs["c_ctx"], np.float32)
    maps = []
    for i in range(ncores):
        m = dict(shared)
        m["x"] = np.ascontiguousarray(x[i * NB:(i + 1) * NB])
        m["ctx"] = np.ascontiguousarray(ctx[i * NB:(i + 1) * NB])
        cond = np.concatenate([c[i * NB:(i + 1) * NB], cc[None, :]], 0)
        m["condT"] = np.ascontiguousarray(cond.T.reshape(8, 128, 3).transpose(1, 0, 2))
        maps.append(m)
    return maps


def kernel(**inputs):
    cfg = {"NB": 2, "CTX": 256, "LAT": 4096, "DEPTH": 4}
    nc = bass.Bass("TRN2", target_bir_lowering=False)
    p = Prog(nc, cfg)
    p.build()
    maps = make_in_maps(inputs, cfg, NCORES)
    res = run_bass_kernel_spmd(nc, maps, core_ids=list(range(NCORES)))
    return np.concatenate([np.asarray(r["out"], np.float32) for r in res.results], axis=0)
```
